# Optimizing a Trainium2 kernel written in Bass

```python
import math
import jax, jax.numpy as jnp
from jax import lax
import numpy as np

D_MODEL = 2048
BATCH = 1
SEQ = 8192
DEPTH = 1

CHUNK = 64
N_META = 16
Q_BLOCK = 128
N_BRANCH = 2
BRANCH_WIDTH = D_MODEL // 2
HD_A = 128
H_A = BRANCH_WIDTH // HD_A
KV_A = H_A // 2
ROT_A = HD_A // 4
H_IDX = D_MODEL // 128
D_IDX = 64
ROT_IDX = D_IDX // 4
TOPK_MAX = 256
DK_B = 128
H_B = BRANCH_WIDTH // DK_B
DV_B = BRANCH_WIDTH // H_B
ROPE_THETA = 500000.0
EPS = 1e-6
SPLIT_SIZES = (H_A * HD_A, KV_A * HD_A, KV_A * HD_A, BRANCH_WIDTH,
               H_IDX * D_IDX, D_IDX, H_IDX,
               H_B * DK_B, H_B * DK_B, H_B * DV_B, BRANCH_WIDTH,
               N_BRANCH * D_MODEL)
N_IN = sum(SPLIT_SIZES)

kernel_name = "hybrid_dsa_hgrn2_gated_merge"


def rms_norm(x, gain):
    xf = x.astype(jnp.float32)
    y = xf * lax.rsqrt(jnp.mean(xf * xf, axis=-1, keepdims=True) + EPS)
    return (y * gain.astype(jnp.float32)).astype(x.dtype)


def rope_tables(pos, rot_dim):
    inv = jnp.power(ROPE_THETA, -jnp.arange(0, rot_dim, 2, dtype=jnp.float32) / rot_dim)
    ang = pos.astype(jnp.float32)[:, None] * inv[None, :]
    return jnp.cos(ang), jnp.sin(ang)


def apply_partial_rope(x, cos, sin):
    half = cos.shape[-1]
    xr = x[..., :2 * half].astype(jnp.float32)
    x1, x2 = xr[..., :half], xr[..., half:]
    c, s = cos[None, :, None, :], sin[None, :, None, :]
    rot = jnp.concatenate([x1 * c - x2 * s, x2 * c + x1 * s], axis=-1)
    return jnp.concatenate([rot.astype(x.dtype), x[..., 2 * half:]], axis=-1)


def dsa_attention(q, k, v, qi, ki, wi, chunk_ids, topk):
    B, T, H, d = q.shape
    G = k.shape[2]
    R = H // G
    nb = T // Q_BLOCK
    scale = d ** -0.5
    ki32 = ki.astype(jnp.float32)

    def block(i):
        s0 = i * Q_BLOCK
        qb = lax.dynamic_slice_in_dim(q, s0, Q_BLOCK, axis=1)
        qib = lax.dynamic_slice_in_dim(qi, s0, Q_BLOCK, axis=1).astype(jnp.float32)
        wib = lax.dynamic_slice_in_dim(wi, s0, Q_BLOCK, axis=1).astype(jnp.float32)
        cq = lax.dynamic_slice_in_dim(chunk_ids, s0, Q_BLOCK)
        admissible = chunk_ids[None, :] <= cq[:, None]
        idx_logits = jnp.einsum('bqhd,bsd->bqhs', qib, ki32)
        score = jnp.einsum('bqhs,bqh->bqs', jax.nn.relu(idx_logits), wib)
        score = jnp.where(admissible[None], score, -jnp.inf)
        _, sel = lax.top_k(score, topk)
        valid = chunk_ids[sel] <= cq[None, :, None]
        k_sel = jax.vmap(lambda kk, ii: kk[ii])(k, sel)
        v_sel = jax.vmap(lambda vv, ii: vv[ii])(v, sel)
        qg = qb.reshape(B, Q_BLOCK, G, R, d)
        logits = jnp.einsum('bqgrd,bqkgd->bqgrk', qg, k_sel).astype(jnp.float32) * scale
        logits = jnp.where(valid[:, :, None, None, :], logits, -jnp.inf)
        p = jax.nn.softmax(logits, axis=-1).astype(v.dtype)
        o = jnp.einsum('bqgrk,bqkgd->bqgrd', p, v_sel)
        return o.reshape(B, Q_BLOCK, H, d)

    o = lax.map(block, jnp.arange(nb))
    return o.transpose(1, 0, 2, 3, 4).reshape(B, T, H, d)


def hgrn2_chunkwise(q, k, v, log_f):
    B, T, H, dk = q.shape
    dv = v.shape[-1]
    n = T // CHUNK

    def to_chunks(a):
        return a.reshape(B, n, CHUNK, H, a.shape[-1]).transpose(1, 0, 2, 3, 4)

    causal = jnp.tril(jnp.ones((CHUNK, CHUNK), dtype=bool))[None, :, :, None, None]

    def step(S, inp):
        q_, k_, v_, g_ = inp
        b = jnp.cumsum(g_, axis=1)
        diff = b[:, :, None] - b[:, None, :]
        decay = jnp.exp(jnp.where(causal, diff, -jnp.inf))
        attn = jnp.einsum('bthd,bshd,btshd->bhts', q_, k_, decay)
        o_intra = jnp.einsum('bhts,bshv->bthv', attn, v_)
        o_inter = jnp.einsum('bthd,bhdv->bthv', q_ * jnp.exp(b), S)
        b_last = b[:, -1]
        k_dec = k_ * jnp.exp(b_last[:, None] - b)
        S_new = jnp.exp(b_last)[..., None] * S + jnp.einsum('bshd,bshv->bhdv', k_dec, v_)
        return S_new, o_intra + o_inter

    S0 = jnp.zeros((B, H, dk, dv), jnp.float32)
    _, o = lax.scan(step, S0, (to_chunks(q), to_chunks(k), to_chunks(v), to_chunks(log_f)))
    return o.transpose(1, 0, 2, 3, 4).reshape(B, T, H, dv)


def hybrid_layer(x, lb, norm_g, w_in, q_norm_g, k_norm_g, idx_k_norm_g, hgrn_out_norm_g,
                 w_branch, w_out, rope_a, rope_i, chunk_ids, topk):
    B, T, D = x.shape
    h = rms_norm(x, norm_g)
    proj = h @ w_in
    cuts = np.cumsum(SPLIT_SIZES)[:-1].tolist()
    (q_a, k_a, v_a, z_a, q_i, k_i, w_i,
     q_b, f_b, i_b, z_b, gates) = jnp.split(proj, cuts, axis=-1)

    q = apply_partial_rope(rms_norm(q_a.reshape(B, T, H_A, HD_A), q_norm_g), *rope_a)
    k = apply_partial_rope(rms_norm(k_a.reshape(B, T, KV_A, HD_A), k_norm_g), *rope_a)
    v = v_a.reshape(B, T, KV_A, HD_A)
    qi = apply_partial_rope(q_i.reshape(B, T, H_IDX, D_IDX), *rope_i)
    ki = apply_partial_rope(rms_norm(k_i, idx_k_norm_g)[:, :, None, :], *rope_i)[:, :, 0, :]
    wi = w_i * (H_IDX ** -0.5 * D_IDX ** -0.5)
    o_a = dsa_attention(q, k, v, qi, ki, wi, chunk_ids, topk)
    y_a = o_a.reshape(B, T, BRANCH_WIDTH) * jax.nn.silu(z_a)

    qb = jax.nn.silu(q_b.astype(jnp.float32)).reshape(B, T, H_B, DK_B)
    lbh = lb.reshape(H_B, DK_B)
    f = lbh + (1.0 - lbh) * jax.nn.sigmoid(f_b.astype(jnp.float32).reshape(B, T, H_B, DK_B))
    vb = i_b.astype(jnp.float32).reshape(B, T, H_B, DV_B)
    o_b = hgrn2_chunkwise(qb, 1.0 - f, vb, jnp.log(f)).astype(x.dtype)
    o_b = rms_norm(o_b, hgrn_out_norm_g)
    y_b = o_b.reshape(B, T, BRANCH_WIDTH) * jax.nn.silu(z_b)

    y = jnp.stack([y_a, y_b], axis=2)
    p = jnp.einsum('btnc,ncd->btnd', y, w_branch)
    g = jax.nn.sigmoid(gates.reshape(B, T, N_BRANCH, D))
    merged = jnp.sum(g * p, axis=2)
    return x + merged @ w_out


def setup_inputs(seed: int = 0) -> dict:
    key = jax.random.key(seed)
    ks = jax.random.split(key, 12)
    f32 = jnp.float32
    D = D_MODEL
    return {
        "x": jax.random.normal(ks[0], (BATCH, SEQ, D), f32),
        "meta_tokens": jax.random.normal(ks[1], (N_META, D), f32),
        "hgrn_lb_logits": 0.1 * jax.random.normal(ks[2], (DEPTH + 1, H_B * DK_B), f32),
        "norm_g": 1.0 + 0.02 * jax.random.normal(ks[3], (DEPTH, D), f32),
        "w_in": jax.random.normal(ks[4], (DEPTH, D, N_IN), f32) * D ** -0.5,
        "q_norm_g": 1.0 + 0.02 * jax.random.normal(ks[5], (DEPTH, HD_A), f32),
        "k_norm_g": 1.0 + 0.02 * jax.random.normal(ks[6], (DEPTH, HD_A), f32),
        "idx_k_norm_g": 1.0 + 0.02 * jax.random.normal(ks[7], (DEPTH, D_IDX), f32),
        "hgrn_out_norm_g": 1.0 + 0.02 * jax.random.normal(ks[8], (DEPTH, DV_B), f32),
        "w_branch": jax.random.normal(ks[9], (DEPTH, N_BRANCH, BRANCH_WIDTH, D), f32) * BRANCH_WIDTH ** -0.5,
        "w_out": jax.random.normal(ks[10], (DEPTH, D, D), f32) * D ** -0.5,
    }


def reference(x, meta_tokens, hgrn_lb_logits, norm_g, w_in, q_norm_g, k_norm_g, idx_k_norm_g,
              hgrn_out_norm_g, w_branch, w_out):
    B, S, D = x.shape
    T = S + N_META
    Tp = -(-T // Q_BLOCK) * Q_BLOCK
    meta = jnp.broadcast_to(meta_tokens.astype(x.dtype)[None], (B, N_META, D))
    h = jnp.concatenate([meta, x, jnp.zeros((B, Tp - T, D), x.dtype)], axis=1)

    pos = jnp.arange(Tp, dtype=jnp.int32)
    chunk_ids = jnp.where(pos < N_META, 0, 1 + (pos - N_META) // CHUNK).astype(jnp.int32)
    rope_a = rope_tables(pos, ROT_A)
    rope_i = rope_tables(pos, ROT_IDX)
    topk = min(TOPK_MAX, S // 4)

    lb_all = jnp.cumsum(jax.nn.softmax(hgrn_lb_logits.astype(jnp.float32), axis=0), axis=0)

    for l in range(DEPTH):
        h = hybrid_layer(h, lb_all[l], norm_g[l], w_in[l], q_norm_g[l], k_norm_g[l],
                         idx_k_norm_g[l], hgrn_out_norm_g[l], w_branch[l], w_out[l],
                         rope_a, rope_i, chunk_ids, topk)
    return h[:, N_META:N_META + S]
```

```python
import contextlib
import numpy as np
import concourse.bass as bass
import concourse.mybir as mybir
from concourse.bass_utils import run_bass_kernel_spmd

F32 = mybir.dt.float32
BF16 = mybir.dt.bfloat16
AF = mybir.ActivationFunctionType
ALU = mybir.AluOpType
AX = mybir.AxisListType

D = 2048
SEQ = 8192
NMETA = 16
TP = 8320
NT = 65
NOWN = 1152
NIN = 12368
EPS = 1e-6
NEG = -30000.0

C_QA, C_KA, C_VA, C_ZA, C_QI, C_KI, C_WI, C_QB, C_FB, C_IB, C_ZB, C_G = (
    0, 1024, 1536, 2048, 3072, 4096, 4160, 4176, 5200, 6224, 7248, 8272)
PA_COLS = 4160
PA_BLOCKS = [(C_KA, 512, 0), (C_VA, 512, 512), (C_KI, 64, 1024)] + \
            [(C_QB + i * 512, 512, 1088 + i * 512) for i in range(6)]
PO_COLS = 8208
PO_BLOCKS = [(C_QA + i * 512, 512, i * 512) for i in range(2)] + \
            [(C_ZA + i * 512, 512, 1024 + i * 512) for i in range(4)] + \
            [(C_WI, 16, 3072)] + \
            [(C_ZB + i * 512, 512, 3088 + i * 512) for i in range(10)]


class Buf:
    __slots__ = ("name", "w", "r")

    def __init__(self, name):
        self.name = name
        self.w = None
        self.r = []


class Eng:
    def __init__(self, name, h, sem):
        self.name, self.h, self.sem = name, h, sem
        self.count = 0
        self.seen = {}

    def wait(self, ev):
        if ev is None:
            return
        sem, val = ev
        if self.seen.get(id(sem), 0) >= val:
            return
        if self.name == "pe" and sem is self.sem:
            return
        self.h.wait_ge(sem, val)
        self.seen[id(sem)] = val


class Sched:
    NDMA = 8

    def __init__(self, nc, sems):
        self.nc = nc
        self.free_sems = list(sems)
        self.E = {}
        for name, h in (("pe", nc.tensor), ("act", nc.scalar), ("dve", nc.vector),
                        ("pool", nc.gpsimd), ("sp", nc.sync)):
            self.E[name] = Eng(name, h, self.free_sems.pop())
        self.dq = {}
        for q in ("sp", "pool", "act"):
            self.dq[q] = {"sems": [self.free_sems.pop() for _ in range(self.NDMA)],
                          "n": [0] * self.NDMA, "i": 0}

    def _deps(self, eng, reads, writes):
        for b in reads:
            eng.wait(b.w)
        for b in writes:
            eng.wait(b.w)
            for ev in b.r:
                eng.wait(ev)

    def _commit(self, ev, reads, writes):
        for b in reads:
            b.r = [e for e in b.r if e[0] is not ev[0]] + [ev]
        for b in writes:
            b.w = ev
            b.r = []

    def op(self, ename, fn, reads=(), writes=()):
        eng = self.E[ename]
        self._deps(eng, reads, writes)
        ins = fn(eng.h)
        eng.count += 1
        ins.then_inc(eng.sem, 1)
        ev = (eng.sem, eng.count)
        self._commit(ev, reads, writes)
        return ev

    def dma(self, q, out, in_, reads=(), writes=(), **kw):
        eng = self.E[q]
        d = self.dq[q]
        k = d["i"] % self.NDMA
        d["i"] += 1
        sem = d["sems"][k]
        if d["n"][k] > 0:
            eng.wait((sem, 16 * d["n"][k]))
        self._deps(eng, reads, writes)
        ins = eng.h.dma_start(out=out, in_=in_, **kw)
        d["n"][k] += 1
        ins.then_inc(sem, 16)
        ev = (sem, 16 * d["n"][k])
        self._commit(ev, reads, writes)
        return ev

    def all_events(self):
        evs = []
        for e in self.E.values():
            if e.count:
                evs.append((e.sem, e.count))
        for d in self.dq.values():
            for s, n in zip(d["sems"], d["n"]):
                if n:
                    evs.append((s, 16 * n))
        return evs

    def barrier(self, engines=None):
        evs = self.all_events()
        for name, e in self.E.items():
            if engines is not None and name not in engines:
                continue
            for ev in evs:
                e.wait(ev)


class Ctx:
    def __init__(self, nc, S):
        self.nc, self.S = nc, S

    n = 0

    def sb(self, es, name, shape, dt):
        Ctx.n += 1
        name = f"t{Ctx.n}_{name}"
        t = es.enter_context(self.nc.sbuf_tensor(name, list(shape), dt))
        return t, Buf(name)

    def ps(self, es, name, shape, dt):
        Ctx.n += 1
        name = f"t{Ctx.n}_{name}"
        n = int(np.prod(shape[1:]))
        be = 2048 // (4 if dt == F32 else 2)
        nb = -(-n // be)
        flat = es.enter_context(self.nc.psum_tensor(name, [shape[0], nb * be], dt))
        v = flat[:, 0:n]
        if len(shape) == 3:
            v = v.rearrange("p (a b) -> p a b", a=shape[1])
        elif len(shape) == 4:
            v = v.rearrange("p (a b c) -> p a b c", a=shape[1], b=shape[2])
        return v, Buf(name)


def build_nc(stages=("norm", "gemm", "post", "hgrn", "attn", "merge"), debug_out=()):
    nc = bass.Bass("TRN2", target_bir_lowering=False)
    dt_in = lambda n, s, d=F32: nc.dram_tensor(n, list(s), d, kind="ExternalInput")
    h_all = dt_in("h_all", [TP, D])
    x_own = dt_in("x_own", [1024, D])
    sel_in = dt_in("sel", [128, 16])
    w_in = dt_in("w_in", [D, NIN])
    norm_g = dt_in("norm_g", [D])
    out_d = nc.dram_tensor("out", [1024, D], F32, kind="ExternalOutput")
    gq_in = dt_in("gq_bc", [128, 128]); gk_in = dt_in("gk_bc", [128, 128]); gi_in = dt_in("gi_bc", [128, 64])
    go_in = dt_in("go_bc", [128, 128])
    ropeK_in = dt_in("ropeK", [TP, 128]); ropeKI_in = dt_in("ropeKI", [TP, 16])
    ropeQ_in = dt_in("ropeQ", [NOWN, 256]); ropeQI_in = dt_in("ropeQI", [NOWN, 256])
    U_in = dt_in("U", [128, 128]); W_in = dt_in("Wm", [128, 128]); chi_in = dt_in("chi", [128, 2])
    lb0_in = dt_in("lb0", [128, 1024]); lb1_in = dt_in("lb1", [128, 1024])
    AM_in = dt_in("AM", [128, 1024])
    wbr_in = dt_in("w_branch", [2, 1024, D]); wout_in = dt_in("w_out", [D, D])

    hT_all = nc.dram_tensor("hT_all", [NT, 128, 2048], BF16)
    hT_own = nc.dram_tensor("hT_own", [9, 128, 2048], BF16)
    kind_dbg = lambda n: "ExternalOutput" if n in debug_out else "Internal"
    P_all = nc.dram_tensor("P_all", [TP, PA_COLS], F32, kind=kind_dbg("P_all"))
    P_own = nc.dram_tensor("P_own", [NOWN, PO_COLS], F32, kind=kind_dbg("P_own"))

    KT_d = nc.dram_tensor("KT_d", [128, 4, TP], BF16)
    V_d = nc.dram_tensor("V_d", [TP, 512], BF16)
    kiT_d = nc.dram_tensor("kiT_d", [128, TP], BF16)
    QT_d = nc.dram_tensor("QT_d", [128, 8, NOWN], BF16)
    zaT_d = nc.dram_tensor("zaT_d", [128, 8, NOWN], BF16)
    zbT_d = nc.dram_tensor("zbT_d", [128, 8, NOWN], BF16)
    qiT_d = nc.dram_tensor("qiT_d", [128, 8, NOWN], BF16)
    w_d = nc.dram_tensor("w_d", [NOWN, 16], F32)
    ybT_d = nc.dram_tensor("ybT_d", [128, 8, NOWN], BF16, kind=kind_dbg("ybT_d"))
    yaT_d = nc.dram_tensor("yaT_d", [128, 8, 1024], BF16, kind=kind_dbg("yaT_d"))
    B_KT, B_V, B_kiT, B_QT, B_zaT, B_zbT, B_qiT, B_w, B_ybT, B_yaT = [Buf(n) for n in
        "KT V kiT QT zaT zbT qiT w ybT yaT".split()]

    with contextlib.ExitStack() as top:
        sems = [top.enter_context(nc.semaphore(f"s{i}")) for i in range(5 + 3 * Sched.NDMA + 2)]
        S = Sched(nc, sems)
        C = Ctx(nc, S)
        B_hT_all, B_hT_own, B_P_all, B_P_own, B_out = (Buf("hT_all"), Buf("hT_own"), Buf("P_all"),
                                                       Buf("P_own"), Buf("out"))

        identf, b_identf = C.sb(top, "identf", [128, 128], F32)
        ident, b_ident = C.sb(top, "ident", [128, 128], BF16)
        self_f, b_self_f = C.sb(top, "self_f", [128, 16], F32)
        selb, b_selb = C.sb(top, "selb", [128, 16], BF16)
        gcol, b_gcol = C.sb(top, "gcol", [128, 16], F32)
        S.op("pool", lambda e: e.memset(identf[:], 0.0), writes=[b_identf])
        S.op("pool", lambda e: e.affine_select(out=identf[:], in_=identf[:], pattern=[[-1, 128]],
                                                compare_op=ALU.not_equal, fill=1.0, base=0,
                                                channel_multiplier=1), reads=[b_identf], writes=[b_identf])
        S.op("dve", lambda e: e.tensor_copy(out=ident[:], in_=identf[:]), reads=[b_identf], writes=[b_ident])
        S.dma("sp", self_f[:], sel_in[:, :], writes=[b_self_f])
        S.op("dve", lambda e: e.tensor_copy(out=selb[:], in_=self_f[:]), reads=[b_self_f], writes=[b_selb])
        S.dma("sp", gcol[:], norm_g.ap().rearrange("(k p) -> p k", p=128), writes=[b_gcol],
              allow_slow_non_contiguous=True)

        if "norm" in stages:
            with contextlib.ExitStack() as es:
                NB = 2
                xt = [C.sb(es, f"xt{i}", [128, D], F32) for i in range(NB)]
                hn = [C.sb(es, f"hn{i}", [128, D], BF16) for i in range(NB)]
                hT = [C.sb(es, f"hT{i}", [128, D], BF16) for i in range(NB)]
                junk, b_junk = C.sb(es, "junk", [128, D], BF16)
                ss = [C.sb(es, f"ss{i}", [128, 1], F32) for i in range(NB)]
                rs = [C.sb(es, f"rs{i}", [128, 1], F32) for i in range(NB)]
                hown, b_hown = C.sb(es, "hown", [128, 9, 16, 128], BF16)
                pT = [C.ps(es, f"pT{i}", [128, D], BF16) for i in range(NB)]
                pO = [C.ps(es, f"pO{i}", [128, 16, 16], F32) for i in range(NB)]
                S.op("pool", lambda e: e.memset(hown[:], 0.0), writes=[b_hown])
                for t in range(NT):
                    i = t % NB
                    (x_, bx), (hn_, bhn), (hT_, bhT), (ss_, bss), (rs_, brs) = xt[i], hn[i], hT[i], ss[i], rs[i]
                    (pT_, bpT), (pO_, bpO) = pT[i], pO[i]
                    S.dma("sp", x_[:], h_all[t * 128:(t + 1) * 128, :], writes=[bx])
                    S.op("act", lambda e: e.activation(out=junk[:], in_=x_[:], func=AF.Square, accum_out=ss_[:]),
                         reads=[bx], writes=[b_junk, bss])
                    S.op("act", lambda e: e.activation(out=rs_[:], in_=ss_[:], func=AF.Sqrt, scale=1.0 / D,
                                                       bias=EPS), reads=[bss], writes=[brs])
                    S.op("dve", lambda e: e.reciprocal(out=rs_[:], in_=rs_[:]), reads=[brs], writes=[brs])
                    S.op("dve", lambda e: e.tensor_scalar(out=hn_[:], in0=x_[:], scalar1=rs_[:, 0:1], scalar2=None,
                                                          op0=ALU.mult), reads=[bx, brs], writes=[bhn])
                    for k in range(16):
                        S.op("pe", lambda e: e.transpose(out=pT_[:, k * 128:(k + 1) * 128],
                                                         in_=hn_[:, k * 128:(k + 1) * 128], identity=ident[:]),
                             reads=[bhn, b_ident], writes=[bpT])
                    S.op("act", lambda e: e.copy(out=hT_[:, 0:1024], in_=pT_[:, 0:1024]), reads=[bpT], writes=[bhT])
                    S.op("dve", lambda e: e.tensor_copy(out=hT_[:, 1024:2048], in_=pT_[:, 1024:2048]),
                         reads=[bpT], writes=[bhT])
                    S.dma("pool", hT_all[t], hT_[:], reads=[bhT], writes=[B_hT_all])
                    for k in range(16):
                        S.op("pe", lambda e: e.matmul(pO_[:, k, :], lhsT=hn_[:, k * 128:(k + 1) * 128], rhs=selb[:],
                                                      start=True, stop=True),
                             reads=[bhn, b_selb], writes=[bpO])
                    S.op("dve", lambda e: e.tensor_copy(out=hown[:, t // 8, :, (t % 8) * 16:(t % 8) * 16 + 16],
                                                        in_=pO_[:]), reads=[bpO], writes=[b_hown])
                for u in range(9):
                    S.dma("sp", hT_own[u].rearrange("p (k t) -> p k t", k=16), hown[:, u, :, :],
                          reads=[b_hown], writes=[B_hT_own])
                S.barrier()

        if "gemm" in stages:
            with contextlib.ExitStack() as es:
                wf = [C.sb(es, f"wf{i}", [128, 16, 512], F32) for i in range(2)]
                wb = [C.sb(es, f"wb{i}", [128, 16, 512], BF16) for i in range(2)]
                NH = 6
                hT = [C.sb(es, f"ghT{i}", [128, 16, 128], BF16) for i in range(NH)]
                ob = [C.sb(es, f"ob{i}", [128, 512], F32) for i in range(4)]
                pp = [C.ps(es, f"pp{i}", [128, 512], F32) for i in range(4)]
                w3 = w_in.ap().rearrange("(k p) c -> p k c", p=128)
                blocks = []
                for (src, ntiles, blks, dst, bdst, bsrc) in ((hT_all, NT, PA_BLOCKS, P_all, B_P_all, B_hT_all),
                                                             (hT_own, 9, PO_BLOCKS, P_own, B_P_own, B_hT_own)):
                    for (c0, wdt, d0) in blks:
                        blocks.append((src, ntiles, dst, bdst, bsrc, c0, wdt, d0))

                def load_w(bi):
                    (_, _, _, _, _, c0, wdt, _) = blocks[bi]
                    (wf_, bwf), (wb_, bwb) = wf[bi % 2], wb[bi % 2]
                    S.dma("pool", wf_[:, 0:5, 0:wdt], w3[:, 0:5, c0:c0 + wdt], writes=[bwf])
                    S.dma("sp", wf_[:, 5:11, 0:wdt], w3[:, 5:11, c0:c0 + wdt], writes=[bwf])
                    S.dma("act", wf_[:, 11:16, 0:wdt], w3[:, 11:16, c0:c0 + wdt], writes=[bwf])
                    for k in range(16):
                        S.op("dve", lambda e: e.tensor_scalar(out=wb_[:, k, 0:wdt], in0=wf_[:, k, 0:wdt],
                                                              scalar1=gcol[:, k:k + 1], scalar2=None, op0=ALU.mult),
                             reads=[bwf, b_gcol], writes=[bwb])

                nt = 0
                load_w(0)
                for bi, (src, ntiles, dst, bdst, bsrc, c0, wdt, d0) in enumerate(blocks):
                    if bi + 1 < len(blocks):
                        load_w(bi + 1)
                    (wb_, bwb) = wb[bi % 2]
                    for t in range(ntiles):
                        (h_, bh), (o_, bo), (p_, bp) = hT[nt % NH], ob[nt % 4], pp[nt % 4]
                        nt += 1
                        S.dma("sp", h_[:], src[t].rearrange("p (k t) -> p k t", k=16), reads=[bsrc], writes=[bh])
                        for k in range(16):
                            S.op("pe", lambda e: e.matmul(p_[:, 0:wdt], lhsT=h_[:, k, :], rhs=wb_[:, k, 0:wdt],
                                                          start=(k == 0), stop=(k == 15)),
                                 reads=[bh, bwb], writes=[bp])
                        S.op("act", lambda e: e.copy(out=o_[:, 0:wdt], in_=p_[:, 0:wdt]), reads=[bp], writes=[bo])
                        S.dma("pool", dst[t * 128:(t + 1) * 128, d0:d0 + wdt], o_[:, 0:wdt], reads=[bo], writes=[bdst])
                S.barrier()

        if "post" in stages:
            with contextlib.ExitStack() as es:
                gq_bc, b_gq = C.sb(es, "gq_sb", [128, 128], F32)
                gk_bc, b_gk = C.sb(es, "gk_sb", [128, 128], F32)
                gi_bc, b_gi = C.sb(es, "gi_sb", [128, 64], F32)
                S.dma("sp", gq_bc[:], gq_in[:, :], writes=[b_gq])
                S.dma("sp", gk_bc[:], gk_in[:, :], writes=[b_gk])
                S.dma("sp", gi_bc[:], gi_in[:, :], writes=[b_gi])
                junks = [C.sb(es, f"pjunk{i}", [128, 128], F32) for i in range(3)]
                NB = 3
                pa = [C.sb(es, f"pa{i}", [128, 1088], F32) for i in range(NB)]
                csk = [C.sb(es, f"csk{i}", [128, 2, 8, 16], F32) for i in range(NB)]
                csi = [C.sb(es, f"csi{i}", [128, 2, 16, 8], F32) for i in range(NB)]
                ssq = [C.sb(es, f"ssq{i}", [128, 8], F32) for i in range(NB)]
                kn = [C.sb(es, f"kn{i}", [128, 8, 128], F32) for i in range(NB)]
                tt = [C.sb(es, f"tt{i}", [128, 4, 8, 16], F32) for i in range(NB)]
                kr = [C.sb(es, f"kr{i}", [128, 8, 128], BF16) for i in range(NB)]
                kT = [C.sb(es, f"kT{i}", [128, 8, 128], BF16) for i in range(NB)]
                vb = [C.sb(es, f"vb{i}", [128, 512], BF16) for i in range(NB)]
                kib = [C.sb(es, f"kib{i}", [128, 128], BF16) for i in range(NB)]
                kiT = [C.sb(es, f"kiT{i}", [128, 128], BF16) for i in range(NB)]
                pT = [C.ps(es, f"ppT{i}", [128, 8, 128], BF16) for i in range(NB)]
                pI = [C.ps(es, f"ppI{i}", [128, 128], BF16) for i in range(NB)]
                po_ = [C.sb(es, f"po{i}", [128, 4112], F32) for i in range(NB)]
                zs = [C.sb(es, f"zs{i}", [128, 1024], F32) for i in range(NB)]
                zb_ = [C.sb(es, f"zbb{i}", [128, 8, 128], BF16) for i in range(NB)]
                wv = [C.sb(es, f"wv{i}", [128, 16], F32) for i in range(NB)]

                def headnorm_rope(i, src3, H, hd, g_bc, cs_tile, half, b_src, b_cs, do_norm=True):
                    (ss_, bss), (kn_, bkn), (tt_, btt), (kr_, bkr) = ssq[i], kn[i], tt[i], kr[i]
                    (junk, b_junk) = junks[i]
                    if hd == 128:
                        knv = kn_[:, 0:H, :]
                        krv = kr_[:, 0:H, :]
                    else:
                        knv = kn_[:].rearrange("p a b -> p (a b)")[:, 0:H * hd].rearrange("p (h d) -> p h d", h=H)
                        krv = kr_[:].rearrange("p a b -> p (a b)")[:, 0:H * hd].rearrange("p (h d) -> p h d", h=H)
                    if do_norm:
                        for h in range(H):
                            S.op("act", lambda e: e.activation(out=junk[:, 0:hd], in_=src3[:, h, :], func=AF.Square,
                                                               accum_out=ss_[:, h:h + 1]),
                                 reads=[b_src], writes=[b_junk, bss])
                        S.op("act", lambda e: e.activation(out=ss_[:, 0:H], in_=ss_[:, 0:H], func=AF.Sqrt,
                                                           scale=1.0 / hd, bias=EPS), reads=[bss], writes=[bss])
                        yield
                        S.op("dve", lambda e: e.reciprocal(out=ss_[:, 0:H], in_=ss_[:, 0:H]), reads=[bss], writes=[bss])
                        for h in range(H):
                            S.op("dve", lambda e: e.scalar_tensor_tensor(out=knv[:, h, :], in0=src3[:, h, :],
                                                                         scalar=ss_[:, h:h + 1], in1=g_bc[:, 0:hd],
                                                                         op0=ALU.mult, op1=ALU.mult),
                                 reads=[b_src, bss], writes=[bkn])
                        srcn, bsn = knv, bkn
                    else:
                        srcn, bsn = src3, b_src
                    if hd == 128:
                        cosv, sinv = cs_tile[:, 0, 0:H, :], cs_tile[:, 1, 0:H, :]
                        tv = [tt_[:, q, 0:H, :] for q in range(4)]
                    else:
                        cosv, sinv = cs_tile[:, 0, 0:H, :], cs_tile[:, 1, 0:H, :]
                        tflat = tt_[:].rearrange("p a b c -> p a (b c)")
                        tv = [tflat[:, q, 0:H * half].rearrange("p (h d) -> p h d", h=H) for q in range(4)]
                    yield
                    x1, x2 = srcn[:, :, 0:half], srcn[:, :, half:2 * half]
                    S.op("dve", lambda e: e.tensor_tensor(out=tv[0], in0=x1, in1=cosv, op=ALU.mult),
                         reads=[bsn, b_cs], writes=[btt])
                    S.op("pool", lambda e: e.tensor_tensor(out=tv[1], in0=x2, in1=sinv, op=ALU.mult),
                         reads=[bsn, b_cs], writes=[btt])
                    S.op("dve", lambda e: e.tensor_tensor(out=tv[2], in0=x2, in1=cosv, op=ALU.mult),
                         reads=[bsn, b_cs], writes=[btt])
                    S.op("pool", lambda e: e.tensor_tensor(out=tv[3], in0=x1, in1=sinv, op=ALU.mult),
                         reads=[bsn, b_cs], writes=[btt])
                    S.op("act", lambda e: e.copy(out=krv, in_=srcn), reads=[bsn], writes=[bkr])
                    yield
                    S.op("dve", lambda e: e.tensor_tensor(out=krv[:, :, 0:half], in0=tv[0], in1=tv[1], op=ALU.subtract),
                         reads=[btt], writes=[bkr])
                    S.op("dve", lambda e: e.tensor_tensor(out=krv[:, :, half:2 * half], in0=tv[2], in1=tv[3], op=ALU.add),
                         reads=[btt], writes=[bkr])
                    yield
                    return krv, bkr

                def all_tile(t):
                    i = t % NB
                    (pa_, bpa), (csk_, bcsk), (csi_, bcsi) = pa[i], csk[i], csi[i]
                    rows = slice(t * 128, (t + 1) * 128)
                    S.dma("sp", pa_[:], P_all[rows, 0:1088], reads=[B_P_all], writes=[bpa])
                    S.dma("sp", csk_[:, :, 0:4, :], ropeK_in[rows].rearrange("p (a h d) -> p a h d", a=2, h=4),
                          writes=[bcsk])
                    S.dma("sp", csi_[:, :, 0:1, :], ropeKI_in[rows].rearrange("p (a h d) -> p a h d", a=2, h=1),
                          writes=[bcsi])
                    k3 = pa_[:, 0:512].rearrange("p (h d) -> p h d", h=4)
                    krv, bkr = yield from headnorm_rope(i, k3, 4, 128, gk_bc, csk_, 16, bpa, bcsk)
                    (pT_, bpT), (kT_, bkT) = pT[i], kT[i]
                    for g in range(4):
                        S.op("pe", lambda e: e.transpose(out=pT_[:, g, :], in_=krv[:, g, :], identity=ident[:]),
                             reads=[bkr, b_ident], writes=[bpT])
                    S.op("act", lambda e: e.copy(out=kT_[:, 0:4, :], in_=pT_[:, 0:4, :]), reads=[bpT], writes=[bkT])
                    S.dma("pool", KT_d[:, :, rows], kT_[:, 0:4, :], reads=[bkT], writes=[B_KT])
                    (vb_, bvb) = vb[i]
                    S.op("pool", lambda e: e.tensor_copy(out=vb_[:], in_=pa_[:, 512:1024]), reads=[bpa], writes=[bvb])
                    S.dma("pool", V_d[rows, :], vb_[:], reads=[bvb], writes=[B_V])
                    ki3 = pa_[:, 1024:1088].rearrange("p (h d) -> p h d", h=1)
                    yield
                    kiv, bkr = yield from headnorm_rope(i, ki3, 1, 64, gi_bc, csi_, 8, bpa, bcsi)
                    (kib_, bkib), (pI_, bpI), (kiT_, bkiT) = kib[i], pI[i], kiT[i]
                    S.op("dve", lambda e: e.tensor_copy(out=kib_[:, 0:64], in_=kiv[:, 0, :]), reads=[bkr], writes=[bkib])
                    S.op("pool", lambda e: e.tensor_copy(out=kib_[:, 64:128], in_=kiv[:, 0, :]), reads=[bkr], writes=[bkib])
                    S.op("pe", lambda e: e.transpose(out=pI_[:], in_=kib_[:], identity=ident[:]),
                         reads=[bkib, b_ident], writes=[bpI])
                    S.op("act", lambda e: e.copy(out=kiT_[:], in_=pI_[:]), reads=[bpI], writes=[bkiT])
                    S.dma("pool", kiT_d[:, rows], kiT_[:], reads=[bkiT], writes=[B_kiT])

                def run_interleaved(gens, width):
                    active = []
                    gens = list(gens)
                    while gens or active:
                        while gens and len(active) < width:
                            active.append(gens.pop(0))
                        for g in list(active):
                            try:
                                next(g)
                            except StopIteration:
                                active.remove(g)

                run_interleaved([all_tile(t) for t in range(NT)], 3)

                def own_tile(u):
                    i = u % NB
                    (po__, bpo), (csk_, bcsk), (csi_, bcsi) = po_[i], csk[i], csi[i]
                    rows = slice(u * 128, (u + 1) * 128)
                    S.dma("sp", po__[:], P_own[rows, 0:4112], reads=[B_P_own], writes=[bpo])
                    S.dma("sp", csk_[:], ropeQ_in[rows].rearrange("p (a h d) -> p a h d", a=2, h=8), writes=[bcsk])
                    S.dma("sp", csi_[:], ropeQI_in[rows].rearrange("p (a h d) -> p a h d", a=2, h=16), writes=[bcsi])
                    (pT_, bpT), (kT_, bkT) = pT[i], kT[i]
                    q3 = po__[:, 0:1024].rearrange("p (h d) -> p h d", h=8)
                    krv, bkr = yield from headnorm_rope(i, q3, 8, 128, gq_bc, csk_, 16, bpo, bcsk)
                    for h in range(8):
                        S.op("pe", lambda e: e.transpose(out=pT_[:, h, :], in_=krv[:, h, :], identity=ident[:]),
                             reads=[bkr, b_ident], writes=[bpT])
                    S.op("act", lambda e: e.copy(out=kT_[:], in_=pT_[:]), reads=[bpT], writes=[bkT])
                    S.dma("pool", QT_d[:, :, rows], kT_[:], reads=[bkT], writes=[B_QT])
                    yield
                    for (c0, dstd, bdst) in ((1024, zaT_d, B_zaT), (3088, zbT_d, B_zbT)):
                        (zs_, bzs), (zb__, bzb) = zs[i], zb_[i]
                        S.op("act", lambda e: e.activation(out=zs_[:], in_=po__[:, c0:c0 + 1024], func=AF.Sigmoid),
                             reads=[bpo], writes=[bzs])
                        S.op("dve", lambda e: e.tensor_tensor(out=zb__[:].rearrange("p a b -> p (a b)"), in0=zs_[:],
                                                              in1=po__[:, c0:c0 + 1024], op=ALU.mult),
                             reads=[bzs, bpo], writes=[bzb])
                        for h in range(8):
                            S.op("pe", lambda e: e.transpose(out=pT_[:, h, :], in_=zb__[:, h, :], identity=ident[:]),
                                 reads=[bzb, b_ident], writes=[bpT])
                        S.op("act", lambda e: e.copy(out=kT_[:], in_=pT_[:]), reads=[bpT], writes=[bkT])
                        S.dma("pool", dstd[:, :, rows], kT_[:], reads=[bkT], writes=[bdst])
                    qi3 = po__[:, 2048:3072].rearrange("p (h d) -> p h d", h=16)
                    yield
                    qv, bkr = yield from headnorm_rope(i, qi3, 16, 64, None, csi_, 8, bpo, bcsi, do_norm=False)
                    qflat = kr[i][0][:]
                    for h in range(8):
                        S.op("pe", lambda e: e.transpose(out=pT_[:, h, :], in_=qflat[:, h, :], identity=ident[:]),
                             reads=[bkr, b_ident], writes=[bpT])
                    S.op("act", lambda e: e.copy(out=kT_[:], in_=pT_[:]), reads=[bpT], writes=[bkT])
                    S.dma("pool", qiT_d[:, :, rows], kT_[:], reads=[bkT], writes=[B_qiT])
                    (wv_, bwv) = wv[i]
                    S.op("dve", lambda e: e.tensor_scalar(out=wv_[:], in0=po__[:, 3072:3088], scalar1=1.0 / 32.0,
                                                          scalar2=None, op0=ALU.mult), reads=[bpo], writes=[bwv])
                    S.dma("pool", w_d[rows, :], wv_[:], reads=[bwv], writes=[B_w])
                    yield

                run_interleaved([own_tile(u) for u in range(9)], 3)
                S.barrier()

        if "hgrn" in stages:
            with contextlib.ExitStack() as es:
                Uf, b_U = C.sb(es, "Uf", [128, 128], F32)
                Wf, b_W = C.sb(es, "Wf", [128, 128], F32)
                chi, b_chi = C.sb(es, "chi", [128, 2], F32)
                lb, b_lb = C.sb(es, "lb_sb", [128, 1024], F32)
                oml, b_oml = C.sb(es, "oml", [128, 1024], F32)
                l1, b_l1 = C.sb(es, "l1", [128, 1024], F32)
                go_bc, b_go = C.sb(es, "go_sb", [128, 128], F32)
                S.dma("sp", Uf[:], U_in[:, :], writes=[b_U])
                S.dma("sp", Wf[:], W_in[:, :], writes=[b_W])
                S.dma("sp", chi[:], chi_in[:, :], writes=[b_chi])
                S.dma("sp", go_bc[:], go_in[:, :], writes=[b_go])
                S.dma("sp", lb[:], lb0_in[:, :], writes=[b_lb])
                S.dma("sp", l1[:], lb1_in[:, :], writes=[b_l1])
                S.op("dve", lambda e: e.tensor_tensor(out=lb[:], in0=lb[:], in1=l1[:], op=ALU.subtract),
                     reads=[b_lb, b_l1], writes=[b_lb])
                S.op("act", lambda e: e.activation(out=lb[:], in_=lb[:], func=AF.Sigmoid), reads=[b_lb], writes=[b_lb])
                S.op("dve", lambda e: e.tensor_scalar(out=oml[:], in0=lb[:], scalar1=-1.0, scalar2=1.0, op0=ALU.mult,
                                                      op1=ALU.add), reads=[b_lb], writes=[b_oml])
                Sf, b_Sf = C.sb(es, "Sf", [128, 8, 128], F32)
                S0b, b_S0b = C.sb(es, "S0b", [128, 8, 128], BF16)
                ybT, b_ybT = C.sb(es, "ybT", [128, 8, NOWN], BF16)
                S.op("pool", lambda e: e.memset(Sf[:], 0.0), writes=[b_Sf])
                S.op("pool", lambda e: e.memset(S0b[:], 0.0), writes=[b_S0b])
                S.op("pool", lambda e: e.memset(ybT[:], 0.0), writes=[b_ybT])
                NB = 2
                qb = [C.sb(es, f"qb{i}", [128, 1024], F32) for i in range(NB)]
                fb = [C.sb(es, f"fb{i}", [128, 1024], F32) for i in range(NB)]
                ib = [C.sb(es, f"ib{i}", [128, 1024], F32) for i in range(NB)]
                gg = [C.sb(es, f"gg{i}", [128, 1024], F32) for i in range(NB)]
                kk = [C.sb(es, f"kk{i}", [128, 1024], F32) for i in range(NB)]
                e1 = [C.sb(es, f"e1{i}", [128, 512], F32) for i in range(NB)]
                e2 = [C.sb(es, f"e2{i}", [128, 512], F32) for i in range(NB)]
                e3 = [C.sb(es, f"e3{i}", [128, 512], F32) for i in range(NB)]
                qt = [C.sb(es, f"qt{i}", [128, 1024], BF16) for i in range(NB)]
                kt = [C.sb(es, f"kt{i}", [128, 1024], BF16) for i in range(NB)]
                kd = [C.sb(es, f"kd{i}", [128, 1024], BF16) for i in range(NB)]
                vv = [C.sb(es, f"vv{i}", [128, 1024], BF16) for i in range(NB)]
                ebl = [C.sb(es, f"ebl{i}", [128, 16], F32) for i in range(NB)]
                qkT8 = [C.sb(es, f"qkT8{i}", [128, 8, 256], BF16) for i in range(2)]
                AT8, b_AT8 = C.sb(es, "AT8", [128, 8, 128], BF16)
                S1f8, b_S1f8 = C.sb(es, "S1f8", [128, 8, 128], F32)
                Sdec, b_Sdec = C.sb(es, "Sdec", [128, 8, 128], F32)
                S1b8, b_S1b8 = C.sb(es, "S1b8", [128, 8, 128], BF16)
                S0b2 = [(S0b, b_S0b), C.sb(es, "S0b_1", [128, 8, 128], BF16)]
                on8, b_on8 = C.sb(es, "on8", [128, 8, 128], BF16)
                sq8, b_sq8 = C.sb(es, "sq8", [128, 8, 128], F32)
                ss8, b_ss8 = C.sb(es, "ss8", [128, 8], F32)
                U8, b_U8 = C.sb(es, "U8", [128, 8, 128], F32)
                for h in range(8):
                    S.op("dve", lambda e: e.tensor_copy(out=U8[:, h, :], in_=Uf[:]), reads=[b_U], writes=[b_U8])
                X1, b_X1 = C.ps(es, "X1", [128, 1024], F32)
                X2, b_X2 = C.ps(es, "X2", [128, 8, 128], F32)
                X3, b_X3 = C.ps(es, "X3", [128, 8, 128], F32)
                pCS, b_pCS = C.ps(es, "pCS", [128, 144], F32)
                pQ4, b_pQ4 = C.ps(es, "pQ4", [128, 4, 256], BF16)
                pB, pBD = X1[:, 0:512], X1[:, 512:1024]
                b_pB = b_pBD = b_X1
                X1v = X1.rearrange("p (a b) -> p a b", a=8)
                pC = pCS[:, 0:16]
                b_pC = b_pCS
                pS3 = pCS[:, 16:144].rearrange("p (a b) -> p a b", a=8)
                def hg_prep(t):
                        i = t % NB
                        rows = slice(t * 128, (t + 1) * 128)
                        (qb_, bqb), (fb_, bfb), (ib_, bib), (gg_, bgg), (kk_, bkk) = qb[i], fb[i], ib[i], gg[i], kk[i]
                        (qt_, bqt), (kt_, bkt), (kd_, bkd), (vv_, bvv), (ebl_, bebl) = qt[i], kt[i], kd[i], vv[i], ebl[i]
                        S.dma("sp", qb_[:], P_all[rows, 1088:2112], reads=[B_P_all], writes=[bqb])
                        S.dma("sp", fb_[:], P_all[rows, 2112:3136], reads=[B_P_all], writes=[bfb])
                        S.dma("sp", ib_[:], P_all[rows, 3136:4160], reads=[B_P_all], writes=[bib])
                        S.op("act", lambda e: e.copy(out=vv_[:], in_=ib_[:]), reads=[bib], writes=[bvv])
                        S.op("act", lambda e: e.activation(out=fb_[:], in_=fb_[:], func=AF.Sigmoid), reads=[bfb], writes=[bfb])
                        S.op("act", lambda e: e.activation(out=ib_[:], in_=qb_[:], func=AF.Sigmoid), reads=[bqb, bvv],
                             writes=[bib])
                        S.op("dve", lambda e: e.tensor_tensor(out=fb_[:], in0=fb_[:], in1=oml[:], op=ALU.mult),
                             reads=[bfb, b_oml], writes=[bfb])
                        S.op("pool", lambda e: e.tensor_tensor(out=fb_[:], in0=fb_[:], in1=lb[:], op=ALU.add),
                             reads=[bfb, b_lb], writes=[bfb])
                        S.op("act", lambda e: e.activation(out=gg_[:], in_=fb_[:], func=AF.Ln), reads=[bfb], writes=[bgg])
                        S.op("pool", lambda e: e.tensor_scalar(out=kk_[:], in0=fb_[:], scalar1=-1.0, scalar2=1.0,
                                                               op0=ALU.mult, op1=ALU.add), reads=[bfb], writes=[bkk])

                        S.op("dve", lambda e: e.tensor_tensor(out=qb_[:], in0=qb_[:], in1=ib_[:], op=ALU.mult),
                             reads=[bqb, bib], writes=[bqb])
                        for hf in range(2):
                            cs = slice(hf * 512, (hf + 1) * 512)
                            (e1_, be1), (e2_, be2), (e3_, be3) = e1[hf], e2[hf], e3[hf]
                            S.op("pe", lambda e: e.matmul(pB[:], lhsT=Uf[:], rhs=gg_[:, cs], start=True, stop=True),
                                 reads=[b_U, bgg], writes=[b_pB])
                            S.op("pe", lambda e: e.matmul(pBD[:], lhsT=Wf[:], rhs=gg_[:, cs], start=True, stop=True),
                                 reads=[b_W, bgg], writes=[b_pBD])
                            S.op("act", lambda e: e.activation(out=e1_[:], in_=pB[:], func=AF.Exp), reads=[b_pB], writes=[be1])
                            S.op("act", lambda e: e.activation(out=e2_[:], in_=pB[:], func=AF.Exp, scale=-1.0),
                                 reads=[b_pB], writes=[be2])
                            S.op("act", lambda e: e.activation(out=e3_[:], in_=pBD[:], func=AF.Exp), reads=[b_pBD],
                                 writes=[be3])
                            S.op("dve", lambda e: e.tensor_tensor(out=qt_[:, cs], in0=qb_[:, cs], in1=e1_[:], op=ALU.mult),
                                 reads=[bqb, be1], writes=[bqt])
                            S.op("pool", lambda e: e.tensor_tensor(out=kt_[:, cs], in0=kk_[:, cs], in1=e2_[:], op=ALU.mult),
                                 reads=[bkk, be2], writes=[bkt])
                            S.op("dve", lambda e: e.tensor_tensor(out=kd_[:, cs], in0=kk_[:, cs], in1=e3_[:], op=ALU.mult),
                                 reads=[bkk, be3], writes=[bkd])
                        for h in range(8):
                            S.op("pe", lambda e: e.matmul(pC[:, 2 * h:2 * h + 2], lhsT=gg_[:, h * 128:(h + 1) * 128],
                                                          rhs=chi[:], start=True, stop=True),
                                 reads=[bgg, b_chi], writes=[b_pC])
                        S.op("act", lambda e: e.activation(out=ebl_[:], in_=pC[:], func=AF.Exp), reads=[b_pC], writes=[bebl])

                def hg_tail(t):
                        i = t % NB
                        (qt_, bqt), (kt_, bkt), (kd_, bkd), (vv_, bvv), (ebl_, bebl) = qt[i], kt[i], kd[i], vv[i], ebl[i]
                        (qk_, bqk) = qkT8[t % 2]
                        (S0c, bS0c), (S0n, bS0n) = S0b2[t % 2], S0b2[(t + 1) % 2]
                        hcs = [slice(h * 128, (h + 1) * 128) for h in range(8)]
                        for h in range(8):
                            S.op("pe", lambda e: e.matmul(X2[:, h, :], lhsT=kd_[0:64, hcs[h]], rhs=vv_[0:64, hcs[h]],
                                                          start=True, stop=True), reads=[bkd, bvv], writes=[b_X2])
                        for half in range(2):
                            for hh in range(4):
                                h = 4 * half + hh
                                S.op("pe", lambda e: e.transpose(out=pQ4[:, hh, 0:128], in_=qt_[:, hcs[h]], identity=ident[:]),
                                     reads=[bqt, b_ident], writes=[b_pQ4])
                                S.op("pe", lambda e: e.transpose(out=pQ4[:, hh, 128:256], in_=kt_[:, hcs[h]],
                                                                 identity=ident[:]), reads=[bkt, b_ident], writes=[b_pQ4])
                            if half == 0:
                                S.op("act", lambda e: e.copy(out=qk_[:, 0:4, :], in_=pQ4[:]), reads=[b_pQ4], writes=[bqk])
                                for h in range(8):
                                    S.op("pool", lambda e: e.tensor_scalar(out=Sdec[:, h, :], in0=Sf[:, h, :],
                                                                           scalar1=ebl_[:, 2 * h:2 * h + 1], scalar2=1.0,
                                                                           op0=ALU.mult, op1=ALU.mult),
                                         reads=[b_Sf, bebl], writes=[b_Sdec])
                                S.op("dve", lambda e: e.tensor_tensor(out=S1f8[:], in0=X2[:], in1=Sdec[:], op=ALU.add),
                                     reads=[b_X2, b_Sdec], writes=[b_S1f8])
                                S.op("act", lambda e: e.copy(out=S1b8[:], in_=S1f8[:]), reads=[b_S1f8], writes=[b_S1b8])
                            else:
                                S.op("dve", lambda e: e.tensor_copy(out=qk_[:, 4:8, :], in_=pQ4[:]), reads=[b_pQ4],
                                     writes=[bqk])
                        for h in range(8):
                            S.op("pe", lambda e: e.matmul(X2[:, h, :], lhsT=kd_[64:128, hcs[h]], rhs=vv_[64:128, hcs[h]],
                                                          start=True, stop=True), reads=[bkd, bvv], writes=[b_X2])
                        for h in range(8):
                            S.op("pool", lambda e: e.tensor_scalar(out=Sdec[:, h, :], in0=S1f8[:, h, :],
                                                                   scalar1=ebl_[:, 2 * h + 1:2 * h + 2], scalar2=1.0,
                                                                   op0=ALU.mult, op1=ALU.mult),
                                 reads=[b_S1f8, bebl], writes=[b_Sdec])
                        S.op("dve", lambda e: e.tensor_tensor(out=Sf[:], in0=X2[:], in1=Sdec[:], op=ALU.add),
                             reads=[b_X2, b_Sdec], writes=[b_Sf])
                        S.op("act", lambda e: e.copy(out=S0n[:], in_=Sf[:]), reads=[b_Sf], writes=[bS0n])
                        for h in range(8):
                            S.op("pe", lambda e: e.matmul(X1v[:, h, :], lhsT=qk_[:, h, 128:256], rhs=qk_[:, h, 0:128],
                                                          start=True, stop=True), reads=[bqk], writes=[b_X1])
                        S.op("dve", lambda e: e.tensor_tensor(out=AT8[:], in0=X1v, in1=U8[:], op=ALU.mult),
                             reads=[b_X1, b_U8], writes=[b_AT8])
                        for h in range(8):
                            S.op("pe", lambda e: e.matmul(X3[:, h, :], lhsT=AT8[:, h, :], rhs=vv_[:, hcs[h]], start=True,
                                                          stop=False, skip_group_check=True),
                                 reads=[b_AT8, bvv], writes=[b_X3])
                            S.op("pe", lambda e: e.matmul(X3[0:64, h, :], lhsT=qk_[:, h, 0:64], rhs=S0c[:, h, :], start=False,
                                                          stop=True, skip_group_check=True),
                                 reads=[bqk, bS0c], writes=[b_X3])
                            S.op("pe", lambda e: e.matmul(X3[64:128, h, :], lhsT=qk_[:, h, 64:128], rhs=S1b8[:, h, :],
                                                          start=False, stop=True, skip_group_check=True),
                                 reads=[bqk, b_S1b8], writes=[b_X3])
                        S.op("act", lambda e: e.activation(out=sq8[:], in_=X3[:], func=AF.Square), reads=[b_X3],
                             writes=[b_sq8])
                        S.op("dve", lambda e: e.tensor_reduce(out=ss8[:], in_=sq8[:], axis=AX.X, op=ALU.add),
                             reads=[b_sq8], writes=[b_ss8])
                        S.op("act", lambda e: e.activation(out=ss8[:], in_=ss8[:], func=AF.Sqrt, scale=1.0 / 128, bias=EPS),
                             reads=[b_ss8], writes=[b_ss8])
                        S.op("dve", lambda e: e.reciprocal(out=ss8[:], in_=ss8[:]), reads=[b_ss8], writes=[b_ss8])
                        for h in range(8):
                            S.op("act", lambda e: e.activation(out=on8[:, h, :], in_=X3[:, h, :], func=AF.Copy,
                                                               scale=ss8[:, h:h + 1]),
                                 reads=[b_X3, b_ss8], writes=[b_on8])
                        for h in range(8):
                            S.op("pe", lambda e: e.matmul(pS3[:, h, :], lhsT=on8[:, h, :], rhs=selb[:], start=True, stop=True),
                                 reads=[b_on8, b_selb], writes=[b_pCS])
                        S.op("act", lambda e: e.copy(out=ybT[:, :, 16 * t:16 * t + 16], in_=pS3), reads=[b_pCS],
                             writes=[b_ybT])

                hg_prep(0)
                for t in range(NT):
                    if t + 1 < NT:
                        hg_prep(t + 1)
                    hg_tail(t)
                S.dma("pool", ybT_d[:, :, :], ybT[:], reads=[b_ybT], writes=[B_ybT])
                S.barrier()

        if "attn" in stages:
            with contextlib.ExitStack() as es:
                ki2, b_ki2 = C.sb(es, "ki2", [128, TP], BF16)
                for q4 in range(5):
                    S.dma("sp", ki2[:, q4 * 1664:(q4 + 1) * 1664], kiT_d[:, q4 * 1664:(q4 + 1) * 1664],
                          reads=[B_kiT], writes=[b_ki2])
                AM, b_AM = C.sb(es, "AM_sb", [128, 1024], F32)
                S.dma("sp", AM[:], AM_in[:, :], writes=[b_AM])
                I4, b_I4 = C.sb(es, "I4", [128, 512], BF16)
                for r in range(4):
                    S.op("dve", lambda e: e.tensor_copy(out=I4[:, r * 128:(r + 1) * 128], in_=identf[:]),
                         reads=[b_identf], writes=[b_I4])
                ones_b, b_ones = C.sb(es, "ones_b", [128, 128], BF16)
                S.op("pool", lambda e: e.memset(ones_b[:], 1.0), writes=[b_ones])
                gq_bc, b_gq = C.sb(es, "gq_bc2", [128, 128], F32)
                gk_bc, b_gk = C.sb(es, "gk_bc2", [128, 128], F32)
                mq, b_mq = C.sb(es, "mq", [128, 1], F32)
                mk, b_mk = C.sb(es, "mk", [128, 1], F32)
                S.dma("sp", gq_bc[:], gq_in[:, :], writes=[b_gq])
                S.dma("sp", gk_bc[:], gk_in[:, :], writes=[b_gk])
                S.op("dve", lambda e: e.tensor_reduce(out=mq[:], in_=gq_bc[:], axis=AX.X, op=ALU.max,
                                                      apply_absolute_value=True), reads=[b_gq], writes=[b_mq])
                S.op("dve", lambda e: e.tensor_reduce(out=mk[:], in_=gk_bc[:], axis=AX.X, op=ALU.max,
                                                      apply_absolute_value=True), reads=[b_gk], writes=[b_mk])
                S.op("dve", lambda e: e.tensor_tensor(out=mq[:], in0=mq[:], in1=mk[:], op=ALU.mult),
                     reads=[b_mq, b_mk], writes=[b_mq])
                S.op("dve", lambda e: e.tensor_scalar(out=mq[:], in0=mq[:], scalar1=-(128.0 ** 0.5), scalar2=None,
                                                      op0=ALU.mult), reads=[b_mq], writes=[b_mq])
                score, b_score = C.sb(es, "score", [128, 8208], F32)
                cjunk, b_cjunk = C.sb(es, "cjunk", [128, 8208], BF16)
                MB, b_MB = C.sb(es, "MB", [128, 8208], BF16)
                QTj, b_QTj = C.sb(es, "QTj", [128, 8, 128], BF16)
                qiTj, b_qiTj = C.sb(es, "qiTj", [128, 8, 128], BF16)
                zaTj, b_zaTj = C.sb(es, "zaTj", [128, 8, 128], BF16)
                wj, b_wj = C.sb(es, "wj", [128, 16], F32)
                Dg, b_Dg = C.sb(es, "Dg", [128, 16, 128], BF16)
                NR = 4
                Rl = [C.sb(es, f"Rl{i}", [128, 512], BF16) for i in range(8)]
                PT = [C.sb(es, f"PT{i}", [128, 512], BF16) for i in range(NR)]
                KTc = [C.sb(es, f"KTc{i}", [128, 4, 512], BF16) for i in range(2)]
                Vc = [C.sb(es, f"Vc{i}", [128, 4, 512], BF16) for i in range(2)]
                sm = {n: C.sb(es, "bs_" + n, [128, 1], F32) for n in ("lo", "hi", "mid", "cnt", "ge", "d1", "d2", "B", "nmid", "sga")}
                ajunk, b_ajunk = C.sb(es, "ajunk", [128, 4608], BF16)
                rden, b_rden = C.sb(es, "rden", [128, 512], F32)
                yaT, b_yaT = C.sb(es, "yaT", [128, 8, 128], BF16)
                oT, b_oT = C.sb(es, "oT", [128, 512], F32)
                pL = [C.ps(es, f"pL{i}", [128, 512], F32) for i in range(3)]
                pSc, b_pSc = C.ps(es, "pSc", [128, 512], F32)
                pOA = [C.ps(es, f"pOA{i}", [128, 512], F32) for i in range(2)]
                pDn = [C.ps(es, f"pDn{i}", [128, 512], F32) for i in range(2)]
                pLx = pL + pOA + pDn
                nrl = 0
                npl = 0
                nplx = 0
                npt = 0
                nkc = 0
                for j in range(8):
                    s0 = 2 + 128 * j
                    NJ = 16 + 1024 * (j + 1)
                    S.dma("sp", QTj[:], QT_d[:, :, s0:s0 + 128], reads=[B_QT], writes=[b_QTj])
                    S.dma("sp", qiTj[:], qiT_d[:, :, s0:s0 + 128], reads=[B_qiT], writes=[b_qiTj])
                    S.dma("sp", zaTj[:], zaT_d[:, :, s0:s0 + 128], reads=[B_zaT], writes=[b_zaTj])
                    S.dma("sp", wj[:], w_d[s0:s0 + 128, :], reads=[B_w], writes=[b_wj])
                    for h in range(16):
                        S.op("dve", lambda e: e.tensor_scalar(out=Dg[:, h, :], in0=identf[:], scalar1=wj[:, h:h + 1],
                                                              scalar2=None, op0=ALU.mult),
                             reads=[b_identf, b_wj], writes=[b_Dg])
                    items = []
                    c0 = 0
                    while c0 < NJ:
                        cw = min(512, NJ - c0)
                        for h in range(16):
                            items.append((c0, cw, h))
                        c0 += cw
                    LAG = 4
                    slots = {}
                    order = []
                    for base in range(0, len(items) + LAG, 2):
                        order += [("L", base), ("L", base + 1), ("A", base - LAG), ("A", base + 1 - LAG)]
                    for kind, idx in order:
                        if kind == "L" and idx < len(items):
                            c0, cw, h = items[idx]
                            (pl_, bpl) = pLx[nplx % 7]
                            nplx += 1
                            (rl_, brl) = Rl[nrl % 8]
                            nrl += 1
                            slots[idx] = (rl_, brl)
                            pr = slice((h % 2) * 64, (h % 2) * 64 + 64)
                            S.op("pe", lambda e: e.matmul(pl_[:, 0:cw], lhsT=qiTj[pr, h // 2, :], rhs=ki2[pr, c0:c0 + cw],
                                                          start=True, stop=True),
                                 reads=[b_qiTj, b_ki2], writes=[bpl])
                            if h % 2 == 0:
                                S.op("act", lambda e: e.activation(out=rl_[:, 0:cw], in_=pl_[:, 0:cw], func=AF.Relu),
                                     reads=[bpl], writes=[brl])
                            else:
                                S.op("dve", lambda e: e.tensor_scalar(out=rl_[:, 0:cw], in0=pl_[:, 0:cw], scalar1=0.0,
                                                                      scalar2=None, op0=ALU.max),
                                     reads=[bpl], writes=[brl])
                        if kind == "A" and 0 <= idx < len(items):
                            c0, cw, h = items[idx]
                            (rl_, brl) = slots.pop(idx)
                            S.op("pe", lambda e: e.matmul(pSc[:, 0:cw], lhsT=Dg[:, h, :], rhs=rl_[:, 0:cw],
                                                          start=(h == 0), stop=(h == 15)),
                                 reads=[b_Dg, brl], writes=[b_pSc])
                            if h == 15:
                                S.op("act", lambda e: e.copy(out=score[:, c0:c0 + cw], in_=pSc[:, 0:cw]),
                                     reads=[b_pSc], writes=[b_score])
                    g_ = lambda n: sm[n][0]
                    bb = lambda n: sm[n][1]
                    S.op("dve", lambda e: e.tensor_reduce(out=g_("B")[:], in_=score[:, 0:NJ], axis=AX.X, op=ALU.max,
                                                          apply_absolute_value=True), reads=[b_score], writes=[bb("B")])
                    S.op("dve", lambda e: e.tensor_scalar(out=g_("hi")[:], in0=g_("B")[:], scalar1=1.001, scalar2=1e-6,
                                                          op0=ALU.mult, op1=ALU.add), reads=[bb("B")], writes=[bb("hi")])
                    S.op("dve", lambda e: e.tensor_scalar(out=g_("lo")[:], in0=g_("hi")[:], scalar1=-1.0, scalar2=None,
                                                          op0=ALU.mult), reads=[bb("hi")], writes=[bb("lo")])
                    S.op("dve", lambda e: e.tensor_tensor(out=score[:, NJ - 1024:NJ], in0=score[:, NJ - 1024:NJ],
                                                          in1=AM[:], op=ALU.add), reads=[b_score, b_AM], writes=[b_score])
                    S.op("dve", lambda e: e.tensor_tensor(out=g_("d2")[:], in0=g_("hi")[:], in1=g_("lo")[:],
                                                          op=ALU.subtract), reads=[bb("hi"), bb("lo")], writes=[bb("d2")])
                    ND = (NJ * 9 // 20) // 16 * 16
                    NA = NJ - ND
                    for it in range(24):
                        cit = 0.5 ** (it + 1)
                        S.op("dve", lambda e: e.tensor_scalar(out=g_("mid")[:], in0=g_("d2")[:], scalar1=cit,
                                                              scalar2=g_("lo")[:, 0:1], op0=ALU.mult, op1=ALU.add),
                             reads=[bb("d2"), bb("lo")], writes=[bb("mid")])
                        S.op("act", lambda e: e.activation(out=ajunk[:, 0:NA], in_=score[:, ND:NJ], func=AF.Sign,
                                                           scale=-1.0, bias=g_("mid")[:, 0:1], accum_out=g_("sga")[:]),
                             reads=[b_score, bb("mid")], writes=[b_ajunk, bb("sga")])
                        S.op("dve", lambda e: e.tensor_scalar(out=cjunk[:, 0:ND], in0=score[:, 0:ND],
                                                              scalar1=g_("mid")[:, 0:1], scalar2=None, op0=ALU.is_ge,
                                                              op1=ALU.add, accum_out=g_("cnt")[:]),
                             reads=[b_score, bb("mid")], writes=[b_cjunk, bb("cnt")])
                        S.op("dve", lambda e: e.scalar_tensor_tensor(out=g_("cnt")[:], in0=g_("cnt")[:], scalar=2.0,
                                                                     in1=g_("sga")[:], op0=ALU.mult, op1=ALU.subtract),
                             reads=[bb("cnt"), bb("sga")], writes=[bb("cnt")])
                        S.op("dve", lambda e: e.tensor_scalar(out=g_("ge")[:], in0=g_("cnt")[:], scalar1=float(511 - NA),
                                                              scalar2=None, op0=ALU.is_ge), reads=[bb("cnt")],
                             writes=[bb("ge")])
                        S.op("dve", lambda e: e.tensor_scalar(out=g_("d1")[:], in0=g_("mid")[:], scalar1=g_("lo")[:, 0:1],
                                                              scalar2=g_("ge")[:, 0:1], op0=ALU.subtract, op1=ALU.mult),
                             reads=[bb("mid"), bb("lo"), bb("ge")], writes=[bb("d1")])
                        S.op("dve", lambda e: e.tensor_tensor(out=g_("lo")[:], in0=g_("lo")[:], in1=g_("d1")[:],
                                                              op=ALU.add), reads=[bb("lo"), bb("d1")], writes=[bb("lo")])
                    S.op("dve", lambda e: e.tensor_scalar(out=MB[:, 0:NJ], in0=score[:, 0:NJ], scalar1=g_("lo")[:, 0:1],
                                                          scalar2=NEG, op0=ALU.is_lt, op1=ALU.mult),
                         reads=[b_score, bb("lo")], writes=[b_MB])
                    nkt = (NJ + 127) // 128
                    aitems = [(kt_, G) for kt_ in range(nkt) for G in range(2)]
                    chunkbuf = {}
                    pend = {}

                    def emit_pv(ii):
                        kt_, G = aitems[ii]
                        (pt_, bpt) = pend.pop(ii)
                        (Vc_, bVc) = chunkbuf[kt_ // 4][1]
                        q = kt_ % 4
                        kw = min(128, NJ - kt_ * 128)
                        first, last = (kt_ == 0), (kt_ == nkt - 1)
                        for g2 in range(2):
                            g = 2 * G + g2
                            S.op("pe", lambda e: e.matmul(pOA[G][0][:, g2 * 256:(g2 + 1) * 256],
                                                          lhsT=Vc_[0:kw, q, g * 128:(g + 1) * 128],
                                                          rhs=pt_[0:kw, g2 * 256:(g2 + 1) * 256],
                                                          start=(first and g2 == 0), stop=last, skip_group_check=True),
                                 reads=[bVc, bpt], writes=[pOA[G][1]])
                        S.op("pe", lambda e: e.matmul(pDn[G][0][:], lhsT=ones_b[0:kw, :], rhs=pt_[0:kw, :],
                                                      start=first, stop=last, skip_group_check=True),
                             reads=[b_ones, bpt], writes=[pDn[G][1]])

                    for ii, (kt_, G) in enumerate(aitems):
                        if kt_ % 4 == 0 and G == 0:
                            (KTc_, bKTc), (Vc_, bVc) = KTc[nkc % 2], Vc[nkc % 2]
                            nkc += 1
                            chunkbuf[kt_ // 4] = ((KTc_, bKTc), (Vc_, bVc))
                            k0 = kt_ * 128
                            kwid = min(512, NJ - k0)
                            S.dma("sp", KTc_[:, :, 0:kwid], KT_d[:, :, k0:k0 + kwid], reads=[B_KT], writes=[bKTc])
                            ntl = (kwid + 127) // 128
                            for q in range(ntl):
                                kw_ = min(128, kwid - q * 128)
                                S.dma("act", Vc_[0:kw_, q, :], V_d[k0 + q * 128:k0 + q * 128 + kw_, :], reads=[B_V],
                                      writes=[bVc])
                        (KTc_, bKTc) = chunkbuf[kt_ // 4][0]
                        q = kt_ % 4
                        kw = min(128, NJ - kt_ * 128)
                        ks = slice(kt_ * 128, kt_ * 128 + kw)
                        kl = slice(q * 128, q * 128 + kw)
                        (pl_, bpl) = pL[npl % 3]
                        npl += 1
                        (pt_, bpt) = PT[npt % NR]
                        npt += 1
                        S.op("pe", lambda e: e.matmul(pl_[0:kw, :], lhsT=MB[:, ks], rhs=I4[:], start=True, stop=False,
                                                      skip_group_check=True), reads=[b_MB, b_I4], writes=[bpl])
                        for g2 in range(2):
                            g = 2 * G + g2
                            S.op("pe", lambda e: e.matmul(
                                pl_[0:kw, g2 * 256:(g2 + 1) * 256], lhsT=KTc_[:, g, kl],
                                rhs=QTj[:].rearrange("p a b -> p (a b)")[:, 2 * g * 128:(2 * g + 2) * 128],
                                start=False, stop=True, skip_group_check=True),
                                 reads=[bKTc, b_QTj], writes=[bpl])
                        S.op("act", lambda e: e.activation(out=pt_[0:kw, :], in_=pl_[0:kw, :], func=AF.Exp,
                                                           scale=128.0 ** -0.5, bias=mq[0:kw, 0:1]),
                             reads=[bpl, b_mq], writes=[bpt])
                        pend[ii] = (pt_, bpt)
                        if ii >= 2:
                            emit_pv(ii - 2)
                    for ii in range(max(0, len(aitems) - 2), len(aitems)):
                        emit_pv(ii)
                    for G in range(2):
                        S.op("dve", lambda e: e.reciprocal(out=rden[:], in_=pDn[G][0][:]), reads=[pDn[G][1]],
                             writes=[b_rden])
                        S.op("dve", lambda e: e.tensor_tensor(out=oT[:], in0=pOA[G][0][:], in1=rden[:], op=ALU.mult),
                             reads=[pOA[G][1], b_rden], writes=[b_oT])
                        S.op("dve", lambda e: e.tensor_tensor(
                            out=yaT[:, 4 * G:4 * G + 4, :].rearrange("p a b -> p (a b)"), in0=oT[:],
                            in1=zaTj[:, 4 * G:4 * G + 4, :].rearrange("p a b -> p (a b)"), op=ALU.mult),
                             reads=[b_oT, b_zaTj], writes=[b_yaT])
                    S.dma("pool", yaT_d[:, :, j * 128:(j + 1) * 128], yaT[:], reads=[b_yaT], writes=[B_yaT])
                S.barrier()

        if "merge" in stages:
            with contextlib.ExitStack() as es0:
                mT_all, b_mT = C.sb(es0, "mT_all", [128, 8, 2048], BF16)
                wst = [C.sb(es0, f"wst{i}", [128, 2048], F32) for i in range(2)]
                nw = 0
                with contextlib.ExitStack() as es:
                    Wb = [C.sb(es, f"Wbr{i}", [128, 8, 2048], BF16) for i in range(2)]
                    gob, b_gob = C.sb(es, "gob", [128, 128], F32)
                    gocol, b_gocol = C.sb(es, "gocol", [128, 1], F32)
                    S.dma("sp", gob[:], go_in[:, :], writes=[b_gob])
                    S.op("dve", lambda e: e.tensor_tensor(out=gob[:], in0=gob[:], in1=identf[:], op=ALU.mult),
                         reads=[b_gob, b_identf], writes=[b_gob])
                    S.op("dve", lambda e: e.tensor_reduce(out=gocol[:], in_=gob[:], axis=AX.X, op=ALU.add),
                         reads=[b_gob], writes=[b_gocol])
                    for br in range(2):
                        for k in range(8):
                            (ws_, bws) = wst[nw % 2]
                            nw += 1
                            S.dma("sp", ws_[:], wbr_in[br, k * 128:(k + 1) * 128, :], writes=[bws])
                            if br == 1:
                                if k % 2:
                                    S.op("act", lambda e: e.activation(out=Wb[1][0][:, k, :], in_=ws_[:], func=AF.Copy,
                                                                       scale=gocol[:, 0:1]),
                                         reads=[bws, b_gocol], writes=[Wb[1][1]])
                                else:
                                    S.op("dve", lambda e: e.tensor_scalar(out=Wb[1][0][:, k, :], in0=ws_[:],
                                                                          scalar1=gocol[:, 0:1], scalar2=None,
                                                                          op0=ALU.mult),
                                         reads=[bws, b_gocol], writes=[Wb[1][1]])
                                continue
                            S.op("act" if k % 2 else "dve",
                                 (lambda e: e.copy(out=Wb[br][0][:, k, :], in_=ws_[:])) if k % 2 else
                                 (lambda e: e.tensor_copy(out=Wb[br][0][:, k, :], in_=ws_[:])),
                                 reads=[bws], writes=[Wb[br][1]])
                    yaTj, b_yaTj = C.sb(es, "yaTj", [128, 8, 128], BF16)
                    ybTj, b_ybTj = C.sb(es, "ybTj", [128, 8, 128], BF16)
                    zbTj, b_zbTj = C.sb(es, "zbTj", [128, 8, 128], BF16)
                    gts, b_gts = C.sb(es, "gts", [128, 4096], F32)
                    mg, b_mg = C.sb(es, "mg", [128, 2048], F32)
                    t2, b_t2 = C.sb(es, "t2", [128, 512], F32)
                    mgb, b_mgb = C.sb(es, "mgb", [128, 2048], BF16)
                    pP = [C.ps(es, f"pP{i}", [128, 512], F32) for i in range(4)]
                    pTm, b_pTm = C.ps(es, "pTm", [128, 2048], BF16)
                    npp = 0
                    for j in range(8):
                        s0 = 2 + 128 * j
                        S.dma("sp", yaTj[:], yaT_d[:, :, j * 128:(j + 1) * 128], reads=[B_yaT], writes=[b_yaTj])
                        S.dma("sp", ybTj[:], ybT_d[:, :, s0:s0 + 128], reads=[B_ybT], writes=[b_ybTj])
                        S.dma("sp", zbTj[:], zbT_d[:, :, s0:s0 + 128], reads=[B_zbT], writes=[b_zbTj])
                        S.dma("sp", gts[:], P_own[s0:s0 + 128, 4112:8208], reads=[B_P_own], writes=[b_gts])
                        S.op("dve", lambda e: e.tensor_tensor(out=ybTj[:], in0=ybTj[:], in1=zbTj[:], op=ALU.mult),
                             reads=[b_ybTj, b_zbTj], writes=[b_ybTj])
                        S.op("act", lambda e: e.activation(out=gts[:], in_=gts[:], func=AF.Sigmoid), reads=[b_gts],
                             writes=[b_gts])
                        for cb in range(4):
                            cs = slice(cb * 512, (cb + 1) * 512)
                            for br, (yT_, byT) in enumerate(((yaTj, b_yaTj), (ybTj, b_ybTj))):
                                (pp_, bpp) = pP[npp % 4]
                                npp += 1
                                for k in range(8):
                                    S.op("pe", lambda e: e.matmul(pp_[:], lhsT=yT_[:, k, :], rhs=Wb[br][0][:, k, cs],
                                                                  start=(k == 0), stop=(k == 7)),
                                         reads=[byT, Wb[br][1]], writes=[bpp])
                                if br == 0:
                                    S.op("dve", lambda e: e.tensor_tensor(out=mg[:, cs], in0=pp_[:], in1=gts[:, cs],
                                                                          op=ALU.mult), reads=[bpp, b_gts], writes=[b_mg])
                                else:
                                    S.op("dve", lambda e: e.tensor_tensor(
                                        out=t2[:], in0=pp_[:], in1=gts[:, 2048 + cb * 512:2048 + (cb + 1) * 512],
                                        op=ALU.mult), reads=[bpp, b_gts], writes=[b_t2])
                                    S.op("pool", lambda e: e.tensor_tensor(out=mgb[:, cs], in0=mg[:, cs], in1=t2[:],
                                                                           op=ALU.add), reads=[b_mg, b_t2], writes=[b_mgb])
                        for k in range(16):
                            S.op("pe", lambda e: e.transpose(out=pTm[:, k * 128:(k + 1) * 128],
                                                             in_=mgb[:, k * 128:(k + 1) * 128], identity=ident[:]),
                                 reads=[b_mgb, b_ident], writes=[b_pTm])
                        S.op("act", lambda e: e.copy(out=mT_all[:, j, 0:1024], in_=pTm[:, 0:1024]), reads=[b_pTm],
                             writes=[b_mT])
                        S.op("dve", lambda e: e.tensor_copy(out=mT_all[:, j, 1024:2048], in_=pTm[:, 1024:2048]),
                             reads=[b_pTm], writes=[b_mT])
                    S.barrier()
                with contextlib.ExitStack() as es:
                    Wo, b_Wo = C.sb(es, "Wo", [128, 16, 2048], BF16)
                    for k in range(16):
                        (ws_, bws) = wst[nw % 2]
                        nw += 1
                        S.dma("sp", ws_[:], wout_in[k * 128:(k + 1) * 128, :], writes=[bws])
                        S.op("act" if k % 2 else "dve",
                             (lambda e: e.copy(out=Wo[:, k, :], in_=ws_[:])) if k % 2 else
                             (lambda e: e.tensor_copy(out=Wo[:, k, :], in_=ws_[:])),
                             reads=[bws], writes=[b_Wo])
                    xo = [C.sb(es, f"xo{i}", [128, 2048], F32) for i in range(2)]
                    pP = [C.ps(es, f"pP2{i}", [128, 512], F32) for i in range(4)]
                    npp = 0
                    for j in range(8):
                        (xo_, bxo) = xo[j % 2]
                        S.dma("sp", xo_[:], x_own[j * 128:(j + 1) * 128, :], writes=[bxo])
                        for cb in range(4):
                            cs = slice(cb * 512, (cb + 1) * 512)
                            (pp_, bpp) = pP[npp % 4]
                            npp += 1
                            for k in range(16):
                                S.op("pe", lambda e: e.matmul(pp_[:], lhsT=mT_all[:, j, k * 128:(k + 1) * 128],
                                                              rhs=Wo[:, k, cs], start=(k == 0), stop=(k == 15)),
                                     reads=[b_mT, b_Wo], writes=[bpp])
                            S.op("dve", lambda e: e.tensor_tensor(out=xo_[:, cs], in0=xo_[:, cs], in1=pp_[:], op=ALU.add),
                                 reads=[bxo, bpp], writes=[bxo])
                        S.dma("pool", out_d[j * 128:(j + 1) * 128, :], xo_[:], reads=[bxo], writes=[B_out])
                    S.barrier()

        S.barrier(engines=("sp",))
    return nc


def host_inputs(x, meta_tokens, hgrn_lb_logits, norm_g, w_in, q_norm_g, k_norm_g, idx_k_norm_g,
                hgrn_out_norm_g, w_branch, w_out):
    x = np.asarray(x, np.float32)
    h_all = np.zeros((TP, D), np.float32)
    h_all[:NMETA] = np.asarray(meta_tokens, np.float32)
    h_all[NMETA:NMETA + SEQ] = x[0]
    common = {
        "h_all": h_all,
        "w_in": np.ascontiguousarray(np.asarray(w_in, np.float32)[0]),
        "norm_g": np.ascontiguousarray(np.asarray(norm_g, np.float32)[0]),
    }
    f32 = np.float32
    tile = lambda v, n: np.ascontiguousarray(np.tile(np.asarray(v, f32).reshape(1, -1), (n, 1)))
    common["gq_bc"] = tile(q_norm_g[0], 128)
    common["gk_bc"] = tile(k_norm_g[0], 128)
    common["gi_bc"] = tile(idx_k_norm_g[0], 128)
    common["go_bc"] = tile(hgrn_out_norm_g[0], 128)
    common["lb0"] = tile(np.asarray(hgrn_lb_logits)[0], 128)
    common["lb1"] = tile(np.asarray(hgrn_lb_logits)[1], 128)
    common["w_branch"] = np.ascontiguousarray(np.asarray(w_branch, f32)[0])
    common["w_out"] = np.ascontiguousarray(np.asarray(w_out, f32)[0])

    def rope_tab(pos, rot, heads):
        inv = np.power(np.float32(500000.0), -np.arange(0, rot, 2, dtype=f32) / np.float32(rot)).astype(f32)
        ang = pos.astype(f32)[:, None] * inv[None, :]
        cos, sin = np.cos(ang).astype(f32), np.sin(ang).astype(f32)
        cs = np.stack([np.repeat(cos[:, None, :], heads, 1), np.repeat(sin[:, None, :], heads, 1)], 1)
        return np.ascontiguousarray(cs.reshape(len(pos), -1))
    pos_all = np.arange(TP)
    common["ropeK"] = rope_tab(pos_all, 32, 4)
    common["ropeKI"] = rope_tab(pos_all, 16, 1)
    si, ti = np.arange(128)[:, None], np.arange(128)[None, :]
    same = (si // 64) == (ti // 64)
    common["U"] = (same & (si <= ti)).astype(f32)
    common["Wm"] = (same & (si > ti)).astype(f32)
    common["chi"] = (np.arange(128)[:, None] // 64 == np.arange(2)[None, :]).astype(f32)
    m_, p_ = np.arange(128)[:, None], np.arange(1024)[None, :]
    common["AM"] = np.where((p_ // 64) <= (m_ // 8), 0.0, -1e9).astype(f32)
    maps = []
    for c in range(8):
        sel = np.zeros((128, 16), np.float32)
        sel[c + 8 * np.arange(16), np.arange(16)] = 1.0
        m = dict(common)
        m["sel"] = sel
        m["x_own"] = np.ascontiguousarray(x[0, c::8])
        slot = np.arange(NOWN)
        pos_own = np.where(slot < 1040, 8 * slot + c, 0)
        m["ropeQ"] = rope_tab(pos_own, 32, 8)
        m["ropeQI"] = rope_tab(pos_own, 16, 16)
        maps.append(m)
    return maps


def kernel(**inputs):
    maps = host_inputs(**inputs)
    nc = build_nc()
    res = run_bass_kernel_spmd(nc, maps, core_ids=list(range(8)))
    out = np.zeros((1, SEQ, D), np.float32)
    for c in range(8):
        out[0, c::8] = res.results[c]["out"]
    return out
```

```python
import contextlib
import numpy as np
import concourse.bass as bass
import concourse.mybir as mybir
from concourse.bass_utils import run_bass_kernel_spmd

F32 = mybir.dt.float32
BF16 = mybir.dt.bfloat16
AF = mybir.ActivationFunctionType
ALU = mybir.AluOpType
AX = mybir.AxisListType

D = 2048
SEQ = 8192
NMETA = 16
TP = 8320
NT = 65
NOWN = 1152
NIN = 12368
EPS = 1e-6
NEG = -30000.0

C_QA, C_KA, C_VA, C_ZA, C_QI, C_KI, C_WI, C_QB, C_FB, C_IB, C_ZB, C_G = (
    0, 1024, 1536, 2048, 3072, 4096, 4160, 4176, 5200, 6224, 7248, 8272)
PA_COLS = 4160
PA_BLOCKS = [(C_KA, 512, 0), (C_VA, 512, 512), (C_KI, 64, 1024)] + \
            [(C_QB + i * 512, 512, 1088 + i * 512) for i in range(6)]
PO_COLS = 8208
PO_BLOCKS = [(C_QA + i * 512, 512, i * 512) for i in range(2)] + \
            [(C_ZA + i * 512, 512, 1024 + i * 512) for i in range(4)] + \
            [(C_WI, 16, 3072)] + \
            [(C_ZB + i * 512, 512, 3088 + i * 512) for i in range(10)]


class Buf:
    __slots__ = ("name", "w", "r")

    def __init__(self, name):
        self.name = name
        self.w = None
        self.r = []


class Eng:
    def __init__(self, name, h, sem):
        self.name, self.h, self.sem = name, h, sem
        self.count = 0
        self.seen = {}

    def wait(self, ev):
        if ev is None:
            return
        sem, val = ev
        if self.seen.get(id(sem), 0) >= val:
            return
        if self.name == "pe" and sem is self.sem:
            return
        self.h.wait_ge(sem, val)
        self.seen[id(sem)] = val


class Sched:
    NDMA = 8

    def __init__(self, nc, sems):
        self.nc = nc
        self.free_sems = list(sems)
        self.E = {}
        for name, h in (("pe", nc.tensor), ("act", nc.scalar), ("dve", nc.vector),
                        ("pool", nc.gpsimd), ("sp", nc.sync)):
            self.E[name] = Eng(name, h, self.free_sems.pop())
        self.dq = {}
        for q in ("sp", "pool", "act"):
            self.dq[q] = {"sems": [self.free_sems.pop() for _ in range(self.NDMA)],
                          "n": [0] * self.NDMA, "i": 0}

    def _deps(self, eng, reads, writes):
        for b in reads:
            eng.wait(b.w)
        for b in writes:
            eng.wait(b.w)
            for ev in b.r:
                eng.wait(ev)

    def _commit(self, ev, reads, writes):
        for b in reads:
            b.r = [e for e in b.r if e[0] is not ev[0]] + [ev]
        for b in writes:
            b.w = ev
            b.r = []

    def op(self, ename, fn, reads=(), writes=()):
        eng = self.E[ename]
        self._deps(eng, reads, writes)
        ins = fn(eng.h)
        eng.count += 1
        ins.then_inc(eng.sem, 1)
        ev = (eng.sem, eng.count)
        self._commit(ev, reads, writes)
        return ev

    def dma(self, q, out, in_, reads=(), writes=(), **kw):
        eng = self.E[q]
        d = self.dq[q]
        k = d["i"] % self.NDMA
        d["i"] += 1
        sem = d["sems"][k]
        if d["n"][k] > 0:
            eng.wait((sem, 16 * d["n"][k]))
        self._deps(eng, reads, writes)
        ins = eng.h.dma_start(out=out, in_=in_, **kw)
        d["n"][k] += 1
        ins.then_inc(sem, 16)
        ev = (sem, 16 * d["n"][k])
        self._commit(ev, reads, writes)
        return ev

    def all_events(self):
        evs = []
        for e in self.E.values():
            if e.count:
                evs.append((e.sem, e.count))
        for d in self.dq.values():
            for s, n in zip(d["sems"], d["n"]):
                if n:
                    evs.append((s, 16 * n))
        return evs

    def barrier(self, engines=None):
        evs = self.all_events()
        for name, e in self.E.items():
            if engines is not None and name not in engines:
                continue
            for ev in evs:
                e.wait(ev)


class Ctx:
    def __init__(self, nc, S):
        self.nc, self.S = nc, S

    n = 0

    def sb(self, es, name, shape, dt):
        Ctx.n += 1
        name = f"t{Ctx.n}_{name}"
        t = es.enter_context(self.nc.sbuf_tensor(name, list(shape), dt))
        return t, Buf(name)

    def ps(self, es, name, shape, dt):
        Ctx.n += 1
        name = f"t{Ctx.n}_{name}"
        n = int(np.prod(shape[1:]))
        be = 2048 // (4 if dt == F32 else 2)
        nb = -(-n // be)
        flat = es.enter_context(self.nc.psum_tensor(name, [shape[0], nb * be], dt))
        v = flat[:, 0:n]
        if len(shape) == 3:
            v = v.rearrange("p (a b) -> p a b", a=shape[1])
        elif len(shape) == 4:
            v = v.rearrange("p (a b c) -> p a b c", a=shape[1], b=shape[2])
        return v, Buf(name)


def build_nc(stages=("norm", "gemm", "post", "hgrn", "attn", "merge"), debug_out=()):
    nc = bass.Bass("TRN2", target_bir_lowering=False)
    dt_in = lambda n, s, d=F32: nc.dram_tensor(n, list(s), d, kind="ExternalInput")
    h_all = dt_in("h_all", [TP, D])
    x_own = dt_in("x_own", [1024, D])
    sel_in = dt_in("sel", [128, 16])
    w_in = dt_in("w_in", [D, NIN])
    norm_g = dt_in("norm_g", [D])
    out_d = nc.dram_tensor("out", [1024, D], F32, kind="ExternalOutput")
    gq_in = dt_in("gq_bc", [128, 128]); gk_in = dt_in("gk_bc", [128, 128]); gi_in = dt_in("gi_bc", [128, 64])
    go_in = dt_in("go_bc", [128, 128])
    ropeK_in = dt_in("ropeK", [TP, 128]); ropeKI_in = dt_in("ropeKI", [TP, 16])
    ropeQ_in = dt_in("ropeQ", [NOWN, 256]); ropeQI_in = dt_in("ropeQI", [NOWN, 256])
    U_in = dt_in("U", [128, 128]); W_in = dt_in("Wm", [128, 128]); chi_in = dt_in("chi", [128, 2])
    lb0_in = dt_in("lb0", [128, 1024]); lb1_in = dt_in("lb1", [128, 1024])
    AM_in = dt_in("AM", [128, 1024])
    wbr_in = dt_in("w_branch", [2, 1024, D]); wout_in = dt_in("w_out", [D, D])

    hT_all = nc.dram_tensor("hT_all", [NT, 128, 2048], BF16)
    hT_own = nc.dram_tensor("hT_own", [9, 128, 2048], BF16)
    kind_dbg = lambda n: "ExternalOutput" if n in debug_out else "Internal"
    P_all = nc.dram_tensor("P_all", [TP, PA_COLS], F32, kind=kind_dbg("P_all"))
    P_own = nc.dram_tensor("P_own", [NOWN, PO_COLS], F32, kind=kind_dbg("P_own"))

    KT_d = nc.dram_tensor("KT_d", [128, 4, TP], BF16)
    V_d = nc.dram_tensor("V_d", [TP, 512], BF16)
    kiT_d = nc.dram_tensor("kiT_d", [128, TP], BF16)
    QT_d = nc.dram_tensor("QT_d", [128, 8, NOWN], BF16)
    zaT_d = nc.dram_tensor("zaT_d", [128, 8, NOWN], BF16)
    zbT_d = nc.dram_tensor("zbT_d", [128, 8, NOWN], BF16)
    qiT_d = nc.dram_tensor("qiT_d", [128, 8, NOWN], BF16)
    w_d = nc.dram_tensor("w_d", [NOWN, 16], F32)
    ybT_d = nc.dram_tensor("ybT_d", [128, 8, NOWN], BF16, kind=kind_dbg("ybT_d"))
    yaT_d = nc.dram_tensor("yaT_d", [128, 8, 1024], BF16, kind=kind_dbg("yaT_d"))
    B_KT, B_V, B_kiT, B_QT, B_zaT, B_zbT, B_qiT, B_w, B_ybT, B_yaT = [Buf(n) for n in
        "KT V kiT QT zaT zbT qiT w ybT yaT".split()]

    with contextlib.ExitStack() as top:
        sems = [top.enter_context(nc.semaphore(f"s{i}")) for i in range(5 + 3 * Sched.NDMA + 2)]
        S = Sched(nc, sems)
        C = Ctx(nc, S)
        B_hT_all, B_hT_own, B_P_all, B_P_own, B_out = (Buf("hT_all"), Buf("hT_own"), Buf("P_all"),
                                                       Buf("P_own"), Buf("out"))

        identf, b_identf = C.sb(top, "identf", [128, 128], F32)
        ident, b_ident = C.sb(top, "ident", [128, 128], BF16)
        self_f, b_self_f = C.sb(top, "self_f", [128, 16], F32)
        selb, b_selb = C.sb(top, "selb", [128, 16], BF16)
        gcol, b_gcol = C.sb(top, "gcol", [128, 16], F32)
        S.op("pool", lambda e: e.memset(identf[:], 0.0), writes=[b_identf])
        S.op("pool", lambda e: e.affine_select(out=identf[:], in_=identf[:], pattern=[[-1, 128]],
                                                compare_op=ALU.not_equal, fill=1.0, base=0,
                                                channel_multiplier=1), reads=[b_identf], writes=[b_identf])
        S.op("dve", lambda e: e.tensor_copy(out=ident[:], in_=identf[:]), reads=[b_identf], writes=[b_ident])
        S.dma("sp", self_f[:], sel_in[:, :], writes=[b_self_f])
        S.op("dve", lambda e: e.tensor_copy(out=selb[:], in_=self_f[:]), reads=[b_self_f], writes=[b_selb])
        S.dma("sp", gcol[:], norm_g.ap().rearrange("(k p) -> p k", p=128), writes=[b_gcol],
              allow_slow_non_contiguous=True)

        if "norm" in stages:
            with contextlib.ExitStack() as es:
                NB = 2
                xt = [C.sb(es, f"xt{i}", [128, D], F32) for i in range(NB)]
                hn = [C.sb(es, f"hn{i}", [128, D], BF16) for i in range(NB)]
                hT = [C.sb(es, f"hT{i}", [128, D], BF16) for i in range(NB)]
                junk, b_junk = C.sb(es, "junk", [128, D], BF16)
                ss = [C.sb(es, f"ss{i}", [128, 1], F32) for i in range(NB)]
                rs = [C.sb(es, f"rs{i}", [128, 1], F32) for i in range(NB)]
                hown, b_hown = C.sb(es, "hown", [128, 9, 16, 128], BF16)
                pT = [C.ps(es, f"pT{i}", [128, D], BF16) for i in range(NB)]
                pO = [C.ps(es, f"pO{i}", [128, 16, 16], F32) for i in range(NB)]
                S.op("pool", lambda e: e.memset(hown[:], 0.0), writes=[b_hown])
                for t in range(NT):
                    i = t % NB
                    (x_, bx), (hn_, bhn), (hT_, bhT), (ss_, bss), (rs_, brs) = xt[i], hn[i], hT[i], ss[i], rs[i]
                    (pT_, bpT), (pO_, bpO) = pT[i], pO[i]
                    S.dma("sp", x_[:], h_all[t * 128:(t + 1) * 128, :], writes=[bx])
                    S.op("act", lambda e: e.activation(out=junk[:], in_=x_[:], func=AF.Square, accum_out=ss_[:]),
                         reads=[bx], writes=[b_junk, bss])
                    S.op("act", lambda e: e.activation(out=rs_[:], in_=ss_[:], func=AF.Sqrt, scale=1.0 / D,
                                                       bias=EPS), reads=[bss], writes=[brs])
                    S.op("dve", lambda e: e.reciprocal(out=rs_[:], in_=rs_[:]), reads=[brs], writes=[brs])
                    S.op("dve", lambda e: e.tensor_scalar(out=hn_[:], in0=x_[:], scalar1=rs_[:, 0:1], scalar2=None,
                                                          op0=ALU.mult), reads=[bx, brs], writes=[bhn])
                    for k in range(16):
                        S.op("pe", lambda e: e.transpose(out=pT_[:, k * 128:(k + 1) * 128],
                                                         in_=hn_[:, k * 128:(k + 1) * 128], identity=ident[:]),
                             reads=[bhn, b_ident], writes=[bpT])
                    S.op("act", lambda e: e.copy(out=hT_[:, 0:1024], in_=pT_[:, 0:1024]), reads=[bpT], writes=[bhT])
                    S.op("dve", lambda e: e.tensor_copy(out=hT_[:, 1024:2048], in_=pT_[:, 1024:2048]),
                         reads=[bpT], writes=[bhT])
                    S.dma("pool", hT_all[t], hT_[:], reads=[bhT], writes=[B_hT_all])
                    for k in range(16):
                        S.op("pe", lambda e: e.matmul(pO_[:, k, :], lhsT=hn_[:, k * 128:(k + 1) * 128], rhs=selb[:],
                                                      start=True, stop=True),
                             reads=[bhn, b_selb], writes=[bpO])
                    S.op("dve", lambda e: e.tensor_copy(out=hown[:, t // 8, :, (t % 8) * 16:(t % 8) * 16 + 16],
                                                        in_=pO_[:]), reads=[bpO], writes=[b_hown])
                for u in range(9):
                    S.dma("sp", hT_own[u].rearrange("p (k t) -> p k t", k=16), hown[:, u, :, :],
                          reads=[b_hown], writes=[B_hT_own])
                S.barrier()

        if "gemm" in stages:
            with contextlib.ExitStack() as es:
                wf = [C.sb(es, f"wf{i}", [128, 16, 512], F32) for i in range(2)]
                wb = [C.sb(es, f"wb{i}", [128, 16, 512], BF16) for i in range(2)]
                NH = 6
                hT = [C.sb(es, f"ghT{i}", [128, 16, 128], BF16) for i in range(NH)]
                ob = [C.sb(es, f"ob{i}", [128, 512], F32) for i in range(4)]
                pp = [C.ps(es, f"pp{i}", [128, 512], F32) for i in range(4)]
                w3 = w_in.ap().rearrange("(k p) c -> p k c", p=128)
                blocks = []
                for (src, ntiles, blks, dst, bdst, bsrc) in ((hT_all, NT, PA_BLOCKS, P_all, B_P_all, B_hT_all),
                                                             (hT_own, 9, PO_BLOCKS, P_own, B_P_own, B_hT_own)):
                    for (c0, wdt, d0) in blks:
                        blocks.append((src, ntiles, dst, bdst, bsrc, c0, wdt, d0))

                def load_w(bi):
                    (_, _, _, _, _, c0, wdt, _) = blocks[bi]
                    (wf_, bwf), (wb_, bwb) = wf[bi % 2], wb[bi % 2]
                    S.dma("pool", wf_[:, 0:8, 0:wdt], w3[:, 0:8, c0:c0 + wdt], writes=[bwf])
                    S.dma("sp", wf_[:, 8:16, 0:wdt], w3[:, 8:16, c0:c0 + wdt], writes=[bwf])
                    for k in range(16):
                        S.op("dve", lambda e: e.tensor_scalar(out=wb_[:, k, 0:wdt], in0=wf_[:, k, 0:wdt],
                                                              scalar1=gcol[:, k:k + 1], scalar2=None, op0=ALU.mult),
                             reads=[bwf, b_gcol], writes=[bwb])

                nt = 0
                load_w(0)
                for bi, (src, ntiles, dst, bdst, bsrc, c0, wdt, d0) in enumerate(blocks):
                    if bi + 1 < len(blocks):
                        load_w(bi + 1)
                    (wb_, bwb) = wb[bi % 2]
                    for t in range(ntiles):
                        (h_, bh), (o_, bo), (p_, bp) = hT[nt % NH], ob[nt % 4], pp[nt % 4]
                        nt += 1
                        S.dma("sp", h_[:], src[t].rearrange("p (k t) -> p k t", k=16), reads=[bsrc], writes=[bh])
                        for k in range(16):
                            S.op("pe", lambda e: e.matmul(p_[:, 0:wdt], lhsT=h_[:, k, :], rhs=wb_[:, k, 0:wdt],
                                                          start=(k == 0), stop=(k == 15)),
                                 reads=[bh, bwb], writes=[bp])
                        S.op("act", lambda e: e.copy(out=o_[:, 0:wdt], in_=p_[:, 0:wdt]), reads=[bp], writes=[bo])
                        S.dma("pool", dst[t * 128:(t + 1) * 128, d0:d0 + wdt], o_[:, 0:wdt], reads=[bo], writes=[bdst])
                S.barrier()

        if "post" in stages:
            with contextlib.ExitStack() as es:
                gq_bc, b_gq = C.sb(es, "gq_sb", [128, 128], F32)
                gk_bc, b_gk = C.sb(es, "gk_sb", [128, 128], F32)
                gi_bc, b_gi = C.sb(es, "gi_sb", [128, 64], F32)
                S.dma("sp", gq_bc[:], gq_in[:, :], writes=[b_gq])
                S.dma("sp", gk_bc[:], gk_in[:, :], writes=[b_gk])
                S.dma("sp", gi_bc[:], gi_in[:, :], writes=[b_gi])
                junks = [C.sb(es, f"pjunk{i}", [128, 128], F32) for i in range(3)]
                NB = 3
                pa = [C.sb(es, f"pa{i}", [128, 1088], F32) for i in range(NB)]
                csk = [C.sb(es, f"csk{i}", [128, 2, 8, 16], F32) for i in range(NB)]
                csi = [C.sb(es, f"csi{i}", [128, 2, 16, 8], F32) for i in range(NB)]
                ssq = [C.sb(es, f"ssq{i}", [128, 8], F32) for i in range(NB)]
                kn = [C.sb(es, f"kn{i}", [128, 8, 128], F32) for i in range(NB)]
                tt = [C.sb(es, f"tt{i}", [128, 4, 8, 16], F32) for i in range(NB)]
                kr = [C.sb(es, f"kr{i}", [128, 8, 128], BF16) for i in range(NB)]
                kT = [C.sb(es, f"kT{i}", [128, 8, 128], BF16) for i in range(NB)]
                vb = [C.sb(es, f"vb{i}", [128, 512], BF16) for i in range(NB)]
                kib = [C.sb(es, f"kib{i}", [128, 128], BF16) for i in range(NB)]
                kiT = [C.sb(es, f"kiT{i}", [128, 128], BF16) for i in range(NB)]
                pT = [C.ps(es, f"ppT{i}", [128, 8, 128], BF16) for i in range(NB)]
                pI = [C.ps(es, f"ppI{i}", [128, 128], BF16) for i in range(NB)]
                po_ = [C.sb(es, f"po{i}", [128, 4112], F32) for i in range(NB)]
                zs = [C.sb(es, f"zs{i}", [128, 1024], F32) for i in range(NB)]
                zb_ = [C.sb(es, f"zbb{i}", [128, 8, 128], BF16) for i in range(NB)]
                wv = [C.sb(es, f"wv{i}", [128, 16], F32) for i in range(NB)]

                def headnorm_rope(i, src3, H, hd, g_bc, cs_tile, half, b_src, b_cs, do_norm=True):
                    (ss_, bss), (kn_, bkn), (tt_, btt), (kr_, bkr) = ssq[i], kn[i], tt[i], kr[i]
                    (junk, b_junk) = junks[i]
                    if hd == 128:
                        knv = kn_[:, 0:H, :]
                        krv = kr_[:, 0:H, :]
                    else:
                        knv = kn_[:].rearrange("p a b -> p (a b)")[:, 0:H * hd].rearrange("p (h d) -> p h d", h=H)
                        krv = kr_[:].rearrange("p a b -> p (a b)")[:, 0:H * hd].rearrange("p (h d) -> p h d", h=H)
                    if do_norm:
                        for h in range(H):
                            S.op("act", lambda e: e.activation(out=junk[:, 0:hd], in_=src3[:, h, :], func=AF.Square,
                                                               accum_out=ss_[:, h:h + 1]),
                                 reads=[b_src], writes=[b_junk, bss])
                        S.op("act", lambda e: e.activation(out=ss_[:, 0:H], in_=ss_[:, 0:H], func=AF.Sqrt,
                                                           scale=1.0 / hd, bias=EPS), reads=[bss], writes=[bss])
                        yield
                        S.op("dve", lambda e: e.reciprocal(out=ss_[:, 0:H], in_=ss_[:, 0:H]), reads=[bss], writes=[bss])
                        for h in range(H):
                            S.op("dve", lambda e: e.scalar_tensor_tensor(out=knv[:, h, :], in0=src3[:, h, :],
                                                                         scalar=ss_[:, h:h + 1], in1=g_bc[:, 0:hd],
                                                                         op0=ALU.mult, op1=ALU.mult),
                                 reads=[b_src, bss], writes=[bkn])
                        srcn, bsn = knv, bkn
                    else:
                        srcn, bsn = src3, b_src
                    if hd == 128:
                        cosv, sinv = cs_tile[:, 0, 0:H, :], cs_tile[:, 1, 0:H, :]
                        tv = [tt_[:, q, 0:H, :] for q in range(4)]
                    else:
                        cosv, sinv = cs_tile[:, 0, 0:H, :], cs_tile[:, 1, 0:H, :]
                        tflat = tt_[:].rearrange("p a b c -> p a (b c)")
                        tv = [tflat[:, q, 0:H * half].rearrange("p (h d) -> p h d", h=H) for q in range(4)]
                    yield
                    x1, x2 = srcn[:, :, 0:half], srcn[:, :, half:2 * half]
                    S.op("dve", lambda e: e.tensor_tensor(out=tv[0], in0=x1, in1=cosv, op=ALU.mult),
                         reads=[bsn, b_cs], writes=[btt])
                    S.op("pool", lambda e: e.tensor_tensor(out=tv[1], in0=x2, in1=sinv, op=ALU.mult),
                         reads=[bsn, b_cs], writes=[btt])
                    S.op("dve", lambda e: e.tensor_tensor(out=tv[2], in0=x2, in1=cosv, op=ALU.mult),
                         reads=[bsn, b_cs], writes=[btt])
                    S.op("pool", lambda e: e.tensor_tensor(out=tv[3], in0=x1, in1=sinv, op=ALU.mult),
                         reads=[bsn, b_cs], writes=[btt])
                    S.op("act", lambda e: e.copy(out=krv, in_=srcn), reads=[bsn], writes=[bkr])
                    yield
                    S.op("dve", lambda e: e.tensor_tensor(out=krv[:, :, 0:half], in0=tv[0], in1=tv[1], op=ALU.subtract),
                         reads=[btt], writes=[bkr])
                    S.op("dve", lambda e: e.tensor_tensor(out=krv[:, :, half:2 * half], in0=tv[2], in1=tv[3], op=ALU.add),
                         reads=[btt], writes=[bkr])
                    yield
                    return krv, bkr

                def all_tile(t):
                    i = t % NB
                    (pa_, bpa), (csk_, bcsk), (csi_, bcsi) = pa[i], csk[i], csi[i]
                    rows = slice(t * 128, (t + 1) * 128)
                    S.dma("sp", pa_[:], P_all[rows, 0:1088], reads=[B_P_all], writes=[bpa])
                    S.dma("sp", csk_[:, :, 0:4, :], ropeK_in[rows].rearrange("p (a h d) -> p a h d", a=2, h=4),
                          writes=[bcsk])
                    S.dma("sp", csi_[:, :, 0:1, :], ropeKI_in[rows].rearrange("p (a h d) -> p a h d", a=2, h=1),
                          writes=[bcsi])
                    k3 = pa_[:, 0:512].rearrange("p (h d) -> p h d", h=4)
                    krv, bkr = yield from headnorm_rope(i, k3, 4, 128, gk_bc, csk_, 16, bpa, bcsk)
                    (pT_, bpT), (kT_, bkT) = pT[i], kT[i]
                    for g in range(4):
                        S.op("pe", lambda e: e.transpose(out=pT_[:, g, :], in_=krv[:, g, :], identity=ident[:]),
                             reads=[bkr, b_ident], writes=[bpT])
                    S.op("act", lambda e: e.copy(out=kT_[:, 0:4, :], in_=pT_[:, 0:4, :]), reads=[bpT], writes=[bkT])
                    S.dma("pool", KT_d[:, :, rows], kT_[:, 0:4, :], reads=[bkT], writes=[B_KT])
                    (vb_, bvb) = vb[i]
                    S.op("pool", lambda e: e.tensor_copy(out=vb_[:], in_=pa_[:, 512:1024]), reads=[bpa], writes=[bvb])
                    S.dma("pool", V_d[rows, :], vb_[:], reads=[bvb], writes=[B_V])
                    ki3 = pa_[:, 1024:1088].rearrange("p (h d) -> p h d", h=1)
                    yield
                    kiv, bkr = yield from headnorm_rope(i, ki3, 1, 64, gi_bc, csi_, 8, bpa, bcsi)
                    (kib_, bkib), (pI_, bpI), (kiT_, bkiT) = kib[i], pI[i], kiT[i]
                    S.op("dve", lambda e: e.tensor_copy(out=kib_[:, 0:64], in_=kiv[:, 0, :]), reads=[bkr], writes=[bkib])
                    S.op("pool", lambda e: e.tensor_copy(out=kib_[:, 64:128], in_=kiv[:, 0, :]), reads=[bkr], writes=[bkib])
                    S.op("pe", lambda e: e.transpose(out=pI_[:], in_=kib_[:], identity=ident[:]),
                         reads=[bkib, b_ident], writes=[bpI])
                    S.op("act", lambda e: e.copy(out=kiT_[:], in_=pI_[:]), reads=[bpI], writes=[bkiT])
                    S.dma("pool", kiT_d[:, rows], kiT_[:], reads=[bkiT], writes=[B_kiT])

                def run_interleaved(gens, width):
                    active = []
                    gens = list(gens)
                    while gens or active:
                        while gens and len(active) < width:
                            active.append(gens.pop(0))
                        for g in list(active):
                            try:
                                next(g)
                            except StopIteration:
                                active.remove(g)

                run_interleaved([all_tile(t) for t in range(NT)], 3)

                def own_tile(u):
                    i = u % NB
                    (po__, bpo), (csk_, bcsk), (csi_, bcsi) = po_[i], csk[i], csi[i]
                    rows = slice(u * 128, (u + 1) * 128)
                    S.dma("sp", po__[:], P_own[rows, 0:4112], reads=[B_P_own], writes=[bpo])
                    S.dma("sp", csk_[:], ropeQ_in[rows].rearrange("p (a h d) -> p a h d", a=2, h=8), writes=[bcsk])
                    S.dma("sp", csi_[:], ropeQI_in[rows].rearrange("p (a h d) -> p a h d", a=2, h=16), writes=[bcsi])
                    (pT_, bpT), (kT_, bkT) = pT[i], kT[i]
                    q3 = po__[:, 0:1024].rearrange("p (h d) -> p h d", h=8)
                    krv, bkr = yield from headnorm_rope(i, q3, 8, 128, gq_bc, csk_, 16, bpo, bcsk)
                    for h in range(8):
                        S.op("pe", lambda e: e.transpose(out=pT_[:, h, :], in_=krv[:, h, :], identity=ident[:]),
                             reads=[bkr, b_ident], writes=[bpT])
                    S.op("act", lambda e: e.copy(out=kT_[:], in_=pT_[:]), reads=[bpT], writes=[bkT])
                    S.dma("pool", QT_d[:, :, rows], kT_[:], reads=[bkT], writes=[B_QT])
                    yield
                    for (c0, dstd, bdst) in ((1024, zaT_d, B_zaT), (3088, zbT_d, B_zbT)):
                        (zs_, bzs), (zb__, bzb) = zs[i], zb_[i]
                        S.op("act", lambda e: e.activation(out=zs_[:], in_=po__[:, c0:c0 + 1024], func=AF.Sigmoid),
                             reads=[bpo], writes=[bzs])
                        S.op("dve", lambda e: e.tensor_tensor(out=zb__[:].rearrange("p a b -> p (a b)"), in0=zs_[:],
                                                              in1=po__[:, c0:c0 + 1024], op=ALU.mult),
                             reads=[bzs, bpo], writes=[bzb])
                        for h in range(8):
                            S.op("pe", lambda e: e.transpose(out=pT_[:, h, :], in_=zb__[:, h, :], identity=ident[:]),
                                 reads=[bzb, b_ident], writes=[bpT])
                        S.op("act", lambda e: e.copy(out=kT_[:], in_=pT_[:]), reads=[bpT], writes=[bkT])
                        S.dma("pool", dstd[:, :, rows], kT_[:], reads=[bkT], writes=[bdst])
                    qi3 = po__[:, 2048:3072].rearrange("p (h d) -> p h d", h=16)
                    yield
                    qv, bkr = yield from headnorm_rope(i, qi3, 16, 64, None, csi_, 8, bpo, bcsi, do_norm=False)
                    qflat = kr[i][0][:]
                    for h in range(8):
                        S.op("pe", lambda e: e.transpose(out=pT_[:, h, :], in_=qflat[:, h, :], identity=ident[:]),
                             reads=[bkr, b_ident], writes=[bpT])
                    S.op("act", lambda e: e.copy(out=kT_[:], in_=pT_[:]), reads=[bpT], writes=[bkT])
                    S.dma("pool", qiT_d[:, :, rows], kT_[:], reads=[bkT], writes=[B_qiT])
                    (wv_, bwv) = wv[i]
                    S.op("dve", lambda e: e.tensor_scalar(out=wv_[:], in0=po__[:, 3072:3088], scalar1=1.0 / 32.0,
                                                          scalar2=None, op0=ALU.mult), reads=[bpo], writes=[bwv])
                    S.dma("pool", w_d[rows, :], wv_[:], reads=[bwv], writes=[B_w])
                    yield

                run_interleaved([own_tile(u) for u in range(9)], 3)
                S.barrier()

        if "hgrn" in stages:
            with contextlib.ExitStack() as es:
                Uf, b_U = C.sb(es, "Uf", [128, 128], F32)
                Wf, b_W = C.sb(es, "Wf", [128, 128], F32)
                chi, b_chi = C.sb(es, "chi", [128, 2], F32)
                lb, b_lb = C.sb(es, "lb_sb", [128, 1024], F32)
                oml, b_oml = C.sb(es, "oml", [128, 1024], F32)
                l1, b_l1 = C.sb(es, "l1", [128, 1024], F32)
                go_bc, b_go = C.sb(es, "go_sb", [128, 128], F32)
                S.dma("sp", Uf[:], U_in[:, :], writes=[b_U])
                S.dma("sp", Wf[:], W_in[:, :], writes=[b_W])
                S.dma("sp", chi[:], chi_in[:, :], writes=[b_chi])
                S.dma("sp", go_bc[:], go_in[:, :], writes=[b_go])
                S.dma("sp", lb[:], lb0_in[:, :], writes=[b_lb])
                S.dma("sp", l1[:], lb1_in[:, :], writes=[b_l1])
                S.op("dve", lambda e: e.tensor_tensor(out=lb[:], in0=lb[:], in1=l1[:], op=ALU.subtract),
                     reads=[b_lb, b_l1], writes=[b_lb])
                S.op("act", lambda e: e.activation(out=lb[:], in_=lb[:], func=AF.Sigmoid), reads=[b_lb], writes=[b_lb])
                S.op("dve", lambda e: e.tensor_scalar(out=oml[:], in0=lb[:], scalar1=-1.0, scalar2=1.0, op0=ALU.mult,
                                                      op1=ALU.add), reads=[b_lb], writes=[b_oml])
                Sf, b_Sf = C.sb(es, "Sf", [128, 8, 128], F32)
                S0b, b_S0b = C.sb(es, "S0b", [128, 8, 128], BF16)
                ybT, b_ybT = C.sb(es, "ybT", [128, 8, NOWN], BF16)
                S.op("pool", lambda e: e.memset(Sf[:], 0.0), writes=[b_Sf])
                S.op("pool", lambda e: e.memset(S0b[:], 0.0), writes=[b_S0b])
                S.op("pool", lambda e: e.memset(ybT[:], 0.0), writes=[b_ybT])
                NB = 2
                qb = [C.sb(es, f"qb{i}", [128, 1024], F32) for i in range(NB)]
                fb = [C.sb(es, f"fb{i}", [128, 1024], F32) for i in range(NB)]
                ib = [C.sb(es, f"ib{i}", [128, 1024], F32) for i in range(NB)]
                gg = [C.sb(es, f"gg{i}", [128, 1024], F32) for i in range(NB)]
                kk = [C.sb(es, f"kk{i}", [128, 1024], F32) for i in range(NB)]
                e1 = [C.sb(es, f"e1{i}", [128, 512], F32) for i in range(NB)]
                e2 = [C.sb(es, f"e2{i}", [128, 512], F32) for i in range(NB)]
                e3 = [C.sb(es, f"e3{i}", [128, 512], F32) for i in range(NB)]
                qt = [C.sb(es, f"qt{i}", [128, 1024], BF16) for i in range(NB)]
                kt = [C.sb(es, f"kt{i}", [128, 1024], BF16) for i in range(NB)]
                kd = [C.sb(es, f"kd{i}", [128, 1024], BF16) for i in range(NB)]
                vv = [C.sb(es, f"vv{i}", [128, 1024], BF16) for i in range(NB)]
                ebl = [C.sb(es, f"ebl{i}", [128, 16], F32) for i in range(NB)]
                qkT8 = [C.sb(es, f"qkT8{i}", [128, 8, 256], BF16) for i in range(2)]
                AT8, b_AT8 = C.sb(es, "AT8", [128, 8, 128], BF16)
                S1f8, b_S1f8 = C.sb(es, "S1f8", [128, 8, 128], F32)
                S1b8, b_S1b8 = C.sb(es, "S1b8", [128, 8, 128], BF16)
                S0b2 = [(S0b, b_S0b), C.sb(es, "S0b_1", [128, 8, 128], BF16)]
                on8, b_on8 = C.sb(es, "on8", [128, 8, 128], BF16)
                sq8, b_sq8 = C.sb(es, "sq8", [128, 8, 128], F32)
                ss8, b_ss8 = C.sb(es, "ss8", [128, 8], F32)
                U8, b_U8 = C.sb(es, "U8", [128, 8, 128], F32)
                for h in range(8):
                    S.op("dve", lambda e: e.tensor_copy(out=U8[:, h, :], in_=Uf[:]), reads=[b_U], writes=[b_U8])
                X1, b_X1 = C.ps(es, "X1", [128, 1024], F32)
                X2, b_X2 = C.ps(es, "X2", [128, 8, 128], F32)
                X3, b_X3 = C.ps(es, "X3", [128, 8, 128], F32)
                pCS, b_pCS = C.ps(es, "pCS", [128, 144], F32)
                pQ4, b_pQ4 = C.ps(es, "pQ4", [128, 4, 256], BF16)
                pB, pBD = X1[:, 0:512], X1[:, 512:1024]
                b_pB = b_pBD = b_X1
                X1v = X1.rearrange("p (a b) -> p a b", a=8)
                pC = pCS[:, 0:16]
                b_pC = b_pCS
                pS3 = pCS[:, 16:144].rearrange("p (a b) -> p a b", a=8)
                def hg_prep(t):
                        i = t % NB
                        rows = slice(t * 128, (t + 1) * 128)
                        (qb_, bqb), (fb_, bfb), (ib_, bib), (gg_, bgg), (kk_, bkk) = qb[i], fb[i], ib[i], gg[i], kk[i]
                        (qt_, bqt), (kt_, bkt), (kd_, bkd), (vv_, bvv), (ebl_, bebl) = qt[i], kt[i], kd[i], vv[i], ebl[i]
                        S.dma("sp", qb_[:], P_all[rows, 1088:2112], reads=[B_P_all], writes=[bqb])
                        S.dma("sp", fb_[:], P_all[rows, 2112:3136], reads=[B_P_all], writes=[bfb])
                        S.dma("sp", ib_[:], P_all[rows, 3136:4160], reads=[B_P_all], writes=[bib])
                        S.op("act", lambda e: e.copy(out=vv_[:], in_=ib_[:]), reads=[bib], writes=[bvv])
                        S.op("act", lambda e: e.activation(out=fb_[:], in_=fb_[:], func=AF.Sigmoid), reads=[bfb], writes=[bfb])
                        S.op("act", lambda e: e.activation(out=ib_[:], in_=qb_[:], func=AF.Sigmoid), reads=[bqb, bvv],
                             writes=[bib])
                        S.op("dve", lambda e: e.tensor_tensor(out=fb_[:], in0=fb_[:], in1=oml[:], op=ALU.mult),
                             reads=[bfb, b_oml], writes=[bfb])
                        S.op("pool", lambda e: e.tensor_tensor(out=fb_[:], in0=fb_[:], in1=lb[:], op=ALU.add),
                             reads=[bfb, b_lb], writes=[bfb])
                        S.op("act", lambda e: e.activation(out=gg_[:], in_=fb_[:], func=AF.Ln), reads=[bfb], writes=[bgg])
                        S.op("pool", lambda e: e.tensor_scalar(out=kk_[:], in0=fb_[:], scalar1=-1.0, scalar2=1.0,
                                                               op0=ALU.mult, op1=ALU.add), reads=[bfb], writes=[bkk])

                        S.op("dve", lambda e: e.tensor_tensor(out=qb_[:], in0=qb_[:], in1=ib_[:], op=ALU.mult),
                             reads=[bqb, bib], writes=[bqb])
                        for hf in range(2):
                            cs = slice(hf * 512, (hf + 1) * 512)
                            (e1_, be1), (e2_, be2), (e3_, be3) = e1[hf], e2[hf], e3[hf]
                            S.op("pe", lambda e: e.matmul(pB[:], lhsT=Uf[:], rhs=gg_[:, cs], start=True, stop=True),
                                 reads=[b_U, bgg], writes=[b_pB])
                            S.op("pe", lambda e: e.matmul(pBD[:], lhsT=Wf[:], rhs=gg_[:, cs], start=True, stop=True),
                                 reads=[b_W, bgg], writes=[b_pBD])
                            S.op("act", lambda e: e.activation(out=e1_[:], in_=pB[:], func=AF.Exp), reads=[b_pB], writes=[be1])
                            S.op("act", lambda e: e.activation(out=e2_[:], in_=pB[:], func=AF.Exp, scale=-1.0),
                                 reads=[b_pB], writes=[be2])
                            S.op("act", lambda e: e.activation(out=e3_[:], in_=pBD[:], func=AF.Exp), reads=[b_pBD],
                                 writes=[be3])
                            S.op("dve", lambda e: e.tensor_tensor(out=qt_[:, cs], in0=qb_[:, cs], in1=e1_[:], op=ALU.mult),
                                 reads=[bqb, be1], writes=[bqt])
                            S.op("pool", lambda e: e.tensor_tensor(out=kt_[:, cs], in0=kk_[:, cs], in1=e2_[:], op=ALU.mult),
                                 reads=[bkk, be2], writes=[bkt])
                            S.op("dve", lambda e: e.tensor_tensor(out=kd_[:, cs], in0=kk_[:, cs], in1=e3_[:], op=ALU.mult),
                                 reads=[bkk, be3], writes=[bkd])
                        for h in range(8):
                            S.op("pe", lambda e: e.matmul(pC[:, 2 * h:2 * h + 2], lhsT=gg_[:, h * 128:(h + 1) * 128],
                                                          rhs=chi[:], start=True, stop=True),
                                 reads=[bgg, b_chi], writes=[b_pC])
                        S.op("act", lambda e: e.activation(out=ebl_[:], in_=pC[:], func=AF.Exp), reads=[b_pC], writes=[bebl])

                def hg_tail(t):
                        i = t % NB
                        (qt_, bqt), (kt_, bkt), (kd_, bkd), (vv_, bvv), (ebl_, bebl) = qt[i], kt[i], kd[i], vv[i], ebl[i]
                        (qk_, bqk) = qkT8[t % 2]
                        (S0c, bS0c), (S0n, bS0n) = S0b2[t % 2], S0b2[(t + 1) % 2]
                        hcs = [slice(h * 128, (h + 1) * 128) for h in range(8)]
                        for half in range(2):
                            for hh in range(4):
                                h = 4 * half + hh
                                S.op("pe", lambda e: e.transpose(out=pQ4[:, hh, 0:128], in_=qt_[:, hcs[h]], identity=ident[:]),
                                     reads=[bqt, b_ident], writes=[b_pQ4])
                                S.op("pe", lambda e: e.transpose(out=pQ4[:, hh, 128:256], in_=kt_[:, hcs[h]],
                                                                 identity=ident[:]), reads=[bkt, b_ident], writes=[b_pQ4])
                            if half == 0:
                                S.op("act", lambda e: e.copy(out=qk_[:, 0:4, :], in_=pQ4[:]), reads=[b_pQ4], writes=[bqk])
                                for h in range(8):
                                    S.op("pe", lambda e: e.matmul(X2[:, h, :], lhsT=kd_[0:64, hcs[h]], rhs=vv_[0:64, hcs[h]],
                                                                  start=True, stop=True), reads=[bkd, bvv], writes=[b_X2])
                            else:
                                S.op("dve", lambda e: e.tensor_copy(out=qk_[:, 4:8, :], in_=pQ4[:]), reads=[b_pQ4],
                                     writes=[bqk])
                        for h in range(8):
                            S.op("pe", lambda e: e.matmul(X1v[:, h, :], lhsT=qk_[:, h, 128:256], rhs=qk_[:, h, 0:128],
                                                          start=True, stop=True), reads=[bqk], writes=[b_X1])
                        S.op("dve", lambda e: e.tensor_tensor(out=AT8[:], in0=X1v, in1=U8[:], op=ALU.mult),
                             reads=[b_X1, b_U8], writes=[b_AT8])
                        for h in range(8):
                            S.op("dve", lambda e: e.scalar_tensor_tensor(out=S1f8[:, h, :], in0=Sf[:, h, :],
                                                                         scalar=ebl_[:, 2 * h:2 * h + 1],
                                                                         in1=X2[:, h, :], op0=ALU.mult, op1=ALU.add),
                                 reads=[b_Sf, bebl, b_X2], writes=[b_S1f8])
                        S.op("act", lambda e: e.copy(out=S1b8[:], in_=S1f8[:]), reads=[b_S1f8], writes=[b_S1b8])
                        for h in range(8):
                            S.op("pe", lambda e: e.matmul(X3[:, h, :], lhsT=AT8[:, h, :], rhs=vv_[:, hcs[h]],
                                                          start=(h % 4 == 0), stop=False, skip_group_check=True),
                                 reads=[b_AT8, bvv], writes=[b_X3])
                        for h in range(8):
                            S.op("pe", lambda e: e.matmul(X3[0:64, h, :], lhsT=qk_[:, h, 0:64], rhs=S0c[:, h, :], start=False,
                                                          stop=True, skip_group_check=True),
                                 reads=[bqk, bS0c], writes=[b_X3])
                        for h in range(8):
                            S.op("pe", lambda e: e.matmul(X3[64:128, h, :], lhsT=qk_[:, h, 64:128], rhs=S1b8[:, h, :],
                                                          start=False, stop=True, skip_group_check=True),
                                 reads=[bqk, b_S1b8], writes=[b_X3])
                        for h in range(8):
                            S.op("pe", lambda e: e.matmul(X2[:, h, :], lhsT=kd_[64:128, hcs[h]], rhs=vv_[64:128, hcs[h]],
                                                          start=True, stop=True), reads=[bkd, bvv], writes=[b_X2])
                        S.op("act", lambda e: e.activation(out=sq8[:], in_=X3[:], func=AF.Square), reads=[b_X3],
                             writes=[b_sq8])
                        S.op("dve", lambda e: e.tensor_reduce(out=ss8[:], in_=sq8[:], axis=AX.X, op=ALU.add),
                             reads=[b_sq8], writes=[b_ss8])
                        S.op("act", lambda e: e.activation(out=ss8[:], in_=ss8[:], func=AF.Sqrt, scale=1.0 / 128, bias=EPS),
                             reads=[b_ss8], writes=[b_ss8])
                        S.op("dve", lambda e: e.reciprocal(out=ss8[:], in_=ss8[:]), reads=[b_ss8], writes=[b_ss8])
                        for h in range(8):
                            S.op("act", lambda e: e.activation(out=on8[:, h, :], in_=X3[:, h, :], func=AF.Copy,
                                                               scale=ss8[:, h:h + 1]),
                                 reads=[b_X3, b_ss8], writes=[b_on8])
                        for h in range(8):
                            S.op("pe", lambda e: e.matmul(pS3[:, h, :], lhsT=on8[:, h, :], rhs=selb[:], start=True, stop=True),
                                 reads=[b_on8, b_selb], writes=[b_pCS])
                        S.op("act", lambda e: e.copy(out=ybT[:, :, 16 * t:16 * t + 16], in_=pS3), reads=[b_pCS],
                             writes=[b_ybT])
                        for h in range(8):
                            S.op("dve", lambda e: e.scalar_tensor_tensor(out=Sf[:, h, :], in0=S1f8[:, h, :],
                                                                         scalar=ebl_[:, 2 * h + 1:2 * h + 2], in1=X2[:, h, :],
                                                                         op0=ALU.mult, op1=ALU.add),
                                 reads=[b_S1f8, bebl, b_X2], writes=[b_Sf])
                        S.op("act", lambda e: e.copy(out=S0n[:], in_=Sf[:]), reads=[b_Sf], writes=[bS0n])

                hg_prep(0)
                for t in range(NT):
                    if t + 1 < NT:
                        hg_prep(t + 1)
                    hg_tail(t)
                S.dma("pool", ybT_d[:, :, :], ybT[:], reads=[b_ybT], writes=[B_ybT])
                S.barrier()

        if "attn" in stages:
            with contextlib.ExitStack() as es:
                ki2, b_ki2 = C.sb(es, "ki2", [128, TP], BF16)
                for q4 in range(5):
                    S.dma("sp", ki2[:, q4 * 1664:(q4 + 1) * 1664], kiT_d[:, q4 * 1664:(q4 + 1) * 1664],
                          reads=[B_kiT], writes=[b_ki2])
                AM, b_AM = C.sb(es, "AM_sb", [128, 1024], F32)
                S.dma("sp", AM[:], AM_in[:, :], writes=[b_AM])
                I4, b_I4 = C.sb(es, "I4", [128, 512], BF16)
                for r in range(4):
                    S.op("dve", lambda e: e.tensor_copy(out=I4[:, r * 128:(r + 1) * 128], in_=identf[:]),
                         reads=[b_identf], writes=[b_I4])
                ones_b, b_ones = C.sb(es, "ones_b", [128, 128], BF16)
                S.op("pool", lambda e: e.memset(ones_b[:], 1.0), writes=[b_ones])
                gq_bc, b_gq = C.sb(es, "gq_bc2", [128, 128], F32)
                gk_bc, b_gk = C.sb(es, "gk_bc2", [128, 128], F32)
                mq, b_mq = C.sb(es, "mq", [128, 1], F32)
                mk, b_mk = C.sb(es, "mk", [128, 1], F32)
                S.dma("sp", gq_bc[:], gq_in[:, :], writes=[b_gq])
                S.dma("sp", gk_bc[:], gk_in[:, :], writes=[b_gk])
                S.op("dve", lambda e: e.tensor_reduce(out=mq[:], in_=gq_bc[:], axis=AX.X, op=ALU.max,
                                                      apply_absolute_value=True), reads=[b_gq], writes=[b_mq])
                S.op("dve", lambda e: e.tensor_reduce(out=mk[:], in_=gk_bc[:], axis=AX.X, op=ALU.max,
                                                      apply_absolute_value=True), reads=[b_gk], writes=[b_mk])
                S.op("dve", lambda e: e.tensor_tensor(out=mq[:], in0=mq[:], in1=mk[:], op=ALU.mult),
                     reads=[b_mq, b_mk], writes=[b_mq])
                S.op("dve", lambda e: e.tensor_scalar(out=mq[:], in0=mq[:], scalar1=-(128.0 ** 0.5), scalar2=None,
                                                      op0=ALU.mult), reads=[b_mq], writes=[b_mq])
                score, b_score = C.sb(es, "score", [128, 8208], F32)
                cjunk, b_cjunk = C.sb(es, "cjunk", [128, 8208], BF16)
                MB, b_MB = C.sb(es, "MB", [128, 8208], BF16)
                QTj, b_QTj = C.sb(es, "QTj", [128, 8, 128], BF16)
                qiTj, b_qiTj = C.sb(es, "qiTj", [128, 8, 128], BF16)
                zaTj, b_zaTj = C.sb(es, "zaTj", [128, 8, 128], BF16)
                wj, b_wj = C.sb(es, "wj", [128, 16], F32)
                Dg, b_Dg = C.sb(es, "Dg", [128, 16, 128], BF16)
                NR = 4
                Rl = [C.sb(es, f"Rl{i}", [128, 512], BF16) for i in range(8)]
                PT = [C.sb(es, f"PT{i}", [128, 512], BF16) for i in range(NR)]
                KTc = [C.sb(es, f"KTc{i}", [128, 4, 512], BF16) for i in range(2)]
                Vc = [C.sb(es, f"Vc{i}", [128, 4, 512], BF16) for i in range(2)]
                sm = {n: C.sb(es, "bs_" + n, [128, 1], F32) for n in ("lo", "hi", "mid", "cnt", "ge", "d1", "d2", "B", "nmid", "sga")}
                ajunk, b_ajunk = C.sb(es, "ajunk", [128, 4608], BF16)
                rden, b_rden = C.sb(es, "rden", [128, 512], F32)
                yaT, b_yaT = C.sb(es, "yaT", [128, 8, 128], BF16)
                oT, b_oT = C.sb(es, "oT", [128, 512], F32)
                pL = [C.ps(es, f"pL{i}", [128, 512], F32) for i in range(3)]
                pSc, b_pSc = C.ps(es, "pSc", [128, 512], F32)
                pOA = [C.ps(es, f"pOA{i}", [128, 512], F32) for i in range(2)]
                pDn = [C.ps(es, f"pDn{i}", [128, 512], F32) for i in range(2)]
                pLx = pL + pOA + pDn
                nrl = 0
                npl = 0
                nplx = 0
                npt = 0
                nkc = 0
                for j in range(8):
                    s0 = 2 + 128 * j
                    NJ = 16 + 1024 * (j + 1)
                    S.dma("sp", QTj[:], QT_d[:, :, s0:s0 + 128], reads=[B_QT], writes=[b_QTj])
                    S.dma("sp", qiTj[:], qiT_d[:, :, s0:s0 + 128], reads=[B_qiT], writes=[b_qiTj])
                    S.dma("sp", zaTj[:], zaT_d[:, :, s0:s0 + 128], reads=[B_zaT], writes=[b_zaTj])
                    S.dma("sp", wj[:], w_d[s0:s0 + 128, :], reads=[B_w], writes=[b_wj])
                    for h in range(16):
                        S.op("dve", lambda e: e.tensor_scalar(out=Dg[:, h, :], in0=identf[:], scalar1=wj[:, h:h + 1],
                                                              scalar2=None, op0=ALU.mult),
                             reads=[b_identf, b_wj], writes=[b_Dg])
                    items = []
                    c0 = 0
                    while c0 < NJ:
                        cw = min(512, NJ - c0)
                        for h in range(16):
                            items.append((c0, cw, h))
                        c0 += cw
                    LAG = 4
                    slots = {}
                    order = []
                    for base in range(0, len(items) + LAG, 2):
                        order += [("L", base), ("L", base + 1), ("A", base - LAG), ("A", base + 1 - LAG)]
                    for kind, idx in order:
                        if kind == "L" and idx < len(items):
                            c0, cw, h = items[idx]
                            (pl_, bpl) = pLx[nplx % 7]
                            nplx += 1
                            (rl_, brl) = Rl[nrl % 8]
                            nrl += 1
                            slots[idx] = (rl_, brl)
                            pr = slice((h % 2) * 64, (h % 2) * 64 + 64)
                            S.op("pe", lambda e: e.matmul(pl_[:, 0:cw], lhsT=qiTj[pr, h // 2, :], rhs=ki2[pr, c0:c0 + cw],
                                                          start=True, stop=True),
                                 reads=[b_qiTj, b_ki2], writes=[bpl])
                            if h % 2 == 0:
                                S.op("act", lambda e: e.activation(out=rl_[:, 0:cw], in_=pl_[:, 0:cw], func=AF.Relu),
                                     reads=[bpl], writes=[brl])
                            else:
                                S.op("dve", lambda e: e.tensor_scalar(out=rl_[:, 0:cw], in0=pl_[:, 0:cw], scalar1=0.0,
                                                                      scalar2=None, op0=ALU.max),
                                     reads=[bpl], writes=[brl])
                        if kind == "A" and 0 <= idx < len(items):
                            c0, cw, h = items[idx]
                            (rl_, brl) = slots.pop(idx)
                            S.op("pe", lambda e: e.matmul(pSc[:, 0:cw], lhsT=Dg[:, h, :], rhs=rl_[:, 0:cw],
                                                          start=(h == 0), stop=(h == 15)),
                                 reads=[b_Dg, brl], writes=[b_pSc])
                            if h == 15:
                                S.op("act", lambda e: e.copy(out=score[:, c0:c0 + cw], in_=pSc[:, 0:cw]),
                                     reads=[b_pSc], writes=[b_score])
                    g_ = lambda n: sm[n][0]
                    bb = lambda n: sm[n][1]
                    S.op("dve", lambda e: e.tensor_reduce(out=g_("B")[:], in_=score[:, 0:NJ], axis=AX.X, op=ALU.max,
                                                          apply_absolute_value=True), reads=[b_score], writes=[bb("B")])
                    S.op("dve", lambda e: e.tensor_scalar(out=g_("hi")[:], in0=g_("B")[:], scalar1=1.001, scalar2=1e-6,
                                                          op0=ALU.mult, op1=ALU.add), reads=[bb("B")], writes=[bb("hi")])
                    S.op("dve", lambda e: e.tensor_scalar(out=g_("lo")[:], in0=g_("hi")[:], scalar1=-1.0, scalar2=None,
                                                          op0=ALU.mult), reads=[bb("hi")], writes=[bb("lo")])
                    S.op("dve", lambda e: e.tensor_tensor(out=score[:, NJ - 1024:NJ], in0=score[:, NJ - 1024:NJ],
                                                          in1=AM[:], op=ALU.add), reads=[b_score, b_AM], writes=[b_score])
                    S.op("dve", lambda e: e.tensor_tensor(out=g_("d2")[:], in0=g_("hi")[:], in1=g_("lo")[:],
                                                          op=ALU.subtract), reads=[bb("hi"), bb("lo")], writes=[bb("d2")])
                    ND = (NJ * 9 // 20) // 16 * 16
                    NA = NJ - ND
                    for it in range(24):
                        cit = 0.5 ** (it + 1)
                        S.op("dve", lambda e: e.tensor_scalar(out=g_("mid")[:], in0=g_("d2")[:], scalar1=cit,
                                                              scalar2=g_("lo")[:, 0:1], op0=ALU.mult, op1=ALU.add),
                             reads=[bb("d2"), bb("lo")], writes=[bb("mid")])
                        S.op("act", lambda e: e.activation(out=ajunk[:, 0:NA], in_=score[:, ND:NJ], func=AF.Sign,
                                                           scale=-1.0, bias=g_("mid")[:, 0:1], accum_out=g_("sga")[:]),
                             reads=[b_score, bb("mid")], writes=[b_ajunk, bb("sga")])
                        S.op("dve", lambda e: e.tensor_scalar(out=cjunk[:, 0:ND], in0=score[:, 0:ND],
                                                              scalar1=g_("mid")[:, 0:1], scalar2=None, op0=ALU.is_ge,
                                                              op1=ALU.add, accum_out=g_("cnt")[:]),
                             reads=[b_score, bb("mid")], writes=[b_cjunk, bb("cnt")])
                        S.op("dve", lambda e: e.scalar_tensor_tensor(out=g_("cnt")[:], in0=g_("cnt")[:], scalar=2.0,
                                                                     in1=g_("sga")[:], op0=ALU.mult, op1=ALU.subtract),
                             reads=[bb("cnt"), bb("sga")], writes=[bb("cnt")])
                        S.op("dve", lambda e: e.tensor_scalar(out=g_("ge")[:], in0=g_("cnt")[:], scalar1=float(511 - NA),
                                                              scalar2=None, op0=ALU.is_ge), reads=[bb("cnt")],
                             writes=[bb("ge")])
                        S.op("dve", lambda e: e.tensor_scalar(out=g_("d1")[:], in0=g_("mid")[:], scalar1=g_("lo")[:, 0:1],
                                                              scalar2=g_("ge")[:, 0:1], op0=ALU.subtract, op1=ALU.mult),
                             reads=[bb("mid"), bb("lo"), bb("ge")], writes=[bb("d1")])
                        S.op("dve", lambda e: e.tensor_tensor(out=g_("lo")[:], in0=g_("lo")[:], in1=g_("d1")[:],
                                                              op=ALU.add), reads=[bb("lo"), bb("d1")], writes=[bb("lo")])
                    S.op("dve", lambda e: e.tensor_scalar(out=MB[:, 0:NJ], in0=score[:, 0:NJ], scalar1=g_("lo")[:, 0:1],
                                                          scalar2=NEG, op0=ALU.is_lt, op1=ALU.mult),
                         reads=[b_score, bb("lo")], writes=[b_MB])
                    nkt = (NJ + 127) // 128
                    aitems = [(kt_, G) for kt_ in range(nkt) for G in range(2)]
                    chunkbuf = {}
                    pend = {}

                    def emit_pv(ii):
                        kt_, G = aitems[ii]
                        (pt_, bpt) = pend.pop(ii)
                        (Vc_, bVc) = chunkbuf[kt_ // 4][1]
                        q = kt_ % 4
                        kw = min(128, NJ - kt_ * 128)
                        first, last = (kt_ == 0), (kt_ == nkt - 1)
                        for g2 in range(2):
                            g = 2 * G + g2
                            S.op("pe", lambda e: e.matmul(pOA[G][0][:, g2 * 256:(g2 + 1) * 256],
                                                          lhsT=Vc_[0:kw, q, g * 128:(g + 1) * 128],
                                                          rhs=pt_[0:kw, g2 * 256:(g2 + 1) * 256],
                                                          start=(first and g2 == 0), stop=last, skip_group_check=True),
                                 reads=[bVc, bpt], writes=[pOA[G][1]])
                        S.op("pe", lambda e: e.matmul(pDn[G][0][:], lhsT=ones_b[0:kw, :], rhs=pt_[0:kw, :],
                                                      start=first, stop=last, skip_group_check=True),
                             reads=[b_ones, bpt], writes=[pDn[G][1]])

                    for ii, (kt_, G) in enumerate(aitems):
                        if kt_ % 4 == 0 and G == 0:
                            (KTc_, bKTc), (Vc_, bVc) = KTc[nkc % 2], Vc[nkc % 2]
                            nkc += 1
                            chunkbuf[kt_ // 4] = ((KTc_, bKTc), (Vc_, bVc))
                            k0 = kt_ * 128
                            kwid = min(512, NJ - k0)
                            S.dma("sp", KTc_[:, :, 0:kwid], KT_d[:, :, k0:k0 + kwid], reads=[B_KT], writes=[bKTc])
                            ntl = (kwid + 127) // 128
                            for q in range(ntl):
                                kw_ = min(128, kwid - q * 128)
                                S.dma("act", Vc_[0:kw_, q, :], V_d[k0 + q * 128:k0 + q * 128 + kw_, :], reads=[B_V],
                                      writes=[bVc])
                        (KTc_, bKTc) = chunkbuf[kt_ // 4][0]
                        q = kt_ % 4
                        kw = min(128, NJ - kt_ * 128)
                        ks = slice(kt_ * 128, kt_ * 128 + kw)
                        kl = slice(q * 128, q * 128 + kw)
                        (pl_, bpl) = pL[npl % 3]
                        npl += 1
                        (pt_, bpt) = PT[npt % NR]
                        npt += 1
                        S.op("pe", lambda e: e.matmul(pl_[0:kw, :], lhsT=MB[:, ks], rhs=I4[:], start=True, stop=False,
                                                      skip_group_check=True), reads=[b_MB, b_I4], writes=[bpl])
                        for g2 in range(2):
                            g = 2 * G + g2
                            S.op("pe", lambda e: e.matmul(
                                pl_[0:kw, g2 * 256:(g2 + 1) * 256], lhsT=KTc_[:, g, kl],
                                rhs=QTj[:].rearrange("p a b -> p (a b)")[:, 2 * g * 128:(2 * g + 2) * 128],
                                start=False, stop=True, skip_group_check=True),
                                 reads=[bKTc, b_QTj], writes=[bpl])
                        S.op("act", lambda e: e.activation(out=pt_[0:kw, :], in_=pl_[0:kw, :], func=AF.Exp,
                                                           scale=128.0 ** -0.5, bias=mq[0:kw, 0:1]),
                             reads=[bpl, b_mq], writes=[bpt])
                        pend[ii] = (pt_, bpt)
                        if ii >= 2:
                            emit_pv(ii - 2)
                    for ii in range(max(0, len(aitems) - 2), len(aitems)):
                        emit_pv(ii)
                    for G in range(2):
                        S.op("dve", lambda e: e.reciprocal(out=rden[:], in_=pDn[G][0][:]), reads=[pDn[G][1]],
                             writes=[b_rden])
                        S.op("dve", lambda e: e.tensor_tensor(out=oT[:], in0=pOA[G][0][:], in1=rden[:], op=ALU.mult),
                             reads=[pOA[G][1], b_rden], writes=[b_oT])
                        S.op("dve", lambda e: e.tensor_tensor(
                            out=yaT[:, 4 * G:4 * G + 4, :].rearrange("p a b -> p (a b)"), in0=oT[:],
                            in1=zaTj[:, 4 * G:4 * G + 4, :].rearrange("p a b -> p (a b)"), op=ALU.mult),
                             reads=[b_oT, b_zaTj], writes=[b_yaT])
                    S.dma("pool", yaT_d[:, :, j * 128:(j + 1) * 128], yaT[:], reads=[b_yaT], writes=[B_yaT])
                S.barrier()

        if "merge" in stages:
            with contextlib.ExitStack() as es0:
                mT_all, b_mT = C.sb(es0, "mT_all", [128, 8, 2048], BF16)
                wst = [C.sb(es0, f"wst{i}", [128, 2048], F32) for i in range(2)]
                nw = 0
                with contextlib.ExitStack() as es:
                    Wb = [C.sb(es, f"Wbr{i}", [128, 8, 2048], BF16) for i in range(2)]
                    gob, b_gob = C.sb(es, "gob", [128, 128], F32)
                    gocol, b_gocol = C.sb(es, "gocol", [128, 1], F32)
                    S.dma("sp", gob[:], go_in[:, :], writes=[b_gob])
                    S.op("dve", lambda e: e.tensor_tensor(out=gob[:], in0=gob[:], in1=identf[:], op=ALU.mult),
                         reads=[b_gob, b_identf], writes=[b_gob])
                    S.op("dve", lambda e: e.tensor_reduce(out=gocol[:], in_=gob[:], axis=AX.X, op=ALU.add),
                         reads=[b_gob], writes=[b_gocol])
                    for br in range(2):
                        for k in range(8):
                            (ws_, bws) = wst[nw % 2]
                            nw += 1
                            S.dma("sp", ws_[:], wbr_in[br, k * 128:(k + 1) * 128, :], writes=[bws])
                            if br == 1:
                                if k % 2:
                                    S.op("act", lambda e: e.activation(out=Wb[1][0][:, k, :], in_=ws_[:], func=AF.Copy,
                                                                       scale=gocol[:, 0:1]),
                                         reads=[bws, b_gocol], writes=[Wb[1][1]])
                                else:
                                    S.op("dve", lambda e: e.tensor_scalar(out=Wb[1][0][:, k, :], in0=ws_[:],
                                                                          scalar1=gocol[:, 0:1], scalar2=None,
                                                                          op0=ALU.mult),
                                         reads=[bws, b_gocol], writes=[Wb[1][1]])
                                continue
                            S.op("act" if k % 2 else "dve",
                                 (lambda e: e.copy(out=Wb[br][0][:, k, :], in_=ws_[:])) if k % 2 else
                                 (lambda e: e.tensor_copy(out=Wb[br][0][:, k, :], in_=ws_[:])),
                                 reads=[bws], writes=[Wb[br][1]])
                    yaTj, b_yaTj = C.sb(es, "yaTj", [128, 8, 128], BF16)
                    ybTj, b_ybTj = C.sb(es, "ybTj", [128, 8, 128], BF16)
                    zbTj, b_zbTj = C.sb(es, "zbTj", [128, 8, 128], BF16)
                    gts, b_gts = C.sb(es, "gts", [128, 4096], F32)
                    mg, b_mg = C.sb(es, "mg", [128, 2048], F32)
                    t2, b_t2 = C.sb(es, "t2", [128, 512], F32)
                    mgb, b_mgb = C.sb(es, "mgb", [128, 2048], BF16)
                    pP = [C.ps(es, f"pP{i}", [128, 512], F32) for i in range(4)]
                    pTm, b_pTm = C.ps(es, "pTm", [128, 2048], BF16)
                    npp = 0
                    for j in range(8):
                        s0 = 2 + 128 * j
                        S.dma("sp", yaTj[:], yaT_d[:, :, j * 128:(j + 1) * 128], reads=[B_yaT], writes=[b_yaTj])
                        S.dma("sp", ybTj[:], ybT_d[:, :, s0:s0 + 128], reads=[B_ybT], writes=[b_ybTj])
                        S.dma("sp", zbTj[:], zbT_d[:, :, s0:s0 + 128], reads=[B_zbT], writes=[b_zbTj])
                        S.dma("sp", gts[:], P_own[s0:s0 + 128, 4112:8208], reads=[B_P_own], writes=[b_gts])
                        S.op("dve", lambda e: e.tensor_tensor(out=ybTj[:], in0=ybTj[:], in1=zbTj[:], op=ALU.mult),
                             reads=[b_ybTj, b_zbTj], writes=[b_ybTj])
                        S.op("act", lambda e: e.activation(out=gts[:], in_=gts[:], func=AF.Sigmoid), reads=[b_gts],
                             writes=[b_gts])
                        for cb in range(4):
                            cs = slice(cb * 512, (cb + 1) * 512)
                            for br, (yT_, byT) in enumerate(((yaTj, b_yaTj), (ybTj, b_ybTj))):
                                (pp_, bpp) = pP[npp % 4]
                                npp += 1
                                for k in range(8):
                                    S.op("pe", lambda e: e.matmul(pp_[:], lhsT=yT_[:, k, :], rhs=Wb[br][0][:, k, cs],
                                                                  start=(k == 0), stop=(k == 7)),
                                         reads=[byT, Wb[br][1]], writes=[bpp])
                                if br == 0:
                                    S.op("dve", lambda e: e.tensor_tensor(out=mg[:, cs], in0=pp_[:], in1=gts[:, cs],
                                                                          op=ALU.mult), reads=[bpp, b_gts], writes=[b_mg])
                                else:
                                    S.op("dve", lambda e: e.tensor_tensor(
                                        out=t2[:], in0=pp_[:], in1=gts[:, 2048 + cb * 512:2048 + (cb + 1) * 512],
                                        op=ALU.mult), reads=[bpp, b_gts], writes=[b_t2])
                                    S.op("pool", lambda e: e.tensor_tensor(out=mgb[:, cs], in0=mg[:, cs], in1=t2[:],
                                                                           op=ALU.add), reads=[b_mg, b_t2], writes=[b_mgb])
                        for k in range(16):
                            S.op("pe", lambda e: e.transpose(out=pTm[:, k * 128:(k + 1) * 128],
                                                             in_=mgb[:, k * 128:(k + 1) * 128], identity=ident[:]),
                                 reads=[b_mgb, b_ident], writes=[b_pTm])
                        S.op("act", lambda e: e.copy(out=mT_all[:, j, 0:1024], in_=pTm[:, 0:1024]), reads=[b_pTm],
                             writes=[b_mT])
                        S.op("dve", lambda e: e.tensor_copy(out=mT_all[:, j, 1024:2048], in_=pTm[:, 1024:2048]),
                             reads=[b_pTm], writes=[b_mT])
                    S.barrier()
                with contextlib.ExitStack() as es:
                    Wo, b_Wo = C.sb(es, "Wo", [128, 16, 2048], BF16)
                    for k in range(16):
                        (ws_, bws) = wst[nw % 2]
                        nw += 1
                        S.dma("sp", ws_[:], wout_in[k * 128:(k + 1) * 128, :], writes=[bws])
                        S.op("act" if k % 2 else "dve",
                             (lambda e: e.copy(out=Wo[:, k, :], in_=ws_[:])) if k % 2 else
                             (lambda e: e.tensor_copy(out=Wo[:, k, :], in_=ws_[:])),
                             reads=[bws], writes=[b_Wo])
                    xo = [C.sb(es, f"xo{i}", [128, 2048], F32) for i in range(2)]
                    pP = [C.ps(es, f"pP2{i}", [128, 512], F32) for i in range(4)]
                    npp = 0
                    for j in range(8):
                        (xo_, bxo) = xo[j % 2]
                        S.dma("sp", xo_[:], x_own[j * 128:(j + 1) * 128, :], writes=[bxo])
                        for cb in range(4):
                            cs = slice(cb * 512, (cb + 1) * 512)
                            (pp_, bpp) = pP[npp % 4]
                            npp += 1
                            for k in range(16):
                                S.op("pe", lambda e: e.matmul(pp_[:], lhsT=mT_all[:, j, k * 128:(k + 1) * 128],
                                                              rhs=Wo[:, k, cs], start=(k == 0), stop=(k == 15)),
                                     reads=[b_mT, b_Wo], writes=[bpp])
                            S.op("dve", lambda e: e.tensor_tensor(out=xo_[:, cs], in0=xo_[:, cs], in1=pp_[:], op=ALU.add),
                                 reads=[bxo, bpp], writes=[bxo])
                        S.dma("pool", out_d[j * 128:(j + 1) * 128, :], xo_[:], reads=[bxo], writes=[B_out])
                    S.barrier()

        S.barrier(engines=("sp",))
    return nc


def host_inputs(x, meta_tokens, hgrn_lb_logits, norm_g, w_in, q_norm_g, k_norm_g, idx_k_norm_g,
                hgrn_out_norm_g, w_branch, w_out):
    x = np.asarray(x, np.float32)
    h_all = np.zeros((TP, D), np.float32)
    h_all[:NMETA] = np.asarray(meta_tokens, np.float32)
    h_all[NMETA:NMETA + SEQ] = x[0]
    common = {
        "h_all": h_all,
        "w_in": np.ascontiguousarray(np.asarray(w_in, np.float32)[0]),
        "norm_g": np.ascontiguousarray(np.asarray(norm_g, np.float32)[0]),
    }
    f32 = np.float32
    tile = lambda v, n: np.ascontiguousarray(np.tile(np.asarray(v, f32).reshape(1, -1), (n, 1)))
    common["gq_bc"] = tile(q_norm_g[0], 128)
    common["gk_bc"] = tile(k_norm_g[0], 128)
    common["gi_bc"] = tile(idx_k_norm_g[0], 128)
    common["go_bc"] = tile(hgrn_out_norm_g[0], 128)
    common["lb0"] = tile(np.asarray(hgrn_lb_logits)[0], 128)
    common["lb1"] = tile(np.asarray(hgrn_lb_logits)[1], 128)
    common["w_branch"] = np.ascontiguousarray(np.asarray(w_branch, f32)[0])
    common["w_out"] = np.ascontiguousarray(np.asarray(w_out, f32)[0])

    def rope_tab(pos, rot, heads):
        inv = np.power(np.float32(500000.0), -np.arange(0, rot, 2, dtype=f32) / np.float32(rot)).astype(f32)
        ang = pos.astype(f32)[:, None] * inv[None, :]
        cos, sin = np.cos(ang).astype(f32), np.sin(ang).astype(f32)
        cs = np.stack([np.repeat(cos[:, None, :], heads, 1), np.repeat(sin[:, None, :], heads, 1)], 1)
        return np.ascontiguousarray(cs.reshape(len(pos), -1))
    pos_all = np.arange(TP)
    common["ropeK"] = rope_tab(pos_all, 32, 4)
    common["ropeKI"] = rope_tab(pos_all, 16, 1)
    si, ti = np.arange(128)[:, None], np.arange(128)[None, :]
    same = (si // 64) == (ti // 64)
    common["U"] = (same & (si <= ti)).astype(f32)
    common["Wm"] = (same & (si > ti)).astype(f32)
    common["chi"] = (np.arange(128)[:, None] // 64 == np.arange(2)[None, :]).astype(f32)
    m_, p_ = np.arange(128)[:, None], np.arange(1024)[None, :]
    common["AM"] = np.where((p_ // 64) <= (m_ // 8), 0.0, -1e9).astype(f32)
    maps = []
    for c in range(8):
        sel = np.zeros((128, 16), np.float32)
        sel[c + 8 * np.arange(16), np.arange(16)] = 1.0
        m = dict(common)
        m["sel"] = sel
        m["x_own"] = np.ascontiguousarray(x[0, c::8])
        slot = np.arange(NOWN)
        pos_own = np.where(slot < 1040, 8 * slot + c, 0)
        m["ropeQ"] = rope_tab(pos_own, 32, 8)
        m["ropeQI"] = rope_tab(pos_own, 16, 16)
        maps.append(m)
    return maps


def kernel(**inputs):
    maps = host_inputs(**inputs)
    nc = build_nc()
    res = run_bass_kernel_spmd(nc, maps, core_ids=list(range(8)))
    out = np.zeros((1, SEQ, D), np.float32)
    for c in range(8):
        out[0, c::8] = res.results[c]["out"]
    return out
```

```python
import contextlib
import numpy as np
import concourse.bass as bass
import concourse.mybir as mybir
from concourse.bass_utils import run_bass_kernel_spmd

F32 = mybir.dt.float32
BF16 = mybir.dt.bfloat16
AF = mybir.ActivationFunctionType
ALU = mybir.AluOpType
AX = mybir.AxisListType

D = 2048
SEQ = 8192
NMETA = 16
TP = 8320
NT = 65
NOWN = 1152
NIN = 12368
EPS = 1e-6
NEG = -30000.0

C_QA, C_KA, C_VA, C_ZA, C_QI, C_KI, C_WI, C_QB, C_FB, C_IB, C_ZB, C_G = (
    0, 1024, 1536, 2048, 3072, 4096, 4160, 4176, 5200, 6224, 7248, 8272)
PA_COLS = 4160
PA_BLOCKS = [(C_KA, 512, 0), (C_VA, 512, 512), (C_KI, 64, 1024)] + \
            [(C_QB + i * 512, 512, 1088 + i * 512) for i in range(6)]
PO_COLS = 8208
PO_BLOCKS = [(C_QA + i * 512, 512, i * 512) for i in range(2)] + \
            [(C_ZA + i * 512, 512, 1024 + i * 512) for i in range(4)] + \
            [(C_WI, 16, 3072)] + \
            [(C_ZB + i * 512, 512, 3088 + i * 512) for i in range(10)]


class Buf:
    __slots__ = ("name", "w", "r")

    def __init__(self, name):
        self.name = name
        self.w = None
        self.r = []


class Eng:
    def __init__(self, name, h, sem):
        self.name, self.h, self.sem = name, h, sem
        self.count = 0
        self.seen = {}

    def wait(self, ev):
        if ev is None:
            return
        sem, val = ev
        if self.seen.get(id(sem), 0) >= val:
            return
        if self.name == "pe" and sem is self.sem:
            return
        self.h.wait_ge(sem, val)
        self.seen[id(sem)] = val


class Sched:
    NDMA = 8

    def __init__(self, nc, sems):
        self.nc = nc
        self.free_sems = list(sems)
        self.E = {}
        for name, h in (("pe", nc.tensor), ("act", nc.scalar), ("dve", nc.vector),
                        ("pool", nc.gpsimd), ("sp", nc.sync)):
            self.E[name] = Eng(name, h, self.free_sems.pop())
        self.dq = {}
        for q in ("sp", "pool", "act"):
            self.dq[q] = {"sems": [self.free_sems.pop() for _ in range(self.NDMA)],
                          "n": [0] * self.NDMA, "i": 0}

    def _deps(self, eng, reads, writes):
        for b in reads:
            eng.wait(b.w)
        for b in writes:
            eng.wait(b.w)
            for ev in b.r:
                eng.wait(ev)

    def _commit(self, ev, reads, writes):
        for b in reads:
            b.r = [e for e in b.r if e[0] is not ev[0]] + [ev]
        for b in writes:
            b.w = ev
            b.r = []

    def op(self, ename, fn, reads=(), writes=()):
        eng = self.E[ename]
        self._deps(eng, reads, writes)
        ins = fn(eng.h)
        eng.count += 1
        ins.then_inc(eng.sem, 1)
        ev = (eng.sem, eng.count)
        self._commit(ev, reads, writes)
        return ev

    def dma(self, q, out, in_, reads=(), writes=(), **kw):
        eng = self.E[q]
        d = self.dq[q]
        k = d["i"] % self.NDMA
        d["i"] += 1
        sem = d["sems"][k]
        if d["n"][k] > 0:
            eng.wait((sem, 16 * d["n"][k]))
        self._deps(eng, reads, writes)
        ins = eng.h.dma_start(out=out, in_=in_, **kw)
        d["n"][k] += 1
        ins.then_inc(sem, 16)
        ev = (sem, 16 * d["n"][k])
        self._commit(ev, reads, writes)
        return ev

    def all_events(self):
        evs = []
        for e in self.E.values():
            if e.count:
                evs.append((e.sem, e.count))
        for d in self.dq.values():
            for s, n in zip(d["sems"], d["n"]):
                if n:
                    evs.append((s, 16 * n))
        return evs

    def barrier(self, engines=None):
        evs = self.all_events()
        for name, e in self.E.items():
            if engines is not None and name not in engines:
                continue
            for ev in evs:
                e.wait(ev)


class Ctx:
    def __init__(self, nc, S):
        self.nc, self.S = nc, S

    n = 0

    def sb(self, es, name, shape, dt):
        Ctx.n += 1
        name = f"t{Ctx.n}_{name}"
        t = es.enter_context(self.nc.sbuf_tensor(name, list(shape), dt))
        return t, Buf(name)

    def ps(self, es, name, shape, dt):
        Ctx.n += 1
        name = f"t{Ctx.n}_{name}"
        n = int(np.prod(shape[1:]))
        be = 2048 // (4 if dt == F32 else 2)
        nb = -(-n // be)
        flat = es.enter_context(self.nc.psum_tensor(name, [shape[0], nb * be], dt))
        v = flat[:, 0:n]
        if len(shape) == 3:
            v = v.rearrange("p (a b) -> p a b", a=shape[1])
        elif len(shape) == 4:
            v = v.rearrange("p (a b c) -> p a b c", a=shape[1], b=shape[2])
        return v, Buf(name)


def build_nc(stages=("norm", "gemm", "post", "hgrn", "attn", "merge"), debug_out=()):
    nc = bass.Bass("TRN2", target_bir_lowering=False)
    dt_in = lambda n, s, d=F32: nc.dram_tensor(n, list(s), d, kind="ExternalInput")
    h_all = dt_in("h_all", [TP, D])
    x_own = dt_in("x_own", [1024, D])
    sel_in = dt_in("sel", [128, 16])
    w_in = dt_in("w_in", [D, NIN])
    norm_g = dt_in("norm_g", [D])
    out_d = nc.dram_tensor("out", [1024, D], F32, kind="ExternalOutput")
    gq_in = dt_in("gq_bc", [128, 128]); gk_in = dt_in("gk_bc", [128, 128]); gi_in = dt_in("gi_bc", [128, 64])
    go_in = dt_in("go_bc", [128, 128])
    ropeK_in = dt_in("ropeK", [TP, 128]); ropeKI_in = dt_in("ropeKI", [TP, 16])
    ropeQ_in = dt_in("ropeQ", [NOWN, 256]); ropeQI_in = dt_in("ropeQI", [NOWN, 256])
    U_in = dt_in("U", [128, 128]); W_in = dt_in("Wm", [128, 128]); chi_in = dt_in("chi", [128, 2])
    lb0_in = dt_in("lb0", [128, 1024]); lb1_in = dt_in("lb1", [128, 1024])
    AM_in = dt_in("AM", [128, 1024])
    wbr_in = dt_in("w_branch", [2, 1024, D]); wout_in = dt_in("w_out", [D, D])

    hT_all = nc.dram_tensor("hT_all", [NT, 128, 2048], BF16)
    hT_own = nc.dram_tensor("hT_own", [9, 128, 2048], BF16)
    kind_dbg = lambda n: "ExternalOutput" if n in debug_out else "Internal"
    P_all = nc.dram_tensor("P_all", [TP, PA_COLS], F32, kind=kind_dbg("P_all"))
    P_own = nc.dram_tensor("P_own", [NOWN, PO_COLS], F32, kind=kind_dbg("P_own"))

    KT_d = nc.dram_tensor("KT_d", [128, 4, TP], BF16)
    V_d = nc.dram_tensor("V_d", [TP, 512], BF16)
    kiT_d = nc.dram_tensor("kiT_d", [128, TP], BF16)
    QT_d = nc.dram_tensor("QT_d", [128, 8, NOWN], BF16)
    zaT_d = nc.dram_tensor("zaT_d", [128, 8, NOWN], BF16)
    zbT_d = nc.dram_tensor("zbT_d", [128, 8, NOWN], BF16)
    qiT_d = nc.dram_tensor("qiT_d", [128, 8, NOWN], BF16)
    w_d = nc.dram_tensor("w_d", [NOWN, 16], F32)
    ybT_d = nc.dram_tensor("ybT_d", [128, 8, NOWN], BF16, kind=kind_dbg("ybT_d"))
    yaT_d = nc.dram_tensor("yaT_d", [128, 8, 1024], BF16, kind=kind_dbg("yaT_d"))
    B_KT, B_V, B_kiT, B_QT, B_zaT, B_zbT, B_qiT, B_w, B_ybT, B_yaT = [Buf(n) for n in
        "KT V kiT QT zaT zbT qiT w ybT yaT".split()]

    with contextlib.ExitStack() as top:
        sems = [top.enter_context(nc.semaphore(f"s{i}")) for i in range(5 + 3 * Sched.NDMA + 2)]
        S = Sched(nc, sems)
        C = Ctx(nc, S)
        B_hT_all, B_hT_own, B_P_all, B_P_own, B_out = (Buf("hT_all"), Buf("hT_own"), Buf("P_all"),
                                                       Buf("P_own"), Buf("out"))

        identf, b_identf = C.sb(top, "identf", [128, 128], F32)
        ident, b_ident = C.sb(top, "ident", [128, 128], BF16)
        self_f, b_self_f = C.sb(top, "self_f", [128, 16], F32)
        selb, b_selb = C.sb(top, "selb", [128, 16], BF16)
        gcol, b_gcol = C.sb(top, "gcol", [128, 16], F32)
        S.op("pool", lambda e: e.memset(identf[:], 0.0), writes=[b_identf])
        S.op("pool", lambda e: e.affine_select(out=identf[:], in_=identf[:], pattern=[[-1, 128]],
                                                compare_op=ALU.not_equal, fill=1.0, base=0,
                                                channel_multiplier=1), reads=[b_identf], writes=[b_identf])
        S.op("dve", lambda e: e.tensor_copy(out=ident[:], in_=identf[:]), reads=[b_identf], writes=[b_ident])
        S.dma("sp", self_f[:], sel_in[:, :], writes=[b_self_f])
        S.op("dve", lambda e: e.tensor_copy(out=selb[:], in_=self_f[:]), reads=[b_self_f], writes=[b_selb])
        S.dma("sp", gcol[:], norm_g.ap().rearrange("(k p) -> p k", p=128), writes=[b_gcol],
              allow_slow_non_contiguous=True)

        if "norm" in stages:
            with contextlib.ExitStack() as es:
                NB = 2
                xt = [C.sb(es, f"xt{i}", [128, D], F32) for i in range(NB)]
                hn = [C.sb(es, f"hn{i}", [128, D], BF16) for i in range(NB)]
                hT = [C.sb(es, f"hT{i}", [128, D], BF16) for i in range(NB)]
                junk, b_junk = C.sb(es, "junk", [128, D], BF16)
                ss = [C.sb(es, f"ss{i}", [128, 1], F32) for i in range(NB)]
                rs = [C.sb(es, f"rs{i}", [128, 1], F32) for i in range(NB)]
                hown, b_hown = C.sb(es, "hown", [128, 9, 16, 128], BF16)
                pT = [C.ps(es, f"pT{i}", [128, D], BF16) for i in range(NB)]
                pO = [C.ps(es, f"pO{i}", [128, 16, 16], F32) for i in range(NB)]
                S.op("pool", lambda e: e.memset(hown[:], 0.0), writes=[b_hown])
                for t in range(NT):
                    i = t % NB
                    (x_, bx), (hn_, bhn), (hT_, bhT), (ss_, bss), (rs_, brs) = xt[i], hn[i], hT[i], ss[i], rs[i]
                    (pT_, bpT), (pO_, bpO) = pT[i], pO[i]
                    S.dma("sp", x_[:], h_all[t * 128:(t + 1) * 128, :], writes=[bx])
                    S.op("act", lambda e: e.activation(out=junk[:], in_=x_[:], func=AF.Square, accum_out=ss_[:]),
                         reads=[bx], writes=[b_junk, bss])
                    S.op("act", lambda e: e.activation(out=rs_[:], in_=ss_[:], func=AF.Sqrt, scale=1.0 / D,
                                                       bias=EPS), reads=[bss], writes=[brs])
                    S.op("dve", lambda e: e.reciprocal(out=rs_[:], in_=rs_[:]), reads=[brs], writes=[brs])
                    S.op("dve", lambda e: e.tensor_scalar(out=hn_[:], in0=x_[:], scalar1=rs_[:, 0:1], scalar2=None,
                                                          op0=ALU.mult), reads=[bx, brs], writes=[bhn])
                    for k in range(16):
                        S.op("pe", lambda e: e.transpose(out=pT_[:, k * 128:(k + 1) * 128],
                                                         in_=hn_[:, k * 128:(k + 1) * 128], identity=ident[:]),
                             reads=[bhn, b_ident], writes=[bpT])
                    S.op("act", lambda e: e.copy(out=hT_[:, 0:1024], in_=pT_[:, 0:1024]), reads=[bpT], writes=[bhT])
                    S.op("dve", lambda e: e.tensor_copy(out=hT_[:, 1024:2048], in_=pT_[:, 1024:2048]),
                         reads=[bpT], writes=[bhT])
                    S.dma("pool", hT_all[t], hT_[:], reads=[bhT], writes=[B_hT_all])
                    for k in range(16):
                        S.op("pe", lambda e: e.matmul(pO_[:, k, :], lhsT=hn_[:, k * 128:(k + 1) * 128], rhs=selb[:],
                                                      start=True, stop=True),
                             reads=[bhn, b_selb], writes=[bpO])
                    S.op("dve", lambda e: e.tensor_copy(out=hown[:, t // 8, :, (t % 8) * 16:(t % 8) * 16 + 16],
                                                        in_=pO_[:]), reads=[bpO], writes=[b_hown])
                for u in range(9):
                    S.dma("sp", hT_own[u].rearrange("p (k t) -> p k t", k=16), hown[:, u, :, :],
                          reads=[b_hown], writes=[B_hT_own])
                S.barrier()

        if "gemm" in stages:
            with contextlib.ExitStack() as es:
                wf = [C.sb(es, f"wf{i}", [128, 16, 512], F32) for i in range(2)]
                wb = [C.sb(es, f"wb{i}", [128, 16, 512], BF16) for i in range(2)]
                NH = 6
                hT = [C.sb(es, f"ghT{i}", [128, 16, 128], BF16) for i in range(NH)]
                ob = [C.sb(es, f"ob{i}", [128, 512], F32) for i in range(4)]
                pp = [C.ps(es, f"pp{i}", [128, 512], F32) for i in range(4)]
                w3 = w_in.ap().rearrange("(k p) c -> p k c", p=128)
                blocks = []
                for (src, ntiles, blks, dst, bdst, bsrc) in ((hT_all, NT, PA_BLOCKS, P_all, B_P_all, B_hT_all),
                                                             (hT_own, 9, PO_BLOCKS, P_own, B_P_own, B_hT_own)):
                    for (c0, wdt, d0) in blks:
                        blocks.append((src, ntiles, dst, bdst, bsrc, c0, wdt, d0))

                def load_w(bi):
                    (_, _, _, _, _, c0, wdt, _) = blocks[bi]
                    (wf_, bwf), (wb_, bwb) = wf[bi % 2], wb[bi % 2]
                    S.dma("pool", wf_[:, 0:8, 0:wdt], w3[:, 0:8, c0:c0 + wdt], writes=[bwf])
                    S.dma("sp", wf_[:, 8:16, 0:wdt], w3[:, 8:16, c0:c0 + wdt], writes=[bwf])
                    for k in range(16):
                        S.op("dve", lambda e: e.tensor_scalar(out=wb_[:, k, 0:wdt], in0=wf_[:, k, 0:wdt],
                                                              scalar1=gcol[:, k:k + 1], scalar2=None, op0=ALU.mult),
                             reads=[bwf, b_gcol], writes=[bwb])

                nt = 0
                load_w(0)
                for bi, (src, ntiles, dst, bdst, bsrc, c0, wdt, d0) in enumerate(blocks):
                    if bi + 1 < len(blocks):
                        load_w(bi + 1)
                    (wb_, bwb) = wb[bi % 2]
                    for t in range(ntiles):
                        (h_, bh), (o_, bo), (p_, bp) = hT[nt % NH], ob[nt % 4], pp[nt % 4]
                        nt += 1
                        S.dma("sp", h_[:], src[t].rearrange("p (k t) -> p k t", k=16), reads=[bsrc], writes=[bh])
                        for k in range(16):
                            S.op("pe", lambda e: e.matmul(p_[:, 0:wdt], lhsT=h_[:, k, :], rhs=wb_[:, k, 0:wdt],
                                                          start=(k == 0), stop=(k == 15)),
                                 reads=[bh, bwb], writes=[bp])
                        S.op("act", lambda e: e.copy(out=o_[:, 0:wdt], in_=p_[:, 0:wdt]), reads=[bp], writes=[bo])
                        S.dma("pool", dst[t * 128:(t + 1) * 128, d0:d0 + wdt], o_[:, 0:wdt], reads=[bo], writes=[bdst])
                S.barrier()

        if "post" in stages:
            with contextlib.ExitStack() as es:
                gq_bc, b_gq = C.sb(es, "gq_sb", [128, 128], F32)
                gk_bc, b_gk = C.sb(es, "gk_sb", [128, 128], F32)
                gi_bc, b_gi = C.sb(es, "gi_sb", [128, 64], F32)
                S.dma("sp", gq_bc[:], gq_in[:, :], writes=[b_gq])
                S.dma("sp", gk_bc[:], gk_in[:, :], writes=[b_gk])
                S.dma("sp", gi_bc[:], gi_in[:, :], writes=[b_gi])
                junks = [C.sb(es, f"pjunk{i}", [128, 128], F32) for i in range(3)]
                NB = 3
                pa = [C.sb(es, f"pa{i}", [128, 1088], F32) for i in range(NB)]
                csk = [C.sb(es, f"csk{i}", [128, 2, 8, 16], F32) for i in range(NB)]
                csi = [C.sb(es, f"csi{i}", [128, 2, 16, 8], F32) for i in range(NB)]
                ssq = [C.sb(es, f"ssq{i}", [128, 8], F32) for i in range(NB)]
                kn = [C.sb(es, f"kn{i}", [128, 8, 128], F32) for i in range(NB)]
                tt = [C.sb(es, f"tt{i}", [128, 4, 8, 16], F32) for i in range(NB)]
                kr = [C.sb(es, f"kr{i}", [128, 8, 128], BF16) for i in range(NB)]
                kT = [C.sb(es, f"kT{i}", [128, 8, 128], BF16) for i in range(NB)]
                vb = [C.sb(es, f"vb{i}", [128, 512], BF16) for i in range(NB)]
                kib = [C.sb(es, f"kib{i}", [128, 128], BF16) for i in range(NB)]
                kiT = [C.sb(es, f"kiT{i}", [128, 128], BF16) for i in range(NB)]
                pT = [C.ps(es, f"ppT{i}", [128, 8, 128], BF16) for i in range(NB)]
                pI = [C.ps(es, f"ppI{i}", [128, 128], BF16) for i in range(NB)]
                po_ = [C.sb(es, f"po{i}", [128, 4112], F32) for i in range(NB)]
                zs = [C.sb(es, f"zs{i}", [128, 1024], F32) for i in range(NB)]
                zb_ = [C.sb(es, f"zbb{i}", [128, 8, 128], BF16) for i in range(NB)]
                wv = [C.sb(es, f"wv{i}", [128, 16], F32) for i in range(NB)]

                def headnorm_rope(i, src3, H, hd, g_bc, cs_tile, half, b_src, b_cs, do_norm=True):
                    (ss_, bss), (kn_, bkn), (tt_, btt), (kr_, bkr) = ssq[i], kn[i], tt[i], kr[i]
                    (junk, b_junk) = junks[i]
                    if hd == 128:
                        knv = kn_[:, 0:H, :]
                        krv = kr_[:, 0:H, :]
                    else:
                        knv = kn_[:].rearrange("p a b -> p (a b)")[:, 0:H * hd].rearrange("p (h d) -> p h d", h=H)
                        krv = kr_[:].rearrange("p a b -> p (a b)")[:, 0:H * hd].rearrange("p (h d) -> p h d", h=H)
                    if do_norm:
                        for h in range(H):
                            S.op("act", lambda e: e.activation(out=junk[:, 0:hd], in_=src3[:, h, :], func=AF.Square,
                                                               accum_out=ss_[:, h:h + 1]),
                                 reads=[b_src], writes=[b_junk, bss])
                        S.op("act", lambda e: e.activation(out=ss_[:, 0:H], in_=ss_[:, 0:H], func=AF.Sqrt,
                                                           scale=1.0 / hd, bias=EPS), reads=[bss], writes=[bss])
                        yield
                        S.op("dve", lambda e: e.reciprocal(out=ss_[:, 0:H], in_=ss_[:, 0:H]), reads=[bss], writes=[bss])
                        for h in range(H):
                            S.op("dve", lambda e: e.scalar_tensor_tensor(out=knv[:, h, :], in0=src3[:, h, :],
                                                                         scalar=ss_[:, h:h + 1], in1=g_bc[:, 0:hd],
                                                                         op0=ALU.mult, op1=ALU.mult),
                                 reads=[b_src, bss], writes=[bkn])
                        srcn, bsn = knv, bkn
                    else:
                        srcn, bsn = src3, b_src
                    if hd == 128:
                        cosv, sinv = cs_tile[:, 0, 0:H, :], cs_tile[:, 1, 0:H, :]
                        tv = [tt_[:, q, 0:H, :] for q in range(4)]
                    else:
                        cosv, sinv = cs_tile[:, 0, 0:H, :], cs_tile[:, 1, 0:H, :]
                        tflat = tt_[:].rearrange("p a b c -> p a (b c)")
                        tv = [tflat[:, q, 0:H * half].rearrange("p (h d) -> p h d", h=H) for q in range(4)]
                    yield
                    x1, x2 = srcn[:, :, 0:half], srcn[:, :, half:2 * half]
                    S.op("dve", lambda e: e.tensor_tensor(out=tv[0], in0=x1, in1=cosv, op=ALU.mult),
                         reads=[bsn, b_cs], writes=[btt])
                    S.op("pool", lambda e: e.tensor_tensor(out=tv[1], in0=x2, in1=sinv, op=ALU.mult),
                         reads=[bsn, b_cs], writes=[btt])
                    S.op("dve", lambda e: e.tensor_tensor(out=tv[2], in0=x2, in1=cosv, op=ALU.mult),
                         reads=[bsn, b_cs], writes=[btt])
                    S.op("pool", lambda e: e.tensor_tensor(out=tv[3], in0=x1, in1=sinv, op=ALU.mult),
                         reads=[bsn, b_cs], writes=[btt])
                    S.op("act", lambda e: e.copy(out=krv, in_=srcn), reads=[bsn], writes=[bkr])
                    yield
                    S.op("dve", lambda e: e.tensor_tensor(out=krv[:, :, 0:half], in0=tv[0], in1=tv[1], op=ALU.subtract),
                         reads=[btt], writes=[bkr])
                    S.op("dve", lambda e: e.tensor_tensor(out=krv[:, :, half:2 * half], in0=tv[2], in1=tv[3], op=ALU.add),
                         reads=[btt], writes=[bkr])
                    yield
                    return krv, bkr

                def all_tile(t):
                    i = t % NB
                    (pa_, bpa), (csk_, bcsk), (csi_, bcsi) = pa[i], csk[i], csi[i]
                    rows = slice(t * 128, (t + 1) * 128)
                    S.dma("sp", pa_[:], P_all[rows, 0:1088], reads=[B_P_all], writes=[bpa])
                    S.dma("sp", csk_[:, :, 0:4, :], ropeK_in[rows].rearrange("p (a h d) -> p a h d", a=2, h=4),
                          writes=[bcsk])
                    S.dma("sp", csi_[:, :, 0:1, :], ropeKI_in[rows].rearrange("p (a h d) -> p a h d", a=2, h=1),
                          writes=[bcsi])
                    k3 = pa_[:, 0:512].rearrange("p (h d) -> p h d", h=4)
                    krv, bkr = yield from headnorm_rope(i, k3, 4, 128, gk_bc, csk_, 16, bpa, bcsk)
                    (pT_, bpT), (kT_, bkT) = pT[i], kT[i]
                    for g in range(4):
                        S.op("pe", lambda e: e.transpose(out=pT_[:, g, :], in_=krv[:, g, :], identity=ident[:]),
                             reads=[bkr, b_ident], writes=[bpT])
                    S.op("act", lambda e: e.copy(out=kT_[:, 0:4, :], in_=pT_[:, 0:4, :]), reads=[bpT], writes=[bkT])
                    S.dma("pool", KT_d[:, :, rows], kT_[:, 0:4, :], reads=[bkT], writes=[B_KT])
                    (vb_, bvb) = vb[i]
                    S.op("pool", lambda e: e.tensor_copy(out=vb_[:], in_=pa_[:, 512:1024]), reads=[bpa], writes=[bvb])
                    S.dma("pool", V_d[rows, :], vb_[:], reads=[bvb], writes=[B_V])
                    ki3 = pa_[:, 1024:1088].rearrange("p (h d) -> p h d", h=1)
                    yield
                    kiv, bkr = yield from headnorm_rope(i, ki3, 1, 64, gi_bc, csi_, 8, bpa, bcsi)
                    (kib_, bkib), (pI_, bpI), (kiT_, bkiT) = kib[i], pI[i], kiT[i]
                    S.op("dve", lambda e: e.tensor_copy(out=kib_[:, 0:64], in_=kiv[:, 0, :]), reads=[bkr], writes=[bkib])
                    S.op("pool", lambda e: e.tensor_copy(out=kib_[:, 64:128], in_=kiv[:, 0, :]), reads=[bkr], writes=[bkib])
                    S.op("pe", lambda e: e.transpose(out=pI_[:], in_=kib_[:], identity=ident[:]),
                         reads=[bkib, b_ident], writes=[bpI])
                    S.op("act", lambda e: e.copy(out=kiT_[:], in_=pI_[:]), reads=[bpI], writes=[bkiT])
                    S.dma("pool", kiT_d[:, rows], kiT_[:], reads=[bkiT], writes=[B_kiT])

                def run_interleaved(gens, width):
                    active = []
                    gens = list(gens)
                    while gens or active:
                        while gens and len(active) < width:
                            active.append(gens.pop(0))
                        for g in list(active):
                            try:
                                next(g)
                            except StopIteration:
                                active.remove(g)

                run_interleaved([all_tile(t) for t in range(NT)], 3)

                def own_tile(u):
                    i = u % NB
                    (po__, bpo), (csk_, bcsk), (csi_, bcsi) = po_[i], csk[i], csi[i]
                    rows = slice(u * 128, (u + 1) * 128)
                    S.dma("sp", po__[:], P_own[rows, 0:4112], reads=[B_P_own], writes=[bpo])
                    S.dma("sp", csk_[:], ropeQ_in[rows].rearrange("p (a h d) -> p a h d", a=2, h=8), writes=[bcsk])
                    S.dma("sp", csi_[:], ropeQI_in[rows].rearrange("p (a h d) -> p a h d", a=2, h=16), writes=[bcsi])
                    (pT_, bpT), (kT_, bkT) = pT[i], kT[i]
                    q3 = po__[:, 0:1024].rearrange("p (h d) -> p h d", h=8)
                    krv, bkr = yield from headnorm_rope(i, q3, 8, 128, gq_bc, csk_, 16, bpo, bcsk)
                    for h in range(8):
                        S.op("pe", lambda e: e.transpose(out=pT_[:, h, :], in_=krv[:, h, :], identity=ident[:]),
                             reads=[bkr, b_ident], writes=[bpT])
                    S.op("act", lambda e: e.copy(out=kT_[:], in_=pT_[:]), reads=[bpT], writes=[bkT])
                    S.dma("pool", QT_d[:, :, rows], kT_[:], reads=[bkT], writes=[B_QT])
                    yield
                    for (c0, dstd, bdst) in ((1024, zaT_d, B_zaT), (3088, zbT_d, B_zbT)):
                        (zs_, bzs), (zb__, bzb) = zs[i], zb_[i]
                        S.op("act", lambda e: e.activation(out=zs_[:], in_=po__[:, c0:c0 + 1024], func=AF.Sigmoid),
                             reads=[bpo], writes=[bzs])
                        S.op("dve", lambda e: e.tensor_tensor(out=zb__[:].rearrange("p a b -> p (a b)"), in0=zs_[:],
                                                              in1=po__[:, c0:c0 + 1024], op=ALU.mult),
                             reads=[bzs, bpo], writes=[bzb])
                        for h in range(8):
                            S.op("pe", lambda e: e.transpose(out=pT_[:, h, :], in_=zb__[:, h, :], identity=ident[:]),
                                 reads=[bzb, b_ident], writes=[bpT])
                        S.op("act", lambda e: e.copy(out=kT_[:], in_=pT_[:]), reads=[bpT], writes=[bkT])
                        S.dma("pool", dstd[:, :, rows], kT_[:], reads=[bkT], writes=[bdst])
                    qi3 = po__[:, 2048:3072].rearrange("p (h d) -> p h d", h=16)
                    yield
                    qv, bkr = yield from headnorm_rope(i, qi3, 16, 64, None, csi_, 8, bpo, bcsi, do_norm=False)
                    qflat = kr[i][0][:]
                    for h in range(8):
                        S.op("pe", lambda e: e.transpose(out=pT_[:, h, :], in_=qflat[:, h, :], identity=ident[:]),
                             reads=[bkr, b_ident], writes=[bpT])
                    S.op("act", lambda e: e.copy(out=kT_[:], in_=pT_[:]), reads=[bpT], writes=[bkT])
                    S.dma("pool", qiT_d[:, :, rows], kT_[:], reads=[bkT], writes=[B_qiT])
                    (wv_, bwv) = wv[i]
                    S.op("dve", lambda e: e.tensor_scalar(out=wv_[:], in0=po__[:, 3072:3088], scalar1=1.0 / 32.0,
                                                          scalar2=None, op0=ALU.mult), reads=[bpo], writes=[bwv])
                    S.dma("pool", w_d[rows, :], wv_[:], reads=[bwv], writes=[B_w])
                    yield

                run_interleaved([own_tile(u) for u in range(9)], 3)
                S.barrier()

        if "hgrn" in stages:
            with contextlib.ExitStack() as es:
                Uf, b_U = C.sb(es, "Uf", [128, 128], F32)
                Wf, b_W = C.sb(es, "Wf", [128, 128], F32)
                chi, b_chi = C.sb(es, "chi", [128, 2], F32)
                lb, b_lb = C.sb(es, "lb_sb", [128, 1024], F32)
                oml, b_oml = C.sb(es, "oml", [128, 1024], F32)
                l1, b_l1 = C.sb(es, "l1", [128, 1024], F32)
                go_bc, b_go = C.sb(es, "go_sb", [128, 128], F32)
                S.dma("sp", Uf[:], U_in[:, :], writes=[b_U])
                S.dma("sp", Wf[:], W_in[:, :], writes=[b_W])
                S.dma("sp", chi[:], chi_in[:, :], writes=[b_chi])
                S.dma("sp", go_bc[:], go_in[:, :], writes=[b_go])
                S.dma("sp", lb[:], lb0_in[:, :], writes=[b_lb])
                S.dma("sp", l1[:], lb1_in[:, :], writes=[b_l1])
                S.op("dve", lambda e: e.tensor_tensor(out=lb[:], in0=lb[:], in1=l1[:], op=ALU.subtract),
                     reads=[b_lb, b_l1], writes=[b_lb])
                S.op("act", lambda e: e.activation(out=lb[:], in_=lb[:], func=AF.Sigmoid), reads=[b_lb], writes=[b_lb])
                S.op("dve", lambda e: e.tensor_scalar(out=oml[:], in0=lb[:], scalar1=-1.0, scalar2=1.0, op0=ALU.mult,
                                                      op1=ALU.add), reads=[b_lb], writes=[b_oml])
                Sf, b_Sf = C.sb(es, "Sf", [128, 8, 128], F32)
                S0b, b_S0b = C.sb(es, "S0b", [128, 8, 128], BF16)
                ybT, b_ybT = C.sb(es, "ybT", [128, 8, NOWN], BF16)
                S.op("pool", lambda e: e.memset(Sf[:], 0.0), writes=[b_Sf])
                S.op("pool", lambda e: e.memset(S0b[:], 0.0), writes=[b_S0b])
                S.op("pool", lambda e: e.memset(ybT[:], 0.0), writes=[b_ybT])
                NB = 2
                qb = [C.sb(es, f"qb{i}", [128, 1024], F32) for i in range(NB)]
                fb = [C.sb(es, f"fb{i}", [128, 1024], F32) for i in range(NB)]
                ib = [C.sb(es, f"ib{i}", [128, 1024], F32) for i in range(NB)]
                gg = [C.sb(es, f"gg{i}", [128, 1024], F32) for i in range(NB)]
                kk = [C.sb(es, f"kk{i}", [128, 1024], F32) for i in range(NB)]
                e1 = [C.sb(es, f"e1{i}", [128, 512], F32) for i in range(NB)]
                e2 = [C.sb(es, f"e2{i}", [128, 512], F32) for i in range(NB)]
                e3 = [C.sb(es, f"e3{i}", [128, 512], F32) for i in range(NB)]
                qt = [C.sb(es, f"qt{i}", [128, 1024], BF16) for i in range(NB)]
                kt = [C.sb(es, f"kt{i}", [128, 1024], BF16) for i in range(NB)]
                kd = [C.sb(es, f"kd{i}", [128, 1024], BF16) for i in range(NB)]
                vv = [C.sb(es, f"vv{i}", [128, 1024], BF16) for i in range(NB)]
                ebl = [C.sb(es, f"ebl{i}", [128, 16], F32) for i in range(NB)]
                qkT8 = [C.sb(es, f"qkT8{i}", [128, 8, 256], BF16) for i in range(2)]
                AT8, b_AT8 = C.sb(es, "AT8", [128, 8, 128], BF16)
                S1f8, b_S1f8 = C.sb(es, "S1f8", [128, 8, 128], F32)
                S1b8, b_S1b8 = C.sb(es, "S1b8", [128, 8, 128], BF16)
                S0b2 = [(S0b, b_S0b), C.sb(es, "S0b_1", [128, 8, 128], BF16)]
                on8, b_on8 = C.sb(es, "on8", [128, 8, 128], BF16)
                sq8, b_sq8 = C.sb(es, "sq8", [128, 8, 128], F32)
                ss8, b_ss8 = C.sb(es, "ss8", [128, 8], F32)
                U8, b_U8 = C.sb(es, "U8", [128, 8, 128], F32)
                for h in range(8):
                    S.op("dve", lambda e: e.tensor_copy(out=U8[:, h, :], in_=Uf[:]), reads=[b_U], writes=[b_U8])
                X1, b_X1 = C.ps(es, "X1", [128, 1024], F32)
                X2, b_X2 = C.ps(es, "X2", [128, 8, 128], F32)
                X3, b_X3 = C.ps(es, "X3", [128, 8, 128], F32)
                pCS, b_pCS = C.ps(es, "pCS", [128, 144], F32)
                pQ4, b_pQ4 = C.ps(es, "pQ4", [128, 4, 256], BF16)
                pB, pBD = X1[:, 0:512], X1[:, 512:1024]
                b_pB = b_pBD = b_X1
                X1v = X1.rearrange("p (a b) -> p a b", a=8)
                pC = pCS[:, 0:16]
                b_pC = b_pCS
                pS3 = pCS[:, 16:144].rearrange("p (a b) -> p a b", a=8)
                def hg_prep(t):
                        i = t % NB
                        rows = slice(t * 128, (t + 1) * 128)
                        (qb_, bqb), (fb_, bfb), (ib_, bib), (gg_, bgg), (kk_, bkk) = qb[i], fb[i], ib[i], gg[i], kk[i]
                        (qt_, bqt), (kt_, bkt), (kd_, bkd), (vv_, bvv), (ebl_, bebl) = qt[i], kt[i], kd[i], vv[i], ebl[i]
                        S.dma("sp", qb_[:], P_all[rows, 1088:2112], reads=[B_P_all], writes=[bqb])
                        S.dma("sp", fb_[:], P_all[rows, 2112:3136], reads=[B_P_all], writes=[bfb])
                        S.dma("sp", ib_[:], P_all[rows, 3136:4160], reads=[B_P_all], writes=[bib])
                        S.op("act", lambda e: e.copy(out=vv_[:], in_=ib_[:]), reads=[bib], writes=[bvv])
                        S.op("act", lambda e: e.activation(out=fb_[:], in_=fb_[:], func=AF.Sigmoid), reads=[bfb], writes=[bfb])
                        S.op("act", lambda e: e.activation(out=ib_[:], in_=qb_[:], func=AF.Sigmoid), reads=[bqb, bvv],
                             writes=[bib])
                        S.op("dve", lambda e: e.tensor_tensor(out=fb_[:], in0=fb_[:], in1=oml[:], op=ALU.mult),
                             reads=[bfb, b_oml], writes=[bfb])
                        S.op("pool", lambda e: e.tensor_tensor(out=fb_[:], in0=fb_[:], in1=lb[:], op=ALU.add),
                             reads=[bfb, b_lb], writes=[bfb])
                        S.op("act", lambda e: e.activation(out=gg_[:], in_=fb_[:], func=AF.Ln), reads=[bfb], writes=[bgg])
                        S.op("pool", lambda e: e.tensor_scalar(out=kk_[:], in0=fb_[:], scalar1=-1.0, scalar2=1.0,
                                                               op0=ALU.mult, op1=ALU.add), reads=[bfb], writes=[bkk])

                        S.op("dve", lambda e: e.tensor_tensor(out=qb_[:], in0=qb_[:], in1=ib_[:], op=ALU.mult),
                             reads=[bqb, bib], writes=[bqb])
                        for hf in range(2):
                            cs = slice(hf * 512, (hf + 1) * 512)
                            (e1_, be1), (e2_, be2), (e3_, be3) = e1[hf], e2[hf], e3[hf]
                            S.op("pe", lambda e: e.matmul(pB[:], lhsT=Uf[:], rhs=gg_[:, cs], start=True, stop=True),
                                 reads=[b_U, bgg], writes=[b_pB])
                            S.op("pe", lambda e: e.matmul(pBD[:], lhsT=Wf[:], rhs=gg_[:, cs], start=True, stop=True),
                                 reads=[b_W, bgg], writes=[b_pBD])
                            S.op("act", lambda e: e.activation(out=e1_[:], in_=pB[:], func=AF.Exp), reads=[b_pB], writes=[be1])
                            S.op("act", lambda e: e.activation(out=e2_[:], in_=pB[:], func=AF.Exp, scale=-1.0),
                                 reads=[b_pB], writes=[be2])
                            S.op("act", lambda e: e.activation(out=e3_[:], in_=pBD[:], func=AF.Exp), reads=[b_pBD],
                                 writes=[be3])
                            S.op("dve", lambda e: e.tensor_tensor(out=qt_[:, cs], in0=qb_[:, cs], in1=e1_[:], op=ALU.mult),
                                 reads=[bqb, be1], writes=[bqt])
                            S.op("pool", lambda e: e.tensor_tensor(out=kt_[:, cs], in0=kk_[:, cs], in1=e2_[:], op=ALU.mult),
                                 reads=[bkk, be2], writes=[bkt])
                            S.op("dve", lambda e: e.tensor_tensor(out=kd_[:, cs], in0=kk_[:, cs], in1=e3_[:], op=ALU.mult),
                                 reads=[bkk, be3], writes=[bkd])
                        for h in range(8):
                            S.op("pe", lambda e: e.matmul(pC[:, 2 * h:2 * h + 2], lhsT=gg_[:, h * 128:(h + 1) * 128],
                                                          rhs=chi[:], start=True, stop=True),
                                 reads=[bgg, b_chi], writes=[b_pC])
                        S.op("act", lambda e: e.activation(out=ebl_[:], in_=pC[:], func=AF.Exp), reads=[b_pC], writes=[bebl])

                def hg_tail(t):
                        i = t % NB
                        (qt_, bqt), (kt_, bkt), (kd_, bkd), (vv_, bvv), (ebl_, bebl) = qt[i], kt[i], kd[i], vv[i], ebl[i]
                        (qk_, bqk) = qkT8[t % 2]
                        (S0c, bS0c), (S0n, bS0n) = S0b2[t % 2], S0b2[(t + 1) % 2]
                        hcs = [slice(h * 128, (h + 1) * 128) for h in range(8)]
                        for half in range(2):
                            for hh in range(4):
                                h = 4 * half + hh
                                S.op("pe", lambda e: e.transpose(out=pQ4[:, hh, 0:128], in_=qt_[:, hcs[h]], identity=ident[:]),
                                     reads=[bqt, b_ident], writes=[b_pQ4])
                                S.op("pe", lambda e: e.transpose(out=pQ4[:, hh, 128:256], in_=kt_[:, hcs[h]],
                                                                 identity=ident[:]), reads=[bkt, b_ident], writes=[b_pQ4])
                            if half == 0:
                                S.op("act", lambda e: e.copy(out=qk_[:, 0:4, :], in_=pQ4[:]), reads=[b_pQ4], writes=[bqk])
                                for h in range(8):
                                    S.op("pe", lambda e: e.matmul(X2[:, h, :], lhsT=kd_[0:64, hcs[h]], rhs=vv_[0:64, hcs[h]],
                                                                  start=True, stop=True), reads=[bkd, bvv], writes=[b_X2])
                            else:
                                S.op("dve", lambda e: e.tensor_copy(out=qk_[:, 4:8, :], in_=pQ4[:]), reads=[b_pQ4],
                                     writes=[bqk])
                        for h in range(8):
                            S.op("pe", lambda e: e.matmul(X1v[:, h, :], lhsT=qk_[:, h, 128:256], rhs=qk_[:, h, 0:128],
                                                          start=True, stop=True), reads=[bqk], writes=[b_X1])
                        S.op("dve", lambda e: e.tensor_tensor(out=AT8[:], in0=X1v, in1=U8[:], op=ALU.mult),
                             reads=[b_X1, b_U8], writes=[b_AT8])
                        for h in range(8):
                            S.op("dve", lambda e: e.scalar_tensor_tensor(out=S1f8[:, h, :], in0=Sf[:, h, :],
                                                                         scalar=ebl_[:, 2 * h:2 * h + 1],
                                                                         in1=X2[:, h, :], op0=ALU.mult, op1=ALU.add),
                                 reads=[b_Sf, bebl, b_X2], writes=[b_S1f8])
                        S.op("act", lambda e: e.copy(out=S1b8[:], in_=S1f8[:]), reads=[b_S1f8], writes=[b_S1b8])
                        for h in range(8):
                            S.op("pe", lambda e: e.matmul(X3[:, h, :], lhsT=AT8[:, h, :], rhs=vv_[:, hcs[h]],
                                                          start=(h % 4 == 0), stop=False, skip_group_check=True),
                                 reads=[b_AT8, bvv], writes=[b_X3])
                        for h in range(8):
                            S.op("pe", lambda e: e.matmul(X3[0:64, h, :], lhsT=qk_[:, h, 0:64], rhs=S0c[:, h, :], start=False,
                                                          stop=True, skip_group_check=True),
                                 reads=[bqk, bS0c], writes=[b_X3])
                        for h in range(8):
                            S.op("pe", lambda e: e.matmul(X3[64:128, h, :], lhsT=qk_[:, h, 64:128], rhs=S1b8[:, h, :],
                                                          start=False, stop=True, skip_group_check=True),
                                 reads=[bqk, b_S1b8], writes=[b_X3])
                        for h in range(8):
                            S.op("pe", lambda e: e.matmul(X2[:, h, :], lhsT=kd_[64:128, hcs[h]], rhs=vv_[64:128, hcs[h]],
                                                          start=True, stop=True), reads=[bkd, bvv], writes=[b_X2])
                        S.op("act", lambda e: e.activation(out=sq8[:], in_=X3[:], func=AF.Square), reads=[b_X3],
                             writes=[b_sq8])
                        S.op("dve", lambda e: e.tensor_reduce(out=ss8[:], in_=sq8[:], axis=AX.X, op=ALU.add),
                             reads=[b_sq8], writes=[b_ss8])
                        S.op("act", lambda e: e.activation(out=ss8[:], in_=ss8[:], func=AF.Sqrt, scale=1.0 / 128, bias=EPS),
                             reads=[b_ss8], writes=[b_ss8])
                        S.op("dve", lambda e: e.reciprocal(out=ss8[:], in_=ss8[:]), reads=[b_ss8], writes=[b_ss8])
                        for h in range(8):
                            S.op("act", lambda e: e.activation(out=on8[:, h, :], in_=X3[:, h, :], func=AF.Copy,
                                                               scale=ss8[:, h:h + 1]),
                                 reads=[b_X3, b_ss8], writes=[b_on8])
                        for h in range(8):
                            S.op("pe", lambda e: e.matmul(pS3[:, h, :], lhsT=on8[:, h, :], rhs=selb[:], start=True, stop=True),
                                 reads=[b_on8, b_selb], writes=[b_pCS])
                        S.op("act", lambda e: e.copy(out=ybT[:, :, 16 * t:16 * t + 16], in_=pS3), reads=[b_pCS],
                             writes=[b_ybT])
                        for h in range(8):
                            S.op("dve", lambda e: e.scalar_tensor_tensor(out=Sf[:, h, :], in0=S1f8[:, h, :],
                                                                         scalar=ebl_[:, 2 * h + 1:2 * h + 2], in1=X2[:, h, :],
                                                                         op0=ALU.mult, op1=ALU.add),
                                 reads=[b_S1f8, bebl, b_X2], writes=[b_Sf])
                        S.op("act", lambda e: e.copy(out=S0n[:], in_=Sf[:]), reads=[b_Sf], writes=[bS0n])

                hg_prep(0)
                for t in range(NT):
                    if t + 1 < NT:
                        hg_prep(t + 1)
                    hg_tail(t)
                S.dma("pool", ybT_d[:, :, :], ybT[:], reads=[b_ybT], writes=[B_ybT])
                S.barrier()

        if "attn" in stages:
            with contextlib.ExitStack() as es:
                ki2, b_ki2 = C.sb(es, "ki2", [128, TP], BF16)
                for q4 in range(5):
                    S.dma("sp", ki2[:, q4 * 1664:(q4 + 1) * 1664], kiT_d[:, q4 * 1664:(q4 + 1) * 1664],
                          reads=[B_kiT], writes=[b_ki2])
                AM, b_AM = C.sb(es, "AM_sb", [128, 1024], F32)
                S.dma("sp", AM[:], AM_in[:, :], writes=[b_AM])
                I4, b_I4 = C.sb(es, "I4", [128, 512], BF16)
                for r in range(4):
                    S.op("dve", lambda e: e.tensor_copy(out=I4[:, r * 128:(r + 1) * 128], in_=identf[:]),
                         reads=[b_identf], writes=[b_I4])
                ones_b, b_ones = C.sb(es, "ones_b", [128, 128], BF16)
                S.op("pool", lambda e: e.memset(ones_b[:], 1.0), writes=[b_ones])
                gq_bc, b_gq = C.sb(es, "gq_bc2", [128, 128], F32)
                gk_bc, b_gk = C.sb(es, "gk_bc2", [128, 128], F32)
                mq, b_mq = C.sb(es, "mq", [128, 1], F32)
                mk, b_mk = C.sb(es, "mk", [128, 1], F32)
                S.dma("sp", gq_bc[:], gq_in[:, :], writes=[b_gq])
                S.dma("sp", gk_bc[:], gk_in[:, :], writes=[b_gk])
                S.op("dve", lambda e: e.tensor_reduce(out=mq[:], in_=gq_bc[:], axis=AX.X, op=ALU.max,
                                                      apply_absolute_value=True), reads=[b_gq], writes=[b_mq])
                S.op("dve", lambda e: e.tensor_reduce(out=mk[:], in_=gk_bc[:], axis=AX.X, op=ALU.max,
                                                      apply_absolute_value=True), reads=[b_gk], writes=[b_mk])
                S.op("dve", lambda e: e.tensor_tensor(out=mq[:], in0=mq[:], in1=mk[:], op=ALU.mult),
                     reads=[b_mq, b_mk], writes=[b_mq])
                S.op("dve", lambda e: e.tensor_scalar(out=mq[:], in0=mq[:], scalar1=-(128.0 ** 0.5), scalar2=None,
                                                      op0=ALU.mult), reads=[b_mq], writes=[b_mq])
                score, b_score = C.sb(es, "score", [128, 8208], F32)
                cjunk, b_cjunk = C.sb(es, "cjunk", [128, 8208], BF16)
                MB, b_MB = C.sb(es, "MB", [128, 8208], BF16)
                QTj, b_QTj = C.sb(es, "QTj", [128, 8, 128], BF16)
                qiTj, b_qiTj = C.sb(es, "qiTj", [128, 8, 128], BF16)
                zaTj, b_zaTj = C.sb(es, "zaTj", [128, 8, 128], BF16)
                wj, b_wj = C.sb(es, "wj", [128, 16], F32)
                Dg, b_Dg = C.sb(es, "Dg", [128, 16, 128], BF16)
                NR = 4
                Rl = [C.sb(es, f"Rl{i}", [128, 512], BF16) for i in range(8)]
                PT = [C.sb(es, f"PT{i}", [128, 512], BF16) for i in range(NR)]
                KTc = [C.sb(es, f"KTc{i}", [128, 4, 512], BF16) for i in range(2)]
                Vc = [C.sb(es, f"Vc{i}", [128, 4, 512], BF16) for i in range(2)]
                sm = {n: C.sb(es, "bs_" + n, [128, 1], F32) for n in ("lo", "hi", "mid", "cnt", "ge", "d1", "d2", "B", "nmid", "sga")}
                ajunk, b_ajunk = C.sb(es, "ajunk", [128, 4608], BF16)
                rden, b_rden = C.sb(es, "rden", [128, 512], F32)
                yaT, b_yaT = C.sb(es, "yaT", [128, 8, 128], BF16)
                oT, b_oT = C.sb(es, "oT", [128, 512], F32)
                pL = [C.ps(es, f"pL{i}", [128, 512], F32) for i in range(3)]
                pSc, b_pSc = C.ps(es, "pSc", [128, 512], F32)
                pOA = [C.ps(es, f"pOA{i}", [128, 512], F32) for i in range(2)]
                pDn = [C.ps(es, f"pDn{i}", [128, 512], F32) for i in range(2)]
                pLx = pL + pOA + pDn
                nrl = 0
                npl = 0
                nplx = 0
                npt = 0
                nkc = 0
                score2 = [(score, b_score), C.sb(es, 'score_1', [128, 8208], F32)]
                Dg2 = [(Dg, b_Dg), C.sb(es, 'Dg_1', [128, 16, 128], BF16)]
                qiT2 = [(qiTj, b_qiTj), C.sb(es, 'qiTj_1', [128, 8, 128], BF16)]
                wj2 = [(wj, b_wj), C.sb(es, 'wj_1', [128, 16], F32)]

                def gen_indexer(j):
                    nonlocal nrl, nplx
                    s0 = 2 + 128 * j
                    NJ = 16 + 1024 * (j + 1)
                    (score, b_score), (Dg, b_Dg), (qiTj, b_qiTj), (wj, b_wj) = score2[j % 2], Dg2[j % 2], qiT2[j % 2], wj2[j % 2]
                    S.dma("sp", qiTj[:], qiT_d[:, :, s0:s0 + 128], reads=[B_qiT], writes=[b_qiTj])
                    S.dma("sp", wj[:], w_d[s0:s0 + 128, :], reads=[B_w], writes=[b_wj])
                    for h in range(16):
                        S.op("dve", lambda e: e.tensor_scalar(out=Dg[:, h, :], in0=identf[:], scalar1=wj[:, h:h + 1],
                                                              scalar2=None, op0=ALU.mult),
                             reads=[b_identf, b_wj], writes=[b_Dg])
                    yield
                    items = []
                    c0 = 0
                    while c0 < NJ:
                        cw = min(512, NJ - c0)
                        for h in range(16):
                            items.append((c0, cw, h))
                        c0 += cw
                    LAG = 4
                    slots = {}
                    order = []
                    for base in range(0, len(items) + LAG, 2):
                        order += [("L", base), ("L", base + 1), ("A", base - LAG), ("A", base + 1 - LAG)]
                    for kind, idx in order:
                        if kind == "L" and idx < len(items):
                            c0, cw, h = items[idx]
                            (pl_, bpl) = pLx[nplx % 7]
                            nplx += 1
                            (rl_, brl) = Rl[nrl % 8]
                            nrl += 1
                            slots[idx] = (rl_, brl)
                            pr = slice((h % 2) * 64, (h % 2) * 64 + 64)
                            S.op("pe", lambda e: e.matmul(pl_[:, 0:cw], lhsT=qiTj[pr, h // 2, :], rhs=ki2[pr, c0:c0 + cw],
                                                          start=True, stop=True),
                                 reads=[b_qiTj, b_ki2], writes=[bpl])
                            if h % 2 == 0:
                                S.op("act", lambda e: e.activation(out=rl_[:, 0:cw], in_=pl_[:, 0:cw], func=AF.Relu),
                                     reads=[bpl], writes=[brl])
                            else:
                                S.op("dve", lambda e: e.tensor_scalar(out=rl_[:, 0:cw], in0=pl_[:, 0:cw], scalar1=0.0,
                                                                      scalar2=None, op0=ALU.max),
                                     reads=[bpl], writes=[brl])
                        if kind == "A" and 0 <= idx < len(items):
                            c0, cw, h = items[idx]
                            (rl_, brl) = slots.pop(idx)
                            S.op("pe", lambda e: e.matmul(pSc[:, 0:cw], lhsT=Dg[:, h, :], rhs=rl_[:, 0:cw],
                                                          start=(h == 0), stop=(h == 15)),
                                 reads=[b_Dg, brl], writes=[b_pSc])
                            if h == 15:
                                S.op("act", lambda e: e.copy(out=score[:, c0:c0 + cw], in_=pSc[:, 0:cw]),
                                     reads=[b_pSc], writes=[b_score])
                        if kind == 'A' and idx % 2 == 1:
                            yield

                def gen_bisect(j):
                    NJ = 16 + 1024 * (j + 1)
                    (score, b_score) = score2[j % 2]
                    g_ = lambda n: sm[n][0]
                    bb = lambda n: sm[n][1]
                    S.op("dve", lambda e: e.tensor_reduce(out=g_("B")[:], in_=score[:, 0:NJ], axis=AX.X, op=ALU.max,
                                                          apply_absolute_value=True), reads=[b_score], writes=[bb("B")])
                    S.op("dve", lambda e: e.tensor_scalar(out=g_("hi")[:], in0=g_("B")[:], scalar1=1.001, scalar2=1e-6,
                                                          op0=ALU.mult, op1=ALU.add), reads=[bb("B")], writes=[bb("hi")])
                    S.op("dve", lambda e: e.tensor_scalar(out=g_("lo")[:], in0=g_("hi")[:], scalar1=-1.0, scalar2=None,
                                                          op0=ALU.mult), reads=[bb("hi")], writes=[bb("lo")])
                    S.op("dve", lambda e: e.tensor_tensor(out=score[:, NJ - 1024:NJ], in0=score[:, NJ - 1024:NJ],
                                                          in1=AM[:], op=ALU.add), reads=[b_score, b_AM], writes=[b_score])
                    S.op("dve", lambda e: e.tensor_tensor(out=g_("d2")[:], in0=g_("hi")[:], in1=g_("lo")[:],
                                                          op=ALU.subtract), reads=[bb("hi"), bb("lo")], writes=[bb("d2")])
                    ND = (NJ * 9 // 20) // 16 * 16
                    NA = NJ - ND
                    for it in range(24):
                        cit = 0.5 ** (it + 1)
                        S.op("dve", lambda e: e.tensor_scalar(out=g_("mid")[:], in0=g_("d2")[:], scalar1=cit,
                                                              scalar2=g_("lo")[:, 0:1], op0=ALU.mult, op1=ALU.add),
                             reads=[bb("d2"), bb("lo")], writes=[bb("mid")])
                        S.op("act", lambda e: e.activation(out=ajunk[:, 0:NA], in_=score[:, ND:NJ], func=AF.Sign,
                                                           scale=-1.0, bias=g_("mid")[:, 0:1], accum_out=g_("sga")[:]),
                             reads=[b_score, bb("mid")], writes=[b_ajunk, bb("sga")])
                        S.op("dve", lambda e: e.tensor_scalar(out=cjunk[:, 0:ND], in0=score[:, 0:ND],
                                                              scalar1=g_("mid")[:, 0:1], scalar2=None, op0=ALU.is_ge,
                                                              op1=ALU.add, accum_out=g_("cnt")[:]),
                             reads=[b_score, bb("mid")], writes=[b_cjunk, bb("cnt")])
                        S.op("dve", lambda e: e.scalar_tensor_tensor(out=g_("cnt")[:], in0=g_("cnt")[:], scalar=2.0,
                                                                     in1=g_("sga")[:], op0=ALU.mult, op1=ALU.subtract),
                             reads=[bb("cnt"), bb("sga")], writes=[bb("cnt")])
                        S.op("dve", lambda e: e.tensor_scalar(out=g_("ge")[:], in0=g_("cnt")[:], scalar1=float(511 - NA),
                                                              scalar2=None, op0=ALU.is_ge), reads=[bb("cnt")],
                             writes=[bb("ge")])
                        S.op("dve", lambda e: e.tensor_scalar(out=g_("d1")[:], in0=g_("mid")[:], scalar1=g_("lo")[:, 0:1],
                                                              scalar2=g_("ge")[:, 0:1], op0=ALU.subtract, op1=ALU.mult),
                             reads=[bb("mid"), bb("lo"), bb("ge")], writes=[bb("d1")])
                        S.op("dve", lambda e: e.tensor_tensor(out=g_("lo")[:], in0=g_("lo")[:], in1=g_("d1")[:],
                                                              op=ALU.add), reads=[bb("lo"), bb("d1")], writes=[bb("lo")])
                        yield
                    S.op("dve", lambda e: e.tensor_scalar(out=MB[:, 0:NJ], in0=score[:, 0:NJ], scalar1=g_("lo")[:, 0:1],
                                                          scalar2=NEG, op0=ALU.is_lt, op1=ALU.mult),
                         reads=[b_score, bb("lo")], writes=[b_MB])
                    yield

                def run_pair(ga, gb):
                    la = list_steps = None
                    done_a = done_b = False
                    if gb is None:
                        for _ in ga:
                            pass
                        return
                    while not (done_a and done_b):
                        if not done_a:
                            try:
                                next(ga)
                            except StopIteration:
                                done_a = True
                        for _ in range(RATIO[0]):
                            if not done_b:
                                try:
                                    next(gb)
                                except StopIteration:
                                    done_b = True

                RATIO = [1]
                for _ in gen_indexer(0):
                    pass
                for j in range(8):
                    s0 = 2 + 128 * j
                    NJ = 16 + 1024 * (j + 1)
                    S.dma("sp", QTj[:], QT_d[:, :, s0:s0 + 128], reads=[B_QT], writes=[b_QTj])
                    S.dma("sp", zaTj[:], zaT_d[:, :, s0:s0 + 128], reads=[B_zaT], writes=[b_zaTj])
                    n_idx_steps = (2 * (j + 2) + 1) * 8 + 3
                    RATIO[0] = max(1, -(-n_idx_steps // 26))
                    run_pair(gen_bisect(j), gen_indexer(j + 1) if j < 7 else None)
                    nkt = (NJ + 127) // 128
                    aitems = [(kt_, G) for kt_ in range(nkt) for G in range(2)]
                    chunkbuf = {}
                    pend = {}

                    def emit_pv(ii):
                        kt_, G = aitems[ii]
                        (pt_, bpt) = pend.pop(ii)
                        (Vc_, bVc) = chunkbuf[kt_ // 4][1]
                        q = kt_ % 4
                        kw = min(128, NJ - kt_ * 128)
                        first, last = (kt_ == 0), (kt_ == nkt - 1)
                        for g2 in range(2):
                            g = 2 * G + g2
                            S.op("pe", lambda e: e.matmul(pOA[G][0][:, g2 * 256:(g2 + 1) * 256],
                                                          lhsT=Vc_[0:kw, q, g * 128:(g + 1) * 128],
                                                          rhs=pt_[0:kw, g2 * 256:(g2 + 1) * 256],
                                                          start=(first and g2 == 0), stop=last, skip_group_check=True),
                                 reads=[bVc, bpt], writes=[pOA[G][1]])
                        S.op("pe", lambda e: e.matmul(pDn[G][0][:], lhsT=ones_b[0:kw, :], rhs=pt_[0:kw, :],
                                                      start=first, stop=last, skip_group_check=True),
                             reads=[b_ones, bpt], writes=[pDn[G][1]])

                    for ii, (kt_, G) in enumerate(aitems):
                        if kt_ % 4 == 0 and G == 0:
                            (KTc_, bKTc), (Vc_, bVc) = KTc[nkc % 2], Vc[nkc % 2]
                            nkc += 1
                            chunkbuf[kt_ // 4] = ((KTc_, bKTc), (Vc_, bVc))
                            k0 = kt_ * 128
                            kwid = min(512, NJ - k0)
                            S.dma("sp", KTc_[:, :, 0:kwid], KT_d[:, :, k0:k0 + kwid], reads=[B_KT], writes=[bKTc])
                            ntl = (kwid + 127) // 128
                            for q in range(ntl):
                                kw_ = min(128, kwid - q * 128)
                                S.dma("act", Vc_[0:kw_, q, :], V_d[k0 + q * 128:k0 + q * 128 + kw_, :], reads=[B_V],
                                      writes=[bVc])
                        (KTc_, bKTc) = chunkbuf[kt_ // 4][0]
                        q = kt_ % 4
                        kw = min(128, NJ - kt_ * 128)
                        ks = slice(kt_ * 128, kt_ * 128 + kw)
                        kl = slice(q * 128, q * 128 + kw)
                        (pl_, bpl) = pL[npl % 3]
                        npl += 1
                        (pt_, bpt) = PT[npt % NR]
                        npt += 1
                        S.op("pe", lambda e: e.matmul(pl_[0:kw, :], lhsT=MB[:, ks], rhs=I4[:], start=True, stop=False,
                                                      skip_group_check=True), reads=[b_MB, b_I4], writes=[bpl])
                        for g2 in range(2):
                            g = 2 * G + g2
                            S.op("pe", lambda e: e.matmul(
                                pl_[0:kw, g2 * 256:(g2 + 1) * 256], lhsT=KTc_[:, g, kl],
                                rhs=QTj[:].rearrange("p a b -> p (a b)")[:, 2 * g * 128:(2 * g + 2) * 128],
                                start=False, stop=True, skip_group_check=True),
                                 reads=[bKTc, b_QTj], writes=[bpl])
                        S.op("act", lambda e: e.activation(out=pt_[0:kw, :], in_=pl_[0:kw, :], func=AF.Exp,
                                                           scale=128.0 ** -0.5, bias=mq[0:kw, 0:1]),
                             reads=[bpl, b_mq], writes=[bpt])
                        pend[ii] = (pt_, bpt)
                        if ii >= 2:
                            emit_pv(ii - 2)
                    for ii in range(max(0, len(aitems) - 2), len(aitems)):
                        emit_pv(ii)
                    for G in range(2):
                        S.op("dve", lambda e: e.reciprocal(out=rden[:], in_=pDn[G][0][:]), reads=[pDn[G][1]],
                             writes=[b_rden])
                        S.op("dve", lambda e: e.tensor_tensor(out=oT[:], in0=pOA[G][0][:], in1=rden[:], op=ALU.mult),
                             reads=[pOA[G][1], b_rden], writes=[b_oT])
                        S.op("dve", lambda e: e.tensor_tensor(
                            out=yaT[:, 4 * G:4 * G + 4, :].rearrange("p a b -> p (a b)"), in0=oT[:],
                            in1=zaTj[:, 4 * G:4 * G + 4, :].rearrange("p a b -> p (a b)"), op=ALU.mult),
                             reads=[b_oT, b_zaTj], writes=[b_yaT])
                    S.dma("pool", yaT_d[:, :, j * 128:(j + 1) * 128], yaT[:], reads=[b_yaT], writes=[B_yaT])
                S.barrier()

        if "merge" in stages:
            with contextlib.ExitStack() as es0:
                mT_all, b_mT = C.sb(es0, "mT_all", [128, 8, 2048], BF16)
                wst = [C.sb(es0, f"wst{i}", [128, 2048], F32) for i in range(2)]
                nw = 0
                with contextlib.ExitStack() as es:
                    Wb = [C.sb(es, f"Wbr{i}", [128, 8, 2048], BF16) for i in range(2)]
                    gob, b_gob = C.sb(es, "gob", [128, 128], F32)
                    gocol, b_gocol = C.sb(es, "gocol", [128, 1], F32)
                    S.dma("sp", gob[:], go_in[:, :], writes=[b_gob])
                    S.op("dve", lambda e: e.tensor_tensor(out=gob[:], in0=gob[:], in1=identf[:], op=ALU.mult),
                         reads=[b_gob, b_identf], writes=[b_gob])
                    S.op("dve", lambda e: e.tensor_reduce(out=gocol[:], in_=gob[:], axis=AX.X, op=ALU.add),
                         reads=[b_gob], writes=[b_gocol])
                    for br in range(2):
                        for k in range(8):
                            (ws_, bws) = wst[nw % 2]
                            nw += 1
                            S.dma("sp", ws_[:], wbr_in[br, k * 128:(k + 1) * 128, :], writes=[bws])
                            if br == 1:
                                if k % 2:
                                    S.op("act", lambda e: e.activation(out=Wb[1][0][:, k, :], in_=ws_[:], func=AF.Copy,
                                                                       scale=gocol[:, 0:1]),
                                         reads=[bws, b_gocol], writes=[Wb[1][1]])
                                else:
                                    S.op("dve", lambda e: e.tensor_scalar(out=Wb[1][0][:, k, :], in0=ws_[:],
                                                                          scalar1=gocol[:, 0:1], scalar2=None,
                                                                          op0=ALU.mult),
                                         reads=[bws, b_gocol], writes=[Wb[1][1]])
                                continue
                            S.op("act" if k % 2 else "dve",
                                 (lambda e: e.copy(out=Wb[br][0][:, k, :], in_=ws_[:])) if k % 2 else
                                 (lambda e: e.tensor_copy(out=Wb[br][0][:, k, :], in_=ws_[:])),
                                 reads=[bws], writes=[Wb[br][1]])
                    yaTj, b_yaTj = C.sb(es, "yaTj", [128, 8, 128], BF16)
                    ybTj, b_ybTj = C.sb(es, "ybTj", [128, 8, 128], BF16)
                    zbTj, b_zbTj = C.sb(es, "zbTj", [128, 8, 128], BF16)
                    gts, b_gts = C.sb(es, "gts", [128, 4096], F32)
                    mg, b_mg = C.sb(es, "mg", [128, 2048], F32)
                    t2, b_t2 = C.sb(es, "t2", [128, 512], F32)
                    mgb, b_mgb = C.sb(es, "mgb", [128, 2048], BF16)
                    pP = [C.ps(es, f"pP{i}", [128, 512], F32) for i in range(4)]
                    pTm, b_pTm = C.ps(es, "pTm", [128, 2048], BF16)
                    npp = 0
                    for j in range(8):
                        s0 = 2 + 128 * j
                        S.dma("sp", yaTj[:], yaT_d[:, :, j * 128:(j + 1) * 128], reads=[B_yaT], writes=[b_yaTj])
                        S.dma("sp", ybTj[:], ybT_d[:, :, s0:s0 + 128], reads=[B_ybT], writes=[b_ybTj])
                        S.dma("sp", zbTj[:], zbT_d[:, :, s0:s0 + 128], reads=[B_zbT], writes=[b_zbTj])
                        S.dma("sp", gts[:], P_own[s0:s0 + 128, 4112:8208], reads=[B_P_own], writes=[b_gts])
                        S.op("dve", lambda e: e.tensor_tensor(out=ybTj[:], in0=ybTj[:], in1=zbTj[:], op=ALU.mult),
                             reads=[b_ybTj, b_zbTj], writes=[b_ybTj])
                        S.op("act", lambda e: e.activation(out=gts[:], in_=gts[:], func=AF.Sigmoid), reads=[b_gts],
                             writes=[b_gts])
                        for cb in range(4):
                            cs = slice(cb * 512, (cb + 1) * 512)
                            for br, (yT_, byT) in enumerate(((yaTj, b_yaTj), (ybTj, b_ybTj))):
                                (pp_, bpp) = pP[npp % 4]
                                npp += 1
                                for k in range(8):
                                    S.op("pe", lambda e: e.matmul(pp_[:], lhsT=yT_[:, k, :], rhs=Wb[br][0][:, k, cs],
                                                                  start=(k == 0), stop=(k == 7)),
                                         reads=[byT, Wb[br][1]], writes=[bpp])
                                if br == 0:
                                    S.op("dve", lambda e: e.tensor_tensor(out=mg[:, cs], in0=pp_[:], in1=gts[:, cs],
                                                                          op=ALU.mult), reads=[bpp, b_gts], writes=[b_mg])
                                else:
                                    S.op("dve", lambda e: e.tensor_tensor(
                                        out=t2[:], in0=pp_[:], in1=gts[:, 2048 + cb * 512:2048 + (cb + 1) * 512],
                                        op=ALU.mult), reads=[bpp, b_gts], writes=[b_t2])
                                    S.op("pool", lambda e: e.tensor_tensor(out=mgb[:, cs], in0=mg[:, cs], in1=t2[:],
                                                                           op=ALU.add), reads=[b_mg, b_t2], writes=[b_mgb])
                        for k in range(16):
                            S.op("pe", lambda e: e.transpose(out=pTm[:, k * 128:(k + 1) * 128],
                                                             in_=mgb[:, k * 128:(k + 1) * 128], identity=ident[:]),
                                 reads=[b_mgb, b_ident], writes=[b_pTm])
                        S.op("act", lambda e: e.copy(out=mT_all[:, j, 0:1024], in_=pTm[:, 0:1024]), reads=[b_pTm],
                             writes=[b_mT])
                        S.op("dve", lambda e: e.tensor_copy(out=mT_all[:, j, 1024:2048], in_=pTm[:, 1024:2048]),
                             reads=[b_pTm], writes=[b_mT])
                    S.barrier()
                with contextlib.ExitStack() as es:
                    Wo, b_Wo = C.sb(es, "Wo", [128, 16, 2048], BF16)
                    for k in range(16):
                        (ws_, bws) = wst[nw % 2]
                        nw += 1
                        S.dma("sp", ws_[:], wout_in[k * 128:(k + 1) * 128, :], writes=[bws])
                        S.op("act" if k % 2 else "dve",
                             (lambda e: e.copy(out=Wo[:, k, :], in_=ws_[:])) if k % 2 else
                             (lambda e: e.tensor_copy(out=Wo[:, k, :], in_=ws_[:])),
                             reads=[bws], writes=[b_Wo])
                    xo = [C.sb(es, f"xo{i}", [128, 2048], F32) for i in range(2)]
                    pP = [C.ps(es, f"pP2{i}", [128, 512], F32) for i in range(4)]
                    npp = 0
                    for j in range(8):
                        (xo_, bxo) = xo[j % 2]
                        S.dma("sp", xo_[:], x_own[j * 128:(j + 1) * 128, :], writes=[bxo])
                        for cb in range(4):
                            cs = slice(cb * 512, (cb + 1) * 512)
                            (pp_, bpp) = pP[npp % 4]
                            npp += 1
                            for k in range(16):
                                S.op("pe", lambda e: e.matmul(pp_[:], lhsT=mT_all[:, j, k * 128:(k + 1) * 128],
                                                              rhs=Wo[:, k, cs], start=(k == 0), stop=(k == 15)),
                                     reads=[b_mT, b_Wo], writes=[bpp])
                            S.op("dve", lambda e: e.tensor_tensor(out=xo_[:, cs], in0=xo_[:, cs], in1=pp_[:], op=ALU.add),
                                 reads=[bxo, bpp], writes=[bxo])
                        S.dma("pool", out_d[j * 128:(j + 1) * 128, :], xo_[:], reads=[bxo], writes=[B_out])
                    S.barrier()

        S.barrier(engines=("sp",))
    return nc


def host_inputs(x, meta_tokens, hgrn_lb_logits, norm_g, w_in, q_norm_g, k_norm_g, idx_k_norm_g,
                hgrn_out_norm_g, w_branch, w_out):
    x = np.asarray(x, np.float32)
    h_all = np.zeros((TP, D), np.float32)
    h_all[:NMETA] = np.asarray(meta_tokens, np.float32)
    h_all[NMETA:NMETA + SEQ] = x[0]
    common = {
        "h_all": h_all,
        "w_in": np.ascontiguousarray(np.asarray(w_in, np.float32)[0]),
        "norm_g": np.ascontiguousarray(np.asarray(norm_g, np.float32)[0]),
    }
    f32 = np.float32
    tile = lambda v, n: np.ascontiguousarray(np.tile(np.asarray(v, f32).reshape(1, -1), (n, 1)))
    common["gq_bc"] = tile(q_norm_g[0], 128)
    common["gk_bc"] = tile(k_norm_g[0], 128)
    common["gi_bc"] = tile(idx_k_norm_g[0], 128)
    common["go_bc"] = tile(hgrn_out_norm_g[0], 128)
    common["lb0"] = tile(np.asarray(hgrn_lb_logits)[0], 128)
    common["lb1"] = tile(np.asarray(hgrn_lb_logits)[1], 128)
    common["w_branch"] = np.ascontiguousarray(np.asarray(w_branch, f32)[0])
    common["w_out"] = np.ascontiguousarray(np.asarray(w_out, f32)[0])

    def rope_tab(pos, rot, heads):
        inv = np.power(np.float32(500000.0), -np.arange(0, rot, 2, dtype=f32) / np.float32(rot)).astype(f32)
        ang = pos.astype(f32)[:, None] * inv[None, :]
        cos, sin = np.cos(ang).astype(f32), np.sin(ang).astype(f32)
        cs = np.stack([np.repeat(cos[:, None, :], heads, 1), np.repeat(sin[:, None, :], heads, 1)], 1)
        return np.ascontiguousarray(cs.reshape(len(pos), -1))
    pos_all = np.arange(TP)
    common["ropeK"] = rope_tab(pos_all, 32, 4)
    common["ropeKI"] = rope_tab(pos_all, 16, 1)
    si, ti = np.arange(128)[:, None], np.arange(128)[None, :]
    same = (si // 64) == (ti // 64)
    common["U"] = (same & (si <= ti)).astype(f32)
    common["Wm"] = (same & (si > ti)).astype(f32)
    common["chi"] = (np.arange(128)[:, None] // 64 == np.arange(2)[None, :]).astype(f32)
    m_, p_ = np.arange(128)[:, None], np.arange(1024)[None, :]
    common["AM"] = np.where((p_ // 64) <= (m_ // 8), 0.0, -1e9).astype(f32)
    maps = []
    for c in range(8):
        sel = np.zeros((128, 16), np.float32)
        sel[c + 8 * np.arange(16), np.arange(16)] = 1.0
        m = dict(common)
        m["sel"] = sel
        m["x_own"] = np.ascontiguousarray(x[0, c::8])
        slot = np.arange(NOWN)
        pos_own = np.where(slot < 1040, 8 * slot + c, 0)
        m["ropeQ"] = rope_tab(pos_own, 32, 8)
        m["ropeQI"] = rope_tab(pos_own, 16, 16)
        maps.append(m)
    return maps


def kernel(**inputs):
    maps = host_inputs(**inputs)
    nc = build_nc()
    res = run_bass_kernel_spmd(nc, maps, core_ids=list(range(8)))
    out = np.zeros((1, SEQ, D), np.float32)
    for c in range(8):
        out[0, c::8] = res.results[c]["out"]
    return out
```

```python
import contextlib
import numpy as np
import concourse.bass as bass
import concourse.mybir as mybir
from concourse.bass_utils import run_bass_kernel_spmd

F32 = mybir.dt.float32
BF16 = mybir.dt.bfloat16
AF = mybir.ActivationFunctionType
ALU = mybir.AluOpType
AX = mybir.AxisListType

D = 2048
SEQ = 8192
NMETA = 16
TP = 8320
NT = 65
NOWN = 1152
NIN = 12368
EPS = 1e-6
NEG = -30000.0

C_QA, C_KA, C_VA, C_ZA, C_QI, C_KI, C_WI, C_QB, C_FB, C_IB, C_ZB, C_G = (
    0, 1024, 1536, 2048, 3072, 4096, 4160, 4176, 5200, 6224, 7248, 8272)
PA_COLS = 4160
PA_BLOCKS = [(C_KA, 512, 0), (C_VA, 512, 512), (C_KI, 64, 1024)] + \
            [(C_QB + i * 512, 512, 1088 + i * 512) for i in range(6)]
PO_COLS = 8208
PO_BLOCKS = [(C_QA + i * 512, 512, i * 512) for i in range(2)] + \
            [(C_ZA + i * 512, 512, 1024 + i * 512) for i in range(4)] + \
            [(C_WI, 16, 3072)] + \
            [(C_ZB + i * 512, 512, 3088 + i * 512) for i in range(10)]


class Buf:
    __slots__ = ("name", "w", "r")

    def __init__(self, name):
        self.name = name
        self.w = None
        self.r = []


class Eng:
    def __init__(self, name, h, sem):
        self.name, self.h, self.sem = name, h, sem
        self.count = 0
        self.seen = {}

    def wait(self, ev):
        if ev is None:
            return
        sem, val = ev
        if self.seen.get(id(sem), 0) >= val:
            return
        if self.name == "pe" and sem is self.sem:
            return
        self.h.wait_ge(sem, val)
        self.seen[id(sem)] = val


class Sched:
    NDMA = 8

    def __init__(self, nc, sems):
        self.nc = nc
        self.free_sems = list(sems)
        self.E = {}
        for name, h in (("pe", nc.tensor), ("act", nc.scalar), ("dve", nc.vector),
                        ("pool", nc.gpsimd), ("sp", nc.sync)):
            self.E[name] = Eng(name, h, self.free_sems.pop())
        self.dq = {}
        for q in ("sp", "pool", "act"):
            self.dq[q] = {"sems": [self.free_sems.pop() for _ in range(self.NDMA)],
                          "n": [0] * self.NDMA, "i": 0}

    def _deps(self, eng, reads, writes):
        for b in reads:
            eng.wait(b.w)
        for b in writes:
            eng.wait(b.w)
            for ev in b.r:
                eng.wait(ev)

    def _commit(self, ev, reads, writes):
        for b in reads:
            b.r = [e for e in b.r if e[0] is not ev[0]] + [ev]
        for b in writes:
            b.w = ev
            b.r = []

    def op(self, ename, fn, reads=(), writes=()):
        eng = self.E[ename]
        self._deps(eng, reads, writes)
        ins = fn(eng.h)
        eng.count += 1
        ins.then_inc(eng.sem, 1)
        ev = (eng.sem, eng.count)
        self._commit(ev, reads, writes)
        return ev

    def dma(self, q, out, in_, reads=(), writes=(), **kw):
        eng = self.E[q]
        d = self.dq[q]
        k = d["i"] % self.NDMA
        d["i"] += 1
        sem = d["sems"][k]
        if d["n"][k] > 0:
            eng.wait((sem, 16 * d["n"][k]))
        self._deps(eng, reads, writes)
        ins = eng.h.dma_start(out=out, in_=in_, **kw)
        d["n"][k] += 1
        ins.then_inc(sem, 16)
        ev = (sem, 16 * d["n"][k])
        self._commit(ev, reads, writes)
        return ev

    def all_events(self):
        evs = []
        for e in self.E.values():
            if e.count:
                evs.append((e.sem, e.count))
        for d in self.dq.values():
            for s, n in zip(d["sems"], d["n"]):
                if n:
                    evs.append((s, 16 * n))
        return evs

    def barrier(self, engines=None):
        evs = self.all_events()
        for name, e in self.E.items():
            if engines is not None and name not in engines:
                continue
            for ev in evs:
                e.wait(ev)


class Ctx:
    def __init__(self, nc, S):
        self.nc, self.S = nc, S

    n = 0

    def sb(self, es, name, shape, dt):
        Ctx.n += 1
        name = f"t{Ctx.n}_{name}"
        t = es.enter_context(self.nc.sbuf_tensor(name, list(shape), dt))
        return t, Buf(name)

    def ps(self, es, name, shape, dt):
        Ctx.n += 1
        name = f"t{Ctx.n}_{name}"
        n = int(np.prod(shape[1:]))
        be = 2048 // (4 if dt == F32 else 2)
        nb = -(-n // be)
        flat = es.enter_context(self.nc.psum_tensor(name, [shape[0], nb * be], dt))
        v = flat[:, 0:n]
        if len(shape) == 3:
            v = v.rearrange("p (a b) -> p a b", a=shape[1])
        elif len(shape) == 4:
            v = v.rearrange("p (a b c) -> p a b c", a=shape[1], b=shape[2])
        return v, Buf(name)


def build_nc(stages=("norm", "gemm", "post", "hgrn", "attn", "merge"), debug_out=()):
    nc = bass.Bass("TRN2", target_bir_lowering=False)
    dt_in = lambda n, s, d=F32: nc.dram_tensor(n, list(s), d, kind="ExternalInput")
    h_all = dt_in("h_all", [TP, D])
    x_own = dt_in("x_own", [1024, D])
    sel_in = dt_in("sel", [128, 16])
    w_in = dt_in("w_in", [D, NIN])
    norm_g = dt_in("norm_g", [D])
    out_d = nc.dram_tensor("out", [1024, D], F32, kind="ExternalOutput")
    gq_in = dt_in("gq_bc", [128, 128]); gk_in = dt_in("gk_bc", [128, 128]); gi_in = dt_in("gi_bc", [128, 64])
    go_in = dt_in("go_bc", [128, 128])
    ropeK_in = dt_in("ropeK", [TP, 128]); ropeKI_in = dt_in("ropeKI", [TP, 16])
    ropeQ_in = dt_in("ropeQ", [NOWN, 256]); ropeQI_in = dt_in("ropeQI", [NOWN, 256])
    U_in = dt_in("U", [128, 128]); W_in = dt_in("Wm", [128, 128]); chi_in = dt_in("chi", [128, 2])
    lb0_in = dt_in("lb0", [128, 1024]); lb1_in = dt_in("lb1", [128, 1024])
    AM_in = dt_in("AM", [128, 1024])
    wbr_in = dt_in("w_branch", [2, 1024, D]); wout_in = dt_in("w_out", [D, D])

    hT_all = nc.dram_tensor("hT_all", [NT, 128, 2048], BF16)
    hT_own = nc.dram_tensor("hT_own", [9, 128, 2048], BF16)
    kind_dbg = lambda n: "ExternalOutput" if n in debug_out else "Internal"
    P_all = nc.dram_tensor("P_all", [TP, PA_COLS], F32, kind=kind_dbg("P_all"))
    P_own = nc.dram_tensor("P_own", [NOWN, PO_COLS], F32, kind=kind_dbg("P_own"))

    KT_d = nc.dram_tensor("KT_d", [128, 4, TP], BF16)
    V_d = nc.dram_tensor("V_d", [TP, 512], BF16)
    kiT_d = nc.dram_tensor("kiT_d", [128, TP], BF16)
    QT_d = nc.dram_tensor("QT_d", [128, 8, NOWN], BF16)
    zaT_d = nc.dram_tensor("zaT_d", [128, 8, NOWN], BF16)
    zbT_d = nc.dram_tensor("zbT_d", [128, 8, NOWN], BF16)
    qiT_d = nc.dram_tensor("qiT_d", [128, 8, NOWN], BF16)
    w_d = nc.dram_tensor("w_d", [NOWN, 16], F32)
    ybT_d = nc.dram_tensor("ybT_d", [128, 8, NOWN], BF16, kind=kind_dbg("ybT_d"))
    yaT_d = nc.dram_tensor("yaT_d", [128, 8, 1024], BF16, kind=kind_dbg("yaT_d"))
    B_KT, B_V, B_kiT, B_QT, B_zaT, B_zbT, B_qiT, B_w, B_ybT, B_yaT = [Buf(n) for n in
        "KT V kiT QT zaT zbT qiT w ybT yaT".split()]

    with contextlib.ExitStack() as top:
        sems = [top.enter_context(nc.semaphore(f"s{i}")) for i in range(5 + 3 * Sched.NDMA + 2)]
        S = Sched(nc, sems)
        C = Ctx(nc, S)
        B_hT_all, B_hT_own, B_P_all, B_P_own, B_out = (Buf("hT_all"), Buf("hT_own"), Buf("P_all"),
                                                       Buf("P_own"), Buf("out"))

        identf, b_identf = C.sb(top, "identf", [128, 128], F32)
        ident, b_ident = C.sb(top, "ident", [128, 128], BF16)
        self_f, b_self_f = C.sb(top, "self_f", [128, 16], F32)
        selb, b_selb = C.sb(top, "selb", [128, 16], BF16)
        gcol, b_gcol = C.sb(top, "gcol", [128, 16], F32)
        S.op("pool", lambda e: e.memset(identf[:], 0.0), writes=[b_identf])
        S.op("pool", lambda e: e.affine_select(out=identf[:], in_=identf[:], pattern=[[-1, 128]],
                                                compare_op=ALU.not_equal, fill=1.0, base=0,
                                                channel_multiplier=1), reads=[b_identf], writes=[b_identf])
        S.op("dve", lambda e: e.tensor_copy(out=ident[:], in_=identf[:]), reads=[b_identf], writes=[b_ident])
        S.dma("sp", self_f[:], sel_in[:, :], writes=[b_self_f])
        S.op("dve", lambda e: e.tensor_copy(out=selb[:], in_=self_f[:]), reads=[b_self_f], writes=[b_selb])
        S.dma("sp", gcol[:], norm_g.ap().rearrange("(k p) -> p k", p=128), writes=[b_gcol],
              allow_slow_non_contiguous=True)

        if "norm" in stages:
            with contextlib.ExitStack() as es:
                NB = 2
                xt = [C.sb(es, f"xt{i}", [128, D], F32) for i in range(NB)]
                hn = [C.sb(es, f"hn{i}", [128, D], BF16) for i in range(NB)]
                hT = [C.sb(es, f"hT{i}", [128, D], BF16) for i in range(NB)]
                junk, b_junk = C.sb(es, "junk", [128, D], BF16)
                ss = [C.sb(es, f"ss{i}", [128, 1], F32) for i in range(NB)]
                rs = [C.sb(es, f"rs{i}", [128, 1], F32) for i in range(NB)]
                hown, b_hown = C.sb(es, "hown", [128, 9, 16, 128], BF16)
                pT = [C.ps(es, f"pT{i}", [128, D], BF16) for i in range(NB)]
                pO = [C.ps(es, f"pO{i}", [128, 16, 16], F32) for i in range(NB)]
                S.op("pool", lambda e: e.memset(hown[:], 0.0), writes=[b_hown])
                for t in range(NT):
                    i = t % NB
                    (x_, bx), (hn_, bhn), (hT_, bhT), (ss_, bss), (rs_, brs) = xt[i], hn[i], hT[i], ss[i], rs[i]
                    (pT_, bpT), (pO_, bpO) = pT[i], pO[i]
                    S.dma("sp", x_[:], h_all[t * 128:(t + 1) * 128, :], writes=[bx])
                    S.op("act", lambda e: e.activation(out=junk[:], in_=x_[:], func=AF.Square, accum_out=ss_[:]),
                         reads=[bx], writes=[b_junk, bss])
                    S.op("act", lambda e: e.activation(out=rs_[:], in_=ss_[:], func=AF.Sqrt, scale=1.0 / D,
                                                       bias=EPS), reads=[bss], writes=[brs])
                    S.op("dve", lambda e: e.reciprocal(out=rs_[:], in_=rs_[:]), reads=[brs], writes=[brs])
                    S.op("dve", lambda e: e.tensor_scalar(out=hn_[:], in0=x_[:], scalar1=rs_[:, 0:1], scalar2=None,
                                                          op0=ALU.mult), reads=[bx, brs], writes=[bhn])
                    for k in range(16):
                        S.op("pe", lambda e: e.transpose(out=pT_[:, k * 128:(k + 1) * 128],
                                                         in_=hn_[:, k * 128:(k + 1) * 128], identity=ident[:]),
                             reads=[bhn, b_ident], writes=[bpT])
                    S.op("act", lambda e: e.copy(out=hT_[:, 0:1024], in_=pT_[:, 0:1024]), reads=[bpT], writes=[bhT])
                    S.op("dve", lambda e: e.tensor_copy(out=hT_[:, 1024:2048], in_=pT_[:, 1024:2048]),
                         reads=[bpT], writes=[bhT])
                    S.dma("pool", hT_all[t], hT_[:], reads=[bhT], writes=[B_hT_all])
                    for k in range(16):
                        S.op("pe", lambda e: e.matmul(pO_[:, k, :], lhsT=hn_[:, k * 128:(k + 1) * 128], rhs=selb[:],
                                                      start=True, stop=True),
                             reads=[bhn, b_selb], writes=[bpO])
                    S.op("dve", lambda e: e.tensor_copy(out=hown[:, t // 8, :, (t % 8) * 16:(t % 8) * 16 + 16],
                                                        in_=pO_[:]), reads=[bpO], writes=[b_hown])
                for u in range(9):
                    S.dma("sp", hT_own[u].rearrange("p (k t) -> p k t", k=16), hown[:, u, :, :],
                          reads=[b_hown], writes=[B_hT_own])
                S.barrier()

        if "gemm" in stages:
            with contextlib.ExitStack() as es:
                wf = [C.sb(es, f"wf{i}", [128, 16, 512], F32) for i in range(2)]
                wb = [C.sb(es, f"wb{i}", [128, 16, 512], BF16) for i in range(2)]
                NH = 6
                hT = [C.sb(es, f"ghT{i}", [128, 16, 128], BF16) for i in range(NH)]
                ob = [C.sb(es, f"ob{i}", [128, 512], F32) for i in range(4)]
                pp = [C.ps(es, f"pp{i}", [128, 512], F32) for i in range(4)]
                w3 = w_in.ap().rearrange("(k p) c -> p k c", p=128)
                blocks = []
                for (src, ntiles, blks, dst, bdst, bsrc) in ((hT_all, NT, PA_BLOCKS, P_all, B_P_all, B_hT_all),
                                                             (hT_own, 9, PO_BLOCKS, P_own, B_P_own, B_hT_own)):
                    for (c0, wdt, d0) in blks:
                        blocks.append((src, ntiles, dst, bdst, bsrc, c0, wdt, d0))

                def load_w(bi):
                    (_, _, _, _, _, c0, wdt, _) = blocks[bi]
                    (wf_, bwf), (wb_, bwb) = wf[bi % 2], wb[bi % 2]
                    S.dma("pool", wf_[:, 0:8, 0:wdt], w3[:, 0:8, c0:c0 + wdt], writes=[bwf])
                    S.dma("sp", wf_[:, 8:16, 0:wdt], w3[:, 8:16, c0:c0 + wdt], writes=[bwf])
                    for k in range(16):
                        S.op("dve", lambda e: e.tensor_scalar(out=wb_[:, k, 0:wdt], in0=wf_[:, k, 0:wdt],
                                                              scalar1=gcol[:, k:k + 1], scalar2=None, op0=ALU.mult),
                             reads=[bwf, b_gcol], writes=[bwb])

                nt = 0
                load_w(0)
                for bi, (src, ntiles, dst, bdst, bsrc, c0, wdt, d0) in enumerate(blocks):
                    if bi + 1 < len(blocks):
                        load_w(bi + 1)
                    (wb_, bwb) = wb[bi % 2]
                    for t in range(ntiles):
                        (h_, bh), (o_, bo), (p_, bp) = hT[nt % NH], ob[nt % 4], pp[nt % 4]
                        nt += 1
                        S.dma("sp", h_[:], src[t].rearrange("p (k t) -> p k t", k=16), reads=[bsrc], writes=[bh])
                        for k in range(16):
                            S.op("pe", lambda e: e.matmul(p_[:, 0:wdt], lhsT=h_[:, k, :], rhs=wb_[:, k, 0:wdt],
                                                          start=(k == 0), stop=(k == 15)),
                                 reads=[bh, bwb], writes=[bp])
                        S.op("act", lambda e: e.copy(out=o_[:, 0:wdt], in_=p_[:, 0:wdt]), reads=[bp], writes=[bo])
                        S.dma("pool", dst[t * 128:(t + 1) * 128, d0:d0 + wdt], o_[:, 0:wdt], reads=[bo], writes=[bdst])
                S.barrier()

        if "post" in stages:
            with contextlib.ExitStack() as es:
                gq_bc, b_gq = C.sb(es, "gq_sb", [128, 128], F32)
                gk_bc, b_gk = C.sb(es, "gk_sb", [128, 128], F32)
                gi_bc, b_gi = C.sb(es, "gi_sb", [128, 64], F32)
                S.dma("sp", gq_bc[:], gq_in[:, :], writes=[b_gq])
                S.dma("sp", gk_bc[:], gk_in[:, :], writes=[b_gk])
                S.dma("sp", gi_bc[:], gi_in[:, :], writes=[b_gi])
                junks = [C.sb(es, f"pjunk{i}", [128, 128], F32) for i in range(3)]
                NB = 3
                pa = [C.sb(es, f"pa{i}", [128, 1088], F32) for i in range(NB)]
                csk = [C.sb(es, f"csk{i}", [128, 2, 8, 16], F32) for i in range(NB)]
                csi = [C.sb(es, f"csi{i}", [128, 2, 16, 8], F32) for i in range(NB)]
                ssq = [C.sb(es, f"ssq{i}", [128, 8], F32) for i in range(NB)]
                kn = [C.sb(es, f"kn{i}", [128, 8, 128], F32) for i in range(NB)]
                tt = [C.sb(es, f"tt{i}", [128, 4, 8, 16], F32) for i in range(NB)]
                kr = [C.sb(es, f"kr{i}", [128, 8, 128], BF16) for i in range(NB)]
                kT = [C.sb(es, f"kT{i}", [128, 8, 128], BF16) for i in range(NB)]
                vb = [C.sb(es, f"vb{i}", [128, 512], BF16) for i in range(NB)]
                kib = [C.sb(es, f"kib{i}", [128, 128], BF16) for i in range(NB)]
                kiT = [C.sb(es, f"kiT{i}", [128, 128], BF16) for i in range(NB)]
                pT = [C.ps(es, f"ppT{i}", [128, 8, 128], BF16) for i in range(NB)]
                pI = [C.ps(es, f"ppI{i}", [128, 128], BF16) for i in range(NB)]
                po_ = [C.sb(es, f"po{i}", [128, 4112], F32) for i in range(NB)]
                zs = [C.sb(es, f"zs{i}", [128, 1024], F32) for i in range(NB)]
                zb_ = [C.sb(es, f"zbb{i}", [128, 8, 128], BF16) for i in range(NB)]
                wv = [C.sb(es, f"wv{i}", [128, 16], F32) for i in range(NB)]

                def headnorm_rope(i, src3, H, hd, g_bc, cs_tile, half, b_src, b_cs, do_norm=True):
                    (ss_, bss), (kn_, bkn), (tt_, btt), (kr_, bkr) = ssq[i], kn[i], tt[i], kr[i]
                    (junk, b_junk) = junks[i]
                    if hd == 128:
                        knv = kn_[:, 0:H, :]
                        krv = kr_[:, 0:H, :]
                    else:
                        knv = kn_[:].rearrange("p a b -> p (a b)")[:, 0:H * hd].rearrange("p (h d) -> p h d", h=H)
                        krv = kr_[:].rearrange("p a b -> p (a b)")[:, 0:H * hd].rearrange("p (h d) -> p h d", h=H)
                    if do_norm:
                        for h in range(H):
                            S.op("act", lambda e: e.activation(out=junk[:, 0:hd], in_=src3[:, h, :], func=AF.Square,
                                                               accum_out=ss_[:, h:h + 1]),
                                 reads=[b_src], writes=[b_junk, bss])
                        S.op("act", lambda e: e.activation(out=ss_[:, 0:H], in_=ss_[:, 0:H], func=AF.Sqrt,
                                                           scale=1.0 / hd, bias=EPS), reads=[bss], writes=[bss])
                        yield
                        S.op("dve", lambda e: e.reciprocal(out=ss_[:, 0:H], in_=ss_[:, 0:H]), reads=[bss], writes=[bss])
                        for h in range(H):
                            S.op("dve", lambda e: e.scalar_tensor_tensor(out=knv[:, h, :], in0=src3[:, h, :],
                                                                         scalar=ss_[:, h:h + 1], in1=g_bc[:, 0:hd],
                                                                         op0=ALU.mult, op1=ALU.mult),
                                 reads=[b_src, bss], writes=[bkn])
                        srcn, bsn = knv, bkn
                    else:
                        srcn, bsn = src3, b_src
                    if hd == 128:
                        cosv, sinv = cs_tile[:, 0, 0:H, :], cs_tile[:, 1, 0:H, :]
                        tv = [tt_[:, q, 0:H, :] for q in range(4)]
                    else:
                        cosv, sinv = cs_tile[:, 0, 0:H, :], cs_tile[:, 1, 0:H, :]
                        tflat = tt_[:].rearrange("p a b c -> p a (b c)")
                        tv = [tflat[:, q, 0:H * half].rearrange("p (h d) -> p h d", h=H) for q in range(4)]
                    yield
                    x1, x2 = srcn[:, :, 0:half], srcn[:, :, half:2 * half]
                    S.op("dve", lambda e: e.tensor_tensor(out=tv[0], in0=x1, in1=cosv, op=ALU.mult),
                         reads=[bsn, b_cs], writes=[btt])
                    S.op("pool", lambda e: e.tensor_tensor(out=tv[1], in0=x2, in1=sinv, op=ALU.mult),
                         reads=[bsn, b_cs], writes=[btt])
                    S.op("dve", lambda e: e.tensor_tensor(out=tv[2], in0=x2, in1=cosv, op=ALU.mult),
                         reads=[bsn, b_cs], writes=[btt])
                    S.op("pool", lambda e: e.tensor_tensor(out=tv[3], in0=x1, in1=sinv, op=ALU.mult),
                         reads=[bsn, b_cs], writes=[btt])
                    S.op("act", lambda e: e.copy(out=krv, in_=srcn), reads=[bsn], writes=[bkr])
                    yield
                    S.op("dve", lambda e: e.tensor_tensor(out=krv[:, :, 0:half], in0=tv[0], in1=tv[1], op=ALU.subtract),
                         reads=[btt], writes=[bkr])
                    S.op("dve", lambda e: e.tensor_tensor(out=krv[:, :, half:2 * half], in0=tv[2], in1=tv[3], op=ALU.add),
                         reads=[btt], writes=[bkr])
                    yield
                    return krv, bkr

                def all_tile(t):
                    i = t % NB
                    (pa_, bpa), (csk_, bcsk), (csi_, bcsi) = pa[i], csk[i], csi[i]
                    rows = slice(t * 128, (t + 1) * 128)
                    S.dma("sp", pa_[:], P_all[rows, 0:1088], reads=[B_P_all], writes=[bpa])
                    S.dma("sp", csk_[:, :, 0:4, :], ropeK_in[rows].rearrange("p (a h d) -> p a h d", a=2, h=4),
                          writes=[bcsk])
                    S.dma("sp", csi_[:, :, 0:1, :], ropeKI_in[rows].rearrange("p (a h d) -> p a h d", a=2, h=1),
                          writes=[bcsi])
                    k3 = pa_[:, 0:512].rearrange("p (h d) -> p h d", h=4)
                    krv, bkr = yield from headnorm_rope(i, k3, 4, 128, gk_bc, csk_, 16, bpa, bcsk)
                    (pT_, bpT), (kT_, bkT) = pT[i], kT[i]
                    for g in range(4):
                        S.op("pe", lambda e: e.transpose(out=pT_[:, g, :], in_=krv[:, g, :], identity=ident[:]),
                             reads=[bkr, b_ident], writes=[bpT])
                    S.op("act", lambda e: e.copy(out=kT_[:, 0:4, :], in_=pT_[:, 0:4, :]), reads=[bpT], writes=[bkT])
                    S.dma("pool", KT_d[:, :, rows], kT_[:, 0:4, :], reads=[bkT], writes=[B_KT])
                    (vb_, bvb) = vb[i]
                    S.op("pool", lambda e: e.tensor_copy(out=vb_[:], in_=pa_[:, 512:1024]), reads=[bpa], writes=[bvb])
                    S.dma("pool", V_d[rows, :], vb_[:], reads=[bvb], writes=[B_V])
                    ki3 = pa_[:, 1024:1088].rearrange("p (h d) -> p h d", h=1)
                    yield
                    kiv, bkr = yield from headnorm_rope(i, ki3, 1, 64, gi_bc, csi_, 8, bpa, bcsi)
                    (kib_, bkib), (pI_, bpI), (kiT_, bkiT) = kib[i], pI[i], kiT[i]
                    S.op("dve", lambda e: e.tensor_copy(out=kib_[:, 0:64], in_=kiv[:, 0, :]), reads=[bkr], writes=[bkib])
                    S.op("pool", lambda e: e.tensor_copy(out=kib_[:, 64:128], in_=kiv[:, 0, :]), reads=[bkr], writes=[bkib])
                    S.op("pe", lambda e: e.transpose(out=pI_[:], in_=kib_[:], identity=ident[:]),
                         reads=[bkib, b_ident], writes=[bpI])
                    S.op("act", lambda e: e.copy(out=kiT_[:], in_=pI_[:]), reads=[bpI], writes=[bkiT])
                    S.dma("pool", kiT_d[:, rows], kiT_[:], reads=[bkiT], writes=[B_kiT])

                def run_interleaved(gens, width):
                    active = []
                    gens = list(gens)
                    while gens or active:
                        while gens and len(active) < width:
                            active.append(gens.pop(0))
                        for g in list(active):
                            try:
                                next(g)
                            except StopIteration:
                                active.remove(g)

                run_interleaved([all_tile(t) for t in range(NT)], 3)

                def own_tile(u):
                    i = u % NB
                    (po__, bpo), (csk_, bcsk), (csi_, bcsi) = po_[i], csk[i], csi[i]
                    rows = slice(u * 128, (u + 1) * 128)
                    S.dma("sp", po__[:], P_own[rows, 0:4112], reads=[B_P_own], writes=[bpo])
                    S.dma("sp", csk_[:], ropeQ_in[rows].rearrange("p (a h d) -> p a h d", a=2, h=8), writes=[bcsk])
                    S.dma("sp", csi_[:], ropeQI_in[rows].rearrange("p (a h d) -> p a h d", a=2, h=16), writes=[bcsi])
                    (pT_, bpT), (kT_, bkT) = pT[i], kT[i]
                    q3 = po__[:, 0:1024].rearrange("p (h d) -> p h d", h=8)
                    krv, bkr = yield from headnorm_rope(i, q3, 8, 128, gq_bc, csk_, 16, bpo, bcsk)
                    for h in range(8):
                        S.op("pe", lambda e: e.transpose(out=pT_[:, h, :], in_=krv[:, h, :], identity=ident[:]),
                             reads=[bkr, b_ident], writes=[bpT])
                    S.op("act", lambda e: e.copy(out=kT_[:], in_=pT_[:]), reads=[bpT], writes=[bkT])
                    S.dma("pool", QT_d[:, :, rows], kT_[:], reads=[bkT], writes=[B_QT])
                    yield
                    for (c0, dstd, bdst) in ((1024, zaT_d, B_zaT), (3088, zbT_d, B_zbT)):
                        (zs_, bzs), (zb__, bzb) = zs[i], zb_[i]
                        S.op("act", lambda e: e.activation(out=zs_[:], in_=po__[:, c0:c0 + 1024], func=AF.Sigmoid),
                             reads=[bpo], writes=[bzs])
                        S.op("dve", lambda e: e.tensor_tensor(out=zb__[:].rearrange("p a b -> p (a b)"), in0=zs_[:],
                                                              in1=po__[:, c0:c0 + 1024], op=ALU.mult),
                             reads=[bzs, bpo], writes=[bzb])
                        for h in range(8):
                            S.op("pe", lambda e: e.transpose(out=pT_[:, h, :], in_=zb__[:, h, :], identity=ident[:]),
                                 reads=[bzb, b_ident], writes=[bpT])
                        S.op("act", lambda e: e.copy(out=kT_[:], in_=pT_[:]), reads=[bpT], writes=[bkT])
                        S.dma("pool", dstd[:, :, rows], kT_[:], reads=[bkT], writes=[bdst])
                    qi3 = po__[:, 2048:3072].rearrange("p (h d) -> p h d", h=16)
                    yield
                    qv, bkr = yield from headnorm_rope(i, qi3, 16, 64, None, csi_, 8, bpo, bcsi, do_norm=False)
                    qflat = kr[i][0][:]
                    for h in range(8):
                        S.op("pe", lambda e: e.transpose(out=pT_[:, h, :], in_=qflat[:, h, :], identity=ident[:]),
                             reads=[bkr, b_ident], writes=[bpT])
                    S.op("act", lambda e: e.copy(out=kT_[:], in_=pT_[:]), reads=[bpT], writes=[bkT])
                    S.dma("pool", qiT_d[:, :, rows], kT_[:], reads=[bkT], writes=[B_qiT])
                    (wv_, bwv) = wv[i]
                    S.op("dve", lambda e: e.tensor_scalar(out=wv_[:], in0=po__[:, 3072:3088], scalar1=1.0 / 32.0,
                                                          scalar2=None, op0=ALU.mult), reads=[bpo], writes=[bwv])
                    S.dma("pool", w_d[rows, :], wv_[:], reads=[bwv], writes=[B_w])
                    yield

                run_interleaved([own_tile(u) for u in range(9)], 3)
                S.barrier()

        if "hgrn" in stages:
            with contextlib.ExitStack() as es:
                Uf, b_U = C.sb(es, "Uf", [128, 128], F32)
                Wf, b_W = C.sb(es, "Wf", [128, 128], F32)
                chi, b_chi = C.sb(es, "chi", [128, 2], F32)
                lb, b_lb = C.sb(es, "lb_sb", [128, 1024], F32)
                oml, b_oml = C.sb(es, "oml", [128, 1024], F32)
                l1, b_l1 = C.sb(es, "l1", [128, 1024], F32)
                go_bc, b_go = C.sb(es, "go_sb", [128, 128], F32)
                S.dma("sp", Uf[:], U_in[:, :], writes=[b_U])
                S.dma("sp", Wf[:], W_in[:, :], writes=[b_W])
                S.dma("sp", chi[:], chi_in[:, :], writes=[b_chi])
                S.dma("sp", go_bc[:], go_in[:, :], writes=[b_go])
                S.dma("sp", lb[:], lb0_in[:, :], writes=[b_lb])
                S.dma("sp", l1[:], lb1_in[:, :], writes=[b_l1])
                S.op("dve", lambda e: e.tensor_tensor(out=lb[:], in0=lb[:], in1=l1[:], op=ALU.subtract),
                     reads=[b_lb, b_l1], writes=[b_lb])
                S.op("act", lambda e: e.activation(out=lb[:], in_=lb[:], func=AF.Sigmoid), reads=[b_lb], writes=[b_lb])
                S.op("dve", lambda e: e.tensor_scalar(out=oml[:], in0=lb[:], scalar1=-1.0, scalar2=1.0, op0=ALU.mult,
                                                      op1=ALU.add), reads=[b_lb], writes=[b_oml])
                Sf, b_Sf = C.sb(es, "Sf", [128, 8, 128], F32)
                S0b, b_S0b = C.sb(es, "S0b", [128, 8, 128], BF16)
                ybT, b_ybT = C.sb(es, "ybT", [128, 8, NOWN], BF16)
                S.op("pool", lambda e: e.memset(Sf[:], 0.0), writes=[b_Sf])
                S.op("pool", lambda e: e.memset(S0b[:], 0.0), writes=[b_S0b])
                S.op("pool", lambda e: e.memset(ybT[:], 0.0), writes=[b_ybT])
                NB = 2
                qb = [C.sb(es, f"qb{i}", [128, 1024], F32) for i in range(NB)]
                fb = [C.sb(es, f"fb{i}", [128, 1024], F32) for i in range(NB)]
                ib = [C.sb(es, f"ib{i}", [128, 1024], F32) for i in range(NB)]
                gg = [C.sb(es, f"gg{i}", [128, 1024], F32) for i in range(NB)]
                kk = [C.sb(es, f"kk{i}", [128, 1024], F32) for i in range(NB)]
                e1 = [C.sb(es, f"e1{i}", [128, 512], F32) for i in range(NB)]
                e2 = [C.sb(es, f"e2{i}", [128, 512], F32) for i in range(NB)]
                e3 = [C.sb(es, f"e3{i}", [128, 512], F32) for i in range(NB)]
                qt = [C.sb(es, f"qt{i}", [128, 1024], BF16) for i in range(NB)]
                kt = [C.sb(es, f"kt{i}", [128, 1024], BF16) for i in range(NB)]
                kd = [C.sb(es, f"kd{i}", [128, 1024], BF16) for i in range(NB)]
                vv = [C.sb(es, f"vv{i}", [128, 1024], BF16) for i in range(NB)]
                ebl = [C.sb(es, f"ebl{i}", [128, 16], F32) for i in range(NB)]
                qkT8 = [C.sb(es, f"qkT8{i}", [128, 8, 256], BF16) for i in range(2)]
                AT8, b_AT8 = C.sb(es, "AT8", [128, 8, 128], BF16)
                S1f8, b_S1f8 = C.sb(es, "S1f8", [128, 8, 128], F32)
                S1b8, b_S1b8 = C.sb(es, "S1b8", [128, 8, 128], BF16)
                S0b2 = [(S0b, b_S0b), C.sb(es, "S0b_1", [128, 8, 128], BF16)]
                on8, b_on8 = C.sb(es, "on8", [128, 8, 128], BF16)
                sq8, b_sq8 = C.sb(es, "sq8", [128, 8, 128], F32)
                ss8, b_ss8 = C.sb(es, "ss8", [128, 8], F32)
                U8, b_U8 = C.sb(es, "U8", [128, 8, 128], F32)
                for h in range(8):
                    S.op("dve", lambda e: e.tensor_copy(out=U8[:, h, :], in_=Uf[:]), reads=[b_U], writes=[b_U8])
                X1, b_X1 = C.ps(es, "X1", [128, 1024], F32)
                X2, b_X2 = C.ps(es, "X2", [128, 8, 128], F32)
                X3, b_X3 = C.ps(es, "X3", [128, 8, 128], F32)
                pCS, b_pCS = C.ps(es, "pCS", [128, 144], F32)
                pQ4, b_pQ4 = C.ps(es, "pQ4", [128, 4, 256], BF16)
                pB, pBD = X1[:, 0:512], X1[:, 512:1024]
                b_pB = b_pBD = b_X1
                X1v = X1.rearrange("p (a b) -> p a b", a=8)
                pC = pCS[:, 0:16]
                b_pC = b_pCS
                pS3 = pCS[:, 16:144].rearrange("p (a b) -> p a b", a=8)
                def hg_prep(t):
                        i = t % NB
                        rows = slice(t * 128, (t + 1) * 128)
                        (qb_, bqb), (fb_, bfb), (ib_, bib), (gg_, bgg), (kk_, bkk) = qb[i], fb[i], ib[i], gg[i], kk[i]
                        (qt_, bqt), (kt_, bkt), (kd_, bkd), (vv_, bvv), (ebl_, bebl) = qt[i], kt[i], kd[i], vv[i], ebl[i]
                        S.dma("sp", qb_[:], P_all[rows, 1088:2112], reads=[B_P_all], writes=[bqb])
                        S.dma("sp", fb_[:], P_all[rows, 2112:3136], reads=[B_P_all], writes=[bfb])
                        S.dma("sp", ib_[:], P_all[rows, 3136:4160], reads=[B_P_all], writes=[bib])
                        S.op("act", lambda e: e.copy(out=vv_[:], in_=ib_[:]), reads=[bib], writes=[bvv])
                        S.op("act", lambda e: e.activation(out=fb_[:], in_=fb_[:], func=AF.Sigmoid), reads=[bfb], writes=[bfb])
                        S.op("act", lambda e: e.activation(out=ib_[:], in_=qb_[:], func=AF.Sigmoid), reads=[bqb, bvv],
                             writes=[bib])
                        S.op("dve", lambda e: e.tensor_tensor(out=fb_[:], in0=fb_[:], in1=oml[:], op=ALU.mult),
                             reads=[bfb, b_oml], writes=[bfb])
                        S.op("pool", lambda e: e.tensor_tensor(out=fb_[:], in0=fb_[:], in1=lb[:], op=ALU.add),
                             reads=[bfb, b_lb], writes=[bfb])
                        S.op("act", lambda e: e.activation(out=gg_[:], in_=fb_[:], func=AF.Ln), reads=[bfb], writes=[bgg])
                        S.op("pool", lambda e: e.tensor_scalar(out=kk_[:], in0=fb_[:], scalar1=-1.0, scalar2=1.0,
                                                               op0=ALU.mult, op1=ALU.add), reads=[bfb], writes=[bkk])

                        S.op("dve", lambda e: e.tensor_tensor(out=qb_[:], in0=qb_[:], in1=ib_[:], op=ALU.mult),
                             reads=[bqb, bib], writes=[bqb])
                        for hf in range(2):
                            cs = slice(hf * 512, (hf + 1) * 512)
                            (e1_, be1), (e2_, be2), (e3_, be3) = e1[hf], e2[hf], e3[hf]
                            S.op("pe", lambda e: e.matmul(pB[:], lhsT=Uf[:], rhs=gg_[:, cs], start=True, stop=True),
                                 reads=[b_U, bgg], writes=[b_pB])
                            S.op("pe", lambda e: e.matmul(pBD[:], lhsT=Wf[:], rhs=gg_[:, cs], start=True, stop=True),
                                 reads=[b_W, bgg], writes=[b_pBD])
                            S.op("act", lambda e: e.activation(out=e1_[:], in_=pB[:], func=AF.Exp), reads=[b_pB], writes=[be1])
                            S.op("act", lambda e: e.activation(out=e2_[:], in_=pB[:], func=AF.Exp, scale=-1.0),
                                 reads=[b_pB], writes=[be2])
                            S.op("act", lambda e: e.activation(out=e3_[:], in_=pBD[:], func=AF.Exp), reads=[b_pBD],
                                 writes=[be3])
                            S.op("dve", lambda e: e.tensor_tensor(out=qt_[:, cs], in0=qb_[:, cs], in1=e1_[:], op=ALU.mult),
                                 reads=[bqb, be1], writes=[bqt])
                            S.op("pool", lambda e: e.tensor_tensor(out=kt_[:, cs], in0=kk_[:, cs], in1=e2_[:], op=ALU.mult),
                                 reads=[bkk, be2], writes=[bkt])
                            S.op("dve", lambda e: e.tensor_tensor(out=kd_[:, cs], in0=kk_[:, cs], in1=e3_[:], op=ALU.mult),
                                 reads=[bkk, be3], writes=[bkd])
                        for h in range(8):
                            S.op("pe", lambda e: e.matmul(pC[:, 2 * h:2 * h + 2], lhsT=gg_[:, h * 128:(h + 1) * 128],
                                                          rhs=chi[:], start=True, stop=True),
                                 reads=[bgg, b_chi], writes=[b_pC])
                        S.op("act", lambda e: e.activation(out=ebl_[:], in_=pC[:], func=AF.Exp), reads=[b_pC], writes=[bebl])

                def hg_tail(t):
                        i = t % NB
                        (qt_, bqt), (kt_, bkt), (kd_, bkd), (vv_, bvv), (ebl_, bebl) = qt[i], kt[i], kd[i], vv[i], ebl[i]
                        (qk_, bqk) = qkT8[t % 2]
                        (S0c, bS0c), (S0n, bS0n) = S0b2[t % 2], S0b2[(t + 1) % 2]
                        hcs = [slice(h * 128, (h + 1) * 128) for h in range(8)]
                        for half in range(2):
                            for hh in range(4):
                                h = 4 * half + hh
                                S.op("pe", lambda e: e.transpose(out=pQ4[:, hh, 0:128], in_=qt_[:, hcs[h]], identity=ident[:]),
                                     reads=[bqt, b_ident], writes=[b_pQ4])
                                S.op("pe", lambda e: e.transpose(out=pQ4[:, hh, 128:256], in_=kt_[:, hcs[h]],
                                                                 identity=ident[:]), reads=[bkt, b_ident], writes=[b_pQ4])
                            if half == 0:
                                S.op("act", lambda e: e.copy(out=qk_[:, 0:4, :], in_=pQ4[:]), reads=[b_pQ4], writes=[bqk])
                                for h in range(8):
                                    S.op("pe", lambda e: e.matmul(X2[:, h, :], lhsT=kd_[0:64, hcs[h]], rhs=vv_[0:64, hcs[h]],
                                                                  start=True, stop=True), reads=[bkd, bvv], writes=[b_X2])
                            else:
                                S.op("dve", lambda e: e.tensor_copy(out=qk_[:, 4:8, :], in_=pQ4[:]), reads=[b_pQ4],
                                     writes=[bqk])
                        for h in range(8):
                            S.op("pe", lambda e: e.matmul(X1v[:, h, :], lhsT=qk_[:, h, 128:256], rhs=qk_[:, h, 0:128],
                                                          start=True, stop=True), reads=[bqk], writes=[b_X1])
                        S.op("dve", lambda e: e.tensor_tensor(out=AT8[:], in0=X1v, in1=U8[:], op=ALU.mult),
                             reads=[b_X1, b_U8], writes=[b_AT8])
                        for h in range(8):
                            S.op("dve", lambda e: e.scalar_tensor_tensor(out=S1f8[:, h, :], in0=Sf[:, h, :],
                                                                         scalar=ebl_[:, 2 * h:2 * h + 1],
                                                                         in1=X2[:, h, :], op0=ALU.mult, op1=ALU.add),
                                 reads=[b_Sf, bebl, b_X2], writes=[b_S1f8])
                        S.op("act", lambda e: e.copy(out=S1b8[:], in_=S1f8[:]), reads=[b_S1f8], writes=[b_S1b8])
                        for h in range(8):
                            S.op("pe", lambda e: e.matmul(X3[:, h, :], lhsT=AT8[:, h, :], rhs=vv_[:, hcs[h]],
                                                          start=(h % 4 == 0), stop=False, skip_group_check=True),
                                 reads=[b_AT8, bvv], writes=[b_X3])
                        for h in range(8):
                            S.op("pe", lambda e: e.matmul(X3[0:64, h, :], lhsT=qk_[:, h, 0:64], rhs=S0c[:, h, :], start=False,
                                                          stop=True, skip_group_check=True),
                                 reads=[bqk, bS0c], writes=[b_X3])
                        for h in range(8):
                            S.op("pe", lambda e: e.matmul(X3[64:128, h, :], lhsT=qk_[:, h, 64:128], rhs=S1b8[:, h, :],
                                                          start=False, stop=True, skip_group_check=True),
                                 reads=[bqk, b_S1b8], writes=[b_X3])
                        for h in range(8):
                            S.op("pe", lambda e: e.matmul(X2[:, h, :], lhsT=kd_[64:128, hcs[h]], rhs=vv_[64:128, hcs[h]],
                                                          start=True, stop=True), reads=[bkd, bvv], writes=[b_X2])
                        S.op("act", lambda e: e.activation(out=sq8[:], in_=X3[:], func=AF.Square), reads=[b_X3],
                             writes=[b_sq8])
                        S.op("dve", lambda e: e.tensor_reduce(out=ss8[:], in_=sq8[:], axis=AX.X, op=ALU.add),
                             reads=[b_sq8], writes=[b_ss8])
                        S.op("act", lambda e: e.activation(out=ss8[:], in_=ss8[:], func=AF.Sqrt, scale=1.0 / 128, bias=EPS),
                             reads=[b_ss8], writes=[b_ss8])
                        S.op("dve", lambda e: e.reciprocal(out=ss8[:], in_=ss8[:]), reads=[b_ss8], writes=[b_ss8])
                        for h in range(8):
                            S.op("act", lambda e: e.activation(out=on8[:, h, :], in_=X3[:, h, :], func=AF.Copy,
                                                               scale=ss8[:, h:h + 1]),
                                 reads=[b_X3, b_ss8], writes=[b_on8])
                        for h in range(8):
                            S.op("pe", lambda e: e.matmul(pS3[:, h, :], lhsT=on8[:, h, :], rhs=selb[:], start=True, stop=True),
                                 reads=[b_on8, b_selb], writes=[b_pCS])
                        S.op("act", lambda e: e.copy(out=ybT[:, :, 16 * t:16 * t + 16], in_=pS3), reads=[b_pCS],
                             writes=[b_ybT])
                        for h in range(8):
                            S.op("dve", lambda e: e.scalar_tensor_tensor(out=Sf[:, h, :], in0=S1f8[:, h, :],
                                                                         scalar=ebl_[:, 2 * h + 1:2 * h + 2], in1=X2[:, h, :],
                                                                         op0=ALU.mult, op1=ALU.add),
                                 reads=[b_S1f8, bebl, b_X2], writes=[b_Sf])
                        S.op("act", lambda e: e.copy(out=S0n[:], in_=Sf[:]), reads=[b_Sf], writes=[bS0n])

                hg_prep(0)
                for t in range(NT):
                    if t + 1 < NT:
                        hg_prep(t + 1)
                    hg_tail(t)
                S.dma("pool", ybT_d[:, :, :], ybT[:], reads=[b_ybT], writes=[B_ybT])
                S.barrier()

        if "attn" in stages:
            with contextlib.ExitStack() as es:
                ki2, b_ki2 = C.sb(es, "ki2", [128, TP], BF16)
                for q4 in range(5):
                    S.dma("sp", ki2[:, q4 * 1664:(q4 + 1) * 1664], kiT_d[:, q4 * 1664:(q4 + 1) * 1664],
                          reads=[B_kiT], writes=[b_ki2])
                AM, b_AM = C.sb(es, "AM_sb", [128, 1024], F32)
                S.dma("sp", AM[:], AM_in[:, :], writes=[b_AM])
                I4, b_I4 = C.sb(es, "I4", [128, 512], BF16)
                for r in range(4):
                    S.op("dve", lambda e: e.tensor_copy(out=I4[:, r * 128:(r + 1) * 128], in_=identf[:]),
                         reads=[b_identf], writes=[b_I4])
                ones_b, b_ones = C.sb(es, "ones_b", [128, 128], BF16)
                S.op("pool", lambda e: e.memset(ones_b[:], 1.0), writes=[b_ones])
                gq_bc, b_gq = C.sb(es, "gq_bc2", [128, 128], F32)
                gk_bc, b_gk = C.sb(es, "gk_bc2", [128, 128], F32)
                mq, b_mq = C.sb(es, "mq", [128, 1], F32)
                mk, b_mk = C.sb(es, "mk", [128, 1], F32)
                S.dma("sp", gq_bc[:], gq_in[:, :], writes=[b_gq])
                S.dma("sp", gk_bc[:], gk_in[:, :], writes=[b_gk])
                S.op("dve", lambda e: e.tensor_reduce(out=mq[:], in_=gq_bc[:], axis=AX.X, op=ALU.max,
                                                      apply_absolute_value=True), reads=[b_gq], writes=[b_mq])
                S.op("dve", lambda e: e.tensor_reduce(out=mk[:], in_=gk_bc[:], axis=AX.X, op=ALU.max,
                                                      apply_absolute_value=True), reads=[b_gk], writes=[b_mk])
                S.op("dve", lambda e: e.tensor_tensor(out=mq[:], in0=mq[:], in1=mk[:], op=ALU.mult),
                     reads=[b_mq, b_mk], writes=[b_mq])
                S.op("dve", lambda e: e.tensor_scalar(out=mq[:], in0=mq[:], scalar1=-(128.0 ** 0.5), scalar2=None,
                                                      op0=ALU.mult), reads=[b_mq], writes=[b_mq])
                score, b_score = C.sb(es, "score", [128, 8208], F32)
                cjunk, b_cjunk = C.sb(es, "cjunk", [128, 8208], BF16)
                MB, b_MB = C.sb(es, "MB", [128, 8208], BF16)
                QTj, b_QTj = C.sb(es, "QTj", [128, 8, 128], BF16)
                qiTj, b_qiTj = C.sb(es, "qiTj", [128, 8, 128], BF16)
                zaTj, b_zaTj = C.sb(es, "zaTj", [128, 8, 128], BF16)
                wj, b_wj = C.sb(es, "wj", [128, 16], F32)
                Dg, b_Dg = C.sb(es, "Dg", [128, 16, 128], BF16)
                NR = 4
                Rl = [C.sb(es, f"Rl{i}", [128, 512], BF16) for i in range(8)]
                PT = [C.sb(es, f"PT{i}", [128, 512], BF16) for i in range(NR)]
                KTc = [C.sb(es, f"KTc{i}", [128, 4, 512], BF16) for i in range(2)]
                Vc = [C.sb(es, f"Vc{i}", [128, 4, 512], BF16) for i in range(2)]
                sm = {n: C.sb(es, "bs_" + n, [128, 1], F32) for n in ("lo", "hi", "mid", "cnt", "ge", "d1", "d2", "B", "nmid", "sga")}
                ajunk, b_ajunk = C.sb(es, "ajunk", [128, 4608], BF16)
                rden, b_rden = C.sb(es, "rden", [128, 512], F32)
                yaT, b_yaT = C.sb(es, "yaT", [128, 8, 128], BF16)
                oT, b_oT = C.sb(es, "oT", [128, 512], F32)
                pL = [C.ps(es, f"pL{i}", [128, 512], F32) for i in range(3)]
                pSc, b_pSc = C.ps(es, "pSc", [128, 512], F32)
                pOA = [C.ps(es, f"pOA{i}", [128, 512], F32) for i in range(2)]
                pDn = [C.ps(es, f"pDn{i}", [128, 512], F32) for i in range(2)]
                pLx = pL + pOA + pDn
                nrl = 0
                npl = 0
                nplx = 0
                npt = 0
                nkc = 0
                score2 = [(score, b_score), C.sb(es, 'score_1', [128, 8208], F32)]
                Dg2 = [(Dg, b_Dg), C.sb(es, 'Dg_1', [128, 16, 128], BF16)]
                qiT2 = [(qiTj, b_qiTj), C.sb(es, 'qiTj_1', [128, 8, 128], BF16)]
                wj2 = [(wj, b_wj), C.sb(es, 'wj_1', [128, 16], F32)]

                def gen_indexer(j):
                    nonlocal nrl, nplx
                    s0 = 2 + 128 * j
                    NJ = 16 + 1024 * (j + 1)
                    (score, b_score), (Dg, b_Dg), (qiTj, b_qiTj), (wj, b_wj) = score2[j % 2], Dg2[j % 2], qiT2[j % 2], wj2[j % 2]
                    S.dma("sp", qiTj[:], qiT_d[:, :, s0:s0 + 128], reads=[B_qiT], writes=[b_qiTj])
                    S.dma("sp", wj[:], w_d[s0:s0 + 128, :], reads=[B_w], writes=[b_wj])
                    for h in range(16):
                        S.op("dve", lambda e: e.tensor_scalar(out=Dg[:, h, :], in0=identf[:], scalar1=wj[:, h:h + 1],
                                                              scalar2=None, op0=ALU.mult),
                             reads=[b_identf, b_wj], writes=[b_Dg])
                    yield
                    items = []
                    c0 = 0
                    while c0 < NJ:
                        cw = min(512, NJ - c0)
                        for h in range(16):
                            items.append((c0, cw, h))
                        c0 += cw
                    LAG = 4
                    slots = {}
                    order = []
                    for base in range(0, len(items) + LAG, 2):
                        order += [("L", base), ("L", base + 1), ("A", base - LAG), ("A", base + 1 - LAG)]
                    for kind, idx in order:
                        if kind == "L" and idx < len(items):
                            c0, cw, h = items[idx]
                            (pl_, bpl) = pLx[nplx % 7]
                            nplx += 1
                            (rl_, brl) = Rl[nrl % 8]
                            nrl += 1
                            slots[idx] = (rl_, brl)
                            pr = slice((h % 2) * 64, (h % 2) * 64 + 64)
                            S.op("pe", lambda e: e.matmul(pl_[:, 0:cw], lhsT=qiTj[pr, h // 2, :], rhs=ki2[pr, c0:c0 + cw],
                                                          start=True, stop=True),
                                 reads=[b_qiTj, b_ki2], writes=[bpl])
                            if h % 2 == 0:
                                S.op("act", lambda e: e.activation(out=rl_[:, 0:cw], in_=pl_[:, 0:cw], func=AF.Relu),
                                     reads=[bpl], writes=[brl])
                            else:
                                S.op("dve", lambda e: e.tensor_scalar(out=rl_[:, 0:cw], in0=pl_[:, 0:cw], scalar1=0.0,
                                                                      scalar2=None, op0=ALU.max),
                                     reads=[bpl], writes=[brl])
                        if kind == "A" and 0 <= idx < len(items):
                            c0, cw, h = items[idx]
                            (rl_, brl) = slots.pop(idx)
                            S.op("pe", lambda e: e.matmul(pSc[:, 0:cw], lhsT=Dg[:, h, :], rhs=rl_[:, 0:cw],
                                                          start=(h == 0), stop=(h == 15)),
                                 reads=[b_Dg, brl], writes=[b_pSc])
                            if h == 15:
                                S.op("act", lambda e: e.copy(out=score[:, c0:c0 + cw], in_=pSc[:, 0:cw]),
                                     reads=[b_pSc], writes=[b_score])
                        if kind == 'A' and idx % 2 == 1:
                            yield

                def gen_bisect(j):
                    NJ = 16 + 1024 * (j + 1)
                    (score, b_score) = score2[j % 2]
                    g_ = lambda n: sm[n][0]
                    bb = lambda n: sm[n][1]
                    S.op("dve", lambda e: e.tensor_reduce(out=g_("B")[:], in_=score[:, 0:NJ], axis=AX.X, op=ALU.max,
                                                          apply_absolute_value=True), reads=[b_score], writes=[bb("B")])
                    S.op("dve", lambda e: e.tensor_scalar(out=g_("hi")[:], in0=g_("B")[:], scalar1=1.001, scalar2=1e-6,
                                                          op0=ALU.mult, op1=ALU.add), reads=[bb("B")], writes=[bb("hi")])
                    S.op("dve", lambda e: e.tensor_scalar(out=g_("lo")[:], in0=g_("hi")[:], scalar1=-1.0, scalar2=None,
                                                          op0=ALU.mult), reads=[bb("hi")], writes=[bb("lo")])
                    S.op("dve", lambda e: e.tensor_tensor(out=score[:, NJ - 1024:NJ], in0=score[:, NJ - 1024:NJ],
                                                          in1=AM[:], op=ALU.add), reads=[b_score, b_AM], writes=[b_score])
                    S.op("dve", lambda e: e.tensor_tensor(out=g_("d2")[:], in0=g_("hi")[:], in1=g_("lo")[:],
                                                          op=ALU.subtract), reads=[bb("hi"), bb("lo")], writes=[bb("d2")])
                    ND = (NJ * 9 // 20) // 16 * 16
                    NA = NJ - ND
                    for it in range(24):
                        cit = 0.5 ** (it + 1)
                        S.op("dve", lambda e: e.tensor_scalar(out=g_("mid")[:], in0=g_("d2")[:], scalar1=cit,
                                                              scalar2=g_("lo")[:, 0:1], op0=ALU.mult, op1=ALU.add),
                             reads=[bb("d2"), bb("lo")], writes=[bb("mid")])
                        S.op("act", lambda e: e.activation(out=ajunk[:, 0:NA], in_=score[:, ND:NJ], func=AF.Sign,
                                                           scale=-1.0, bias=g_("mid")[:, 0:1], accum_out=g_("sga")[:]),
                             reads=[b_score, bb("mid")], writes=[b_ajunk, bb("sga")])
                        S.op("dve", lambda e: e.tensor_scalar(out=cjunk[:, 0:ND], in0=score[:, 0:ND],
                                                              scalar1=g_("mid")[:, 0:1], scalar2=None, op0=ALU.is_ge,
                                                              op1=ALU.add, accum_out=g_("cnt")[:]),
                             reads=[b_score, bb("mid")], writes=[b_cjunk, bb("cnt")])
                        S.op("dve", lambda e: e.scalar_tensor_tensor(out=g_("cnt")[:], in0=g_("cnt")[:], scalar=2.0,
                                                                     in1=g_("sga")[:], op0=ALU.mult, op1=ALU.subtract),
                             reads=[bb("cnt"), bb("sga")], writes=[bb("cnt")])
                        S.op("dve", lambda e: e.tensor_scalar(out=g_("ge")[:], in0=g_("cnt")[:], scalar1=float(511 - NA),
                                                              scalar2=None, op0=ALU.is_ge), reads=[bb("cnt")],
                             writes=[bb("ge")])
                        S.op("dve", lambda e: e.tensor_scalar(out=g_("d1")[:], in0=g_("mid")[:], scalar1=g_("lo")[:, 0:1],
                                                              scalar2=g_("ge")[:, 0:1], op0=ALU.subtract, op1=ALU.mult),
                             reads=[bb("mid"), bb("lo"), bb("ge")], writes=[bb("d1")])
                        S.op("dve", lambda e: e.tensor_tensor(out=g_("lo")[:], in0=g_("lo")[:], in1=g_("d1")[:],
                                                              op=ALU.add), reads=[bb("lo"), bb("d1")], writes=[bb("lo")])
                        yield
                    yield

                def emit_MB(j):
                    NJ = 16 + 1024 * (j + 1)
                    (score, b_score) = score2[j % 2]
                    g_ = lambda n: sm[n][0]
                    bb = lambda n: sm[n][1]
                    S.op("dve", lambda e: e.tensor_scalar(out=MB[:, 0:NJ], in0=score[:, 0:NJ], scalar1=g_("lo")[:, 0:1],
                                                          scalar2=NEG, op0=ALU.is_lt, op1=ALU.mult),
                         reads=[b_score, bb("lo")], writes=[b_MB])

                def run_pair(ga, gb):
                    la = list_steps = None
                    done_a = done_b = False
                    if gb is None:
                        for _ in ga:
                            pass
                        return
                    while not (done_a and done_b):
                        if not done_a:
                            try:
                                next(ga)
                            except StopIteration:
                                done_a = True
                        for _ in range(RATIO[0]):
                            if not done_b:
                                try:
                                    next(gb)
                                except StopIteration:
                                    done_b = True

                def gen_loop(j):
                    nonlocal npl, npt, nkc
                    NJ = 16 + 1024 * (j + 1)
                    nkt = (NJ + 127) // 128
                    aitems = [(kt_, G) for kt_ in range(nkt) for G in range(2)]
                    chunkbuf = {}
                    pend = {}

                    def emit_pv(ii):
                        kt_, G = aitems[ii]
                        (pt_, bpt) = pend.pop(ii)
                        (Vc_, bVc) = chunkbuf[kt_ // 4][1]
                        q = kt_ % 4
                        kw = min(128, NJ - kt_ * 128)
                        first, last = (kt_ == 0), (kt_ == nkt - 1)
                        for g2 in range(2):
                            g = 2 * G + g2
                            S.op("pe", lambda e: e.matmul(pOA[G][0][:, g2 * 256:(g2 + 1) * 256],
                                                          lhsT=Vc_[0:kw, q, g * 128:(g + 1) * 128],
                                                          rhs=pt_[0:kw, g2 * 256:(g2 + 1) * 256],
                                                          start=(first and g2 == 0), stop=last, skip_group_check=True),
                                 reads=[bVc, bpt], writes=[pOA[G][1]])
                        S.op("pe", lambda e: e.matmul(pDn[G][0][:], lhsT=ones_b[0:kw, :], rhs=pt_[0:kw, :],
                                                      start=first, stop=last, skip_group_check=True),
                             reads=[b_ones, bpt], writes=[pDn[G][1]])

                    for ii, (kt_, G) in enumerate(aitems):
                        if kt_ % 4 == 0 and G == 0:
                            (KTc_, bKTc), (Vc_, bVc) = KTc[nkc % 2], Vc[nkc % 2]
                            nkc += 1
                            chunkbuf[kt_ // 4] = ((KTc_, bKTc), (Vc_, bVc))
                            k0 = kt_ * 128
                            kwid = min(512, NJ - k0)
                            S.dma("sp", KTc_[:, :, 0:kwid], KT_d[:, :, k0:k0 + kwid], reads=[B_KT], writes=[bKTc])
                            ntl = (kwid + 127) // 128
                            for q in range(ntl):
                                kw_ = min(128, kwid - q * 128)
                                S.dma("act", Vc_[0:kw_, q, :], V_d[k0 + q * 128:k0 + q * 128 + kw_, :], reads=[B_V],
                                      writes=[bVc])
                        (KTc_, bKTc) = chunkbuf[kt_ // 4][0]
                        q = kt_ % 4
                        kw = min(128, NJ - kt_ * 128)
                        ks = slice(kt_ * 128, kt_ * 128 + kw)
                        kl = slice(q * 128, q * 128 + kw)
                        (pl_, bpl) = pL[npl % 3]
                        npl += 1
                        (pt_, bpt) = PT[npt % NR]
                        npt += 1
                        S.op("pe", lambda e: e.matmul(pl_[0:kw, :], lhsT=MB[:, ks], rhs=I4[:], start=True, stop=False,
                                                      skip_group_check=True), reads=[b_MB, b_I4], writes=[bpl])
                        for g2 in range(2):
                            g = 2 * G + g2
                            S.op("pe", lambda e: e.matmul(
                                pl_[0:kw, g2 * 256:(g2 + 1) * 256], lhsT=KTc_[:, g, kl],
                                rhs=QTj[:].rearrange("p a b -> p (a b)")[:, 2 * g * 128:(2 * g + 2) * 128],
                                start=False, stop=True, skip_group_check=True),
                                 reads=[bKTc, b_QTj], writes=[bpl])
                        S.op("act", lambda e: e.activation(out=pt_[0:kw, :], in_=pl_[0:kw, :], func=AF.Exp,
                                                           scale=128.0 ** -0.5, bias=mq[0:kw, 0:1]),
                             reads=[bpl, b_mq], writes=[bpt])
                        pend[ii] = (pt_, bpt)
                        if ii >= 2:
                            emit_pv(ii - 2)
                            yield
                    for ii in range(max(0, len(aitems) - 2), len(aitems)):
                        emit_pv(ii)
                    yield

                RATIO = [1]
                for _ in gen_indexer(0):
                    pass
                for _ in gen_bisect(0):
                    pass
                emit_MB(0)
                for _ in gen_indexer(1):
                    pass
                for j in range(8):
                    s0 = 2 + 128 * j
                    NJ = 16 + 1024 * (j + 1)
                    S.dma("sp", QTj[:], QT_d[:, :, s0:s0 + 128], reads=[B_QT], writes=[b_QTj])
                    S.dma("sp", zaTj[:], zaT_d[:, :, s0:s0 + 128], reads=[B_zaT], writes=[b_zaTj])
                    n_loop_steps = 2 * ((NJ + 127) // 128)
                    RATIO[0] = max(1, -(-26 // n_loop_steps))
                    run_pair(gen_loop(j), gen_bisect(j + 1) if j < 7 else None)
                    for G in range(2):
                        S.op("dve", lambda e: e.reciprocal(out=rden[:], in_=pDn[G][0][:]), reads=[pDn[G][1]],
                             writes=[b_rden])
                        S.op("dve", lambda e: e.tensor_tensor(out=oT[:], in0=pOA[G][0][:], in1=rden[:], op=ALU.mult),
                             reads=[pOA[G][1], b_rden], writes=[b_oT])
                        S.op("dve", lambda e: e.tensor_tensor(
                            out=yaT[:, 4 * G:4 * G + 4, :].rearrange("p a b -> p (a b)"), in0=oT[:],
                            in1=zaTj[:, 4 * G:4 * G + 4, :].rearrange("p a b -> p (a b)"), op=ALU.mult),
                             reads=[b_oT, b_zaTj], writes=[b_yaT])
                    S.dma("pool", yaT_d[:, :, j * 128:(j + 1) * 128], yaT[:], reads=[b_yaT], writes=[B_yaT])
                    if j < 7:
                        emit_MB(j + 1)
                    if j + 2 < 8:
                        for _ in gen_indexer(j + 2):
                            pass
                S.barrier()

        if "merge" in stages:
            with contextlib.ExitStack() as es0:
                mT_all, b_mT = C.sb(es0, "mT_all", [128, 8, 2048], BF16)
                wst = [C.sb(es0, f"wst{i}", [128, 2048], F32) for i in range(2)]
                nw = 0
                with contextlib.ExitStack() as es:
                    Wb = [C.sb(es, f"Wbr{i}", [128, 8, 2048], BF16) for i in range(2)]
                    gob, b_gob = C.sb(es, "gob", [128, 128], F32)
                    gocol, b_gocol = C.sb(es, "gocol", [128, 1], F32)
                    S.dma("sp", gob[:], go_in[:, :], writes=[b_gob])
                    S.op("dve", lambda e: e.tensor_tensor(out=gob[:], in0=gob[:], in1=identf[:], op=ALU.mult),
                         reads=[b_gob, b_identf], writes=[b_gob])
                    S.op("dve", lambda e: e.tensor_reduce(out=gocol[:], in_=gob[:], axis=AX.X, op=ALU.add),
                         reads=[b_gob], writes=[b_gocol])
                    for br in range(2):
                        for k in range(8):
                            (ws_, bws) = wst[nw % 2]
                            nw += 1
                            S.dma("sp", ws_[:], wbr_in[br, k * 128:(k + 1) * 128, :], writes=[bws])
                            if br == 1:
                                if k % 2:
                                    S.op("act", lambda e: e.activation(out=Wb[1][0][:, k, :], in_=ws_[:], func=AF.Copy,
                                                                       scale=gocol[:, 0:1]),
                                         reads=[bws, b_gocol], writes=[Wb[1][1]])
                                else:
                                    S.op("dve", lambda e: e.tensor_scalar(out=Wb[1][0][:, k, :], in0=ws_[:],
                                                                          scalar1=gocol[:, 0:1], scalar2=None,
                                                                          op0=ALU.mult),
                                         reads=[bws, b_gocol], writes=[Wb[1][1]])
                                continue
                            S.op("act" if k % 2 else "dve",
                                 (lambda e: e.copy(out=Wb[br][0][:, k, :], in_=ws_[:])) if k % 2 else
                                 (lambda e: e.tensor_copy(out=Wb[br][0][:, k, :], in_=ws_[:])),
                                 reads=[bws], writes=[Wb[br][1]])
                    yaTj, b_yaTj = C.sb(es, "yaTj", [128, 8, 128], BF16)
                    ybTj, b_ybTj = C.sb(es, "ybTj", [128, 8, 128], BF16)
                    zbTj, b_zbTj = C.sb(es, "zbTj", [128, 8, 128], BF16)
                    gts, b_gts = C.sb(es, "gts", [128, 4096], F32)
                    mg, b_mg = C.sb(es, "mg", [128, 2048], F32)
                    t2, b_t2 = C.sb(es, "t2", [128, 512], F32)
                    mgb, b_mgb = C.sb(es, "mgb", [128, 2048], BF16)
                    pP = [C.ps(es, f"pP{i}", [128, 512], F32) for i in range(4)]
                    pTm, b_pTm = C.ps(es, "pTm", [128, 2048], BF16)
                    npp = 0
                    for j in range(8):
                        s0 = 2 + 128 * j
                        S.dma("sp", yaTj[:], yaT_d[:, :, j * 128:(j + 1) * 128], reads=[B_yaT], writes=[b_yaTj])
                        S.dma("sp", ybTj[:], ybT_d[:, :, s0:s0 + 128], reads=[B_ybT], writes=[b_ybTj])
                        S.dma("sp", zbTj[:], zbT_d[:, :, s0:s0 + 128], reads=[B_zbT], writes=[b_zbTj])
                        S.dma("sp", gts[:], P_own[s0:s0 + 128, 4112:8208], reads=[B_P_own], writes=[b_gts])
                        S.op("dve", lambda e: e.tensor_tensor(out=ybTj[:], in0=ybTj[:], in1=zbTj[:], op=ALU.mult),
                             reads=[b_ybTj, b_zbTj], writes=[b_ybTj])
                        S.op("act", lambda e: e.activation(out=gts[:], in_=gts[:], func=AF.Sigmoid), reads=[b_gts],
                             writes=[b_gts])
                        for cb in range(4):
                            cs = slice(cb * 512, (cb + 1) * 512)
                            for br, (yT_, byT) in enumerate(((yaTj, b_yaTj), (ybTj, b_ybTj))):
                                (pp_, bpp) = pP[npp % 4]
                                npp += 1
                                for k in range(8):
                                    S.op("pe", lambda e: e.matmul(pp_[:], lhsT=yT_[:, k, :], rhs=Wb[br][0][:, k, cs],
                                                                  start=(k == 0), stop=(k == 7)),
                                         reads=[byT, Wb[br][1]], writes=[bpp])
                                if br == 0:
                                    S.op("dve", lambda e: e.tensor_tensor(out=mg[:, cs], in0=pp_[:], in1=gts[:, cs],
                                                                          op=ALU.mult), reads=[bpp, b_gts], writes=[b_mg])
                                else:
                                    S.op("dve", lambda e: e.tensor_tensor(
                                        out=t2[:], in0=pp_[:], in1=gts[:, 2048 + cb * 512:2048 + (cb + 1) * 512],
                                        op=ALU.mult), reads=[bpp, b_gts], writes=[b_t2])
                                    S.op("pool", lambda e: e.tensor_tensor(out=mgb[:, cs], in0=mg[:, cs], in1=t2[:],
                                                                           op=ALU.add), reads=[b_mg, b_t2], writes=[b_mgb])
                        for k in range(16):
                            S.op("pe", lambda e: e.transpose(out=pTm[:, k * 128:(k + 1) * 128],
                                                             in_=mgb[:, k * 128:(k + 1) * 128], identity=ident[:]),
                                 reads=[b_mgb, b_ident], writes=[b_pTm])
                        S.op("act", lambda e: e.copy(out=mT_all[:, j, 0:1024], in_=pTm[:, 0:1024]), reads=[b_pTm],
                             writes=[b_mT])
                        S.op("dve", lambda e: e.tensor_copy(out=mT_all[:, j, 1024:2048], in_=pTm[:, 1024:2048]),
                             reads=[b_pTm], writes=[b_mT])
                    S.barrier()
                with contextlib.ExitStack() as es:
                    Wo, b_Wo = C.sb(es, "Wo", [128, 16, 2048], BF16)
                    for k in range(16):
                        (ws_, bws) = wst[nw % 2]
                        nw += 1
                        S.dma("sp", ws_[:], wout_in[k * 128:(k + 1) * 128, :], writes=[bws])
                        S.op("act" if k % 2 else "dve",
                             (lambda e: e.copy(out=Wo[:, k, :], in_=ws_[:])) if k % 2 else
                             (lambda e: e.tensor_copy(out=Wo[:, k, :], in_=ws_[:])),
                             reads=[bws], writes=[b_Wo])
                    xo = [C.sb(es, f"xo{i}", [128, 2048], F32) for i in range(2)]
                    pP = [C.ps(es, f"pP2{i}", [128, 512], F32) for i in range(4)]
                    npp = 0
                    for j in range(8):
                        (xo_, bxo) = xo[j % 2]
                        S.dma("sp", xo_[:], x_own[j * 128:(j + 1) * 128, :], writes=[bxo])
                        for cb in range(4):
                            cs = slice(cb * 512, (cb + 1) * 512)
                            (pp_, bpp) = pP[npp % 4]
                            npp += 1
                            for k in range(16):
                                S.op("pe", lambda e: e.matmul(pp_[:], lhsT=mT_all[:, j, k * 128:(k + 1) * 128],
                                                              rhs=Wo[:, k, cs], start=(k == 0), stop=(k == 15)),
                                     reads=[b_mT, b_Wo], writes=[bpp])
                            S.op("dve", lambda e: e.tensor_tensor(out=xo_[:, cs], in0=xo_[:, cs], in1=pp_[:], op=ALU.add),
                                 reads=[bxo, bpp], writes=[bxo])
                        S.dma("pool", out_d[j * 128:(j + 1) * 128, :], xo_[:], reads=[bxo], writes=[B_out])
                    S.barrier()

        S.barrier(engines=("sp",))
    return nc


def host_inputs(x, meta_tokens, hgrn_lb_logits, norm_g, w_in, q_norm_g, k_norm_g, idx_k_norm_g,
                hgrn_out_norm_g, w_branch, w_out):
    x = np.asarray(x, np.float32)
    h_all = np.zeros((TP, D), np.float32)
    h_all[:NMETA] = np.asarray(meta_tokens, np.float32)
    h_all[NMETA:NMETA + SEQ] = x[0]
    common = {
        "h_all": h_all,
        "w_in": np.ascontiguousarray(np.asarray(w_in, np.float32)[0]),
        "norm_g": np.ascontiguousarray(np.asarray(norm_g, np.float32)[0]),
    }
    f32 = np.float32
    tile = lambda v, n: np.ascontiguousarray(np.tile(np.asarray(v, f32).reshape(1, -1), (n, 1)))
    common["gq_bc"] = tile(q_norm_g[0], 128)
    common["gk_bc"] = tile(k_norm_g[0], 128)
    common["gi_bc"] = tile(idx_k_norm_g[0], 128)
    common["go_bc"] = tile(hgrn_out_norm_g[0], 128)
    common["lb0"] = tile(np.asarray(hgrn_lb_logits)[0], 128)
    common["lb1"] = tile(np.asarray(hgrn_lb_logits)[1], 128)
    common["w_branch"] = np.ascontiguousarray(np.asarray(w_branch, f32)[0])
    common["w_out"] = np.ascontiguousarray(np.asarray(w_out, f32)[0])

    def rope_tab(pos, rot, heads):
        inv = np.power(np.float32(500000.0), -np.arange(0, rot, 2, dtype=f32) / np.float32(rot)).astype(f32)
        ang = pos.astype(f32)[:, None] * inv[None, :]
        cos, sin = np.cos(ang).astype(f32), np.sin(ang).astype(f32)
        cs = np.stack([np.repeat(cos[:, None, :], heads, 1), np.repeat(sin[:, None, :], heads, 1)], 1)
        return np.ascontiguousarray(cs.reshape(len(pos), -1))
    pos_all = np.arange(TP)
    common["ropeK"] = rope_tab(pos_all, 32, 4)
    common["ropeKI"] = rope_tab(pos_all, 16, 1)
    si, ti = np.arange(128)[:, None], np.arange(128)[None, :]
    same = (si // 64) == (ti // 64)
    common["U"] = (same & (si <= ti)).astype(f32)
    common["Wm"] = (same & (si > ti)).astype(f32)
    common["chi"] = (np.arange(128)[:, None] // 64 == np.arange(2)[None, :]).astype(f32)
    m_, p_ = np.arange(128)[:, None], np.arange(1024)[None, :]
    common["AM"] = np.where((p_ // 64) <= (m_ // 8), 0.0, -1e9).astype(f32)
    maps = []
    for c in range(8):
        sel = np.zeros((128, 16), np.float32)
        sel[c + 8 * np.arange(16), np.arange(16)] = 1.0
        m = dict(common)
        m["sel"] = sel
        m["x_own"] = np.ascontiguousarray(x[0, c::8])
        slot = np.arange(NOWN)
        pos_own = np.where(slot < 1040, 8 * slot + c, 0)
        m["ropeQ"] = rope_tab(pos_own, 32, 8)
        m["ropeQI"] = rope_tab(pos_own, 16, 16)
        maps.append(m)
    return maps


def kernel(**inputs):
    maps = host_inputs(**inputs)
    nc = build_nc()
    res = run_bass_kernel_spmd(nc, maps, core_ids=list(range(8)))
    out = np.zeros((1, SEQ, D), np.float32)
    for c in range(8):
        out[0, c::8] = res.results[c]["out"]
    return out
```

```python
import contextlib
import numpy as np
import concourse.bass as bass
import concourse.mybir as mybir
from concourse.bass_utils import run_bass_kernel_spmd

F32 = mybir.dt.float32
BF16 = mybir.dt.bfloat16
AF = mybir.ActivationFunctionType
ALU = mybir.AluOpType
AX = mybir.AxisListType

D = 2048
SEQ = 8192
NMETA = 16
TP = 8320
NT = 65
NOWN = 1152
NIN = 12368
EPS = 1e-6
NEG = -30000.0

C_QA, C_KA, C_VA, C_ZA, C_QI, C_KI, C_WI, C_QB, C_FB, C_IB, C_ZB, C_G = (
    0, 1024, 1536, 2048, 3072, 4096, 4160, 4176, 5200, 6224, 7248, 8272)
PA_COLS = 4160
PA_BLOCKS = [(C_KA, 512, 0), (C_VA, 512, 512), (C_KI, 64, 1024)] + \
            [(C_QB + i * 512, 512, 1088 + i * 512) for i in range(6)]
PO_COLS = 8208
PO_BLOCKS = [(C_QA + i * 512, 512, i * 512) for i in range(2)] + \
            [(C_ZA + i * 512, 512, 1024 + i * 512) for i in range(4)] + \
            [(C_WI, 16, 3072)] + \
            [(C_ZB + i * 512, 512, 3088 + i * 512) for i in range(10)]


class Buf:
    __slots__ = ("name", "w", "r")

    def __init__(self, name):
        self.name = name
        self.w = None
        self.r = []


class Eng:
    def __init__(self, name, h, sem):
        self.name, self.h, self.sem = name, h, sem
        self.count = 0
        self.seen = {}

    def wait(self, ev):
        if ev is None:
            return
        sem, val = ev
        if self.seen.get(id(sem), 0) >= val:
            return
        if self.name == "pe" and sem is self.sem:
            return
        self.h.wait_ge(sem, val)
        self.seen[id(sem)] = val


class Sched:
    NDMA = 8

    def __init__(self, nc, sems):
        self.nc = nc
        self.free_sems = list(sems)
        self.E = {}
        for name, h in (("pe", nc.tensor), ("act", nc.scalar), ("dve", nc.vector),
                        ("pool", nc.gpsimd), ("sp", nc.sync)):
            self.E[name] = Eng(name, h, self.free_sems.pop())
        self.dq = {}
        for q in ("sp", "pool", "act"):
            self.dq[q] = {"sems": [self.free_sems.pop() for _ in range(self.NDMA)],
                          "n": [0] * self.NDMA, "i": 0}

    def _deps(self, eng, reads, writes):
        for b in reads:
            eng.wait(b.w)
        for b in writes:
            eng.wait(b.w)
            for ev in b.r:
                eng.wait(ev)

    def _commit(self, ev, reads, writes):
        for b in reads:
            b.r = [e for e in b.r if e[0] is not ev[0]] + [ev]
        for b in writes:
            b.w = ev
            b.r = []

    def op(self, ename, fn, reads=(), writes=()):
        eng = self.E[ename]
        self._deps(eng, reads, writes)
        ins = fn(eng.h)
        eng.count += 1
        ins.then_inc(eng.sem, 1)
        ev = (eng.sem, eng.count)
        self._commit(ev, reads, writes)
        return ev

    def dma(self, q, out, in_, reads=(), writes=(), **kw):
        eng = self.E[q]
        d = self.dq[q]
        k = d["i"] % self.NDMA
        d["i"] += 1
        sem = d["sems"][k]
        if d["n"][k] > 0:
            eng.wait((sem, 16 * d["n"][k]))
        self._deps(eng, reads, writes)
        ins = eng.h.dma_start(out=out, in_=in_, **kw)
        d["n"][k] += 1
        ins.then_inc(sem, 16)
        ev = (sem, 16 * d["n"][k])
        self._commit(ev, reads, writes)
        return ev

    def all_events(self):
        evs = []
        for e in self.E.values():
            if e.count:
                evs.append((e.sem, e.count))
        for d in self.dq.values():
            for s, n in zip(d["sems"], d["n"]):
                if n:
                    evs.append((s, 16 * n))
        return evs

    def barrier(self, engines=None):
        evs = self.all_events()
        for name, e in self.E.items():
            if engines is not None and name not in engines:
                continue
            for ev in evs:
                e.wait(ev)


class Ctx:
    def __init__(self, nc, S):
        self.nc, self.S = nc, S

    n = 0

    def sb(self, es, name, shape, dt):
        Ctx.n += 1
        name = f"t{Ctx.n}_{name}"
        t = es.enter_context(self.nc.sbuf_tensor(name, list(shape), dt))
        return t, Buf(name)

    def ps(self, es, name, shape, dt):
        Ctx.n += 1
        name = f"t{Ctx.n}_{name}"
        n = int(np.prod(shape[1:]))
        be = 2048 // (4 if dt == F32 else 2)
        nb = -(-n // be)
        flat = es.enter_context(self.nc.psum_tensor(name, [shape[0], nb * be], dt))
        v = flat[:, 0:n]
        if len(shape) == 3:
            v = v.rearrange("p (a b) -> p a b", a=shape[1])
        elif len(shape) == 4:
            v = v.rearrange("p (a b c) -> p a b c", a=shape[1], b=shape[2])
        return v, Buf(name)


def build_nc(stages=("norm", "gemm", "post", "hgrn", "attn", "merge"), debug_out=()):
    nc = bass.Bass("TRN2", target_bir_lowering=False)
    dt_in = lambda n, s, d=F32: nc.dram_tensor(n, list(s), d, kind="ExternalInput")
    h_all = dt_in("h_all", [TP, D])
    x_own = dt_in("x_own", [1024, D])
    sel_in = dt_in("sel", [128, 16])
    w_in = dt_in("w_in", [D, NIN])
    norm_g = dt_in("norm_g", [D])
    out_d = nc.dram_tensor("out", [1024, D], F32, kind="ExternalOutput")
    gq_in = dt_in("gq_bc", [128, 128]); gk_in = dt_in("gk_bc", [128, 128]); gi_in = dt_in("gi_bc", [128, 64])
    go_in = dt_in("go_bc", [128, 128])
    ropeK_in = dt_in("ropeK", [TP, 128]); ropeKI_in = dt_in("ropeKI", [TP, 16])
    ropeQ_in = dt_in("ropeQ", [NOWN, 256]); ropeQI_in = dt_in("ropeQI", [NOWN, 256])
    U_in = dt_in("U", [128, 128]); W_in = dt_in("Wm", [128, 128]); chi_in = dt_in("chi", [128, 2])
    lb0_in = dt_in("lb0", [128, 1024]); lb1_in = dt_in("lb1", [128, 1024])
    AM_in = dt_in("AM", [128, 1024])
    wbr_in = dt_in("w_branch", [2, 1024, D]); wout_in = dt_in("w_out", [D, D])

    hT_all = nc.dram_tensor("hT_all", [NT, 128, 2048], BF16)
    hT_own = nc.dram_tensor("hT_own", [9, 128, 2048], BF16)
    kind_dbg = lambda n: "ExternalOutput" if n in debug_out else "Internal"
    P_all = nc.dram_tensor("P_all", [TP, PA_COLS], F32, kind=kind_dbg("P_all"))
    P_own = nc.dram_tensor("P_own", [NOWN, PO_COLS], F32, kind=kind_dbg("P_own"))

    KT_d = nc.dram_tensor("KT_d", [128, 4, TP], BF16)
    V_d = nc.dram_tensor("V_d", [TP, 512], BF16)
    kiT_d = nc.dram_tensor("kiT_d", [128, TP], BF16)
    QT_d = nc.dram_tensor("QT_d", [128, 8, NOWN], BF16)
    zaT_d = nc.dram_tensor("zaT_d", [128, 8, NOWN], BF16)
    zbT_d = nc.dram_tensor("zbT_d", [128, 8, NOWN], BF16)
    qiT_d = nc.dram_tensor("qiT_d", [128, 8, NOWN], BF16)
    w_d = nc.dram_tensor("w_d", [NOWN, 16], F32)
    ybT_d = nc.dram_tensor("ybT_d", [128, 8, NOWN], BF16, kind=kind_dbg("ybT_d"))
    yaT_d = nc.dram_tensor("yaT_d", [128, 8, 1024], BF16, kind=kind_dbg("yaT_d"))
    B_KT, B_V, B_kiT, B_QT, B_zaT, B_zbT, B_qiT, B_w, B_ybT, B_yaT = [Buf(n) for n in
        "KT V kiT QT zaT zbT qiT w ybT yaT".split()]

    with contextlib.ExitStack() as top:
        sems = [top.enter_context(nc.semaphore(f"s{i}")) for i in range(5 + 3 * Sched.NDMA + 2)]
        S = Sched(nc, sems)
        C = Ctx(nc, S)
        B_hT_all, B_hT_own, B_P_all, B_P_own, B_out = (Buf("hT_all"), Buf("hT_own"), Buf("P_all"),
                                                       Buf("P_own"), Buf("out"))

        identf, b_identf = C.sb(top, "identf", [128, 128], F32)
        ident, b_ident = C.sb(top, "ident", [128, 128], BF16)
        self_f, b_self_f = C.sb(top, "self_f", [128, 16], F32)
        selb, b_selb = C.sb(top, "selb", [128, 16], BF16)
        gcol, b_gcol = C.sb(top, "gcol", [128, 16], F32)
        S.op("pool", lambda e: e.memset(identf[:], 0.0), writes=[b_identf])
        S.op("pool", lambda e: e.affine_select(out=identf[:], in_=identf[:], pattern=[[-1, 128]],
                                                compare_op=ALU.not_equal, fill=1.0, base=0,
                                                channel_multiplier=1), reads=[b_identf], writes=[b_identf])
        S.op("dve", lambda e: e.tensor_copy(out=ident[:], in_=identf[:]), reads=[b_identf], writes=[b_ident])
        S.dma("sp", self_f[:], sel_in[:, :], writes=[b_self_f])
        S.op("dve", lambda e: e.tensor_copy(out=selb[:], in_=self_f[:]), reads=[b_self_f], writes=[b_selb])
        S.dma("sp", gcol[:], norm_g.ap().rearrange("(k p) -> p k", p=128), writes=[b_gcol],
              allow_slow_non_contiguous=True)

        if "norm" in stages:
            with contextlib.ExitStack() as es:
                NB = 2
                xt = [C.sb(es, f"xt{i}", [128, D], F32) for i in range(NB)]
                hn = [C.sb(es, f"hn{i}", [128, D], BF16) for i in range(NB)]
                hT = [C.sb(es, f"hT{i}", [128, D], BF16) for i in range(NB)]
                junk, b_junk = C.sb(es, "junk", [128, D], BF16)
                ss = [C.sb(es, f"ss{i}", [128, 1], F32) for i in range(NB)]
                rs = [C.sb(es, f"rs{i}", [128, 1], F32) for i in range(NB)]
                hown, b_hown = C.sb(es, "hown", [128, 9, 16, 128], BF16)
                pT = [C.ps(es, f"pT{i}", [128, D], BF16) for i in range(NB)]
                pO = [C.ps(es, f"pO{i}", [128, 16, 16], F32) for i in range(NB)]
                S.op("pool", lambda e: e.memset(hown[:], 0.0), writes=[b_hown])
                for t in range(NT):
                    i = t % NB
                    (x_, bx), (hn_, bhn), (hT_, bhT), (ss_, bss), (rs_, brs) = xt[i], hn[i], hT[i], ss[i], rs[i]
                    (pT_, bpT), (pO_, bpO) = pT[i], pO[i]
                    S.dma("sp", x_[:], h_all[t * 128:(t + 1) * 128, :], writes=[bx])
                    S.op("act", lambda e: e.activation(out=junk[:], in_=x_[:], func=AF.Square, accum_out=ss_[:]),
                         reads=[bx], writes=[b_junk, bss])
                    S.op("act", lambda e: e.activation(out=rs_[:], in_=ss_[:], func=AF.Sqrt, scale=1.0 / D,
                                                       bias=EPS), reads=[bss], writes=[brs])
                    S.op("dve", lambda e: e.reciprocal(out=rs_[:], in_=rs_[:]), reads=[brs], writes=[brs])
                    S.op("dve", lambda e: e.tensor_scalar(out=hn_[:], in0=x_[:], scalar1=rs_[:, 0:1], scalar2=None,
                                                          op0=ALU.mult), reads=[bx, brs], writes=[bhn])
                    for k in range(16):
                        S.op("pe", lambda e: e.transpose(out=pT_[:, k * 128:(k + 1) * 128],
                                                         in_=hn_[:, k * 128:(k + 1) * 128], identity=ident[:]),
                             reads=[bhn, b_ident], writes=[bpT])
                    S.op("act", lambda e: e.copy(out=hT_[:, 0:1024], in_=pT_[:, 0:1024]), reads=[bpT], writes=[bhT])
                    S.op("dve", lambda e: e.tensor_copy(out=hT_[:, 1024:2048], in_=pT_[:, 1024:2048]),
                         reads=[bpT], writes=[bhT])
                    S.dma("pool", hT_all[t], hT_[:], reads=[bhT], writes=[B_hT_all])
                    for k in range(16):
                        S.op("pe", lambda e: e.matmul(pO_[:, k, :], lhsT=hn_[:, k * 128:(k + 1) * 128], rhs=selb[:],
                                                      start=True, stop=True),
                             reads=[bhn, b_selb], writes=[bpO])
                    S.op("dve", lambda e: e.tensor_copy(out=hown[:, t // 8, :, (t % 8) * 16:(t % 8) * 16 + 16],
                                                        in_=pO_[:]), reads=[bpO], writes=[b_hown])
                for u in range(9):
                    S.dma("sp", hT_own[u].rearrange("p (k t) -> p k t", k=16), hown[:, u, :, :],
                          reads=[b_hown], writes=[B_hT_own])
                S.barrier()

        if "gemm" in stages:
            with contextlib.ExitStack() as es:
                wf = [C.sb(es, f"wf{i}", [128, 16, 512], F32) for i in range(2)]
                wb = [C.sb(es, f"wb{i}", [128, 16, 512], BF16) for i in range(2)]
                NH = 6
                hT = [C.sb(es, f"ghT{i}", [128, 16, 128], BF16) for i in range(NH)]
                ob = [C.sb(es, f"ob{i}", [128, 512], F32) for i in range(4)]
                pp = [C.ps(es, f"pp{i}", [128, 512], F32) for i in range(4)]
                w3 = w_in.ap().rearrange("(k p) c -> p k c", p=128)
                blocks = []
                for (src, ntiles, blks, dst, bdst, bsrc) in ((hT_all, NT, PA_BLOCKS, P_all, B_P_all, B_hT_all),
                                                             (hT_own, 9, PO_BLOCKS, P_own, B_P_own, B_hT_own)):
                    for (c0, wdt, d0) in blks:
                        blocks.append((src, ntiles, dst, bdst, bsrc, c0, wdt, d0))

                def load_w(bi):
                    (_, _, _, _, _, c0, wdt, _) = blocks[bi]
                    (wf_, bwf), (wb_, bwb) = wf[bi % 2], wb[bi % 2]
                    S.dma("pool", wf_[:, 0:8, 0:wdt], w3[:, 0:8, c0:c0 + wdt], writes=[bwf])
                    S.dma("sp", wf_[:, 8:16, 0:wdt], w3[:, 8:16, c0:c0 + wdt], writes=[bwf])
                    for k in range(16):
                        S.op("dve", lambda e: e.tensor_scalar(out=wb_[:, k, 0:wdt], in0=wf_[:, k, 0:wdt],
                                                              scalar1=gcol[:, k:k + 1], scalar2=None, op0=ALU.mult),
                             reads=[bwf, b_gcol], writes=[bwb])

                nt = 0
                load_w(0)
                for bi, (src, ntiles, dst, bdst, bsrc, c0, wdt, d0) in enumerate(blocks):
                    if bi + 1 < len(blocks):
                        load_w(bi + 1)
                    (wb_, bwb) = wb[bi % 2]
                    for t in range(ntiles):
                        (h_, bh), (o_, bo), (p_, bp) = hT[nt % NH], ob[nt % 4], pp[nt % 4]
                        nt += 1
                        S.dma("sp", h_[:], src[t].rearrange("p (k t) -> p k t", k=16), reads=[bsrc], writes=[bh])
                        for k in range(16):
                            S.op("pe", lambda e: e.matmul(p_[:, 0:wdt], lhsT=h_[:, k, :], rhs=wb_[:, k, 0:wdt],
                                                          start=(k == 0), stop=(k == 15)),
                                 reads=[bh, bwb], writes=[bp])
                        S.op("act", lambda e: e.copy(out=o_[:, 0:wdt], in_=p_[:, 0:wdt]), reads=[bp], writes=[bo])
                        S.dma("pool", dst[t * 128:(t + 1) * 128, d0:d0 + wdt], o_[:, 0:wdt], reads=[bo], writes=[bdst])
                S.barrier()

        if "post" in stages:
            with contextlib.ExitStack() as es:
                gq_bc, b_gq = C.sb(es, "gq_sb", [128, 128], F32)
                gk_bc, b_gk = C.sb(es, "gk_sb", [128, 128], F32)
                gi_bc, b_gi = C.sb(es, "gi_sb", [128, 64], F32)
                S.dma("sp", gq_bc[:], gq_in[:, :], writes=[b_gq])
                S.dma("sp", gk_bc[:], gk_in[:, :], writes=[b_gk])
                S.dma("sp", gi_bc[:], gi_in[:, :], writes=[b_gi])
                junks = [C.sb(es, f"pjunk{i}", [128, 128], F32) for i in range(3)]
                NB = 3
                pa = [C.sb(es, f"pa{i}", [128, 1088], F32) for i in range(NB)]
                csk = [C.sb(es, f"csk{i}", [128, 2, 8, 16], F32) for i in range(NB)]
                csi = [C.sb(es, f"csi{i}", [128, 2, 16, 8], F32) for i in range(NB)]
                ssq = [C.sb(es, f"ssq{i}", [128, 8], F32) for i in range(NB)]
                kn = [C.sb(es, f"kn{i}", [128, 8, 128], F32) for i in range(NB)]
                tt = [C.sb(es, f"tt{i}", [128, 4, 8, 16], F32) for i in range(NB)]
                kr = [C.sb(es, f"kr{i}", [128, 8, 128], BF16) for i in range(NB)]
                kT = [C.sb(es, f"kT{i}", [128, 8, 128], BF16) for i in range(NB)]
                vb = [C.sb(es, f"vb{i}", [128, 512], BF16) for i in range(NB)]
                kib = [C.sb(es, f"kib{i}", [128, 128], BF16) for i in range(NB)]
                kiT = [C.sb(es, f"kiT{i}", [128, 128], BF16) for i in range(NB)]
                pT = [C.ps(es, f"ppT{i}", [128, 8, 128], BF16) for i in range(NB)]
                pI = [C.ps(es, f"ppI{i}", [128, 128], BF16) for i in range(NB)]
                po_ = [C.sb(es, f"po{i}", [128, 4112], F32) for i in range(NB)]
                zs = [C.sb(es, f"zs{i}", [128, 1024], F32) for i in range(NB)]
                zb_ = [C.sb(es, f"zbb{i}", [128, 8, 128], BF16) for i in range(NB)]
                wv = [C.sb(es, f"wv{i}", [128, 16], F32) for i in range(NB)]

                def headnorm_rope(i, src3, H, hd, g_bc, cs_tile, half, b_src, b_cs, do_norm=True):
                    (ss_, bss), (kn_, bkn), (tt_, btt), (kr_, bkr) = ssq[i], kn[i], tt[i], kr[i]
                    (junk, b_junk) = junks[i]
                    if hd == 128:
                        knv = kn_[:, 0:H, :]
                        krv = kr_[:, 0:H, :]
                    else:
                        knv = kn_[:].rearrange("p a b -> p (a b)")[:, 0:H * hd].rearrange("p (h d) -> p h d", h=H)
                        krv = kr_[:].rearrange("p a b -> p (a b)")[:, 0:H * hd].rearrange("p (h d) -> p h d", h=H)
                    if do_norm:
                        for h in range(H):
                            S.op("act", lambda e: e.activation(out=junk[:, 0:hd], in_=src3[:, h, :], func=AF.Square,
                                                               accum_out=ss_[:, h:h + 1]),
                                 reads=[b_src], writes=[b_junk, bss])
                        S.op("act", lambda e: e.activation(out=ss_[:, 0:H], in_=ss_[:, 0:H], func=AF.Sqrt,
                                                           scale=1.0 / hd, bias=EPS), reads=[bss], writes=[bss])
                        yield
                        S.op("dve", lambda e: e.reciprocal(out=ss_[:, 0:H], in_=ss_[:, 0:H]), reads=[bss], writes=[bss])
                        for h in range(H):
                            S.op("dve", lambda e: e.scalar_tensor_tensor(out=knv[:, h, :], in0=src3[:, h, :],
                                                                         scalar=ss_[:, h:h + 1], in1=g_bc[:, 0:hd],
                                                                         op0=ALU.mult, op1=ALU.mult),
                                 reads=[b_src, bss], writes=[bkn])
                        srcn, bsn = knv, bkn
                    else:
                        srcn, bsn = src3, b_src
                    if hd == 128:
                        cosv, sinv = cs_tile[:, 0, 0:H, :], cs_tile[:, 1, 0:H, :]
                        tv = [tt_[:, q, 0:H, :] for q in range(4)]
                    else:
                        cosv, sinv = cs_tile[:, 0, 0:H, :], cs_tile[:, 1, 0:H, :]
                        tflat = tt_[:].rearrange("p a b c -> p a (b c)")
                        tv = [tflat[:, q, 0:H * half].rearrange("p (h d) -> p h d", h=H) for q in range(4)]
                    yield
                    x1, x2 = srcn[:, :, 0:half], srcn[:, :, half:2 * half]
                    S.op("dve", lambda e: e.tensor_tensor(out=tv[0], in0=x1, in1=cosv, op=ALU.mult),
                         reads=[bsn, b_cs], writes=[btt])
                    S.op("pool", lambda e: e.tensor_tensor(out=tv[1], in0=x2, in1=sinv, op=ALU.mult),
                         reads=[bsn, b_cs], writes=[btt])
                    S.op("dve", lambda e: e.tensor_tensor(out=tv[2], in0=x2, in1=cosv, op=ALU.mult),
                         reads=[bsn, b_cs], writes=[btt])
                    S.op("pool", lambda e: e.tensor_tensor(out=tv[3], in0=x1, in1=sinv, op=ALU.mult),
                         reads=[bsn, b_cs], writes=[btt])
                    S.op("act", lambda e: e.copy(out=krv, in_=srcn), reads=[bsn], writes=[bkr])
                    yield
                    S.op("dve", lambda e: e.tensor_tensor(out=krv[:, :, 0:half], in0=tv[0], in1=tv[1], op=ALU.subtract),
                         reads=[btt], writes=[bkr])
                    S.op("dve", lambda e: e.tensor_tensor(out=krv[:, :, half:2 * half], in0=tv[2], in1=tv[3], op=ALU.add),
                         reads=[btt], writes=[bkr])
                    yield
                    return krv, bkr

                def all_tile(t):
                    i = t % NB
                    (pa_, bpa), (csk_, bcsk), (csi_, bcsi) = pa[i], csk[i], csi[i]
                    rows = slice(t * 128, (t + 1) * 128)
                    S.dma("sp", pa_[:], P_all[rows, 0:1088], reads=[B_P_all], writes=[bpa])
                    S.dma("sp", csk_[:, :, 0:4, :], ropeK_in[rows].rearrange("p (a h d) -> p a h d", a=2, h=4),
                          writes=[bcsk])
                    S.dma("sp", csi_[:, :, 0:1, :], ropeKI_in[rows].rearrange("p (a h d) -> p a h d", a=2, h=1),
                          writes=[bcsi])
                    k3 = pa_[:, 0:512].rearrange("p (h d) -> p h d", h=4)
                    krv, bkr = yield from headnorm_rope(i, k3, 4, 128, gk_bc, csk_, 16, bpa, bcsk)
                    (pT_, bpT), (kT_, bkT) = pT[i], kT[i]
                    for g in range(4):
                        S.op("pe", lambda e: e.transpose(out=pT_[:, g, :], in_=krv[:, g, :], identity=ident[:]),
                             reads=[bkr, b_ident], writes=[bpT])
                    S.op("act", lambda e: e.copy(out=kT_[:, 0:4, :], in_=pT_[:, 0:4, :]), reads=[bpT], writes=[bkT])
                    S.dma("pool", KT_d[:, :, rows], kT_[:, 0:4, :], reads=[bkT], writes=[B_KT])
                    (vb_, bvb) = vb[i]
                    S.op("pool", lambda e: e.tensor_copy(out=vb_[:], in_=pa_[:, 512:1024]), reads=[bpa], writes=[bvb])
                    S.dma("pool", V_d[rows, :], vb_[:], reads=[bvb], writes=[B_V])
                    ki3 = pa_[:, 1024:1088].rearrange("p (h d) -> p h d", h=1)
                    yield
                    kiv, bkr = yield from headnorm_rope(i, ki3, 1, 64, gi_bc, csi_, 8, bpa, bcsi)
                    (kib_, bkib), (pI_, bpI), (kiT_, bkiT) = kib[i], pI[i], kiT[i]
                    S.op("dve", lambda e: e.tensor_copy(out=kib_[:, 0:64], in_=kiv[:, 0, :]), reads=[bkr], writes=[bkib])
                    S.op("pool", lambda e: e.tensor_copy(out=kib_[:, 64:128], in_=kiv[:, 0, :]), reads=[bkr], writes=[bkib])
                    S.op("pe", lambda e: e.transpose(out=pI_[:], in_=kib_[:], identity=ident[:]),
                         reads=[bkib, b_ident], writes=[bpI])
                    S.op("act", lambda e: e.copy(out=kiT_[:], in_=pI_[:]), reads=[bpI], writes=[bkiT])
                    S.dma("pool", kiT_d[:, rows], kiT_[:], reads=[bkiT], writes=[B_kiT])

                def run_interleaved(gens, width):
                    active = []
                    gens = list(gens)
                    while gens or active:
                        while gens and len(active) < width:
                            active.append(gens.pop(0))
                        for g in list(active):
                            try:
                                next(g)
                            except StopIteration:
                                active.remove(g)

                run_interleaved([all_tile(t) for t in range(NT)], 3)

                def own_tile(u):
                    i = u % NB
                    (po__, bpo), (csk_, bcsk), (csi_, bcsi) = po_[i], csk[i], csi[i]
                    rows = slice(u * 128, (u + 1) * 128)
                    S.dma("sp", po__[:], P_own[rows, 0:4112], reads=[B_P_own], writes=[bpo])
                    S.dma("sp", csk_[:], ropeQ_in[rows].rearrange("p (a h d) -> p a h d", a=2, h=8), writes=[bcsk])
                    S.dma("sp", csi_[:], ropeQI_in[rows].rearrange("p (a h d) -> p a h d", a=2, h=16), writes=[bcsi])
                    (pT_, bpT), (kT_, bkT) = pT[i], kT[i]
                    q3 = po__[:, 0:1024].rearrange("p (h d) -> p h d", h=8)
                    krv, bkr = yield from headnorm_rope(i, q3, 8, 128, gq_bc, csk_, 16, bpo, bcsk)
                    for h in range(8):
                        S.op("pe", lambda e: e.transpose(out=pT_[:, h, :], in_=krv[:, h, :], identity=ident[:]),
                             reads=[bkr, b_ident], writes=[bpT])
                    S.op("act", lambda e: e.copy(out=kT_[:], in_=pT_[:]), reads=[bpT], writes=[bkT])
                    S.dma("pool", QT_d[:, :, rows], kT_[:], reads=[bkT], writes=[B_QT])
                    yield
                    for (c0, dstd, bdst) in ((1024, zaT_d, B_zaT), (3088, zbT_d, B_zbT)):
                        (zs_, bzs), (zb__, bzb) = zs[i], zb_[i]
                        S.op("act", lambda e: e.activation(out=zs_[:], in_=po__[:, c0:c0 + 1024], func=AF.Sigmoid),
                             reads=[bpo], writes=[bzs])
                        S.op("dve", lambda e: e.tensor_tensor(out=zb__[:].rearrange("p a b -> p (a b)"), in0=zs_[:],
                                                              in1=po__[:, c0:c0 + 1024], op=ALU.mult),
                             reads=[bzs, bpo], writes=[bzb])
                        for h in range(8):
                            S.op("pe", lambda e: e.transpose(out=pT_[:, h, :], in_=zb__[:, h, :], identity=ident[:]),
                                 reads=[bzb, b_ident], writes=[bpT])
                        S.op("act", lambda e: e.copy(out=kT_[:], in_=pT_[:]), reads=[bpT], writes=[bkT])
                        S.dma("pool", dstd[:, :, rows], kT_[:], reads=[bkT], writes=[bdst])
                    qi3 = po__[:, 2048:3072].rearrange("p (h d) -> p h d", h=16)
                    yield
                    qv, bkr = yield from headnorm_rope(i, qi3, 16, 64, None, csi_, 8, bpo, bcsi, do_norm=False)
                    qflat = kr[i][0][:]
                    for h in range(8):
                        S.op("pe", lambda e: e.transpose(out=pT_[:, h, :], in_=qflat[:, h, :], identity=ident[:]),
                             reads=[bkr, b_ident], writes=[bpT])
                    S.op("act", lambda e: e.copy(out=kT_[:], in_=pT_[:]), reads=[bpT], writes=[bkT])
                    S.dma("pool", qiT_d[:, :, rows], kT_[:], reads=[bkT], writes=[B_qiT])
                    (wv_, bwv) = wv[i]
                    S.op("dve", lambda e: e.tensor_scalar(out=wv_[:], in0=po__[:, 3072:3088], scalar1=1.0 / 32.0,
                                                          scalar2=None, op0=ALU.mult), reads=[bpo], writes=[bwv])
                    S.dma("pool", w_d[rows, :], wv_[:], reads=[bwv], writes=[B_w])
                    yield

                run_interleaved([own_tile(u) for u in range(9)], 3)
                S.barrier()

        if "hgrn" in stages:
            with contextlib.ExitStack() as es:
                Uf, b_U = C.sb(es, "Uf", [128, 128], F32)
                Wf, b_W = C.sb(es, "Wf", [128, 128], F32)
                chi, b_chi = C.sb(es, "chi", [128, 2], F32)
                lb, b_lb = C.sb(es, "lb_sb", [128, 1024], F32)
                oml, b_oml = C.sb(es, "oml", [128, 1024], F32)
                l1, b_l1 = C.sb(es, "l1", [128, 1024], F32)
                go_bc, b_go = C.sb(es, "go_sb", [128, 128], F32)
                S.dma("sp", Uf[:], U_in[:, :], writes=[b_U])
                S.dma("sp", Wf[:], W_in[:, :], writes=[b_W])
                S.dma("sp", chi[:], chi_in[:, :], writes=[b_chi])
                S.dma("sp", go_bc[:], go_in[:, :], writes=[b_go])
                S.dma("sp", lb[:], lb0_in[:, :], writes=[b_lb])
                S.dma("sp", l1[:], lb1_in[:, :], writes=[b_l1])
                S.op("dve", lambda e: e.tensor_tensor(out=lb[:], in0=lb[:], in1=l1[:], op=ALU.subtract),
                     reads=[b_lb, b_l1], writes=[b_lb])
                S.op("act", lambda e: e.activation(out=lb[:], in_=lb[:], func=AF.Sigmoid), reads=[b_lb], writes=[b_lb])
                S.op("dve", lambda e: e.tensor_scalar(out=oml[:], in0=lb[:], scalar1=-1.0, scalar2=1.0, op0=ALU.mult,
                                                      op1=ALU.add), reads=[b_lb], writes=[b_oml])
                Sf, b_Sf = C.sb(es, "Sf", [128, 8, 128], F32)
                S0b, b_S0b = C.sb(es, "S0b", [128, 8, 128], BF16)
                ybT, b_ybT = C.sb(es, "ybT", [128, 8, NOWN], BF16)
                S.op("pool", lambda e: e.memset(Sf[:], 0.0), writes=[b_Sf])
                S.op("pool", lambda e: e.memset(S0b[:], 0.0), writes=[b_S0b])
                S.op("pool", lambda e: e.memset(ybT[:], 0.0), writes=[b_ybT])
                NB = 2
                qb = [C.sb(es, f"qb{i}", [128, 1024], F32) for i in range(NB)]
                fb = [C.sb(es, f"fb{i}", [128, 1024], F32) for i in range(NB)]
                ib = [C.sb(es, f"ib{i}", [128, 1024], F32) for i in range(NB)]
                gg = [C.sb(es, f"gg{i}", [128, 1024], F32) for i in range(NB)]
                kk = [C.sb(es, f"kk{i}", [128, 1024], F32) for i in range(NB)]
                e1 = [C.sb(es, f"e1{i}", [128, 512], F32) for i in range(NB)]
                e2 = [C.sb(es, f"e2{i}", [128, 512], F32) for i in range(NB)]
                e3 = [C.sb(es, f"e3{i}", [128, 512], F32) for i in range(NB)]
                qt = [C.sb(es, f"qt{i}", [128, 1024], BF16) for i in range(NB)]
                kt = [C.sb(es, f"kt{i}", [128, 1024], BF16) for i in range(NB)]
                kd = [C.sb(es, f"kd{i}", [128, 1024], BF16) for i in range(NB)]
                vv = [C.sb(es, f"vv{i}", [128, 1024], BF16) for i in range(NB)]
                ebl = [C.sb(es, f"ebl{i}", [128, 16], F32) for i in range(NB)]
                qkT8 = [C.sb(es, f"qkT8{i}", [128, 8, 256], BF16) for i in range(2)]
                AT8, b_AT8 = C.sb(es, "AT8", [128, 8, 128], BF16)
                S1f8, b_S1f8 = C.sb(es, "S1f8", [128, 8, 128], F32)
                S1b8, b_S1b8 = C.sb(es, "S1b8", [128, 8, 128], BF16)
                S0b2 = [(S0b, b_S0b), C.sb(es, "S0b_1", [128, 8, 128], BF16)]
                on8, b_on8 = C.sb(es, "on8", [128, 8, 128], BF16)
                sq8, b_sq8 = C.sb(es, "sq8", [128, 8, 128], F32)
                ss8, b_ss8 = C.sb(es, "ss8", [128, 8], F32)
                U8, b_U8 = C.sb(es, "U8", [128, 8, 128], F32)
                for h in range(8):
                    S.op("dve", lambda e: e.tensor_copy(out=U8[:, h, :], in_=Uf[:]), reads=[b_U], writes=[b_U8])
                X1, b_X1 = C.ps(es, "X1", [128, 1024], F32)
                X2, b_X2 = C.ps(es, "X2", [128, 8, 128], F32)
                X3, b_X3 = C.ps(es, "X3", [128, 8, 128], F32)
                pCS, b_pCS = C.ps(es, "pCS", [128, 144], F32)
                pQ4, b_pQ4 = C.ps(es, "pQ4", [128, 4, 256], BF16)
                pB, pBD = X1[:, 0:512], X1[:, 512:1024]
                b_pB = b_pBD = b_X1
                X1v = X1.rearrange("p (a b) -> p a b", a=8)
                pC = pCS[:, 0:16]
                b_pC = b_pCS
                pS3 = pCS[:, 16:144].rearrange("p (a b) -> p a b", a=8)
                def hg_prep(t):
                        i = t % NB
                        rows = slice(t * 128, (t + 1) * 128)
                        (qb_, bqb), (fb_, bfb), (ib_, bib), (gg_, bgg), (kk_, bkk) = qb[i], fb[i], ib[i], gg[i], kk[i]
                        (qt_, bqt), (kt_, bkt), (kd_, bkd), (vv_, bvv), (ebl_, bebl) = qt[i], kt[i], kd[i], vv[i], ebl[i]
                        S.dma("sp", qb_[:], P_all[rows, 1088:2112], reads=[B_P_all], writes=[bqb])
                        S.dma("sp", fb_[:], P_all[rows, 2112:3136], reads=[B_P_all], writes=[bfb])
                        S.dma("sp", ib_[:], P_all[rows, 3136:4160], reads=[B_P_all], writes=[bib])
                        S.op("act", lambda e: e.copy(out=vv_[:], in_=ib_[:]), reads=[bib], writes=[bvv])
                        S.op("act", lambda e: e.activation(out=fb_[:], in_=fb_[:], func=AF.Sigmoid), reads=[bfb], writes=[bfb])
                        yield
                        S.op("act", lambda e: e.activation(out=ib_[:], in_=qb_[:], func=AF.Sigmoid), reads=[bqb, bvv],
                             writes=[bib])
                        S.op("dve", lambda e: e.tensor_tensor(out=fb_[:], in0=fb_[:], in1=oml[:], op=ALU.mult),
                             reads=[bfb, b_oml], writes=[bfb])
                        S.op("pool", lambda e: e.tensor_tensor(out=fb_[:], in0=fb_[:], in1=lb[:], op=ALU.add),
                             reads=[bfb, b_lb], writes=[bfb])
                        S.op("act", lambda e: e.activation(out=gg_[:], in_=fb_[:], func=AF.Ln), reads=[bfb], writes=[bgg])
                        S.op("pool", lambda e: e.tensor_scalar(out=kk_[:], in0=fb_[:], scalar1=-1.0, scalar2=1.0,
                                                               op0=ALU.mult, op1=ALU.add), reads=[bfb], writes=[bkk])

                        S.op("dve", lambda e: e.tensor_tensor(out=qb_[:], in0=qb_[:], in1=ib_[:], op=ALU.mult),
                             reads=[bqb, bib], writes=[bqb])
                        for hf in range(2):
                            yield
                            cs = slice(hf * 512, (hf + 1) * 512)
                            (e1_, be1), (e2_, be2), (e3_, be3) = e1[hf], e2[hf], e3[hf]
                            S.op("pe", lambda e: e.matmul(pB[:], lhsT=Uf[:], rhs=gg_[:, cs], start=True, stop=True),
                                 reads=[b_U, bgg], writes=[b_pB])
                            S.op("pe", lambda e: e.matmul(pBD[:], lhsT=Wf[:], rhs=gg_[:, cs], start=True, stop=True),
                                 reads=[b_W, bgg], writes=[b_pBD])
                            S.op("act", lambda e: e.activation(out=e1_[:], in_=pB[:], func=AF.Exp), reads=[b_pB], writes=[be1])
                            S.op("act", lambda e: e.activation(out=e2_[:], in_=pB[:], func=AF.Exp, scale=-1.0),
                                 reads=[b_pB], writes=[be2])
                            S.op("act", lambda e: e.activation(out=e3_[:], in_=pBD[:], func=AF.Exp), reads=[b_pBD],
                                 writes=[be3])
                            S.op("dve", lambda e: e.tensor_tensor(out=qt_[:, cs], in0=qb_[:, cs], in1=e1_[:], op=ALU.mult),
                                 reads=[bqb, be1], writes=[bqt])
                            S.op("pool", lambda e: e.tensor_tensor(out=kt_[:, cs], in0=kk_[:, cs], in1=e2_[:], op=ALU.mult),
                                 reads=[bkk, be2], writes=[bkt])
                            S.op("dve", lambda e: e.tensor_tensor(out=kd_[:, cs], in0=kk_[:, cs], in1=e3_[:], op=ALU.mult),
                                 reads=[bkk, be3], writes=[bkd])
                        for h in range(8):
                            S.op("pe", lambda e: e.matmul(pC[:, 2 * h:2 * h + 2], lhsT=gg_[:, h * 128:(h + 1) * 128],
                                                          rhs=chi[:], start=True, stop=True),
                                 reads=[bgg, b_chi], writes=[b_pC])
                        S.op("act", lambda e: e.activation(out=ebl_[:], in_=pC[:], func=AF.Exp), reads=[b_pC], writes=[bebl])
                        yield

                def hg_tail(t):
                        i = t % NB
                        (qt_, bqt), (kt_, bkt), (kd_, bkd), (vv_, bvv), (ebl_, bebl) = qt[i], kt[i], kd[i], vv[i], ebl[i]
                        (qk_, bqk) = qkT8[t % 2]
                        (S0c, bS0c), (S0n, bS0n) = S0b2[t % 2], S0b2[(t + 1) % 2]
                        hcs = [slice(h * 128, (h + 1) * 128) for h in range(8)]
                        yield
                        for half in range(2):
                            for hh in range(4):
                                h = 4 * half + hh
                                S.op("pe", lambda e: e.transpose(out=pQ4[:, hh, 0:128], in_=qt_[:, hcs[h]], identity=ident[:]),
                                     reads=[bqt, b_ident], writes=[b_pQ4])
                                S.op("pe", lambda e: e.transpose(out=pQ4[:, hh, 128:256], in_=kt_[:, hcs[h]],
                                                                 identity=ident[:]), reads=[bkt, b_ident], writes=[b_pQ4])
                            if half == 0:
                                S.op("act", lambda e: e.copy(out=qk_[:, 0:4, :], in_=pQ4[:]), reads=[b_pQ4], writes=[bqk])
                                for h in range(8):
                                    S.op("pe", lambda e: e.matmul(X2[:, h, :], lhsT=kd_[0:64, hcs[h]], rhs=vv_[0:64, hcs[h]],
                                                                  start=True, stop=True), reads=[bkd, bvv], writes=[b_X2])
                            else:
                                S.op("dve", lambda e: e.tensor_copy(out=qk_[:, 4:8, :], in_=pQ4[:]), reads=[b_pQ4],
                                     writes=[bqk])
                        for h in range(8):
                            S.op("pe", lambda e: e.matmul(X1v[:, h, :], lhsT=qk_[:, h, 128:256], rhs=qk_[:, h, 0:128],
                                                          start=True, stop=True), reads=[bqk], writes=[b_X1])
                        S.op("dve", lambda e: e.tensor_tensor(out=AT8[:], in0=X1v, in1=U8[:], op=ALU.mult),
                             reads=[b_X1, b_U8], writes=[b_AT8])
                        yield
                        for h in range(8):
                            S.op("dve", lambda e: e.scalar_tensor_tensor(out=S1f8[:, h, :], in0=Sf[:, h, :],
                                                                         scalar=ebl_[:, 2 * h:2 * h + 1],
                                                                         in1=X2[:, h, :], op0=ALU.mult, op1=ALU.add),
                                 reads=[b_Sf, bebl, b_X2], writes=[b_S1f8])
                        S.op("act", lambda e: e.copy(out=S1b8[:], in_=S1f8[:]), reads=[b_S1f8], writes=[b_S1b8])
                        yield
                        yield
                        for h in range(8):
                            S.op("pe", lambda e: e.matmul(X3[:, h, :], lhsT=AT8[:, h, :], rhs=vv_[:, hcs[h]],
                                                          start=(h % 4 == 0), stop=False, skip_group_check=True),
                                 reads=[b_AT8, bvv], writes=[b_X3])
                        for h in range(8):
                            S.op("pe", lambda e: e.matmul(X3[0:64, h, :], lhsT=qk_[:, h, 0:64], rhs=S0c[:, h, :], start=False,
                                                          stop=True, skip_group_check=True),
                                 reads=[bqk, bS0c], writes=[b_X3])
                        for h in range(8):
                            S.op("pe", lambda e: e.matmul(X3[64:128, h, :], lhsT=qk_[:, h, 64:128], rhs=S1b8[:, h, :],
                                                          start=False, stop=True, skip_group_check=True),
                                 reads=[bqk, b_S1b8], writes=[b_X3])
                        yield
                        for h in range(8):
                            S.op("pe", lambda e: e.matmul(X2[:, h, :], lhsT=kd_[64:128, hcs[h]], rhs=vv_[64:128, hcs[h]],
                                                          start=True, stop=True), reads=[bkd, bvv], writes=[b_X2])
                        yield
                        S.op("act", lambda e: e.activation(out=sq8[:], in_=X3[:], func=AF.Square), reads=[b_X3],
                             writes=[b_sq8])
                        S.op("dve", lambda e: e.tensor_reduce(out=ss8[:], in_=sq8[:], axis=AX.X, op=ALU.add),
                             reads=[b_sq8], writes=[b_ss8])
                        S.op("act", lambda e: e.activation(out=ss8[:], in_=ss8[:], func=AF.Sqrt, scale=1.0 / 128, bias=EPS),
                             reads=[b_ss8], writes=[b_ss8])
                        S.op("dve", lambda e: e.reciprocal(out=ss8[:], in_=ss8[:]), reads=[b_ss8], writes=[b_ss8])
                        for h in range(8):
                            S.op("act", lambda e: e.activation(out=on8[:, h, :], in_=X3[:, h, :], func=AF.Copy,
                                                               scale=ss8[:, h:h + 1]),
                                 reads=[b_X3, b_ss8], writes=[b_on8])
                        for h in range(8):
                            S.op("pe", lambda e: e.matmul(pS3[:, h, :], lhsT=on8[:, h, :], rhs=selb[:], start=True, stop=True),
                                 reads=[b_on8, b_selb], writes=[b_pCS])
                        S.op("act", lambda e: e.copy(out=ybT[:, :, 16 * t:16 * t + 16], in_=pS3), reads=[b_pCS],
                             writes=[b_ybT])
                        for h in range(8):
                            S.op("dve", lambda e: e.scalar_tensor_tensor(out=Sf[:, h, :], in0=S1f8[:, h, :],
                                                                         scalar=ebl_[:, 2 * h + 1:2 * h + 2], in1=X2[:, h, :],
                                                                         op0=ALU.mult, op1=ALU.add),
                                 reads=[b_S1f8, bebl, b_X2], writes=[b_Sf])
                        S.op("act", lambda e: e.copy(out=S0n[:], in_=Sf[:]), reads=[b_Sf], writes=[bS0n])
                        yield


                for _ in hg_prep(0):
                    pass
                for t in range(NT):
                    ga = hg_tail(t)
                    gb = hg_prep(t + 1) if t + 1 < NT else iter(())
                    da = db = False
                    while not (da and db):
                        if not da:
                            try:
                                next(ga)
                            except StopIteration:
                                da = True
                        if not db:
                            try:
                                next(gb)
                            except StopIteration:
                                db = True
                S.dma("pool", ybT_d[:, :, :], ybT[:], reads=[b_ybT], writes=[B_ybT])
                S.barrier()

        if "attn" in stages:
            with contextlib.ExitStack() as es:
                ki2, b_ki2 = C.sb(es, "ki2", [128, TP], BF16)
                for q4 in range(5):
                    S.dma("sp", ki2[:, q4 * 1664:(q4 + 1) * 1664], kiT_d[:, q4 * 1664:(q4 + 1) * 1664],
                          reads=[B_kiT], writes=[b_ki2])
                AM, b_AM = C.sb(es, "AM_sb", [128, 1024], F32)
                S.dma("sp", AM[:], AM_in[:, :], writes=[b_AM])
                I4, b_I4 = C.sb(es, "I4", [128, 512], BF16)
                for r in range(4):
                    S.op("dve", lambda e: e.tensor_copy(out=I4[:, r * 128:(r + 1) * 128], in_=identf[:]),
                         reads=[b_identf], writes=[b_I4])
                ones_b, b_ones = C.sb(es, "ones_b", [128, 128], BF16)
                S.op("pool", lambda e: e.memset(ones_b[:], 1.0), writes=[b_ones])
                gq_bc, b_gq = C.sb(es, "gq_bc2", [128, 128], F32)
                gk_bc, b_gk = C.sb(es, "gk_bc2", [128, 128], F32)
                mq, b_mq = C.sb(es, "mq", [128, 1], F32)
                mk, b_mk = C.sb(es, "mk", [128, 1], F32)
                S.dma("sp", gq_bc[:], gq_in[:, :], writes=[b_gq])
                S.dma("sp", gk_bc[:], gk_in[:, :], writes=[b_gk])
                S.op("dve", lambda e: e.tensor_reduce(out=mq[:], in_=gq_bc[:], axis=AX.X, op=ALU.max,
                                                      apply_absolute_value=True), reads=[b_gq], writes=[b_mq])
                S.op("dve", lambda e: e.tensor_reduce(out=mk[:], in_=gk_bc[:], axis=AX.X, op=ALU.max,
                                                      apply_absolute_value=True), reads=[b_gk], writes=[b_mk])
                S.op("dve", lambda e: e.tensor_tensor(out=mq[:], in0=mq[:], in1=mk[:], op=ALU.mult),
                     reads=[b_mq, b_mk], writes=[b_mq])
                S.op("dve", lambda e: e.tensor_scalar(out=mq[:], in0=mq[:], scalar1=-(128.0 ** 0.5), scalar2=None,
                                                      op0=ALU.mult), reads=[b_mq], writes=[b_mq])
                score, b_score = C.sb(es, "score", [128, 8208], F32)
                cjunk, b_cjunk = C.sb(es, "cjunk", [128, 8208], BF16)
                MB, b_MB = C.sb(es, "MB", [128, 8208], BF16)
                QTj, b_QTj = C.sb(es, "QTj", [128, 8, 128], BF16)
                qiTj, b_qiTj = C.sb(es, "qiTj", [128, 8, 128], BF16)
                zaTj, b_zaTj = C.sb(es, "zaTj", [128, 8, 128], BF16)
                wj, b_wj = C.sb(es, "wj", [128, 16], F32)
                Dg, b_Dg = C.sb(es, "Dg", [128, 16, 128], BF16)
                NR = 4
                Rl = [C.sb(es, f"Rl{i}", [128, 512], BF16) for i in range(8)]
                PT = [C.sb(es, f"PT{i}", [128, 512], BF16) for i in range(NR)]
                KTc = [C.sb(es, f"KTc{i}", [128, 4, 512], BF16) for i in range(2)]
                Vc = [C.sb(es, f"Vc{i}", [128, 4, 512], BF16) for i in range(2)]
                sm = {n: C.sb(es, "bs_" + n, [128, 1], F32) for n in ("lo", "hi", "mid", "cnt", "ge", "d1", "d2", "B", "nmid", "sga")}
                ajunk, b_ajunk = C.sb(es, "ajunk", [128, 4608], BF16)
                rden, b_rden = C.sb(es, "rden", [128, 512], F32)
                yaT, b_yaT = C.sb(es, "yaT", [128, 8, 128], BF16)
                oT, b_oT = C.sb(es, "oT", [128, 512], F32)
                pL = [C.ps(es, f"pL{i}", [128, 512], F32) for i in range(3)]
                pSc, b_pSc = C.ps(es, "pSc", [128, 512], F32)
                pOA = [C.ps(es, f"pOA{i}", [128, 512], F32) for i in range(2)]
                pDn = [C.ps(es, f"pDn{i}", [128, 512], F32) for i in range(2)]
                pLx = pL + pOA + pDn
                nrl = 0
                npl = 0
                nplx = 0
                npt = 0
                nkc = 0
                score2 = [(score, b_score), C.sb(es, 'score_1', [128, 8208], F32)]
                Dg2 = [(Dg, b_Dg), C.sb(es, 'Dg_1', [128, 16, 128], BF16)]
                qiT2 = [(qiTj, b_qiTj), C.sb(es, 'qiTj_1', [128, 8, 128], BF16)]
                wj2 = [(wj, b_wj), C.sb(es, 'wj_1', [128, 16], F32)]

                def gen_indexer(j):
                    nonlocal nrl, nplx
                    s0 = 2 + 128 * j
                    NJ = 16 + 1024 * (j + 1)
                    (score, b_score), (Dg, b_Dg), (qiTj, b_qiTj), (wj, b_wj) = score2[j % 2], Dg2[j % 2], qiT2[j % 2], wj2[j % 2]
                    S.dma("sp", qiTj[:], qiT_d[:, :, s0:s0 + 128], reads=[B_qiT], writes=[b_qiTj])
                    S.dma("sp", wj[:], w_d[s0:s0 + 128, :], reads=[B_w], writes=[b_wj])
                    for h in range(16):
                        S.op("dve", lambda e: e.tensor_scalar(out=Dg[:, h, :], in0=identf[:], scalar1=wj[:, h:h + 1],
                                                              scalar2=None, op0=ALU.mult),
                             reads=[b_identf, b_wj], writes=[b_Dg])
                    yield
                    items = []
                    c0 = 0
                    while c0 < NJ:
                        cw = min(512, NJ - c0)
                        for h in range(16):
                            items.append((c0, cw, h))
                        c0 += cw
                    LAG = 4
                    slots = {}
                    order = []
                    for base in range(0, len(items) + LAG, 2):
                        order += [("L", base), ("L", base + 1), ("A", base - LAG), ("A", base + 1 - LAG)]
                    for kind, idx in order:
                        if kind == "L" and idx < len(items):
                            c0, cw, h = items[idx]
                            (pl_, bpl) = pLx[nplx % 7]
                            nplx += 1
                            (rl_, brl) = Rl[nrl % 8]
                            nrl += 1
                            slots[idx] = (rl_, brl)
                            pr = slice((h % 2) * 64, (h % 2) * 64 + 64)
                            S.op("pe", lambda e: e.matmul(pl_[:, 0:cw], lhsT=qiTj[pr, h // 2, :], rhs=ki2[pr, c0:c0 + cw],
                                                          start=True, stop=True),
                                 reads=[b_qiTj, b_ki2], writes=[bpl])
                            if h % 2 == 0:
                                S.op("act", lambda e: e.activation(out=rl_[:, 0:cw], in_=pl_[:, 0:cw], func=AF.Relu),
                                     reads=[bpl], writes=[brl])
                            else:
                                S.op("dve", lambda e: e.tensor_scalar(out=rl_[:, 0:cw], in0=pl_[:, 0:cw], scalar1=0.0,
                                                                      scalar2=None, op0=ALU.max),
                                     reads=[bpl], writes=[brl])
                        if kind == "A" and 0 <= idx < len(items):
                            c0, cw, h = items[idx]
                            (rl_, brl) = slots.pop(idx)
                            S.op("pe", lambda e: e.matmul(pSc[:, 0:cw], lhsT=Dg[:, h, :], rhs=rl_[:, 0:cw],
                                                          start=(h == 0), stop=(h == 15)),
                                 reads=[b_Dg, brl], writes=[b_pSc])
                            if h == 15:
                                S.op("act", lambda e: e.copy(out=score[:, c0:c0 + cw], in_=pSc[:, 0:cw]),
                                     reads=[b_pSc], writes=[b_score])
                        if kind == 'A' and idx % 2 == 1:
                            yield

                def gen_bisect(j):
                    NJ = 16 + 1024 * (j + 1)
                    (score, b_score) = score2[j % 2]
                    g_ = lambda n: sm[n][0]
                    bb = lambda n: sm[n][1]
                    S.op("dve", lambda e: e.tensor_reduce(out=g_("B")[:], in_=score[:, 0:NJ], axis=AX.X, op=ALU.max,
                                                          apply_absolute_value=True), reads=[b_score], writes=[bb("B")])
                    S.op("dve", lambda e: e.tensor_scalar(out=g_("hi")[:], in0=g_("B")[:], scalar1=1.001, scalar2=1e-6,
                                                          op0=ALU.mult, op1=ALU.add), reads=[bb("B")], writes=[bb("hi")])
                    S.op("dve", lambda e: e.tensor_scalar(out=g_("lo")[:], in0=g_("hi")[:], scalar1=-1.0, scalar2=None,
                                                          op0=ALU.mult), reads=[bb("hi")], writes=[bb("lo")])
                    S.op("dve", lambda e: e.tensor_tensor(out=score[:, NJ - 1024:NJ], in0=score[:, NJ - 1024:NJ],
                                                          in1=AM[:], op=ALU.add), reads=[b_score, b_AM], writes=[b_score])
                    S.op("dve", lambda e: e.tensor_tensor(out=g_("d2")[:], in0=g_("hi")[:], in1=g_("lo")[:],
                                                          op=ALU.subtract), reads=[bb("hi"), bb("lo")], writes=[bb("d2")])
                    ND = (NJ * 9 // 20) // 16 * 16
                    NA = NJ - ND
                    for it in range(24):
                        cit = 0.5 ** (it + 1)
                        S.op("dve", lambda e: e.tensor_scalar(out=g_("mid")[:], in0=g_("d2")[:], scalar1=cit,
                                                              scalar2=g_("lo")[:, 0:1], op0=ALU.mult, op1=ALU.add),
                             reads=[bb("d2"), bb("lo")], writes=[bb("mid")])
                        S.op("act", lambda e: e.activation(out=ajunk[:, 0:NA], in_=score[:, ND:NJ], func=AF.Sign,
                                                           scale=-1.0, bias=g_("mid")[:, 0:1], accum_out=g_("sga")[:]),
                             reads=[b_score, bb("mid")], writes=[b_ajunk, bb("sga")])
                        S.op("dve", lambda e: e.tensor_scalar(out=cjunk[:, 0:ND], in0=score[:, 0:ND],
                                                              scalar1=g_("mid")[:, 0:1], scalar2=None, op0=ALU.is_ge,
                                                              op1=ALU.add, accum_out=g_("cnt")[:]),
                             reads=[b_score, bb("mid")], writes=[b_cjunk, bb("cnt")])
                        S.op("dve", lambda e: e.scalar_tensor_tensor(out=g_("cnt")[:], in0=g_("cnt")[:], scalar=2.0,
                                                                     in1=g_("sga")[:], op0=ALU.mult, op1=ALU.subtract),
                             reads=[bb("cnt"), bb("sga")], writes=[bb("cnt")])
                        S.op("dve", lambda e: e.tensor_scalar(out=g_("ge")[:], in0=g_("cnt")[:], scalar1=float(511 - NA),
                                                              scalar2=None, op0=ALU.is_ge), reads=[bb("cnt")],
                             writes=[bb("ge")])
                        S.op("dve", lambda e: e.tensor_scalar(out=g_("d1")[:], in0=g_("mid")[:], scalar1=g_("lo")[:, 0:1],
                                                              scalar2=g_("ge")[:, 0:1], op0=ALU.subtract, op1=ALU.mult),
                             reads=[bb("mid"), bb("lo"), bb("ge")], writes=[bb("d1")])
                        S.op("dve", lambda e: e.tensor_tensor(out=g_("lo")[:], in0=g_("lo")[:], in1=g_("d1")[:],
                                                              op=ALU.add), reads=[bb("lo"), bb("d1")], writes=[bb("lo")])
                        yield
                    yield

                def emit_MB(j):
                    NJ = 16 + 1024 * (j + 1)
                    (score, b_score) = score2[j % 2]
                    g_ = lambda n: sm[n][0]
                    bb = lambda n: sm[n][1]
                    S.op("dve", lambda e: e.tensor_scalar(out=MB[:, 0:NJ], in0=score[:, 0:NJ], scalar1=g_("lo")[:, 0:1],
                                                          scalar2=NEG, op0=ALU.is_lt, op1=ALU.mult),
                         reads=[b_score, bb("lo")], writes=[b_MB])

                def run_pair(ga, gb):
                    la = list_steps = None
                    done_a = done_b = False
                    if gb is None:
                        for _ in ga:
                            pass
                        return
                    while not (done_a and done_b):
                        if not done_a:
                            try:
                                next(ga)
                            except StopIteration:
                                done_a = True
                        for _ in range(RATIO[0]):
                            if not done_b:
                                try:
                                    next(gb)
                                except StopIteration:
                                    done_b = True

                def gen_loop(j):
                    nonlocal npl, npt, nkc
                    NJ = 16 + 1024 * (j + 1)
                    nkt = (NJ + 127) // 128
                    aitems = [(kt_, G) for kt_ in range(nkt) for G in range(2)]
                    chunkbuf = {}
                    pend = {}

                    def emit_pv(ii):
                        kt_, G = aitems[ii]
                        (pt_, bpt) = pend.pop(ii)
                        (Vc_, bVc) = chunkbuf[kt_ // 4][1]
                        q = kt_ % 4
                        kw = min(128, NJ - kt_ * 128)
                        first, last = (kt_ == 0), (kt_ == nkt - 1)
                        for g2 in range(2):
                            g = 2 * G + g2
                            S.op("pe", lambda e: e.matmul(pOA[G][0][:, g2 * 256:(g2 + 1) * 256],
                                                          lhsT=Vc_[0:kw, q, g * 128:(g + 1) * 128],
                                                          rhs=pt_[0:kw, g2 * 256:(g2 + 1) * 256],
                                                          start=(first and g2 == 0), stop=last, skip_group_check=True),
                                 reads=[bVc, bpt], writes=[pOA[G][1]])
                        S.op("pe", lambda e: e.matmul(pDn[G][0][:], lhsT=ones_b[0:kw, :], rhs=pt_[0:kw, :],
                                                      start=first, stop=last, skip_group_check=True),
                             reads=[b_ones, bpt], writes=[pDn[G][1]])

                    for ii, (kt_, G) in enumerate(aitems):
                        if kt_ % 4 == 0 and G == 0:
                            (KTc_, bKTc), (Vc_, bVc) = KTc[nkc % 2], Vc[nkc % 2]
                            nkc += 1
                            chunkbuf[kt_ // 4] = ((KTc_, bKTc), (Vc_, bVc))
                            k0 = kt_ * 128
                            kwid = min(512, NJ - k0)
                            S.dma("sp", KTc_[:, :, 0:kwid], KT_d[:, :, k0:k0 + kwid], reads=[B_KT], writes=[bKTc])
                            ntl = (kwid + 127) // 128
                            for q in range(ntl):
                                kw_ = min(128, kwid - q * 128)
                                S.dma("act", Vc_[0:kw_, q, :], V_d[k0 + q * 128:k0 + q * 128 + kw_, :], reads=[B_V],
                                      writes=[bVc])
                        (KTc_, bKTc) = chunkbuf[kt_ // 4][0]
                        q = kt_ % 4
                        kw = min(128, NJ - kt_ * 128)
                        ks = slice(kt_ * 128, kt_ * 128 + kw)
                        kl = slice(q * 128, q * 128 + kw)
                        (pl_, bpl) = pL[npl % 3]
                        npl += 1
                        (pt_, bpt) = PT[npt % NR]
                        npt += 1
                        S.op("pe", lambda e: e.matmul(pl_[0:kw, :], lhsT=MB[:, ks], rhs=I4[:], start=True, stop=False,
                                                      skip_group_check=True), reads=[b_MB, b_I4], writes=[bpl])
                        for g2 in range(2):
                            g = 2 * G + g2
                            S.op("pe", lambda e: e.matmul(
                                pl_[0:kw, g2 * 256:(g2 + 1) * 256], lhsT=KTc_[:, g, kl],
                                rhs=QTj[:].rearrange("p a b -> p (a b)")[:, 2 * g * 128:(2 * g + 2) * 128],
                                start=False, stop=True, skip_group_check=True),
                                 reads=[bKTc, b_QTj], writes=[bpl])
                        S.op("act", lambda e: e.activation(out=pt_[0:kw, :], in_=pl_[0:kw, :], func=AF.Exp,
                                                           scale=128.0 ** -0.5, bias=mq[0:kw, 0:1]),
                             reads=[bpl, b_mq], writes=[bpt])
                        pend[ii] = (pt_, bpt)
                        if ii >= 2:
                            emit_pv(ii - 2)
                            yield
                    for ii in range(max(0, len(aitems) - 2), len(aitems)):
                        emit_pv(ii)
                    yield

                RATIO = [1]
                for _ in gen_indexer(0):
                    pass
                for _ in gen_bisect(0):
                    pass
                emit_MB(0)
                for _ in gen_indexer(1):
                    pass
                for j in range(8):
                    s0 = 2 + 128 * j
                    NJ = 16 + 1024 * (j + 1)
                    S.dma("sp", QTj[:], QT_d[:, :, s0:s0 + 128], reads=[B_QT], writes=[b_QTj])
                    S.dma("sp", zaTj[:], zaT_d[:, :, s0:s0 + 128], reads=[B_zaT], writes=[b_zaTj])
                    n_loop_steps = 2 * ((NJ + 127) // 128)
                    RATIO[0] = max(1, -(-26 // n_loop_steps))
                    run_pair(gen_loop(j), gen_bisect(j + 1) if j < 7 else None)
                    for G in range(2):
                        S.op("dve", lambda e: e.reciprocal(out=rden[:], in_=pDn[G][0][:]), reads=[pDn[G][1]],
                             writes=[b_rden])
                        S.op("dve", lambda e: e.tensor_tensor(out=oT[:], in0=pOA[G][0][:], in1=rden[:], op=ALU.mult),
                             reads=[pOA[G][1], b_rden], writes=[b_oT])
                        S.op("dve", lambda e: e.tensor_tensor(
                            out=yaT[:, 4 * G:4 * G + 4, :].rearrange("p a b -> p (a b)"), in0=oT[:],
                            in1=zaTj[:, 4 * G:4 * G + 4, :].rearrange("p a b -> p (a b)"), op=ALU.mult),
                             reads=[b_oT, b_zaTj], writes=[b_yaT])
                    S.dma("pool", yaT_d[:, :, j * 128:(j + 1) * 128], yaT[:], reads=[b_yaT], writes=[B_yaT])
                    if j < 7:
                        emit_MB(j + 1)
                    if j + 2 < 8:
                        for _ in gen_indexer(j + 2):
                            pass
                S.barrier()

        if "merge" in stages:
            with contextlib.ExitStack() as es0:
                mT_all, b_mT = C.sb(es0, "mT_all", [128, 8, 2048], BF16)
                wst = [C.sb(es0, f"wst{i}", [128, 2048], F32) for i in range(2)]
                nw = 0
                with contextlib.ExitStack() as es:
                    Wb = [C.sb(es, f"Wbr{i}", [128, 8, 2048], BF16) for i in range(2)]
                    gob, b_gob = C.sb(es, "gob", [128, 128], F32)
                    gocol, b_gocol = C.sb(es, "gocol", [128, 1], F32)
                    S.dma("sp", gob[:], go_in[:, :], writes=[b_gob])
                    S.op("dve", lambda e: e.tensor_tensor(out=gob[:], in0=gob[:], in1=identf[:], op=ALU.mult),
                         reads=[b_gob, b_identf], writes=[b_gob])
                    S.op("dve", lambda e: e.tensor_reduce(out=gocol[:], in_=gob[:], axis=AX.X, op=ALU.add),
                         reads=[b_gob], writes=[b_gocol])
                    for br in range(2):
                        for k in range(8):
                            (ws_, bws) = wst[nw % 2]
                            nw += 1
                            S.dma("sp", ws_[:], wbr_in[br, k * 128:(k + 1) * 128, :], writes=[bws])
                            if br == 1:
                                if k % 2:
                                    S.op("act", lambda e: e.activation(out=Wb[1][0][:, k, :], in_=ws_[:], func=AF.Copy,
                                                                       scale=gocol[:, 0:1]),
                                         reads=[bws, b_gocol], writes=[Wb[1][1]])
                                else:
                                    S.op("dve", lambda e: e.tensor_scalar(out=Wb[1][0][:, k, :], in0=ws_[:],
                                                                          scalar1=gocol[:, 0:1], scalar2=None,
                                                                          op0=ALU.mult),
                                         reads=[bws, b_gocol], writes=[Wb[1][1]])
                                continue
                            S.op("act" if k % 2 else "dve",
                                 (lambda e: e.copy(out=Wb[br][0][:, k, :], in_=ws_[:])) if k % 2 else
                                 (lambda e: e.tensor_copy(out=Wb[br][0][:, k, :], in_=ws_[:])),
                                 reads=[bws], writes=[Wb[br][1]])
                    yaTj, b_yaTj = C.sb(es, "yaTj", [128, 8, 128], BF16)
                    ybTj, b_ybTj = C.sb(es, "ybTj", [128, 8, 128], BF16)
                    zbTj, b_zbTj = C.sb(es, "zbTj", [128, 8, 128], BF16)
                    gts, b_gts = C.sb(es, "gts", [128, 4096], F32)
                    mg, b_mg = C.sb(es, "mg", [128, 2048], F32)
                    t2, b_t2 = C.sb(es, "t2", [128, 512], F32)
                    mgb, b_mgb = C.sb(es, "mgb", [128, 2048], BF16)
                    pP = [C.ps(es, f"pP{i}", [128, 512], F32) for i in range(4)]
                    pTm, b_pTm = C.ps(es, "pTm", [128, 2048], BF16)
                    npp = 0
                    for j in range(8):
                        s0 = 2 + 128 * j
                        S.dma("sp", yaTj[:], yaT_d[:, :, j * 128:(j + 1) * 128], reads=[B_yaT], writes=[b_yaTj])
                        S.dma("sp", ybTj[:], ybT_d[:, :, s0:s0 + 128], reads=[B_ybT], writes=[b_ybTj])
                        S.dma("sp", zbTj[:], zbT_d[:, :, s0:s0 + 128], reads=[B_zbT], writes=[b_zbTj])
                        S.dma("sp", gts[:], P_own[s0:s0 + 128, 4112:8208], reads=[B_P_own], writes=[b_gts])
                        S.op("dve", lambda e: e.tensor_tensor(out=ybTj[:], in0=ybTj[:], in1=zbTj[:], op=ALU.mult),
                             reads=[b_ybTj, b_zbTj], writes=[b_ybTj])
                        S.op("act", lambda e: e.activation(out=gts[:], in_=gts[:], func=AF.Sigmoid), reads=[b_gts],
                             writes=[b_gts])
                        for cb in range(4):
                            cs = slice(cb * 512, (cb + 1) * 512)
                            for br, (yT_, byT) in enumerate(((yaTj, b_yaTj), (ybTj, b_ybTj))):
                                (pp_, bpp) = pP[npp % 4]
                                npp += 1
                                for k in range(8):
                                    S.op("pe", lambda e: e.matmul(pp_[:], lhsT=yT_[:, k, :], rhs=Wb[br][0][:, k, cs],
                                                                  start=(k == 0), stop=(k == 7)),
                                         reads=[byT, Wb[br][1]], writes=[bpp])
                                if br == 0:
                                    S.op("dve", lambda e: e.tensor_tensor(out=mg[:, cs], in0=pp_[:], in1=gts[:, cs],
                                                                          op=ALU.mult), reads=[bpp, b_gts], writes=[b_mg])
                                else:
                                    S.op("dve", lambda e: e.tensor_tensor(
                                        out=t2[:], in0=pp_[:], in1=gts[:, 2048 + cb * 512:2048 + (cb + 1) * 512],
                                        op=ALU.mult), reads=[bpp, b_gts], writes=[b_t2])
                                    S.op("pool", lambda e: e.tensor_tensor(out=mgb[:, cs], in0=mg[:, cs], in1=t2[:],
                                                                           op=ALU.add), reads=[b_mg, b_t2], writes=[b_mgb])
                        for k in range(16):
                            S.op("pe", lambda e: e.transpose(out=pTm[:, k * 128:(k + 1) * 128],
                                                             in_=mgb[:, k * 128:(k + 1) * 128], identity=ident[:]),
                                 reads=[b_mgb, b_ident], writes=[b_pTm])
                        S.op("act", lambda e: e.copy(out=mT_all[:, j, 0:1024], in_=pTm[:, 0:1024]), reads=[b_pTm],
                             writes=[b_mT])
                        S.op("dve", lambda e: e.tensor_copy(out=mT_all[:, j, 1024:2048], in_=pTm[:, 1024:2048]),
                             reads=[b_pTm], writes=[b_mT])
                    S.barrier()
                with contextlib.ExitStack() as es:
                    Wo, b_Wo = C.sb(es, "Wo", [128, 16, 2048], BF16)
                    for k in range(16):
                        (ws_, bws) = wst[nw % 2]
                        nw += 1
                        S.dma("sp", ws_[:], wout_in[k * 128:(k + 1) * 128, :], writes=[bws])
                        S.op("act" if k % 2 else "dve",
                             (lambda e: e.copy(out=Wo[:, k, :], in_=ws_[:])) if k % 2 else
                             (lambda e: e.tensor_copy(out=Wo[:, k, :], in_=ws_[:])),
                             reads=[bws], writes=[b_Wo])
                    xo = [C.sb(es, f"xo{i}", [128, 2048], F32) for i in range(2)]
                    pP = [C.ps(es, f"pP2{i}", [128, 512], F32) for i in range(4)]
                    npp = 0
                    for j in range(8):
                        (xo_, bxo) = xo[j % 2]
                        S.dma("sp", xo_[:], x_own[j * 128:(j + 1) * 128, :], writes=[bxo])
                        for cb in range(4):
                            cs = slice(cb * 512, (cb + 1) * 512)
                            (pp_, bpp) = pP[npp % 4]
                            npp += 1
                            for k in range(16):
                                S.op("pe", lambda e: e.matmul(pp_[:], lhsT=mT_all[:, j, k * 128:(k + 1) * 128],
                                                              rhs=Wo[:, k, cs], start=(k == 0), stop=(k == 15)),
                                     reads=[b_mT, b_Wo], writes=[bpp])
                            S.op("dve", lambda e: e.tensor_tensor(out=xo_[:, cs], in0=xo_[:, cs], in1=pp_[:], op=ALU.add),
                                 reads=[bxo, bpp], writes=[bxo])
                        S.dma("pool", out_d[j * 128:(j + 1) * 128, :], xo_[:], reads=[bxo], writes=[B_out])
                    S.barrier()

        S.barrier(engines=("sp",))
    return nc


def host_inputs(x, meta_tokens, hgrn_lb_logits, norm_g, w_in, q_norm_g, k_norm_g, idx_k_norm_g,
                hgrn_out_norm_g, w_branch, w_out):
    x = np.asarray(x, np.float32)
    h_all = np.zeros((TP, D), np.float32)
    h_all[:NMETA] = np.asarray(meta_tokens, np.float32)
    h_all[NMETA:NMETA + SEQ] = x[0]
    common = {
        "h_all": h_all,
        "w_in": np.ascontiguousarray(np.asarray(w_in, np.float32)[0]),
        "norm_g": np.ascontiguousarray(np.asarray(norm_g, np.float32)[0]),
    }
    f32 = np.float32
    tile = lambda v, n: np.ascontiguousarray(np.tile(np.asarray(v, f32).reshape(1, -1), (n, 1)))
    common["gq_bc"] = tile(q_norm_g[0], 128)
    common["gk_bc"] = tile(k_norm_g[0], 128)
    common["gi_bc"] = tile(idx_k_norm_g[0], 128)
    common["go_bc"] = tile(hgrn_out_norm_g[0], 128)
    common["lb0"] = tile(np.asarray(hgrn_lb_logits)[0], 128)
    common["lb1"] = tile(np.asarray(hgrn_lb_logits)[1], 128)
    common["w_branch"] = np.ascontiguousarray(np.asarray(w_branch, f32)[0])
    common["w_out"] = np.ascontiguousarray(np.asarray(w_out, f32)[0])

    def rope_tab(pos, rot, heads):
        inv = np.power(np.float32(500000.0), -np.arange(0, rot, 2, dtype=f32) / np.float32(rot)).astype(f32)
        ang = pos.astype(f32)[:, None] * inv[None, :]
        cos, sin = np.cos(ang).astype(f32), np.sin(ang).astype(f32)
        cs = np.stack([np.repeat(cos[:, None, :], heads, 1), np.repeat(sin[:, None, :], heads, 1)], 1)
        return np.ascontiguousarray(cs.reshape(len(pos), -1))
    pos_all = np.arange(TP)
    common["ropeK"] = rope_tab(pos_all, 32, 4)
    common["ropeKI"] = rope_tab(pos_all, 16, 1)
    si, ti = np.arange(128)[:, None], np.arange(128)[None, :]
    same = (si // 64) == (ti // 64)
    common["U"] = (same & (si <= ti)).astype(f32)
    common["Wm"] = (same & (si > ti)).astype(f32)
    common["chi"] = (np.arange(128)[:, None] // 64 == np.arange(2)[None, :]).astype(f32)
    m_, p_ = np.arange(128)[:, None], np.arange(1024)[None, :]
    common["AM"] = np.where((p_ // 64) <= (m_ // 8), 0.0, -1e9).astype(f32)
    maps = []
    for c in range(8):
        sel = np.zeros((128, 16), np.float32)
        sel[c + 8 * np.arange(16), np.arange(16)] = 1.0
        m = dict(common)
        m["sel"] = sel
        m["x_own"] = np.ascontiguousarray(x[0, c::8])
        slot = np.arange(NOWN)
        pos_own = np.where(slot < 1040, 8 * slot + c, 0)
        m["ropeQ"] = rope_tab(pos_own, 32, 8)
        m["ropeQI"] = rope_tab(pos_own, 16, 16)
        maps.append(m)
    return maps


def kernel(**inputs):
    maps = host_inputs(**inputs)
    nc = build_nc()
    res = run_bass_kernel_spmd(nc, maps, core_ids=list(range(8)))
    out = np.zeros((1, SEQ, D), np.float32)
    for c in range(8):
        out[0, c::8] = res.results[c]["out"]
    return out
```

```python
import contextlib
import numpy as np
import concourse.bass as bass
import concourse.mybir as mybir
from concourse.bass_utils import run_bass_kernel_spmd

F32 = mybir.dt.float32
BF16 = mybir.dt.bfloat16
AF = mybir.ActivationFunctionType
ALU = mybir.AluOpType
AX = mybir.AxisListType

D = 2048
SEQ = 8192
NMETA = 16
TP = 8320
NT = 65
NOWN = 1152
NIN = 12368
EPS = 1e-6
NEG = -30000.0

C_QA, C_KA, C_VA, C_ZA, C_QI, C_KI, C_WI, C_QB, C_FB, C_IB, C_ZB, C_G = (
    0, 1024, 1536, 2048, 3072, 4096, 4160, 4176, 5200, 6224, 7248, 8272)
PA_COLS = 4160
PA_BLOCKS = [(C_KA, 512, 0), (C_VA, 512, 512), (C_KI, 64, 1024)] + \
            [(C_QB + i * 512, 512, 1088 + i * 512) for i in range(6)]
PO_COLS = 8208
PO_BLOCKS = [(C_QA + i * 512, 512, i * 512) for i in range(2)] + \
            [(C_ZA + i * 512, 512, 1024 + i * 512) for i in range(4)] + \
            [(C_WI, 16, 3072)] + \
            [(C_ZB + i * 512, 512, 3088 + i * 512) for i in range(10)]


class Buf:
    __slots__ = ("name", "w", "r")

    def __init__(self, name):
        self.name = name
        self.w = None
        self.r = []


class Eng:
    def __init__(self, name, h, sem):
        self.name, self.h, self.sem = name, h, sem
        self.count = 0
        self.seen = {}

    def wait(self, ev):
        if ev is None:
            return
        sem, val = ev
        if self.seen.get(id(sem), 0) >= val:
            return
        if self.name == "pe" and sem is self.sem:
            return
        self.h.wait_ge(sem, val)
        self.seen[id(sem)] = val


class Sched:
    NDMA = 8

    def __init__(self, nc, sems):
        self.nc = nc
        self.free_sems = list(sems)
        self.E = {}
        for name, h in (("pe", nc.tensor), ("act", nc.scalar), ("dve", nc.vector),
                        ("pool", nc.gpsimd), ("sp", nc.sync)):
            self.E[name] = Eng(name, h, self.free_sems.pop())
        self.dq = {}
        for q in ("sp", "pool", "act"):
            self.dq[q] = {"sems": [self.free_sems.pop() for _ in range(self.NDMA)],
                          "n": [0] * self.NDMA, "i": 0}

    def _deps(self, eng, reads, writes):
        for b in reads:
            eng.wait(b.w)
        for b in writes:
            eng.wait(b.w)
            for ev in b.r:
                eng.wait(ev)

    def _commit(self, ev, reads, writes):
        for b in reads:
            b.r = [e for e in b.r if e[0] is not ev[0]] + [ev]
        for b in writes:
            b.w = ev
            b.r = []

    def op(self, ename, fn, reads=(), writes=()):
        eng = self.E[ename]
        self._deps(eng, reads, writes)
        ins = fn(eng.h)
        eng.count += 1
        ins.then_inc(eng.sem, 1)
        ev = (eng.sem, eng.count)
        self._commit(ev, reads, writes)
        return ev

    def dma(self, q, out, in_, reads=(), writes=(), **kw):
        eng = self.E[q]
        d = self.dq[q]
        k = d["i"] % self.NDMA
        d["i"] += 1
        sem = d["sems"][k]
        if d["n"][k] > 0:
            eng.wait((sem, 16 * d["n"][k]))
        self._deps(eng, reads, writes)
        ins = eng.h.dma_start(out=out, in_=in_, **kw)
        d["n"][k] += 1
        ins.then_inc(sem, 16)
        ev = (sem, 16 * d["n"][k])
        self._commit(ev, reads, writes)
        return ev

    def all_events(self):
        evs = []
        for e in self.E.values():
            if e.count:
                evs.append((e.sem, e.count))
        for d in self.dq.values():
            for s, n in zip(d["sems"], d["n"]):
                if n:
                    evs.append((s, 16 * n))
        return evs

    def barrier(self, engines=None):
        evs = self.all_events()
        for name, e in self.E.items():
            if engines is not None and name not in engines:
                continue
            for ev in evs:
                e.wait(ev)


class Ctx:
    def __init__(self, nc, S):
        self.nc, self.S = nc, S

    n = 0

    def sb(self, es, name, shape, dt):
        Ctx.n += 1
        name = f"t{Ctx.n}_{name}"
        t = es.enter_context(self.nc.sbuf_tensor(name, list(shape), dt))
        return t, Buf(name)

    def ps(self, es, name, shape, dt):
        Ctx.n += 1
        name = f"t{Ctx.n}_{name}"
        n = int(np.prod(shape[1:]))
        be = 2048 // (4 if dt == F32 else 2)
        nb = -(-n // be)
        flat = es.enter_context(self.nc.psum_tensor(name, [shape[0], nb * be], dt))
        v = flat[:, 0:n]
        if len(shape) == 3:
            v = v.rearrange("p (a b) -> p a b", a=shape[1])
        elif len(shape) == 4:
            v = v.rearrange("p (a b c) -> p a b c", a=shape[1], b=shape[2])
        return v, Buf(name)


def build_nc(stages=("norm", "gemm", "post", "hgrn", "attn", "merge"), debug_out=()):
    nc = bass.Bass("TRN2", target_bir_lowering=False)
    dt_in = lambda n, s, d=F32: nc.dram_tensor(n, list(s), d, kind="ExternalInput")
    h_all = dt_in("h_all", [TP, D])
    x_own = dt_in("x_own", [1024, D])
    sel_in = dt_in("sel", [128, 16])
    w_in = dt_in("w_in", [D, NIN])
    norm_g = dt_in("norm_g", [D])
    out_d = nc.dram_tensor("out", [1024, D], F32, kind="ExternalOutput")
    gq_in = dt_in("gq_bc", [128, 128]); gk_in = dt_in("gk_bc", [128, 128]); gi_in = dt_in("gi_bc", [128, 64])
    go_in = dt_in("go_bc", [128, 128])
    ropeK_in = dt_in("ropeK", [TP, 128]); ropeKI_in = dt_in("ropeKI", [TP, 16])
    ropeQ_in = dt_in("ropeQ", [NOWN, 256]); ropeQI_in = dt_in("ropeQI", [NOWN, 256])
    U_in = dt_in("U", [128, 128]); W_in = dt_in("Wm", [128, 128]); chi_in = dt_in("chi", [128, 2])
    lb0_in = dt_in("lb0", [128, 1024]); lb1_in = dt_in("lb1", [128, 1024])
    AM_in = dt_in("AM", [128, 1024])
    wbr_in = dt_in("w_branch", [2, 1024, D]); wout_in = dt_in("w_out", [D, D])

    hT_all = nc.dram_tensor("hT_all", [NT, 128, 2048], BF16)
    hT_own = nc.dram_tensor("hT_own", [9, 128, 2048], BF16)
    kind_dbg = lambda n: "ExternalOutput" if n in debug_out else "Internal"
    P_all = nc.dram_tensor("P_all", [TP, PA_COLS], F32, kind=kind_dbg("P_all"))
    P_own = nc.dram_tensor("P_own", [NOWN, PO_COLS], F32, kind=kind_dbg("P_own"))

    KT_d = nc.dram_tensor("KT_d", [128, 4, TP], BF16)
    V_d = nc.dram_tensor("V_d", [TP, 512], BF16)
    kiT_d = nc.dram_tensor("kiT_d", [128, TP], BF16)
    QT_d = nc.dram_tensor("QT_d", [128, 8, NOWN], BF16)
    zaT_d = nc.dram_tensor("zaT_d", [128, 8, NOWN], BF16)
    zbT_d = nc.dram_tensor("zbT_d", [128, 8, NOWN], BF16)
    qiT_d = nc.dram_tensor("qiT_d", [128, 8, NOWN], BF16)
    w_d = nc.dram_tensor("w_d", [NOWN, 16], F32)
    ybT_d = nc.dram_tensor("ybT_d", [128, 8, NOWN], BF16, kind=kind_dbg("ybT_d"))
    yaT_d = nc.dram_tensor("yaT_d", [128, 8, 1024], BF16, kind=kind_dbg("yaT_d"))
    B_KT, B_V, B_kiT, B_QT, B_zaT, B_zbT, B_qiT, B_w, B_ybT, B_yaT = [Buf(n) for n in
        "KT V kiT QT zaT zbT qiT w ybT yaT".split()]

    with contextlib.ExitStack() as top:
        sems = [top.enter_context(nc.semaphore(f"s{i}")) for i in range(5 + 3 * Sched.NDMA + 2)]
        S = Sched(nc, sems)
        C = Ctx(nc, S)
        B_hT_all, B_hT_own, B_P_all, B_P_own, B_out = (Buf("hT_all"), Buf("hT_own"), Buf("P_all"),
                                                       Buf("P_own"), Buf("out"))

        identf, b_identf = C.sb(top, "identf", [128, 128], F32)
        ident, b_ident = C.sb(top, "ident", [128, 128], BF16)
        self_f, b_self_f = C.sb(top, "self_f", [128, 16], F32)
        selb, b_selb = C.sb(top, "selb", [128, 16], BF16)
        gcol, b_gcol = C.sb(top, "gcol", [128, 16], F32)
        S.op("pool", lambda e: e.memset(identf[:], 0.0), writes=[b_identf])
        S.op("pool", lambda e: e.affine_select(out=identf[:], in_=identf[:], pattern=[[-1, 128]],
                                                compare_op=ALU.not_equal, fill=1.0, base=0,
                                                channel_multiplier=1), reads=[b_identf], writes=[b_identf])
        S.op("dve", lambda e: e.tensor_copy(out=ident[:], in_=identf[:]), reads=[b_identf], writes=[b_ident])
        S.dma("sp", self_f[:], sel_in[:, :], writes=[b_self_f])
        S.op("dve", lambda e: e.tensor_copy(out=selb[:], in_=self_f[:]), reads=[b_self_f], writes=[b_selb])
        S.dma("sp", gcol[:], norm_g.ap().rearrange("(k p) -> p k", p=128), writes=[b_gcol],
              allow_slow_non_contiguous=True)

        if "norm" in stages:
            with contextlib.ExitStack() as es:
                NB = 2
                xt = [C.sb(es, f"xt{i}", [128, D], F32) for i in range(NB)]
                hn = [C.sb(es, f"hn{i}", [128, D], BF16) for i in range(NB)]
                hT = [C.sb(es, f"hT{i}", [128, D], BF16) for i in range(NB)]
                junk, b_junk = C.sb(es, "junk", [128, D], BF16)
                ss = [C.sb(es, f"ss{i}", [128, 1], F32) for i in range(NB)]
                rs = [C.sb(es, f"rs{i}", [128, 1], F32) for i in range(NB)]
                hown, b_hown = C.sb(es, "hown", [128, 9, 16, 128], BF16)
                pT = [C.ps(es, f"pT{i}", [128, D], BF16) for i in range(NB)]
                pO = [C.ps(es, f"pO{i}", [128, 16, 16], F32) for i in range(NB)]
                S.op("pool", lambda e: e.memset(hown[:], 0.0), writes=[b_hown])
                for t in range(NT):
                    i = t % NB
                    (x_, bx), (hn_, bhn), (hT_, bhT), (ss_, bss), (rs_, brs) = xt[i], hn[i], hT[i], ss[i], rs[i]
                    (pT_, bpT), (pO_, bpO) = pT[i], pO[i]
                    S.dma("sp", x_[:], h_all[t * 128:(t + 1) * 128, :], writes=[bx])
                    S.op("act", lambda e: e.activation(out=junk[:], in_=x_[:], func=AF.Square, accum_out=ss_[:]),
                         reads=[bx], writes=[b_junk, bss])
                    S.op("act", lambda e: e.activation(out=rs_[:], in_=ss_[:], func=AF.Sqrt, scale=1.0 / D,
                                                       bias=EPS), reads=[bss], writes=[brs])
                    S.op("dve", lambda e: e.reciprocal(out=rs_[:], in_=rs_[:]), reads=[brs], writes=[brs])
                    S.op("dve", lambda e: e.tensor_scalar(out=hn_[:], in0=x_[:], scalar1=rs_[:, 0:1], scalar2=None,
                                                          op0=ALU.mult), reads=[bx, brs], writes=[bhn])
                    for k in range(16):
                        S.op("pe", lambda e: e.transpose(out=pT_[:, k * 128:(k + 1) * 128],
                                                         in_=hn_[:, k * 128:(k + 1) * 128], identity=ident[:]),
                             reads=[bhn, b_ident], writes=[bpT])
                    S.op("act", lambda e: e.copy(out=hT_[:, 0:1024], in_=pT_[:, 0:1024]), reads=[bpT], writes=[bhT])
                    S.op("dve", lambda e: e.tensor_copy(out=hT_[:, 1024:2048], in_=pT_[:, 1024:2048]),
                         reads=[bpT], writes=[bhT])
                    S.dma("pool", hT_all[t], hT_[:], reads=[bhT], writes=[B_hT_all])
                    for k in range(16):
                        S.op("pe", lambda e: e.matmul(pO_[:, k, :], lhsT=hn_[:, k * 128:(k + 1) * 128], rhs=selb[:],
                                                      start=True, stop=True),
                             reads=[bhn, b_selb], writes=[bpO])
                    S.op("dve", lambda e: e.tensor_copy(out=hown[:, t // 8, :, (t % 8) * 16:(t % 8) * 16 + 16],
                                                        in_=pO_[:]), reads=[bpO], writes=[b_hown])
                for u in range(9):
                    S.dma("sp", hT_own[u].rearrange("p (k t) -> p k t", k=16), hown[:, u, :, :],
                          reads=[b_hown], writes=[B_hT_own])
                S.barrier()

        if "gemm" in stages:
            with contextlib.ExitStack() as es:
                wf = [C.sb(es, f"wf{i}", [128, 16, 512], F32) for i in range(2)]
                wb = [C.sb(es, f"wb{i}", [128, 16, 512], BF16) for i in range(2)]
                NH = 6
                hT = [C.sb(es, f"ghT{i}", [128, 16, 128], BF16) for i in range(NH)]
                ob = [C.sb(es, f"ob{i}", [128, 512], F32) for i in range(4)]
                pp = [C.ps(es, f"pp{i}", [128, 512], F32) for i in range(4)]
                w3 = w_in.ap().rearrange("(k p) c -> p k c", p=128)
                blocks = []
                for (src, ntiles, blks, dst, bdst, bsrc) in ((hT_all, NT, PA_BLOCKS, P_all, B_P_all, B_hT_all),
                                                             (hT_own, 9, PO_BLOCKS, P_own, B_P_own, B_hT_own)):
                    for (c0, wdt, d0) in blks:
                        blocks.append((src, ntiles, dst, bdst, bsrc, c0, wdt, d0))

                def load_w(bi):
                    (_, _, _, _, _, c0, wdt, _) = blocks[bi]
                    (wf_, bwf), (wb_, bwb) = wf[bi % 2], wb[bi % 2]
                    S.dma("pool", wf_[:, 0:8, 0:wdt], w3[:, 0:8, c0:c0 + wdt], writes=[bwf])
                    S.dma("sp", wf_[:, 8:16, 0:wdt], w3[:, 8:16, c0:c0 + wdt], writes=[bwf])
                    for k in range(16):
                        S.op("dve", lambda e: e.tensor_scalar(out=wb_[:, k, 0:wdt], in0=wf_[:, k, 0:wdt],
                                                              scalar1=gcol[:, k:k + 1], scalar2=None, op0=ALU.mult),
                             reads=[bwf, b_gcol], writes=[bwb])

                nt = 0
                load_w(0)
                for bi, (src, ntiles, dst, bdst, bsrc, c0, wdt, d0) in enumerate(blocks):
                    if bi + 1 < len(blocks):
                        load_w(bi + 1)
                    (wb_, bwb) = wb[bi % 2]
                    for t in range(ntiles):
                        (h_, bh), (o_, bo), (p_, bp) = hT[nt % NH], ob[nt % 4], pp[nt % 4]
                        nt += 1
                        S.dma("sp", h_[:], src[t].rearrange("p (k t) -> p k t", k=16), reads=[bsrc], writes=[bh])
                        for k in range(16):
                            S.op("pe", lambda e: e.matmul(p_[:, 0:wdt], lhsT=h_[:, k, :], rhs=wb_[:, k, 0:wdt],
                                                          start=(k == 0), stop=(k == 15)),
                                 reads=[bh, bwb], writes=[bp])
                        S.op("act", lambda e: e.copy(out=o_[:, 0:wdt], in_=p_[:, 0:wdt]), reads=[bp], writes=[bo])
                        S.dma("pool", dst[t * 128:(t + 1) * 128, d0:d0 + wdt], o_[:, 0:wdt], reads=[bo], writes=[bdst])
                S.barrier()

        if "post" in stages:
            with contextlib.ExitStack() as es:
                gq_bc, b_gq = C.sb(es, "gq_sb", [128, 128], F32)
                gk_bc, b_gk = C.sb(es, "gk_sb", [128, 128], F32)
                gi_bc, b_gi = C.sb(es, "gi_sb", [128, 64], F32)
                S.dma("sp", gq_bc[:], gq_in[:, :], writes=[b_gq])
                S.dma("sp", gk_bc[:], gk_in[:, :], writes=[b_gk])
                S.dma("sp", gi_bc[:], gi_in[:, :], writes=[b_gi])
                junks = [C.sb(es, f"pjunk{i}", [128, 128], F32) for i in range(3)]
                NB = 3
                pa = [C.sb(es, f"pa{i}", [128, 1088], F32) for i in range(NB)]
                csk = [C.sb(es, f"csk{i}", [128, 2, 8, 16], F32) for i in range(NB)]
                csi = [C.sb(es, f"csi{i}", [128, 2, 16, 8], F32) for i in range(NB)]
                ssq = [C.sb(es, f"ssq{i}", [128, 8], F32) for i in range(NB)]
                kn = [C.sb(es, f"kn{i}", [128, 8, 128], F32) for i in range(NB)]
                tt = [C.sb(es, f"tt{i}", [128, 4, 8, 16], F32) for i in range(NB)]
                kr = [C.sb(es, f"kr{i}", [128, 8, 128], BF16) for i in range(NB)]
                kT = [C.sb(es, f"kT{i}", [128, 8, 128], BF16) for i in range(NB)]
                vb = [C.sb(es, f"vb{i}", [128, 512], BF16) for i in range(NB)]
                kib = [C.sb(es, f"kib{i}", [128, 128], BF16) for i in range(NB)]
                kiT = [C.sb(es, f"kiT{i}", [128, 128], BF16) for i in range(NB)]
                pT = [C.ps(es, f"ppT{i}", [128, 8, 128], BF16) for i in range(NB)]
                pI = [C.ps(es, f"ppI{i}", [128, 128], BF16) for i in range(NB)]
                po_ = [C.sb(es, f"po{i}", [128, 4112], F32) for i in range(NB)]
                zs = [C.sb(es, f"zs{i}", [128, 1024], F32) for i in range(NB)]
                zb_ = [C.sb(es, f"zbb{i}", [128, 8, 128], BF16) for i in range(NB)]
                wv = [C.sb(es, f"wv{i}", [128, 16], F32) for i in range(NB)]

                def headnorm_rope(i, src3, H, hd, g_bc, cs_tile, half, b_src, b_cs, do_norm=True):
                    (ss_, bss), (kn_, bkn), (tt_, btt), (kr_, bkr) = ssq[i], kn[i], tt[i], kr[i]
                    (junk, b_junk) = junks[i]
                    if hd == 128:
                        knv = kn_[:, 0:H, :]
                        krv = kr_[:, 0:H, :]
                    else:
                        knv = kn_[:].rearrange("p a b -> p (a b)")[:, 0:H * hd].rearrange("p (h d) -> p h d", h=H)
                        krv = kr_[:].rearrange("p a b -> p (a b)")[:, 0:H * hd].rearrange("p (h d) -> p h d", h=H)
                    if do_norm:
                        for h in range(H):
                            S.op("act", lambda e: e.activation(out=junk[:, 0:hd], in_=src3[:, h, :], func=AF.Square,
                                                               accum_out=ss_[:, h:h + 1]),
                                 reads=[b_src], writes=[b_junk, bss])
                        S.op("act", lambda e: e.activation(out=ss_[:, 0:H], in_=ss_[:, 0:H], func=AF.Sqrt,
                                                           scale=1.0 / hd, bias=EPS), reads=[bss], writes=[bss])
                        yield
                        S.op("dve", lambda e: e.reciprocal(out=ss_[:, 0:H], in_=ss_[:, 0:H]), reads=[bss], writes=[bss])
                        for h in range(H):
                            S.op("dve", lambda e: e.scalar_tensor_tensor(out=knv[:, h, :], in0=src3[:, h, :],
                                                                         scalar=ss_[:, h:h + 1], in1=g_bc[:, 0:hd],
                                                                         op0=ALU.mult, op1=ALU.mult),
                                 reads=[b_src, bss], writes=[bkn])
                        srcn, bsn = knv, bkn
                    else:
                        srcn, bsn = src3, b_src
                    if hd == 128:
                        cosv, sinv = cs_tile[:, 0, 0:H, :], cs_tile[:, 1, 0:H, :]
                        tv = [tt_[:, q, 0:H, :] for q in range(4)]
                    else:
                        cosv, sinv = cs_tile[:, 0, 0:H, :], cs_tile[:, 1, 0:H, :]
                        tflat = tt_[:].rearrange("p a b c -> p a (b c)")
                        tv = [tflat[:, q, 0:H * half].rearrange("p (h d) -> p h d", h=H) for q in range(4)]
                    yield
                    x1, x2 = srcn[:, :, 0:half], srcn[:, :, half:2 * half]
                    S.op("dve", lambda e: e.tensor_tensor(out=tv[0], in0=x1, in1=cosv, op=ALU.mult),
                         reads=[bsn, b_cs], writes=[btt])
                    S.op("pool", lambda e: e.tensor_tensor(out=tv[1], in0=x2, in1=sinv, op=ALU.mult),
                         reads=[bsn, b_cs], writes=[btt])
                    S.op("dve", lambda e: e.tensor_tensor(out=tv[2], in0=x2, in1=cosv, op=ALU.mult),
                         reads=[bsn, b_cs], writes=[btt])
                    S.op("pool", lambda e: e.tensor_tensor(out=tv[3], in0=x1, in1=sinv, op=ALU.mult),
                         reads=[bsn, b_cs], writes=[btt])
                    S.op("act", lambda e: e.copy(out=krv, in_=srcn), reads=[bsn], writes=[bkr])
                    yield
                    S.op("dve", lambda e: e.tensor_tensor(out=krv[:, :, 0:half], in0=tv[0], in1=tv[1], op=ALU.subtract),
                         reads=[btt], writes=[bkr])
                    S.op("dve", lambda e: e.tensor_tensor(out=krv[:, :, half:2 * half], in0=tv[2], in1=tv[3], op=ALU.add),
                         reads=[btt], writes=[bkr])
                    yield
                    return krv, bkr

                def all_tile(t):
                    i = t % NB
                    (pa_, bpa), (csk_, bcsk), (csi_, bcsi) = pa[i], csk[i], csi[i]
                    rows = slice(t * 128, (t + 1) * 128)
                    S.dma("sp", pa_[:], P_all[rows, 0:1088], reads=[B_P_all], writes=[bpa])
                    S.dma("sp", csk_[:, :, 0:4, :], ropeK_in[rows].rearrange("p (a h d) -> p a h d", a=2, h=4),
                          writes=[bcsk])
                    S.dma("sp", csi_[:, :, 0:1, :], ropeKI_in[rows].rearrange("p (a h d) -> p a h d", a=2, h=1),
                          writes=[bcsi])
                    k3 = pa_[:, 0:512].rearrange("p (h d) -> p h d", h=4)
                    krv, bkr = yield from headnorm_rope(i, k3, 4, 128, gk_bc, csk_, 16, bpa, bcsk)
                    (pT_, bpT), (kT_, bkT) = pT[i], kT[i]
                    for g in range(4):
                        S.op("pe", lambda e: e.transpose(out=pT_[:, g, :], in_=krv[:, g, :], identity=ident[:]),
                             reads=[bkr, b_ident], writes=[bpT])
                    S.op("act", lambda e: e.copy(out=kT_[:, 0:4, :], in_=pT_[:, 0:4, :]), reads=[bpT], writes=[bkT])
                    S.dma("pool", KT_d[:, :, rows], kT_[:, 0:4, :], reads=[bkT], writes=[B_KT])
                    (vb_, bvb) = vb[i]
                    S.op("pool", lambda e: e.tensor_copy(out=vb_[:], in_=pa_[:, 512:1024]), reads=[bpa], writes=[bvb])
                    S.dma("pool", V_d[rows, :], vb_[:], reads=[bvb], writes=[B_V])
                    ki3 = pa_[:, 1024:1088].rearrange("p (h d) -> p h d", h=1)
                    yield
                    kiv, bkr = yield from headnorm_rope(i, ki3, 1, 64, gi_bc, csi_, 8, bpa, bcsi)
                    (kib_, bkib), (pI_, bpI), (kiT_, bkiT) = kib[i], pI[i], kiT[i]
                    S.op("dve", lambda e: e.tensor_copy(out=kib_[:, 0:64], in_=kiv[:, 0, :]), reads=[bkr], writes=[bkib])
                    S.op("pool", lambda e: e.tensor_copy(out=kib_[:, 64:128], in_=kiv[:, 0, :]), reads=[bkr], writes=[bkib])
                    S.op("pe", lambda e: e.transpose(out=pI_[:], in_=kib_[:], identity=ident[:]),
                         reads=[bkib, b_ident], writes=[bpI])
                    S.op("act", lambda e: e.copy(out=kiT_[:], in_=pI_[:]), reads=[bpI], writes=[bkiT])
                    S.dma("pool", kiT_d[:, rows], kiT_[:], reads=[bkiT], writes=[B_kiT])

                def run_interleaved(gens, width):
                    active = []
                    gens = list(gens)
                    while gens or active:
                        while gens and len(active) < width:
                            active.append(gens.pop(0))
                        for g in list(active):
                            try:
                                next(g)
                            except StopIteration:
                                active.remove(g)

                run_interleaved([all_tile(t) for t in range(NT)], 3)

                def own_tile(u):
                    i = u % NB
                    (po__, bpo), (csk_, bcsk), (csi_, bcsi) = po_[i], csk[i], csi[i]
                    rows = slice(u * 128, (u + 1) * 128)
                    S.dma("sp", po__[:], P_own[rows, 0:4112], reads=[B_P_own], writes=[bpo])
                    S.dma("sp", csk_[:], ropeQ_in[rows].rearrange("p (a h d) -> p a h d", a=2, h=8), writes=[bcsk])
                    S.dma("sp", csi_[:], ropeQI_in[rows].rearrange("p (a h d) -> p a h d", a=2, h=16), writes=[bcsi])
                    (pT_, bpT), (kT_, bkT) = pT[i], kT[i]
                    q3 = po__[:, 0:1024].rearrange("p (h d) -> p h d", h=8)
                    krv, bkr = yield from headnorm_rope(i, q3, 8, 128, gq_bc, csk_, 16, bpo, bcsk)
                    for h in range(8):
                        S.op("pe", lambda e: e.transpose(out=pT_[:, h, :], in_=krv[:, h, :], identity=ident[:]),
                             reads=[bkr, b_ident], writes=[bpT])
                    S.op("act", lambda e: e.copy(out=kT_[:], in_=pT_[:]), reads=[bpT], writes=[bkT])
                    S.dma("pool", QT_d[:, :, rows], kT_[:], reads=[bkT], writes=[B_QT])
                    yield
                    for (c0, dstd, bdst) in ((1024, zaT_d, B_zaT), (3088, zbT_d, B_zbT)):
                        (zs_, bzs), (zb__, bzb) = zs[i], zb_[i]
                        S.op("act", lambda e: e.activation(out=zs_[:], in_=po__[:, c0:c0 + 1024], func=AF.Sigmoid),
                             reads=[bpo], writes=[bzs])
                        S.op("dve", lambda e: e.tensor_tensor(out=zb__[:].rearrange("p a b -> p (a b)"), in0=zs_[:],
                                                              in1=po__[:, c0:c0 + 1024], op=ALU.mult),
                             reads=[bzs, bpo], writes=[bzb])
                        for h in range(8):
                            S.op("pe", lambda e: e.transpose(out=pT_[:, h, :], in_=zb__[:, h, :], identity=ident[:]),
                                 reads=[bzb, b_ident], writes=[bpT])
                        S.op("act", lambda e: e.copy(out=kT_[:], in_=pT_[:]), reads=[bpT], writes=[bkT])
                        S.dma("pool", dstd[:, :, rows], kT_[:], reads=[bkT], writes=[bdst])
                    qi3 = po__[:, 2048:3072].rearrange("p (h d) -> p h d", h=16)
                    yield
                    qv, bkr = yield from headnorm_rope(i, qi3, 16, 64, None, csi_, 8, bpo, bcsi, do_norm=False)
                    qflat = kr[i][0][:]
                    for h in range(8):
                        S.op("pe", lambda e: e.transpose(out=pT_[:, h, :], in_=qflat[:, h, :], identity=ident[:]),
                             reads=[bkr, b_ident], writes=[bpT])
                    S.op("act", lambda e: e.copy(out=kT_[:], in_=pT_[:]), reads=[bpT], writes=[bkT])
                    S.dma("pool", qiT_d[:, :, rows], kT_[:], reads=[bkT], writes=[B_qiT])
                    (wv_, bwv) = wv[i]
                    S.op("dve", lambda e: e.tensor_scalar(out=wv_[:], in0=po__[:, 3072:3088], scalar1=1.0 / 32.0,
                                                          scalar2=None, op0=ALU.mult), reads=[bpo], writes=[bwv])
                    S.dma("pool", w_d[rows, :], wv_[:], reads=[bwv], writes=[B_w])
                    yield

                run_interleaved([own_tile(u) for u in range(9)], 3)
                S.barrier()

        if "hgrn" in stages:
            with contextlib.ExitStack() as es:
                Uf, b_U = C.sb(es, "Uf", [128, 128], F32)
                Wf, b_W = C.sb(es, "Wf", [128, 128], F32)
                chi, b_chi = C.sb(es, "chi", [128, 2], F32)
                lb, b_lb = C.sb(es, "lb_sb", [128, 1024], F32)
                oml, b_oml = C.sb(es, "oml", [128, 1024], F32)
                l1, b_l1 = C.sb(es, "l1", [128, 1024], F32)
                go_bc, b_go = C.sb(es, "go_sb", [128, 128], F32)
                S.dma("sp", Uf[:], U_in[:, :], writes=[b_U])
                S.dma("sp", Wf[:], W_in[:, :], writes=[b_W])
                S.dma("sp", chi[:], chi_in[:, :], writes=[b_chi])
                S.dma("sp", go_bc[:], go_in[:, :], writes=[b_go])
                S.dma("sp", lb[:], lb0_in[:, :], writes=[b_lb])
                S.dma("sp", l1[:], lb1_in[:, :], writes=[b_l1])
                S.op("dve", lambda e: e.tensor_tensor(out=lb[:], in0=lb[:], in1=l1[:], op=ALU.subtract),
                     reads=[b_lb, b_l1], writes=[b_lb])
                S.op("act", lambda e: e.activation(out=lb[:], in_=lb[:], func=AF.Sigmoid), reads=[b_lb], writes=[b_lb])
                S.op("dve", lambda e: e.tensor_scalar(out=oml[:], in0=lb[:], scalar1=-1.0, scalar2=1.0, op0=ALU.mult,
                                                      op1=ALU.add), reads=[b_lb], writes=[b_oml])
                Sf, b_Sf = C.sb(es, "Sf", [128, 8, 128], F32)
                S0b, b_S0b = C.sb(es, "S0b", [128, 8, 128], BF16)
                ybT, b_ybT = C.sb(es, "ybT", [128, 8, NOWN], BF16)
                S.op("pool", lambda e: e.memset(Sf[:], 0.0), writes=[b_Sf])
                S.op("pool", lambda e: e.memset(S0b[:], 0.0), writes=[b_S0b])
                S.op("pool", lambda e: e.memset(ybT[:], 0.0), writes=[b_ybT])
                NB = 2
                qb = [C.sb(es, f"qb{i}", [128, 1024], F32) for i in range(NB)]
                fb = [C.sb(es, f"fb{i}", [128, 1024], F32) for i in range(NB)]
                ib = [C.sb(es, f"ib{i}", [128, 1024], F32) for i in range(NB)]
                gg = [C.sb(es, f"gg{i}", [128, 1024], F32) for i in range(NB)]
                kk = [C.sb(es, f"kk{i}", [128, 1024], F32) for i in range(NB)]
                e1 = [C.sb(es, f"e1{i}", [128, 512], F32) for i in range(NB)]
                e2 = [C.sb(es, f"e2{i}", [128, 512], F32) for i in range(NB)]
                e3 = [C.sb(es, f"e3{i}", [128, 512], F32) for i in range(NB)]
                qt = [C.sb(es, f"qt{i}", [128, 1024], BF16) for i in range(NB)]
                kt = [C.sb(es, f"kt{i}", [128, 1024], BF16) for i in range(NB)]
                kd = [C.sb(es, f"kd{i}", [128, 1024], BF16) for i in range(NB)]
                vv = [C.sb(es, f"vv{i}", [128, 1024], BF16) for i in range(NB)]
                ebl = [C.sb(es, f"ebl{i}", [128, 16], F32) for i in range(NB)]
                qkT8 = [C.sb(es, f"qkT8{i}", [128, 8, 256], BF16) for i in range(2)]
                AT8, b_AT8 = C.sb(es, "AT8", [128, 8, 128], BF16)
                S1f8, b_S1f8 = C.sb(es, "S1f8", [128, 8, 128], F32)
                S1b8, b_S1b8 = C.sb(es, "S1b8", [128, 8, 128], BF16)
                S0b2 = [(S0b, b_S0b), C.sb(es, "S0b_1", [128, 8, 128], BF16)]
                on8, b_on8 = C.sb(es, "on8", [128, 8, 128], BF16)
                sq8, b_sq8 = C.sb(es, "sq8", [128, 8, 128], F32)
                ss8, b_ss8 = C.sb(es, "ss8", [128, 8], F32)
                U8, b_U8 = C.sb(es, "U8", [128, 8, 128], F32)
                for h in range(8):
                    S.op("dve", lambda e: e.tensor_copy(out=U8[:, h, :], in_=Uf[:]), reads=[b_U], writes=[b_U8])
                X1, b_X1 = C.ps(es, "X1", [128, 1024], F32)
                X2, b_X2 = C.ps(es, "X2", [128, 8, 128], F32)
                X3, b_X3 = C.ps(es, "X3", [128, 8, 128], F32)
                pCS, b_pCS = C.ps(es, "pCS", [128, 144], F32)
                pQ4, b_pQ4 = C.ps(es, "pQ4", [128, 4, 256], BF16)
                pB, pBD = X1[:, 0:512], X1[:, 512:1024]
                b_pB = b_pBD = b_X1
                X1v = X1.rearrange("p (a b) -> p a b", a=8)
                pC = pCS[:, 0:16]
                b_pC = b_pCS
                pS3 = pCS[:, 16:144].rearrange("p (a b) -> p a b", a=8)
                def hg_prep(t):
                        i = t % NB
                        rows = slice(t * 128, (t + 1) * 128)
                        (qb_, bqb), (fb_, bfb), (ib_, bib), (gg_, bgg), (kk_, bkk) = qb[i], fb[i], ib[i], gg[i], kk[i]
                        (qt_, bqt), (kt_, bkt), (kd_, bkd), (vv_, bvv), (ebl_, bebl) = qt[i], kt[i], kd[i], vv[i], ebl[i]
                        S.dma("sp", qb_[:], P_all[rows, 1088:2112], reads=[B_P_all], writes=[bqb])
                        S.dma("sp", fb_[:], P_all[rows, 2112:3136], reads=[B_P_all], writes=[bfb])
                        S.dma("sp", ib_[:], P_all[rows, 3136:4160], reads=[B_P_all], writes=[bib])
                        S.op("act", lambda e: e.copy(out=vv_[:], in_=ib_[:]), reads=[bib], writes=[bvv])
                        S.op("act", lambda e: e.activation(out=fb_[:], in_=fb_[:], func=AF.Sigmoid), reads=[bfb], writes=[bfb])
                        yield
                        S.op("act", lambda e: e.activation(out=ib_[:], in_=qb_[:], func=AF.Sigmoid), reads=[bqb, bvv],
                             writes=[bib])
                        S.op("dve", lambda e: e.tensor_tensor(out=fb_[:], in0=fb_[:], in1=oml[:], op=ALU.mult),
                             reads=[bfb, b_oml], writes=[bfb])
                        S.op("pool", lambda e: e.tensor_tensor(out=fb_[:], in0=fb_[:], in1=lb[:], op=ALU.add),
                             reads=[bfb, b_lb], writes=[bfb])
                        S.op("act", lambda e: e.activation(out=gg_[:], in_=fb_[:], func=AF.Ln), reads=[bfb], writes=[bgg])
                        S.op("pool", lambda e: e.tensor_scalar(out=kk_[:], in0=fb_[:], scalar1=-1.0, scalar2=1.0,
                                                               op0=ALU.mult, op1=ALU.add), reads=[bfb], writes=[bkk])

                        S.op("dve", lambda e: e.tensor_tensor(out=qb_[:], in0=qb_[:], in1=ib_[:], op=ALU.mult),
                             reads=[bqb, bib], writes=[bqb])
                        for hf in range(2):
                            yield
                            cs = slice(hf * 512, (hf + 1) * 512)
                            (e1_, be1), (e2_, be2), (e3_, be3) = e1[hf], e2[hf], e3[hf]
                            S.op("pe", lambda e: e.matmul(pB[:], lhsT=Uf[:], rhs=gg_[:, cs], start=True, stop=True),
                                 reads=[b_U, bgg], writes=[b_pB])
                            S.op("pe", lambda e: e.matmul(pBD[:], lhsT=Wf[:], rhs=gg_[:, cs], start=True, stop=True),
                                 reads=[b_W, bgg], writes=[b_pBD])
                            S.op("act", lambda e: e.activation(out=e1_[:], in_=pB[:], func=AF.Exp), reads=[b_pB], writes=[be1])
                            S.op("act", lambda e: e.activation(out=e2_[:], in_=pB[:], func=AF.Exp, scale=-1.0),
                                 reads=[b_pB], writes=[be2])
                            S.op("act", lambda e: e.activation(out=e3_[:], in_=pBD[:], func=AF.Exp), reads=[b_pBD],
                                 writes=[be3])
                            S.op("dve", lambda e: e.tensor_tensor(out=qt_[:, cs], in0=qb_[:, cs], in1=e1_[:], op=ALU.mult),
                                 reads=[bqb, be1], writes=[bqt])
                            S.op("pool", lambda e: e.tensor_tensor(out=kt_[:, cs], in0=kk_[:, cs], in1=e2_[:], op=ALU.mult),
                                 reads=[bkk, be2], writes=[bkt])
                            S.op("dve", lambda e: e.tensor_tensor(out=kd_[:, cs], in0=kk_[:, cs], in1=e3_[:], op=ALU.mult),
                                 reads=[bkk, be3], writes=[bkd])
                        for h in range(8):
                            S.op("pe", lambda e: e.matmul(pC[:, 2 * h:2 * h + 2], lhsT=gg_[:, h * 128:(h + 1) * 128],
                                                          rhs=chi[:], start=True, stop=True),
                                 reads=[bgg, b_chi], writes=[b_pC])
                        S.op("act", lambda e: e.activation(out=ebl_[:], in_=pC[:], func=AF.Exp), reads=[b_pC], writes=[bebl])
                        yield

                def hg_tail(t):
                        i = t % NB
                        (qt_, bqt), (kt_, bkt), (kd_, bkd), (vv_, bvv), (ebl_, bebl) = qt[i], kt[i], kd[i], vv[i], ebl[i]
                        (qk_, bqk) = qkT8[t % 2]
                        (S0c, bS0c), (S0n, bS0n) = S0b2[t % 2], S0b2[(t + 1) % 2]
                        hcs = [slice(h * 128, (h + 1) * 128) for h in range(8)]
                        yield
                        for half in range(2):
                            for hh in range(4):
                                h = 4 * half + hh
                                S.op("pe", lambda e: e.transpose(out=pQ4[:, hh, 0:128], in_=qt_[:, hcs[h]], identity=ident[:]),
                                     reads=[bqt, b_ident], writes=[b_pQ4])
                                S.op("pe", lambda e: e.transpose(out=pQ4[:, hh, 128:256], in_=kt_[:, hcs[h]],
                                                                 identity=ident[:]), reads=[bkt, b_ident], writes=[b_pQ4])
                            if half == 0:
                                S.op("act", lambda e: e.copy(out=qk_[:, 0:4, :], in_=pQ4[:]), reads=[b_pQ4], writes=[bqk])
                                for h in range(8):
                                    S.op("pe", lambda e: e.matmul(X2[:, h, :], lhsT=kd_[0:64, hcs[h]], rhs=vv_[0:64, hcs[h]],
                                                                  start=True, stop=True), reads=[bkd, bvv], writes=[b_X2])
                            else:
                                S.op("dve", lambda e: e.tensor_copy(out=qk_[:, 4:8, :], in_=pQ4[:]), reads=[b_pQ4],
                                     writes=[bqk])
                        for h in range(8):
                            S.op("pe", lambda e: e.matmul(X1v[:, h, :], lhsT=qk_[:, h, 128:256], rhs=qk_[:, h, 0:128],
                                                          start=True, stop=True), reads=[bqk], writes=[b_X1])
                        S.op("dve", lambda e: e.tensor_tensor(out=AT8[:], in0=X1v, in1=U8[:], op=ALU.mult),
                             reads=[b_X1, b_U8], writes=[b_AT8])
                        yield
                        for h in range(8):
                            S.op("dve", lambda e: e.scalar_tensor_tensor(out=S1f8[:, h, :], in0=Sf[:, h, :],
                                                                         scalar=ebl_[:, 2 * h:2 * h + 1],
                                                                         in1=X2[:, h, :], op0=ALU.mult, op1=ALU.add),
                                 reads=[b_Sf, bebl, b_X2], writes=[b_S1f8])
                        S.op("act", lambda e: e.copy(out=S1b8[:], in_=S1f8[:]), reads=[b_S1f8], writes=[b_S1b8])
                        yield
                        yield
                        for h in range(8):
                            S.op("pe", lambda e: e.matmul(X3[:, h, :], lhsT=AT8[:, h, :], rhs=vv_[:, hcs[h]],
                                                          start=(h % 4 == 0), stop=False, skip_group_check=True),
                                 reads=[b_AT8, bvv], writes=[b_X3])
                        for h in range(8):
                            S.op("pe", lambda e: e.matmul(X3[0:64, h, :], lhsT=qk_[:, h, 0:64], rhs=S0c[:, h, :], start=False,
                                                          stop=True, skip_group_check=True),
                                 reads=[bqk, bS0c], writes=[b_X3])
                        for h in range(8):
                            S.op("pe", lambda e: e.matmul(X3[64:128, h, :], lhsT=qk_[:, h, 64:128], rhs=S1b8[:, h, :],
                                                          start=False, stop=True, skip_group_check=True),
                                 reads=[bqk, b_S1b8], writes=[b_X3])
                        yield
                        for h in range(8):
                            S.op("pe", lambda e: e.matmul(X2[:, h, :], lhsT=kd_[64:128, hcs[h]], rhs=vv_[64:128, hcs[h]],
                                                          start=True, stop=True), reads=[bkd, bvv], writes=[b_X2])
                        yield
                        S.op("act", lambda e: e.activation(out=sq8[:], in_=X3[:], func=AF.Square), reads=[b_X3],
                             writes=[b_sq8])
                        S.op("dve", lambda e: e.tensor_reduce(out=ss8[:], in_=sq8[:], axis=AX.X, op=ALU.add),
                             reads=[b_sq8], writes=[b_ss8])
                        S.op("act", lambda e: e.activation(out=ss8[:], in_=ss8[:], func=AF.Sqrt, scale=1.0 / 128, bias=EPS),
                             reads=[b_ss8], writes=[b_ss8])
                        S.op("dve", lambda e: e.reciprocal(out=ss8[:], in_=ss8[:]), reads=[b_ss8], writes=[b_ss8])
                        for h in range(8):
                            S.op("act", lambda e: e.activation(out=on8[:, h, :], in_=X3[:, h, :], func=AF.Copy,
                                                               scale=ss8[:, h:h + 1]),
                                 reads=[b_X3, b_ss8], writes=[b_on8])
                        for h in range(8):
                            S.op("pe", lambda e: e.matmul(pS3[:, h, :], lhsT=on8[:, h, :], rhs=selb[:], start=True, stop=True),
                                 reads=[b_on8, b_selb], writes=[b_pCS])
                        S.op("act", lambda e: e.copy(out=ybT[:, :, 16 * t:16 * t + 16], in_=pS3), reads=[b_pCS],
                             writes=[b_ybT])
                        for h in range(8):
                            S.op("dve", lambda e: e.scalar_tensor_tensor(out=Sf[:, h, :], in0=S1f8[:, h, :],
                                                                         scalar=ebl_[:, 2 * h + 1:2 * h + 2], in1=X2[:, h, :],
                                                                         op0=ALU.mult, op1=ALU.add),
                                 reads=[b_S1f8, bebl, b_X2], writes=[b_Sf])
                        S.op("act", lambda e: e.copy(out=S0n[:], in_=Sf[:]), reads=[b_Sf], writes=[bS0n])
                        yield


                for _ in hg_prep(0):
                    pass
                for t in range(NT):
                    ga = hg_tail(t)
                    gb = hg_prep(t + 1) if t + 1 < NT else iter(())
                    da = db = False
                    while not (da and db):
                        if not da:
                            try:
                                next(ga)
                            except StopIteration:
                                da = True
                        if not db:
                            try:
                                next(gb)
                            except StopIteration:
                                db = True
                S.dma("pool", ybT_d[:, :, :], ybT[:], reads=[b_ybT], writes=[B_ybT])
                S.barrier()

        if "attn" in stages:
            with contextlib.ExitStack() as es:
                ki2, b_ki2 = C.sb(es, "ki2", [128, TP], BF16)
                for q4 in range(5):
                    S.dma("sp", ki2[:, q4 * 1664:(q4 + 1) * 1664], kiT_d[:, q4 * 1664:(q4 + 1) * 1664],
                          reads=[B_kiT], writes=[b_ki2])
                AM, b_AM = C.sb(es, "AM_sb", [128, 1024], F32)
                S.dma("sp", AM[:], AM_in[:, :], writes=[b_AM])
                I4, b_I4 = C.sb(es, "I4", [128, 512], BF16)
                for r in range(4):
                    S.op("dve", lambda e: e.tensor_copy(out=I4[:, r * 128:(r + 1) * 128], in_=identf[:]),
                         reads=[b_identf], writes=[b_I4])
                ones_b, b_ones = C.sb(es, "ones_b", [128, 128], BF16)
                S.op("pool", lambda e: e.memset(ones_b[:], 1.0), writes=[b_ones])
                gq_bc, b_gq = C.sb(es, "gq_bc2", [128, 128], F32)
                gk_bc, b_gk = C.sb(es, "gk_bc2", [128, 128], F32)
                mq, b_mq = C.sb(es, "mq", [128, 1], F32)
                mk, b_mk = C.sb(es, "mk", [128, 1], F32)
                S.dma("sp", gq_bc[:], gq_in[:, :], writes=[b_gq])
                S.dma("sp", gk_bc[:], gk_in[:, :], writes=[b_gk])
                S.op("dve", lambda e: e.tensor_reduce(out=mq[:], in_=gq_bc[:], axis=AX.X, op=ALU.max,
                                                      apply_absolute_value=True), reads=[b_gq], writes=[b_mq])
                S.op("dve", lambda e: e.tensor_reduce(out=mk[:], in_=gk_bc[:], axis=AX.X, op=ALU.max,
                                                      apply_absolute_value=True), reads=[b_gk], writes=[b_mk])
                S.op("dve", lambda e: e.tensor_tensor(out=mq[:], in0=mq[:], in1=mk[:], op=ALU.mult),
                     reads=[b_mq, b_mk], writes=[b_mq])
                S.op("dve", lambda e: e.tensor_scalar(out=mq[:], in0=mq[:], scalar1=-(128.0 ** 0.5), scalar2=None,
                                                      op0=ALU.mult), reads=[b_mq], writes=[b_mq])
                score, b_score = C.sb(es, "score", [128, 8208], F32)
                cjunk, b_cjunk = C.sb(es, "cjunk", [128, 8208], BF16)
                MB, b_MB = C.sb(es, "MB", [128, 8208], BF16)
                QTj, b_QTj = C.sb(es, "QTj", [128, 8, 128], BF16)
                qiTj, b_qiTj = C.sb(es, "qiTj", [128, 8, 128], BF16)
                zaTj, b_zaTj = C.sb(es, "zaTj", [128, 8, 128], BF16)
                wj, b_wj = C.sb(es, "wj", [128, 16], F32)
                Dg, b_Dg = C.sb(es, "Dg", [128, 16, 128], BF16)
                NR = 4
                Rl = [C.sb(es, f"Rl{i}", [128, 512], BF16) for i in range(8)]
                PT = [C.sb(es, f"PT{i}", [128, 512], BF16) for i in range(NR)]
                KTc = [C.sb(es, f"KTc{i}", [128, 4, 512], BF16) for i in range(2)]
                Vc = [C.sb(es, f"Vc{i}", [128, 4, 512], BF16) for i in range(2)]
                sm = {n: C.sb(es, "bs_" + n, [128, 1], F32) for n in ("lo", "hi", "mid", "cnt", "ge", "d1", "d2", "B", "nmid", "sga")}
                ajunk, b_ajunk = C.sb(es, "ajunk", [128, 4608], BF16)
                rden, b_rden = C.sb(es, "rden", [128, 512], F32)
                yaT, b_yaT = C.sb(es, "yaT", [128, 8, 128], BF16)
                oT, b_oT = C.sb(es, "oT", [128, 512], F32)
                pL = [C.ps(es, f"pL{i}", [128, 512], F32) for i in range(3)]
                pSc, b_pSc = C.ps(es, "pSc", [128, 512], F32)
                pOA = [C.ps(es, f"pOA{i}", [128, 512], F32) for i in range(2)]
                pDn = [C.ps(es, f"pDn{i}", [128, 512], F32) for i in range(2)]
                pLx = pL + pOA + pDn
                nrl = 0
                npl = 0
                nplx = 0
                npt = 0
                nkc = 0
                score2 = [(score, b_score), C.sb(es, 'score_1', [128, 8208], F32)]
                Dg2 = [(Dg, b_Dg), C.sb(es, 'Dg_1', [128, 16, 128], BF16)]
                qiT2 = [(qiTj, b_qiTj), C.sb(es, 'qiTj_1', [128, 8, 128], BF16)]
                wj2 = [(wj, b_wj), C.sb(es, 'wj_1', [128, 16], F32)]

                def gen_indexer(j):
                    nonlocal nrl, nplx
                    s0 = 2 + 128 * j
                    NJ = 16 + 1024 * (j + 1)
                    (score, b_score), (Dg, b_Dg), (qiTj, b_qiTj), (wj, b_wj) = score2[j % 2], Dg2[j % 2], qiT2[j % 2], wj2[j % 2]
                    S.dma("sp", qiTj[:], qiT_d[:, :, s0:s0 + 128], reads=[B_qiT], writes=[b_qiTj])
                    S.dma("sp", wj[:], w_d[s0:s0 + 128, :], reads=[B_w], writes=[b_wj])
                    for h in range(16):
                        S.op("dve", lambda e: e.tensor_scalar(out=Dg[:, h, :], in0=identf[:], scalar1=wj[:, h:h + 1],
                                                              scalar2=None, op0=ALU.mult),
                             reads=[b_identf, b_wj], writes=[b_Dg])
                    yield
                    items = []
                    c0 = 0
                    while c0 < NJ:
                        cw = min(512, NJ - c0)
                        for h in range(16):
                            items.append((c0, cw, h))
                        c0 += cw
                    LAG = 4
                    slots = {}
                    order = []
                    for base in range(0, len(items) + LAG, 2):
                        order += [("L", base), ("L", base + 1), ("A", base - LAG), ("A", base + 1 - LAG)]
                    for kind, idx in order:
                        if kind == "L" and idx < len(items):
                            c0, cw, h = items[idx]
                            (pl_, bpl) = pLx[nplx % 7]
                            nplx += 1
                            (rl_, brl) = Rl[nrl % 8]
                            nrl += 1
                            slots[idx] = (rl_, brl)
                            pr = slice((h % 2) * 64, (h % 2) * 64 + 64)
                            S.op("pe", lambda e: e.matmul(pl_[:, 0:cw], lhsT=qiTj[pr, h // 2, :], rhs=ki2[pr, c0:c0 + cw],
                                                          start=True, stop=True),
                                 reads=[b_qiTj, b_ki2], writes=[bpl])
                            if h % 2 == 0:
                                S.op("act", lambda e: e.activation(out=rl_[:, 0:cw], in_=pl_[:, 0:cw], func=AF.Relu),
                                     reads=[bpl], writes=[brl])
                            else:
                                S.op("dve", lambda e: e.tensor_scalar(out=rl_[:, 0:cw], in0=pl_[:, 0:cw], scalar1=0.0,
                                                                      scalar2=None, op0=ALU.max),
                                     reads=[bpl], writes=[brl])
                        if kind == "A" and 0 <= idx < len(items):
                            c0, cw, h = items[idx]
                            (rl_, brl) = slots.pop(idx)
                            S.op("pe", lambda e: e.matmul(pSc[:, 0:cw], lhsT=Dg[:, h, :], rhs=rl_[:, 0:cw],
                                                          start=(h == 0), stop=(h == 15)),
                                 reads=[b_Dg, brl], writes=[b_pSc])
                            if h == 15:
                                S.op("act", lambda e: e.copy(out=score[:, c0:c0 + cw], in_=pSc[:, 0:cw]),
                                     reads=[b_pSc], writes=[b_score])
                        if kind == 'A' and idx % 2 == 1:
                            yield

                def gen_bisect(j):
                    NJ = 16 + 1024 * (j + 1)
                    (score, b_score) = score2[j % 2]
                    g_ = lambda n: sm[n][0]
                    bb = lambda n: sm[n][1]
                    S.op("dve", lambda e: e.tensor_reduce(out=g_("B")[:], in_=score[:, 0:NJ], axis=AX.X, op=ALU.max,
                                                          apply_absolute_value=True), reads=[b_score], writes=[bb("B")])
                    S.op("dve", lambda e: e.tensor_scalar(out=g_("hi")[:], in0=g_("B")[:], scalar1=1.001, scalar2=1e-6,
                                                          op0=ALU.mult, op1=ALU.add), reads=[bb("B")], writes=[bb("hi")])
                    S.op("dve", lambda e: e.tensor_scalar(out=g_("lo")[:], in0=g_("hi")[:], scalar1=-1.0, scalar2=None,
                                                          op0=ALU.mult), reads=[bb("hi")], writes=[bb("lo")])
                    S.op("dve", lambda e: e.tensor_tensor(out=score[:, NJ - 1024:NJ], in0=score[:, NJ - 1024:NJ],
                                                          in1=AM[:], op=ALU.add), reads=[b_score, b_AM], writes=[b_score])
                    S.op("dve", lambda e: e.tensor_tensor(out=g_("d2")[:], in0=g_("hi")[:], in1=g_("lo")[:],
                                                          op=ALU.subtract), reads=[bb("hi"), bb("lo")], writes=[bb("d2")])
                    ND = (NJ * 9 // 20) // 16 * 16
                    NA = NJ - ND
                    for it in range(24):
                        cit = 0.5 ** (it + 1)
                        S.op("dve", lambda e: e.tensor_scalar(out=g_("mid")[:], in0=g_("d2")[:], scalar1=cit,
                                                              scalar2=g_("lo")[:, 0:1], op0=ALU.mult, op1=ALU.add),
                             reads=[bb("d2"), bb("lo")], writes=[bb("mid")])
                        S.op("act", lambda e: e.activation(out=ajunk[:, 0:NA], in_=score[:, ND:NJ], func=AF.Sign,
                                                           scale=-1.0, bias=g_("mid")[:, 0:1], accum_out=g_("sga")[:]),
                             reads=[b_score, bb("mid")], writes=[b_ajunk, bb("sga")])
                        S.op("dve", lambda e: e.tensor_scalar(out=cjunk[:, 0:ND], in0=score[:, 0:ND],
                                                              scalar1=g_("mid")[:, 0:1], scalar2=None, op0=ALU.is_ge,
                                                              op1=ALU.add, accum_out=g_("cnt")[:]),
                             reads=[b_score, bb("mid")], writes=[b_cjunk, bb("cnt")])
                        S.op("dve", lambda e: e.scalar_tensor_tensor(out=g_("cnt")[:], in0=g_("cnt")[:], scalar=2.0,
                                                                     in1=g_("sga")[:], op0=ALU.mult, op1=ALU.subtract),
                             reads=[bb("cnt"), bb("sga")], writes=[bb("cnt")])
                        S.op("dve", lambda e: e.tensor_scalar(out=g_("ge")[:], in0=g_("cnt")[:], scalar1=float(511 - NA),
                                                              scalar2=None, op0=ALU.is_ge), reads=[bb("cnt")],
                             writes=[bb("ge")])
                        S.op("dve", lambda e: e.tensor_scalar(out=g_("d1")[:], in0=g_("mid")[:], scalar1=g_("lo")[:, 0:1],
                                                              scalar2=g_("ge")[:, 0:1], op0=ALU.subtract, op1=ALU.mult),
                             reads=[bb("mid"), bb("lo"), bb("ge")], writes=[bb("d1")])
                        S.op("dve", lambda e: e.tensor_tensor(out=g_("lo")[:], in0=g_("lo")[:], in1=g_("d1")[:],
                                                              op=ALU.add), reads=[bb("lo"), bb("d1")], writes=[bb("lo")])
                        yield
                    yield

                def emit_MB(j):
                    NJ = 16 + 1024 * (j + 1)
                    (score, b_score) = score2[j % 2]
                    g_ = lambda n: sm[n][0]
                    bb = lambda n: sm[n][1]
                    S.op("dve", lambda e: e.tensor_scalar(out=MB[:, 0:NJ], in0=score[:, 0:NJ], scalar1=g_("lo")[:, 0:1],
                                                          scalar2=NEG, op0=ALU.is_lt, op1=ALU.mult),
                         reads=[b_score, bb("lo")], writes=[b_MB])

                def run_pair(ga, gb):
                    la = list_steps = None
                    done_a = done_b = False
                    if gb is None:
                        for _ in ga:
                            pass
                        return
                    while not (done_a and done_b):
                        if not done_a:
                            try:
                                next(ga)
                            except StopIteration:
                                done_a = True
                        for _ in range(RATIO[0]):
                            if not done_b:
                                try:
                                    next(gb)
                                except StopIteration:
                                    done_b = True

                def gen_loop(j):
                    nonlocal npl, npt, nkc
                    NJ = 16 + 1024 * (j + 1)
                    nkt = (NJ + 127) // 128
                    aitems = [(kt_, G) for kt_ in range(nkt) for G in range(2)]
                    chunkbuf = {}
                    pend = {}

                    def emit_pv(ii):
                        kt_, G = aitems[ii]
                        (pt_, bpt) = pend.pop(ii)
                        (Vc_, bVc) = chunkbuf[kt_ // 4][1]
                        q = kt_ % 4
                        kw = min(128, NJ - kt_ * 128)
                        first, last = (kt_ == 0), (kt_ == nkt - 1)
                        for g2 in range(2):
                            g = 2 * G + g2
                            S.op("pe", lambda e: e.matmul(pOA[G][0][:, g2 * 256:(g2 + 1) * 256],
                                                          lhsT=Vc_[0:kw, q, g * 128:(g + 1) * 128],
                                                          rhs=pt_[0:kw, g2 * 256:(g2 + 1) * 256],
                                                          start=(first and g2 == 0), stop=last, skip_group_check=True),
                                 reads=[bVc, bpt], writes=[pOA[G][1]])
                        S.op("pe", lambda e: e.matmul(pDn[G][0][:], lhsT=ones_b[0:kw, :], rhs=pt_[0:kw, :],
                                                      start=first, stop=last, skip_group_check=True),
                             reads=[b_ones, bpt], writes=[pDn[G][1]])

                    for ii, (kt_, G) in enumerate(aitems):
                        if kt_ % 4 == 0 and G == 0:
                            (KTc_, bKTc), (Vc_, bVc) = KTc[nkc % 2], Vc[nkc % 2]
                            nkc += 1
                            chunkbuf[kt_ // 4] = ((KTc_, bKTc), (Vc_, bVc))
                            k0 = kt_ * 128
                            kwid = min(512, NJ - k0)
                            S.dma("sp", KTc_[:, :, 0:kwid], KT_d[:, :, k0:k0 + kwid], reads=[B_KT], writes=[bKTc])
                            ntl = (kwid + 127) // 128
                            for q in range(ntl):
                                kw_ = min(128, kwid - q * 128)
                                S.dma("act", Vc_[0:kw_, q, :], V_d[k0 + q * 128:k0 + q * 128 + kw_, :], reads=[B_V],
                                      writes=[bVc])
                        (KTc_, bKTc) = chunkbuf[kt_ // 4][0]
                        q = kt_ % 4
                        kw = min(128, NJ - kt_ * 128)
                        ks = slice(kt_ * 128, kt_ * 128 + kw)
                        kl = slice(q * 128, q * 128 + kw)
                        (pl_, bpl) = pL[npl % 3]
                        npl += 1
                        (pt_, bpt) = PT[npt % NR]
                        npt += 1
                        S.op("pe", lambda e: e.matmul(pl_[0:kw, :], lhsT=MB[:, ks], rhs=I4[:], start=True, stop=False,
                                                      skip_group_check=True), reads=[b_MB, b_I4], writes=[bpl])
                        for g2 in range(2):
                            g = 2 * G + g2
                            S.op("pe", lambda e: e.matmul(
                                pl_[0:kw, g2 * 256:(g2 + 1) * 256], lhsT=KTc_[:, g, kl],
                                rhs=QTj[:].rearrange("p a b -> p (a b)")[:, 2 * g * 128:(2 * g + 2) * 128],
                                start=False, stop=True, skip_group_check=True),
                                 reads=[bKTc, b_QTj], writes=[bpl])
                        S.op("act", lambda e: e.activation(out=pt_[0:kw, :], in_=pl_[0:kw, :], func=AF.Exp,
                                                           scale=128.0 ** -0.5, bias=mq[0:kw, 0:1]),
                             reads=[bpl, b_mq], writes=[bpt])
                        pend[ii] = (pt_, bpt)
                        if ii >= 2:
                            emit_pv(ii - 2)
                            yield
                    for ii in range(max(0, len(aitems) - 2), len(aitems)):
                        emit_pv(ii)
                    yield

                RATIO = [1]
                for _ in gen_indexer(0):
                    pass
                for _ in gen_bisect(0):
                    pass
                emit_MB(0)
                for _ in gen_indexer(1):
                    pass
                for j in range(8):
                    s0 = 2 + 128 * j
                    NJ = 16 + 1024 * (j + 1)
                    S.dma("sp", QTj[:], QT_d[:, :, s0:s0 + 128], reads=[B_QT], writes=[b_QTj])
                    S.dma("sp", zaTj[:], zaT_d[:, :, s0:s0 + 128], reads=[B_zaT], writes=[b_zaTj])
                    n_loop_steps = 2 * ((NJ + 127) // 128)
                    RATIO[0] = max(1, -(-26 // n_loop_steps))
                    run_pair(gen_loop(j), gen_bisect(j + 1) if j < 7 else None)
                    for G in range(2):
                        S.op("dve", lambda e: e.reciprocal(out=rden[:], in_=pDn[G][0][:]), reads=[pDn[G][1]],
                             writes=[b_rden])
                        S.op("dve", lambda e: e.tensor_tensor(out=oT[:], in0=pOA[G][0][:], in1=rden[:], op=ALU.mult),
                             reads=[pOA[G][1], b_rden], writes=[b_oT])
                        S.op("dve", lambda e: e.tensor_tensor(
                            out=yaT[:, 4 * G:4 * G + 4, :].rearrange("p a b -> p (a b)"), in0=oT[:],
                            in1=zaTj[:, 4 * G:4 * G + 4, :].rearrange("p a b -> p (a b)"), op=ALU.mult),
                             reads=[b_oT, b_zaTj], writes=[b_yaT])
                    S.dma("pool", yaT_d[:, :, j * 128:(j + 1) * 128], yaT[:], reads=[b_yaT], writes=[B_yaT])
                    if j < 7:
                        emit_MB(j + 1)
                    if j + 2 < 8:
                        for _ in gen_indexer(j + 2):
                            pass
                S.barrier()

        if "merge" in stages:
            with contextlib.ExitStack() as es0:
                mT_all, b_mT = C.sb(es0, "mT_all", [128, 8, 2048], BF16)
                wst = [C.sb(es0, f"wst{i}", [128, 2048], F32) for i in range(3)]
                nw = 0
                with contextlib.ExitStack() as es:
                    Wb = [C.sb(es, f"Wbr{i}", [128, 8, 2048], BF16) for i in range(2)]
                    gob, b_gob = C.sb(es, "gob", [128, 128], F32)
                    gocol, b_gocol = C.sb(es, "gocol", [128, 1], F32)
                    S.dma("sp", gob[:], go_in[:, :], writes=[b_gob])
                    S.op("dve", lambda e: e.tensor_tensor(out=gob[:], in0=gob[:], in1=identf[:], op=ALU.mult),
                         reads=[b_gob, b_identf], writes=[b_gob])
                    S.op("dve", lambda e: e.tensor_reduce(out=gocol[:], in_=gob[:], axis=AX.X, op=ALU.add),
                         reads=[b_gob], writes=[b_gocol])
                    for br in range(2):
                        for k in range(8):
                            (ws_, bws) = wst[nw % 3]
                            nw += 1
                            S.dma(("sp", "act", "pool")[nw % 3], ws_[:], wbr_in[br, k * 128:(k + 1) * 128, :], writes=[bws])
                            if br == 1:
                                if k % 2:
                                    S.op("act", lambda e: e.activation(out=Wb[1][0][:, k, :], in_=ws_[:], func=AF.Copy,
                                                                       scale=gocol[:, 0:1]),
                                         reads=[bws, b_gocol], writes=[Wb[1][1]])
                                else:
                                    S.op("dve", lambda e: e.tensor_scalar(out=Wb[1][0][:, k, :], in0=ws_[:],
                                                                          scalar1=gocol[:, 0:1], scalar2=None,
                                                                          op0=ALU.mult),
                                         reads=[bws, b_gocol], writes=[Wb[1][1]])
                                continue
                            S.op("act" if k % 2 else "dve",
                                 (lambda e: e.copy(out=Wb[br][0][:, k, :], in_=ws_[:])) if k % 2 else
                                 (lambda e: e.tensor_copy(out=Wb[br][0][:, k, :], in_=ws_[:])),
                                 reads=[bws], writes=[Wb[br][1]])
                    yaTj, b_yaTj = C.sb(es, "yaTj", [128, 8, 128], BF16)
                    ybTj, b_ybTj = C.sb(es, "ybTj", [128, 8, 128], BF16)
                    zbTj, b_zbTj = C.sb(es, "zbTj", [128, 8, 128], BF16)
                    gts, b_gts = C.sb(es, "gts", [128, 4096], F32)
                    mg, b_mg = C.sb(es, "mg", [128, 2048], F32)
                    t2, b_t2 = C.sb(es, "t2", [128, 512], F32)
                    mgb, b_mgb = C.sb(es, "mgb", [128, 2048], BF16)
                    pP = [C.ps(es, f"pP{i}", [128, 512], F32) for i in range(4)]
                    pTm, b_pTm = C.ps(es, "pTm", [128, 2048], BF16)
                    npp = 0
                    for j in range(8):
                        s0 = 2 + 128 * j
                        S.dma("sp", yaTj[:], yaT_d[:, :, j * 128:(j + 1) * 128], reads=[B_yaT], writes=[b_yaTj])
                        S.dma("sp", ybTj[:], ybT_d[:, :, s0:s0 + 128], reads=[B_ybT], writes=[b_ybTj])
                        S.dma("sp", zbTj[:], zbT_d[:, :, s0:s0 + 128], reads=[B_zbT], writes=[b_zbTj])
                        S.dma("sp", gts[:], P_own[s0:s0 + 128, 4112:8208], reads=[B_P_own], writes=[b_gts])
                        S.op("dve", lambda e: e.tensor_tensor(out=ybTj[:], in0=ybTj[:], in1=zbTj[:], op=ALU.mult),
                             reads=[b_ybTj, b_zbTj], writes=[b_ybTj])
                        S.op("act", lambda e: e.activation(out=gts[:], in_=gts[:], func=AF.Sigmoid), reads=[b_gts],
                             writes=[b_gts])
                        for cb in range(4):
                            cs = slice(cb * 512, (cb + 1) * 512)
                            for br, (yT_, byT) in enumerate(((yaTj, b_yaTj), (ybTj, b_ybTj))):
                                (pp_, bpp) = pP[npp % 4]
                                npp += 1
                                for k in range(8):
                                    S.op("pe", lambda e: e.matmul(pp_[:], lhsT=yT_[:, k, :], rhs=Wb[br][0][:, k, cs],
                                                                  start=(k == 0), stop=(k == 7)),
                                         reads=[byT, Wb[br][1]], writes=[bpp])
                                if br == 0:
                                    S.op("dve", lambda e: e.tensor_tensor(out=mg[:, cs], in0=pp_[:], in1=gts[:, cs],
                                                                          op=ALU.mult), reads=[bpp, b_gts], writes=[b_mg])
                                else:
                                    S.op("dve", lambda e: e.tensor_tensor(
                                        out=t2[:], in0=pp_[:], in1=gts[:, 2048 + cb * 512:2048 + (cb + 1) * 512],
                                        op=ALU.mult), reads=[bpp, b_gts], writes=[b_t2])
                                    S.op("pool", lambda e: e.tensor_tensor(out=mgb[:, cs], in0=mg[:, cs], in1=t2[:],
                                                                           op=ALU.add), reads=[b_mg, b_t2], writes=[b_mgb])
                        for k in range(16):
                            S.op("pe", lambda e: e.transpose(out=pTm[:, k * 128:(k + 1) * 128],
                                                             in_=mgb[:, k * 128:(k + 1) * 128], identity=ident[:]),
                                 reads=[b_mgb, b_ident], writes=[b_pTm])
                        S.op("act", lambda e: e.copy(out=mT_all[:, j, 0:1024], in_=pTm[:, 0:1024]), reads=[b_pTm],
                             writes=[b_mT])
                        S.op("dve", lambda e: e.tensor_copy(out=mT_all[:, j, 1024:2048], in_=pTm[:, 1024:2048]),
                             reads=[b_pTm], writes=[b_mT])
                    S.barrier()
                with contextlib.ExitStack() as es:
                    Wo, b_Wo = C.sb(es, "Wo", [128, 16, 2048], BF16)
                    for k in range(16):
                        (ws_, bws) = wst[nw % 3]
                        nw += 1
                        S.dma(("sp", "act", "pool")[nw % 3], ws_[:], wout_in[k * 128:(k + 1) * 128, :], writes=[bws])
                        S.op("act" if k % 2 else "dve",
                             (lambda e: e.copy(out=Wo[:, k, :], in_=ws_[:])) if k % 2 else
                             (lambda e: e.tensor_copy(out=Wo[:, k, :], in_=ws_[:])),
                             reads=[bws], writes=[b_Wo])
                    xo = [C.sb(es, f"xo{i}", [128, 2048], F32) for i in range(2)]
                    pP = [C.ps(es, f"pP2{i}", [128, 512], F32) for i in range(4)]
                    npp = 0
                    for j in range(8):
                        (xo_, bxo) = xo[j % 2]
                        S.dma("sp", xo_[:], x_own[j * 128:(j + 1) * 128, :], writes=[bxo])
                        for cb in range(4):
                            cs = slice(cb * 512, (cb + 1) * 512)
                            (pp_, bpp) = pP[npp % 4]
                            npp += 1
                            for k in range(16):
                                S.op("pe", lambda e: e.matmul(pp_[:], lhsT=mT_all[:, j, k * 128:(k + 1) * 128],
                                                              rhs=Wo[:, k, cs], start=(k == 0), stop=(k == 15)),
                                     reads=[b_mT, b_Wo], writes=[bpp])
                            S.op("dve", lambda e: e.tensor_tensor(out=xo_[:, cs], in0=xo_[:, cs], in1=pp_[:], op=ALU.add),
                                 reads=[bxo, bpp], writes=[bxo])
                        S.dma("pool", out_d[j * 128:(j + 1) * 128, :], xo_[:], reads=[bxo], writes=[B_out])
                    S.barrier()

        S.barrier(engines=("sp",))
    return nc


def host_inputs(x, meta_tokens, hgrn_lb_logits, norm_g, w_in, q_norm_g, k_norm_g, idx_k_norm_g,
                hgrn_out_norm_g, w_branch, w_out):
    x = np.asarray(x, np.float32)
    h_all = np.zeros((TP, D), np.float32)
    h_all[:NMETA] = np.asarray(meta_tokens, np.float32)
    h_all[NMETA:NMETA + SEQ] = x[0]
    common = {
        "h_all": h_all,
        "w_in": np.ascontiguousarray(np.asarray(w_in, np.float32)[0]),
        "norm_g": np.ascontiguousarray(np.asarray(norm_g, np.float32)[0]),
    }
    f32 = np.float32
    tile = lambda v, n: np.ascontiguousarray(np.tile(np.asarray(v, f32).reshape(1, -1), (n, 1)))
    common["gq_bc"] = tile(q_norm_g[0], 128)
    common["gk_bc"] = tile(k_norm_g[0], 128)
    common["gi_bc"] = tile(idx_k_norm_g[0], 128)
    common["go_bc"] = tile(hgrn_out_norm_g[0], 128)
    common["lb0"] = tile(np.asarray(hgrn_lb_logits)[0], 128)
    common["lb1"] = tile(np.asarray(hgrn_lb_logits)[1], 128)
    common["w_branch"] = np.ascontiguousarray(np.asarray(w_branch, f32)[0])
    common["w_out"] = np.ascontiguousarray(np.asarray(w_out, f32)[0])

    def rope_tab(pos, rot, heads):
        inv = np.power(np.float32(500000.0), -np.arange(0, rot, 2, dtype=f32) / np.float32(rot)).astype(f32)
        ang = pos.astype(f32)[:, None] * inv[None, :]
        cos, sin = np.cos(ang).astype(f32), np.sin(ang).astype(f32)
        cs = np.stack([np.repeat(cos[:, None, :], heads, 1), np.repeat(sin[:, None, :], heads, 1)], 1)
        return np.ascontiguousarray(cs.reshape(len(pos), -1))
    pos_all = np.arange(TP)
    common["ropeK"] = rope_tab(pos_all, 32, 4)
    common["ropeKI"] = rope_tab(pos_all, 16, 1)
    si, ti = np.arange(128)[:, None], np.arange(128)[None, :]
    same = (si // 64) == (ti // 64)
    common["U"] = (same & (si <= ti)).astype(f32)
    common["Wm"] = (same & (si > ti)).astype(f32)
    common["chi"] = (np.arange(128)[:, None] // 64 == np.arange(2)[None, :]).astype(f32)
    m_, p_ = np.arange(128)[:, None], np.arange(1024)[None, :]
    common["AM"] = np.where((p_ // 64) <= (m_ // 8), 0.0, -1e9).astype(f32)
    maps = []
    for c in range(8):
        sel = np.zeros((128, 16), np.float32)
        sel[c + 8 * np.arange(16), np.arange(16)] = 1.0
        m = dict(common)
        m["sel"] = sel
        m["x_own"] = np.ascontiguousarray(x[0, c::8])
        slot = np.arange(NOWN)
        pos_own = np.where(slot < 1040, 8 * slot + c, 0)
        m["ropeQ"] = rope_tab(pos_own, 32, 8)
        m["ropeQI"] = rope_tab(pos_own, 16, 16)
        maps.append(m)
    return maps


def kernel(**inputs):
    maps = host_inputs(**inputs)
    nc = build_nc()
    res = run_bass_kernel_spmd(nc, maps, core_ids=list(range(8)))
    out = np.zeros((1, SEQ, D), np.float32)
    for c in range(8):
        out[0, c::8] = res.results[c]["out"]
    return out
```
